# Optimizing a Trainium2 kernel written in Bass

```python
import math
import jax, jax.numpy as jnp
from jax import lax
import numpy as np

D_MODEL = 1024
BATCH = 8
SEQ = 4096
DEPTH = 1

N_META = 16
SSD_HEAD_DIM = 64
SSD_INNER = D_MODEL
SSD_HEADS = SSD_INNER // SSD_HEAD_DIM
SSD_GROUPS = 2
SSD_STATE = 128
SSD_CONV = 4
CHUNK = 128
SB_HEAD_DIM = 64
SB_WIDTH = D_MODEL
SB_HEADS = SB_WIDTH // SB_HEAD_DIM
Q_BLOCK = 128
MIX_WIDTH = SSD_INNER + SB_WIDTH
D_FF = 256 * ((8 * D_MODEL // 3 + 255) // 256)
FFN_CONV = 3
EPS = 1e-6

XBC_WIDTH = SSD_INNER + 2 * SSD_GROUPS * SSD_STATE
OFF_Z = 0
OFF_XBC = OFF_Z + SSD_INNER
OFF_DT = OFF_XBC + XBC_WIDTH
OFF_Q = OFF_DT + SSD_HEADS
OFF_K = OFF_Q + SB_WIDTH
OFF_V = OFF_K + SB_WIDTH
IN_COLS = OFF_V + SB_WIDTH

kernel_name = "hymba_ssd_stickbreaking_convffn_layer"


def rms_norm(x, g):
    x32 = x.astype(jnp.float32)
    y = x32 * lax.rsqrt(jnp.mean(x32 * x32, axis=-1, keepdims=True) + EPS)
    return (y * g.astype(jnp.float32)).astype(x.dtype)


def causal_dwconv(x, w, b):
    K = w.shape[0]
    L = x.shape[1]
    xp = jnp.pad(x, ((0, 0), (K - 1, 0), (0, 0)))
    y = b
    for k in range(K):
        y = y + xp[:, k:k + L] * w[k]
    return y


def ssd_mixer(z, xbc, dt_raw, conv_w, conv_b, dt_bias, a_log, d_skip, norm_g):
    out_dtype = z.dtype
    Bsz, L, _ = xbc.shape
    H, P, G, N = SSD_HEADS, SSD_HEAD_DIM, SSD_GROUPS, SSD_STATE
    J = H // G
    f32 = jnp.float32
    xbc = jax.nn.silu(causal_dwconv(xbc, conv_w, conv_b)).astype(f32)
    xs = xbc[..., :SSD_INNER].reshape(Bsz, L, H, P)
    Bm = xbc[..., SSD_INNER:SSD_INNER + G * N].reshape(Bsz, L, G, N)
    Cm = xbc[..., SSD_INNER + G * N:].reshape(Bsz, L, G, N)
    dt = jax.nn.softplus(dt_raw.astype(f32) + dt_bias.astype(f32))
    A = -jnp.exp(a_log.astype(f32))

    pad = CHUNK - N_META
    Lp = L + pad
    nc = Lp // CHUNK

    def front_pad(t):
        return jnp.pad(t, ((0, 0), (pad, 0)) + ((0, 0),) * (t.ndim - 2))

    Xdt = front_pad(xs * dt[..., None]).reshape(Bsz, nc, CHUNK, G, J, P)
    Adt = front_pad(dt * A).reshape(Bsz, nc, CHUNK, G, J).transpose(0, 3, 4, 1, 2)
    Bc = front_pad(Bm).reshape(Bsz, nc, CHUNK, G, N)
    Cc = front_pad(Cm).reshape(Bsz, nc, CHUNK, G, N)

    Acs = jnp.cumsum(Adt, axis=-1)
    causal = jnp.tril(jnp.ones((CHUNK, CHUNK), dtype=bool))
    seg = Acs[..., :, None] - Acs[..., None, :]
    Ldec = jnp.where(causal, jnp.exp(jnp.where(causal, seg, 0.0)), 0.0)

    CB = jnp.einsum('bclgn,bcsgn->bgcls', Cc, Bc)
    y_diag = jnp.einsum('bgcls,bgjcls,bcsgjp->bclgjp', CB, Ldec, Xdt)

    decay_states = jnp.exp(Acs[..., -1:] - Acs)
    states = jnp.einsum('bclgn,bgjcl,bclgjp->bcgjpn', Bc, decay_states, Xdt)
    chunk_decay = jnp.exp(Acs[..., -1])

    def step(carry, inp):
        st, dec = inp
        return carry * dec[..., None, None] + st, carry

    init = jnp.zeros((Bsz, G, J, P, N), f32)
    _, prev = lax.scan(step, init, (states.transpose(1, 0, 2, 3, 4, 5),
                                    chunk_decay.transpose(3, 0, 1, 2)))
    prev = prev.transpose(1, 0, 2, 3, 4, 5)

    y_off = jnp.einsum('bclgn,bcgjpn,bgjcl->bclgjp', Cc, prev, jnp.exp(Acs))

    y = (y_diag + y_off).reshape(Bsz, Lp, H, P)[:, pad:]
    y = y + xs * d_skip.astype(f32)[:, None]
    y = y.reshape(Bsz, L, SSD_INNER) * jax.nn.silu(z.astype(f32))
    return rms_norm(y, norm_g).astype(out_dtype)


def stick_breaking_attention(q, k, v):
    out_dtype = q.dtype
    Bsz, L, H, D = q.shape
    f32 = jnp.float32
    scale = 1.0 / math.sqrt(D)
    pad = Q_BLOCK - N_META
    Lp = L + pad
    nb = Lp // Q_BLOCK
    padw = ((0, 0), (pad, 0), (0, 0), (0, 0))
    qp = jnp.pad(q.astype(f32), padw)
    kp = jnp.pad(k.astype(f32), padw)
    vp = jnp.pad(v.astype(f32), padw)
    qb = qp.reshape(Bsz, nb, Q_BLOCK, H, D).transpose(1, 0, 2, 3, 4)
    key_pos = jnp.arange(Lp)

    def block(args):
        i, qi = args
        q_pos = i * Q_BLOCK + jnp.arange(Q_BLOCK)
        valid = (key_pos[None, :] < q_pos[:, None]) & (key_pos[None, :] >= pad)
        zlog = jnp.einsum('bqhd,bkhd->bhqk', qi, kp) * scale
        log_beta = jax.nn.log_sigmoid(zlog)
        log_keep = jnp.where(valid, log_beta - zlog, 0.0)
        after = lax.cumsum(log_keep, axis=3, reverse=True) - log_keep
        w = jnp.where(valid, jnp.exp(log_beta + after), 0.0)
        return jnp.einsum('bhqk,bkhd->bqhd', w, vp)

    o = lax.map(block, (jnp.arange(nb), qb))
    o = o.transpose(1, 0, 2, 3, 4).reshape(Bsz, Lp, H, D)[:, pad:]
    return o.astype(out_dtype)


def setup_inputs(seed: int = 0) -> dict:
    key = jax.random.key(seed)
    ks = jax.random.split(key, 20)
    f32 = jnp.float32

    def gain(k, n):
        return 1.0 + 0.05 * jax.random.normal(k, (DEPTH, n), f32)

    dt0 = jnp.exp(jax.random.uniform(ks[5], (DEPTH, SSD_HEADS), f32,
                                     math.log(1e-3), math.log(1e-1)))
    dt_bias = dt0 + jnp.log(-jnp.expm1(-dt0))
    return {
        "x": jax.random.normal(ks[0], (BATCH, SEQ, D_MODEL), f32),
        "meta_tokens": jax.random.normal(ks[1], (N_META, D_MODEL), f32),
        "mix_pre_g": gain(ks[2], D_MODEL),
        "w_in": jax.random.normal(ks[3], (DEPTH, D_MODEL, IN_COLS), f32) * D_MODEL ** -0.5,
        "ssd_conv_w": jax.random.normal(ks[4], (DEPTH, SSD_CONV, XBC_WIDTH), f32) * SSD_CONV ** -0.5,
        "ssd_conv_b": 0.01 * jax.random.normal(ks[6], (DEPTH, XBC_WIDTH), f32),
        "ssd_dt_bias": dt_bias,
        "ssd_a_log": jnp.log(jax.random.uniform(ks[7], (DEPTH, SSD_HEADS), f32, 1.0, 16.0)),
        "ssd_d": 1.0 + 0.1 * jax.random.normal(ks[8], (DEPTH, SSD_HEADS), f32),
        "ssd_norm_g": gain(ks[9], SSD_INNER),
        "sb_norm_g": gain(ks[10], SB_WIDTH),
        "w_out": jax.random.normal(ks[11], (DEPTH, MIX_WIDTH, D_MODEL), f32) * MIX_WIDTH ** -0.5,
        "mix_post_g": gain(ks[12], D_MODEL),
        "ffn_pre_g": gain(ks[13], D_MODEL),
        "w_up": jax.random.normal(ks[14], (DEPTH, D_MODEL, 2 * D_FF), f32) * D_MODEL ** -0.5,
        "ffn_conv_w": jax.random.normal(ks[15], (DEPTH, FFN_CONV, D_FF), f32) * FFN_CONV ** -0.5,
        "ffn_conv_b": 0.01 * jax.random.normal(ks[16], (DEPTH, D_FF), f32),
        "w_down": jax.random.normal(ks[17], (DEPTH, D_FF, D_MODEL), f32) * D_FF ** -0.5,
        "ffn_post_g": gain(ks[18], D_MODEL),
    }


def reference(x, meta_tokens, mix_pre_g, w_in, ssd_conv_w, ssd_conv_b, ssd_dt_bias,
              ssd_a_log, ssd_d, ssd_norm_g, sb_norm_g, w_out, mix_post_g, ffn_pre_g,
              w_up, ffn_conv_w, ffn_conv_b, w_down, ffn_post_g):
    Bsz = x.shape[0]
    meta = jnp.broadcast_to(meta_tokens.astype(x.dtype)[None], (Bsz, N_META, x.shape[-1]))
    h = jnp.concatenate([meta, x], axis=1)
    L = h.shape[1]
    for l in range(DEPTH):
        xn = rms_norm(h, mix_pre_g[l])
        proj = xn @ w_in[l]
        z = proj[..., OFF_Z:OFF_XBC]
        xbc = proj[..., OFF_XBC:OFF_DT]
        dt_raw = proj[..., OFF_DT:OFF_Q]
        q = proj[..., OFF_Q:OFF_K].reshape(Bsz, L, SB_HEADS, SB_HEAD_DIM)
        k = proj[..., OFF_K:OFF_V].reshape(Bsz, L, SB_HEADS, SB_HEAD_DIM)
        v = proj[..., OFF_V:IN_COLS].reshape(Bsz, L, SB_HEADS, SB_HEAD_DIM)
        y_ssd = ssd_mixer(z, xbc, dt_raw, ssd_conv_w[l], ssd_conv_b[l], ssd_dt_bias[l],
                          ssd_a_log[l], ssd_d[l], ssd_norm_g[l])
        y_sb = rms_norm(stick_breaking_attention(q, k, v).reshape(Bsz, L, SB_WIDTH), sb_norm_g[l])
        mix = jnp.concatenate([y_ssd, y_sb], axis=-1) @ w_out[l]
        h = h + rms_norm(mix, mix_post_g[l])
        xn = rms_norm(h, ffn_pre_g[l])
        gu = xn @ w_up[l]
        g = causal_dwconv(gu[..., :D_FF], ffn_conv_w[l], ffn_conv_b[l])
        f = (jax.nn.gelu(g, approximate=True) * gu[..., D_FF:]) @ w_down[l]
        h = h + rms_norm(f, ffn_post_g[l])
    return h[:, N_META:]
```

```python
import numpy as np
from contextlib import ExitStack
import concourse.bass as bass
import concourse.mybir as mybir
from concourse.bass_utils import run_bass_kernel_spmd

F32 = mybir.dt.float32
BF16 = mybir.dt.bfloat16
AF = mybir.ActivationFunctionType
ALU = mybir.AluOpType

D = 1024
SEQ = 4096
NMETA = 16
NT = 33
LP = NT * 128
PAD = 112
H = 16
DFF = 2816
NFC = 22
EPS = 1e-6
OFF_Z, OFF_XBC, OFF_DT, OFF_Q, OFF_K, OFF_V = 0, 1024, 2560, 2576, 3600, 4624

BLK_Z = 0
BLK_XBC = 8
BLK_Q = 20
BLK_K = 28
BLK_V = 36
BLK_OUT = 44
BLK_GATE = 60
BLK_UP = 82
BLK_DOWN = 104
NBLK = 126

ENGS = ['pe', 'act', 'dve', 'pool', 'sp']
NDSEM = 8


class _Op:
    __slots__ = ('eng', 'fn', 'deps', 'is_dma', 'needed', 'val', 'sem', 'idx')


class Sched:
    def __init__(self):
        self.ops = {e: [] for e in ENGS}
        self.last_w = {}
        self.readers = {}
        self.seen_c = {e: {p: -1 for p in ENGS} for e in ENGS}
        self.seen_d = {e: set() for e in ENGS}
        self.ndma = {e: 0 for e in ENGS}
        self.dma_ops = {e: [] for e in ENGS}

    def add(self, eng, fn, reads=(), writes=(), dma=False, extra=()):
        op = _Op()
        op.eng = eng
        op.fn = fn
        op.is_dma = dma
        op.needed = False
        op.val = None
        op.sem = None
        op.idx = len(self.ops[eng])
        deps = list(extra)
        for k in reads:
            w = self.last_w.get(k)
            if w is not None:
                deps.append(w)
        for k in writes:
            w = self.last_w.get(k)
            if w is not None:
                deps.append(w)
            deps.extend(self.readers.get(k, ()))
        if dma:
            n = self.ndma[eng]
            if n >= NDSEM:
                deps.append(self.dma_ops[eng][n - NDSEM])
            op.sem = ('d', eng, n % NDSEM)
            op.val = 16 * (n // NDSEM + 1)
            self.ndma[eng] = n + 1
            self.dma_ops[eng].append(op)
        cdeps = {}
        ddeps = []
        for d in deps:
            if d is op:
                continue
            if d.is_dma:
                key = (d.eng, d.sem, d.val)
                if key in self.seen_d[eng]:
                    continue
                self.seen_d[eng].add(key)
                ddeps.append(d)
            else:
                if d.eng == eng and eng == 'pe':
                    continue
                if d.idx <= self.seen_c[eng][d.eng]:
                    continue
                if d.eng not in cdeps or cdeps[d.eng].idx < d.idx:
                    cdeps[d.eng] = d
        for p, d in cdeps.items():
            self.seen_c[eng][p] = d.idx
            d.needed = True
        op.deps = list(cdeps.values()) + ddeps
        self.ops[eng].append(op)
        for k in writes:
            self.last_w[k] = op
            self.readers[k] = []
        for k in reads:
            if k in writes:
                continue
            self.readers.setdefault(k, []).append(op)
        return op

    def emit(self, nc, es):
        sem_c = {e: es.enter_context(nc.semaphore('sc_' + e)) for e in ENGS}
        sem_d = {}
        for e in ENGS:
            for i in range(min(NDSEM, self.ndma[e])):
                sem_d[('d', e, i)] = es.enter_context(nc.semaphore('sd_%s_%d' % (e, i)))
        for e in ENGS:
            c = 0
            for op in self.ops[e]:
                if op.is_dma:
                    continue
                if op.needed:
                    c += 1
                    op.val = c
        block = es.enter_context(nc.Block())
        secs = {'pe': block.tensor, 'act': block.scalar, 'dve': block.vector,
                'pool': block.gpsimd, 'sp': block.sync}

        def mk(e):
            def body(eng):
                for op in self.ops[e]:
                    for d in op.deps:
                        if d.is_dma:
                            eng.wait_ge(sem_d[d.sem], d.val)
                        else:
                            eng.wait_ge(sem_c[d.eng], d.val)
                    inst = op.fn(eng)
                    if op.is_dma:
                        inst.then_inc(sem_d[op.sem], 16)
                    elif op.needed:
                        inst.then_inc(sem_c[e], 1)
            return body

        for e in ENGS:
            if self.ops[e]:
                secs[e](mk(e))


def build_nc(stage=99, debug=False):
    nc = bass.Bass("TRN2", target_bir_lowering=False)
    es = ExitStack()

    def din(name, shape):
        return nc.dram_tensor(name, list(shape), F32, kind="ExternalInput").ap()

    x_d = din("x", [SEQ, D])
    meta_d = din("meta", [NMETA, D])
    wblk_d = din("wblk", [NBLK * 128, 1024])
    wdt_d = din("wdt", [128, 8 * 16])
    vecs_d = din("vecs", [8, D])
    pp_d = din("ppd", [128, 128])
    ffnp_d = din("ffnpd", [128, 128])
    small_d = din("small", [1, 64])
    out_d = nc.dram_tensor("out", [SEQ, D], F32, kind="ExternalOutput").ap()
    wscr = nc.dram_tensor("wscr", [NBLK * 128, 1024], BF16, kind="Internal").ap()
    dbg = {}
    if debug:
        dbg['xnT'] = nc.dram_tensor("dbg_xnT", [128, 8 * LP], BF16, kind="ExternalOutput").ap()
        dbg['ysbT'] = nc.dram_tensor("dbg_ysbT", [128, 8 * LP], BF16, kind="ExternalOutput").ap()

    S = Sched()
    with es:
        es2 = ExitStack()

        def sb(name, shape, dt=F32):
            return es.enter_context(nc.sbuf_tensor(name, list(shape), dt))

        def sb2(name, shape, dt=F32):
            return es2.enter_context(nc.sbuf_tensor(name, list(shape), dt))

        ps = es.enter_context(nc.psum_tensor("ps", [128, 4096], F32))

        def bank(b, n=1):
            return ps[:, 512 * b:512 * (b + n)]

        def bkeys(b, n=1):
            return [('ps', b + i) for i in range(n)]

        def dbgdump(name, ap, shape, dt, reads):
            if not debug:
                return
            t = nc.dram_tensor("dbg_" + name, list(shape), dt, kind="ExternalOutput").ap()
            S.add('sp', lambda e: e.dma_start(out=t, in_=ap), reads=reads, dma=True)

        ident_b = sb("ident_b", [128, 128], BF16)
        tri_b = sb("tri_b", [128, 128], BF16)
        tric_b = sb("tric_b", [128, 128], BF16)
        mdiag_b = sb("mdiag_b", [128, 128], BF16)
        padm = sb("padm", [128, 1], F32)
        mhalf = sb("mhalf", [128, 1], F32)
        pp = sb("pp", [128, 128], F32)
        junk = sb("junk", [128, D], BF16)

        def cmask(t, pattern_mult, chan_mult, op, base=0):
            S.add('pool', lambda e: e.memset(t[:], 1.0), writes=[t.name])
            S.add('pool', lambda e: e.affine_select(
                out=t[:], in_=t[:], pattern=[[pattern_mult, t.shape[1]]], compare_op=op,
                fill=0.0, base=base, channel_multiplier=chan_mult), reads=[t.name], writes=[t.name])

        cmask(ident_b, -1, 1, ALU.is_equal)
        cmask(tri_b, -1, 1, ALU.is_ge)
        cmask(tric_b, 1, -1, ALU.is_gt)
        cmask(mdiag_b, 1, -1, ALU.is_gt)
        cmask(padm, 0, 1, ALU.is_ge, base=-PAD)
        S.add('pool', lambda e: e.memset(mhalf[:], -0.5), writes=['mhalf'])
        S.add('sp', lambda e: e.dma_start(out=pp[:], in_=pp_d[:, :]), writes=['pp'], dma=True)

        def prep_blocks(b0, nb):
            S.add('pool', lambda e: e.dma_start(out=wscr[b0 * 128:(b0 + nb) * 128, :],
                                                in_=wblk_d[b0 * 128:(b0 + nb) * 128, :]),
                  writes=[('wscr', b) for b in range(b0, b0 + nb)], dma=True)

        for b0 in (BLK_Q, BLK_K, BLK_V):
            prep_blocks(b0, 8)
        for b0 in range(0, 20, 4):
            prep_blocks(b0, 4)
        for b0 in range(BLK_OUT, NBLK, 4):
            prep_blocks(b0, min(4, NBLK - b0))

        ysbT = sb("ysbT", [128, 8, LP], BF16)
        xnT = sb2("xnT", [128, 8, LP], BF16)
        gpre_bc = sb2("gpre_bc", [128, D], F32)

        S.add('sp', lambda e: e.dma_start(out=gpre_bc[:], in_=vecs_d[0:1, :].partition_broadcast(128)),
              writes=['gpre_bc'], dma=True)
        xt = [sb2("xt%d" % i, [128, D], F32) for i in range(2)]
        xnb = [sb2("xnb%d" % i, [128, D], BF16) for i in range(2)]
        st1 = sb2("st1", [128, NT * 4], F32)

        def load_x_tile(t, buf, key):
            if t == 0:
                S.add('pool', lambda e: e.memset(buf[:], 0.0), writes=[key])
                S.add('sp', lambda e: e.dma_start(out=buf[PAD:128, :], in_=meta_d[:, :]),
                      writes=[key], dma=True)
            else:
                S.add('sp', lambda e: e.dma_start(out=buf[:], in_=x_d[(t - 1) * 128:t * 128, :]),
                      writes=[key], dma=True)

        def rstd_from(src, srckey, col, stt, sttname, junkbuf, junkkey):
            S.add('act', lambda e: e.activation(out=junkbuf[:], in_=src, func=AF.Square,
                                                accum_out=stt[:, col:col + 1]),
                  reads=[srckey], writes=[(sttname, col)])
            S.add('dve', lambda e: e.tensor_scalar(out=stt[:, col + 1:col + 2], in0=stt[:, col:col + 1],
                                                   scalar1=1.0 / D, scalar2=EPS, op0=ALU.mult, op1=ALU.add),
                  reads=[(sttname, col)], writes=[(sttname, col + 1)])
            S.add('pool', lambda e: e.tensor_tensor(out=stt[:, col + 2:col + 3], in0=stt[:, col + 1:col + 2],
                                                    in1=mhalf[:], op=ALU.pow),
                  reads=[(sttname, col + 1), 'mhalf'], writes=[(sttname, col + 2)])

        for t in range(NT):
            i = t % 2
            load_x_tile(t, xt[i], ('xt', i))
            rstd_from(xt[i][:], ('xt', i), 4 * t, st1, 'st1', junk, 'junk')
            S.add('dve', lambda e, i=i, t=t: e.scalar_tensor_tensor(
                out=xnb[i][:], in0=xt[i][:], scalar=st1[:, 4 * t + 2:4 * t + 3], in1=gpre_bc[:],
                op0=ALU.mult, op1=ALU.mult),
                reads=[('xt', i), ('st1', 4 * t + 2), 'gpre_bc'], writes=[('xnb', i)])
            pb = 4 + (t % 2)
            pbv = bank(pb).bitcast(BF16)
            for k in range(8):
                S.add('pe', lambda e, i=i, k=k, pbv=pbv: e.transpose(
                    out=pbv[:, k * 128:(k + 1) * 128], in_=xnb[i][:, k * 128:(k + 1) * 128], identity=ident_b[:]),
                    reads=[('xnb', i), 'ident_b'], writes=bkeys(pb))
            S.add('act', lambda e, t=t, pbv=pbv: e.activation(
                out=xnT[:, :, t * 128:(t + 1) * 128], in_=pbv.rearrange("p (k c) -> p k c", k=8), func=AF.Copy),
                reads=bkeys(pb), writes=[('xnT', t)] + bkeys(pb))

        if debug:
            S.add('sp', lambda e: e.dma_start(out=dbg['xnT'], in_=xnT[:].rearrange("p k c -> p (k c)")),
                  reads=[('xnT', t) for t in range(NT)], dma=True)

        if stage >= 2:
            wq = sb2("wq", [128, 8, 128], BF16)
            wk = sb2("wk", [128, 8, 128], BF16)
            wv = sb2("wv", [128, 8, 128], BF16)
            qT = sb2("qT", [128, LP], BF16)
            kT = sb2("kT", [128, LP], BF16)
            vv = sb2("vv", [128, NT, 128], BF16)
            eb = [sb2("eb%d" % i, [128, 2, 512], BF16) for i in range(4)]
            spb = [sb2("spb%d" % i, [128, 2, 512], BF16) for i in range(3)]
            gb = [sb2("gb%d" % i, [128, 2, 512], BF16) for i in range(2)]
            wb = [sb2("wb%d" % i, [128, 2, 512], BF16) for i in range(2)]
            allx = [('xnT', t) for t in range(NT)]

            def load_wblk(dst, key, blk):
                S.add('sp', lambda e: e.dma_start(
                    out=dst[:].rearrange("p k c -> p (k c)"), in_=wscr[blk * 128:(blk + 1) * 128, :]),
                    reads=[('wscr', blk)], writes=[key], dma=True)

            gstep = 0
            for hp in range(8):
                load_wblk(wq, 'wq', BLK_Q + hp)
                load_wblk(wk, 'wk', BLK_K + hp)
                load_wblk(wv, 'wv', BLK_V + hp)
                pcnt = 0
                for (dst, dkey, wt, wkey, scale) in ((qT, 'qT', wq, 'wq', 0.125), (kT, 'kT', wk, 'wk', 1.0)):
                    for tb in range(9):
                        c0 = tb * 512
                        n = min(512, LP - c0)
                        b = pcnt % 4
                        pcnt += 1
                        for k in range(8):
                            S.add('pe', lambda e, b=b, n=n, wt=wt, k=k, c0=c0: e.matmul(
                                bank(b)[:, 0:n], lhsT=wt[:, k, :], rhs=xnT[:, k, c0:c0 + n],
                                start=(k == 0), stop=(k == 7)),
                                reads=[wkey] + allx[c0 // 128:(c0 + n) // 128], writes=bkeys(b))
                        S.add('act', lambda e, b=b, n=n, dst=dst, c0=c0, scale=scale: e.activation(
                            out=dst[:, c0:c0 + n], in_=bank(b)[:, 0:n], func=AF.Copy, scale=scale),
                            reads=bkeys(b), writes=[(dkey, j) for j in range(c0 // 128, (c0 + n) // 128)] + bkeys(b))
                for t0 in range(0, NT, 4):
                    nt = min(4, NT - t0)
                    b = pcnt % 4
                    pcnt += 1
                    for j in range(nt):
                        t = t0 + j
                        for k in range(8):
                            S.add('pe', lambda e, b=b, j=j, t=t, k=k: e.matmul(
                                bank(b)[:, j * 128:(j + 1) * 128], lhsT=xnT[:, k, t * 128:(t + 1) * 128],
                                rhs=wv[:, k, :], start=(k == 0), stop=(k == 7)),
                                reads=['wv', ('xnT', t)], writes=bkeys(b))
                    S.add('dve', lambda e, b=b, nt=nt, t0=t0: e.tensor_copy(
                        out=vv[:, t0:t0 + nt, :], in_=bank(b)[:, 0:nt * 128].rearrange("p (t c) -> p t c", t=nt)),
                        reads=bkeys(b), writes=[('vv', t) for t in range(t0, t0 + nt)] + bkeys(b))

                steps = []
                for qg in range(9):
                    qb0 = qg * 4
                    nq = min(4, NT - qb0)
                    qb1 = qb0 + nq - 1
                    for kb in range(qb1, -1, -1):
                        steps.append(dict(qg=qg, kb=kb, qb0=qb0, qb1=qb1, NQ=nq * 128, qc0=qb0 * 128,
                                          ob=6 + (qg % 2), co=max(0, kb - qb0) * 128,
                                          first=(kb == qb1), last=(kb == 0)))
                cb = 4
                c3 = bank(cb, 2).rearrange("p (h c) -> p h c", h=2)

                def fZ(st, gs):
                    co, NQ, kb, qc0 = st['co'], st['NQ'], st['kb'], st['qc0']
                    zs = (gs % 2) * 2
                    z3 = bank(zs, 2).rearrange("p (h c) -> p h c", h=2)
                    for h in range(2):
                        r0 = 64 * h
                        S.add('pe', lambda e, h=h, r0=r0, z3=z3: e.matmul(
                            z3[:, h, co:NQ], lhsT=kT[r0:r0 + 64, kb * 128:(kb + 1) * 128],
                            rhs=qT[r0:r0 + 64, qc0 + co:qc0 + NQ], start=True, stop=True),
                            reads=[('kT', kb)] + [('qT', j) for j in range(st['qb0'] + co // 128, st['qb1'] + 1)],
                            writes=bkeys(zs + h))

                def fE(st, gs):
                    co, NQ, kb = st['co'], st['NQ'], st['kb']
                    zs = (gs % 2) * 2
                    i = gs % 4
                    z3 = bank(zs, 2).rearrange("p (h c) -> p h c", h=2)
                    S.add('act', lambda e: e.activation(
                        out=eb[i][:, :, co:NQ], in_=z3[:, :, co:NQ], func=AF.Exp),
                        reads=bkeys(zs, 2), writes=[('eb', i)] + bkeys(zs, 2))
                    if kb >= st['qb0']:
                        S.add('dve', lambda e: e.tensor_tensor(
                            out=eb[i][:, :, co:co + 128], in0=eb[i][:, :, co:co + 128],
                            in1=mdiag_b[:].unsqueeze(1).broadcast_to([128, 2, 128]), op=ALU.mult),
                            reads=[('eb', i), 'mdiag_b'], writes=[('eb', i)])
                    if kb == 0:
                        S.add('dve', lambda e: e.tensor_scalar(
                            out=eb[i][:, :, co:NQ], in0=eb[i][:, :, co:NQ], scalar1=padm[:, 0:1],
                            scalar2=None, op0=ALU.mult),
                            reads=[('eb', i), 'padm'], writes=[('eb', i)])

                def fL(st, gs):
                    co, NQ = st['co'], st['NQ']
                    i = gs % 3
                    ie = gs % 4
                    S.add('act', lambda e: e.activation(
                        out=spb[i][:, :, co:NQ], in_=eb[ie][:, :, co:NQ], func=AF.Ln, bias=1.0),
                        reads=[('eb', ie)], writes=[('spb', i)])

                def fT(st, gs):
                    co, NQ = st['co'], st['NQ']
                    i = gs % 3
                    for h in range(2):
                        S.add('pe', lambda e, h=h: e.matmul(
                            c3[:, h, co:NQ], lhsT=tri_b[:], rhs=spb[i][:, h, co:NQ],
                            start=st['first'], stop=False, skip_group_check=True),
                            reads=[('spb', i), 'tri_b'], writes=bkeys(cb + h))

                def fG(st, gs):
                    co, NQ = st['co'], st['NQ']
                    i = gs % 2
                    S.add('act', lambda e: e.activation(
                        out=gb[i][:, :, co:NQ], in_=c3[:, :, co:NQ], func=AF.Exp, scale=-1.0),
                        reads=bkeys(cb, 2), writes=[('gb', i)] + bkeys(cb, 2))

                def fT2(st, gs):
                    co, NQ = st['co'], st['NQ']
                    i = gs % 3
                    if st['last']:
                        return
                    for h in range(2):
                        S.add('pe', lambda e, h=h: e.matmul(
                            c3[:, h, co:NQ], lhsT=tric_b[:], rhs=spb[i][:, h, co:NQ],
                            start=False, stop=False, skip_group_check=True),
                            reads=[('spb', i), 'tric_b'], writes=bkeys(cb + h))

                def fW(st, gs):
                    co, NQ = st['co'], st['NQ']
                    i = gs % 2
                    S.add('dve', lambda e: e.tensor_tensor(
                        out=wb[i][:, :, co:NQ], in0=eb[gs % 4][:, :, co:NQ], in1=gb[i][:, :, co:NQ], op=ALU.mult),
                        reads=[('eb', gs % 4), ('gb', i)], writes=[('wb', i)])

                def fV(st, gs, hp=hp):
                    co, NQ, kb, ob, qc0 = st['co'], st['NQ'], st['kb'], st['ob'], st['qc0']
                    i = gs % 2
                    for h in range(2):
                        r0 = 64 * h
                        S.add('pe', lambda e, h=h, r0=r0: e.matmul(
                            bank(ob)[r0:r0 + 64, co:NQ], lhsT=vv[:, kb, r0:r0 + 64], rhs=wb[i][:, h, co:NQ],
                            start=st['first'], stop=False, skip_group_check=True),
                            reads=[('wb', i), ('vv', kb)], writes=bkeys(ob))
                    if st['last']:
                        S.add('dve', lambda e: e.tensor_copy(
                            out=ysbT[:, hp, qc0:qc0 + NQ], in_=bank(ob)[:, 0:NQ]),
                            reads=bkeys(ob), writes=[('ysbT', hp, st['qg'])] + bkeys(ob))

                n = len(steps)
                for i in range(-2, n + 2):
                    if 0 <= i - 1 < n:
                        fT(steps[i - 1], gstep + i - 1)
                    if 0 <= i + 2 < n:
                        fZ(steps[i + 2], gstep + i + 2)
                    if 0 <= i - 2 < n:
                        fV(steps[i - 2], gstep + i - 2)
                    if 0 <= i + 1 < n:
                        fE(steps[i + 1], gstep + i + 1)
                    if 0 <= i < n:
                        fL(steps[i], gstep + i)
                    if 0 <= i - 1 < n:
                        fG(steps[i - 1], gstep + i - 1)
                        fT2(steps[i - 1], gstep + i - 1)
                        fW(steps[i - 1], gstep + i - 1)
                gstep += n

            if debug:
                S.add('sp', lambda e: e.dma_start(out=dbg['ysbT'], in_=ysbT[:].rearrange("p k c -> p (k c)")),
                      reads=[('ysbT', hp, qg) for hp in range(8) for qg in range(9)], dma=True)

        def barrier():
            tails = {e: [op for op in S.ops[e] if not op.is_dma][-1:] for e in ENGS}
            dts = [op for q in ENGS for op in S.dma_ops[q][-NDSEM:]]
            for e in ENGS:
                ex = [o for p in ENGS if p != e for o in tails[p]] + dts
                S.add(e, lambda eng: eng.nop(), extra=ex)

        if stage >= 3:
            barrier()
            es2.close()
            NG = 384
            ident_f = sb("ident_f", [128, 128], F32)
            UTf = sb("UTf", [128, 128], F32)
            SGf = sb("SGf", [128, 128], F32)
            ONESf = sb("ONESf", [128, 128], F32)
            ones_b = sb("ones_b", [128, 128], BF16)
            cmask(ident_f, -1, 1, ALU.is_equal)
            cmask(UTf, 1, -1, ALU.is_ge)
            cmask(SGf, -1, 1, ALU.is_gt)
            S.add('pool', lambda e: e.memset(ONESf[:], 1.0), writes=['ONESf'])
            S.add('pool', lambda e: e.memset(ones_b[:], 1.0), writes=['ones_b'])
            ffnp = sb("ffnp", [128, 128], F32)
            S.add('sp', lambda e: e.dma_start(out=ffnp[:], in_=ffnp_d[:, :]), writes=['ffnp'], dma=True)
            smallbc = sb("smallbc", [128, 64], F32)
            S.add('sp', lambda e: e.dma_start(out=smallbc[:], in_=small_d[0:1, :].partition_broadcast(128)),
                  writes=['smallbc'], dma=True)
            Abc = sb("Abc", [128, 16], F32)
            S.add('act', lambda e: e.activation(out=Abc[:], in_=smallbc[:, 16:32], func=AF.Exp),
                  reads=['smallbc'], writes=['Abc'])
            S.add('dve', lambda e: e.tensor_scalar(out=Abc[:], in0=Abc[:], scalar1=-1.0, scalar2=None, op0=ALU.mult),
                  reads=['Abc'], writes=['Abc'])
            wdt_b = sb("wdt_b", [128, 8, 16], BF16)
            S.add('pool', lambda e: e.dma_start(out=wdt_b[:].rearrange("p k c -> p (k c)"), in_=wdt_d[:, :]),
                  writes=['wdt_b'], dma=True)
            gpost_bc = sb("gpost_bc", [128, D], F32)
            gfpost_bc = sb("gfpost_bc", [128, D], F32)
            S.add('sp', lambda e: e.dma_start(out=gpost_bc[:], in_=vecs_d[1:2, :].partition_broadcast(128)),
                  writes=['gpost_bc'], dma=True)
            S.add('sp', lambda e: e.dma_start(out=gfpost_bc[:], in_=vecs_d[3:4, :].partition_broadcast(128)),
                  writes=['gfpost_bc'], dma=True)

            NWB = 6
            wbuf = [sb("wbuf%d" % i, [128, 1024], BF16) for i in range(NWB)]
            wcnt = [0]

            def wload(blk):
                i = wcnt[0] % NWB
                wcnt[0] += 1
                S.add('sp', lambda e: e.dma_start(out=wbuf[i][:], in_=wscr[blk * 128:(blk + 1) * 128, :]),
                      reads=[('wscr', blk)], writes=[('wbuf', i)], dma=True)
                return wbuf[i], ('wbuf', i)

            state = sb("state", [128, 1024], F32)
            state_b = sb("state_b", [128, 1024], BF16)
            S.add('pool', lambda e: e.memset(state[:], 0.0), writes=['state'])
            S.add('pool', lambda e: e.memset(state_b[:], 0.0), writes=['state_b'])
            halo = sb("halo", [128, 12, 3], F32)
            ghalo = sb("ghalo", [128, NFC, 2], F32)
            S.add('pool', lambda e: e.memset(halo[:], 0.0), writes=['halo'])
            S.add('pool', lambda e: e.memset(ghalo[:], 0.0), writes=['ghalo'])

            hbuf = sb("hbuf", [128, 3, D], F32)
            actT = sb("actT", [128, 8, NG], BF16)
            R1 = sb("R1", [128, 6 * D], F32)
            zs = R1[:, 0:3 * D].rearrange("p (j c) -> p j c", j=3)
            Xtok = R1[:, 3 * D:6 * D].rearrange("p (j c) -> p j c", j=3)
            aT = R1[:, 0:NFC * NG // 2].bitcast(BF16).rearrange("p (f c) -> p f c", f=NFC)
            BT = sb("BT", [128, 2, NG], BF16)
            CT = sb("CT", [128, 2, NG], BF16)
            Btok = sb("Btok", [128, 3, 256], BF16)
            cbuf = [sb("cbuf%d" % i, [128, NG + 3], F32) for i in range(2)]
            cacc = [sb("cacc%d" % i, [128, NG], F32) for i in range(2)]
            sfm = [sb("sfm%d" % i, [128, NG], F32) for i in range(2)]
            t1 = sb("t1", [128, D], F32)
            RH = sb("RH", [128, 16, 128], F32)
            Xdt = sb("Xdt", [128, D], BF16)
            Xdec = sb("Xdec", [128, D], BF16)
            Eb = [sb("Eb%d" % i, [128, 512], F32) for i in range(2)]
            MTb = [sb("MTb%d" % i, [128, 4, 128], BF16) for i in range(2)]
            CBm = sb("CBm", [128, 2, 128], F32)
            tokbf = sb("tokbf", [128, D], BF16)
            ygnT = sb("ygnT", [128, 8, NG], BF16)
            ysbn = sb("ysbn", [128, 8, NG], BF16)
            r_bc = sb("r_bc", [128, NG], F32)
            lnr = sb("lnr", [128, NG], F32)
            sqb = [sb("sqb%d" % i, [128, NG], BF16) for i in range(2)]
            dtt = sb("dtt", [128, 3, 16], F32)
            att = sb("att", [128, 3, 16], F32)
            dte = sb("dte", [128, 3, 16], F32)
            ex48 = sb("ex48", [128, 48], F32)
            dtd = sb("dtd", [128, 16], F32)
            st3 = sb("st3", [128, 16], F32)
            gcb = cbuf
            glb = sfm

            def bc3(ap2, n):
                return ap2.unsqueeze(2).broadcast_to([128, 16, n])

            def v3(ap2):
                return ap2.rearrange("p (h d) -> p h d", h=16)

            def rstd3(src, srcreads, col, key):
                S.add('act', lambda e: e.activation(out=junk[:], in_=src, func=AF.Square,
                                                    accum_out=st3[:, col:col + 1]),
                      reads=srcreads, writes=[(key, 0)])
                S.add('dve', lambda e: e.tensor_scalar(out=st3[:, col + 1:col + 2], in0=st3[:, col:col + 1],
                                                       scalar1=1.0 / D, scalar2=EPS, op0=ALU.mult, op1=ALU.add),
                      reads=[(key, 0)], writes=[(key, 1)])
                S.add('pool', lambda e: e.tensor_tensor(out=st3[:, col + 2:col + 3], in0=st3[:, col + 1:col + 2],
                                                        in1=mhalf[:], op=ALU.pow),
                      reads=[(key, 1), 'mhalf'], writes=[(key, 2)])
                return st3[:, col + 2:col + 3], (key, 2)

            def norm_transpose(j, src, srckeys, rcol, rkey, gcol, dstT, dkey):
                S.add('dve', lambda e: e.tensor_scalar(out=tokbf[:], in0=src, scalar1=rcol, scalar2=None,
                                                       op0=ALU.mult),
                      reads=srckeys + [rkey], writes=['tokbf'])
                pbv = bank(7).bitcast(BF16)
                for k in range(8):
                    S.add('pe', lambda e, k=k: e.transpose(
                        out=pbv[:, k * 128:(k + 1) * 128], in_=tokbf[:, k * 128:(k + 1) * 128], identity=ident_b[:]),
                        reads=['tokbf', 'ident_b'], writes=bkeys(7))
                for k in range(8):
                    S.add('act', lambda e, k=k: e.activation(
                        out=dstT[:, k, j * 128:(j + 1) * 128], in_=pbv[:, k * 128:(k + 1) * 128],
                        func=AF.Copy, scale=pp[:, gcol + k:gcol + k + 1]),
                        reads=bkeys(7) + ['pp'], writes=[(dkey, j)] + bkeys(7))

            def do_group(g):
                t0 = 3 * g
                gc0 = t0 * 128
                hk = [('hbuf', j) for j in range(3)]
                for j in range(3):
                    load_x_tile(t0 + j, hbuf[:, j, :], ('hbuf', j))
                    rc, rk = rstd3(hbuf[:, j, :], [('hbuf', j)], 0, 'st3a')
                    norm_transpose(j, hbuf[:, j, :], [('hbuf', j)], rc, rk, 76, actT, 'actT')
                ak = [('actT', j) for j in range(3)]
                if g == 0:
                    dbgdump('actT', actT[:].rearrange("p k c -> p (k c)"), [128, 8 * NG], BF16, ak)
                for cbk in range(8):
                    w, wkey = wload(BLK_Z + cbk)
                    w3 = w[:].rearrange("p (k c) -> p k c", k=8)
                    for j in range(3):
                        b = 2 * j + cbk // 4
                        for k in range(8):
                            S.add('pe', lambda e, b=b, j=j, k=k, w3=w3, cbk=cbk: e.matmul(
                                bank(b)[:, (cbk % 4) * 128:(cbk % 4 + 1) * 128], lhsT=actT[:, k, j * 128:(j + 1) * 128],
                                rhs=w3[:, k, :], start=(k == 0), stop=(k == 7)),
                                reads=[wkey, ('actT', j)], writes=bkeys(b))
                for j in range(3):
                    S.add('act', lambda e, j=j: e.activation(out=zs[:, j, :], in_=bank(2 * j, 2), func=AF.Silu),
                          reads=bkeys(2 * j, 2), writes=[('zs', j)] + bkeys(2 * j, 2))
                for j in range(3):
                    for k in range(8):
                        S.add('pe', lambda e, j=j, k=k: e.matmul(
                            bank(6)[:, j * 16:(j + 1) * 16], lhsT=actT[:, k, j * 128:(j + 1) * 128],
                            rhs=wdt_b[:, k, :], start=(k == 0), stop=(k == 7)),
                            reads=['wdt_b', ('actT', j)], writes=bkeys(6))
                S.add('dve', lambda e: e.tensor_tensor(
                    out=dte[:], in0=bank(6)[:, 0:48].rearrange("p (j h) -> p j h", j=3),
                    in1=smallbc[:, 0:16].unsqueeze(1).broadcast_to([128, 3, 16]), op=ALU.add),
                    reads=bkeys(6) + ['smallbc'], writes=['dte'] + bkeys(6))
                S.add('act', lambda e: e.activation(out=dte[:], in_=dte[:], func=AF.Exp), reads=['dte'], writes=['dte'])
                S.add('act', lambda e: e.activation(out=dtt[:], in_=dte[:], func=AF.Ln, bias=1.0),
                      reads=['dte'], writes=['dtt'])
                if g == 0:
                    S.add('dve', lambda e: e.tensor_scalar(out=dtt[:, 0, :], in0=dtt[:, 0, :], scalar1=padm[:, 0:1],
                                                           scalar2=None, op0=ALU.mult),
                          reads=['dtt', 'padm'], writes=['dtt'])
                S.add('dve', lambda e: e.tensor_tensor(
                    out=att[:], in0=dtt[:], in1=Abc[:].unsqueeze(1).broadcast_to([128, 3, 16]), op=ALU.mult),
                    reads=['dtt', 'Abc'], writes=['att'])
                for jb in range(12):
                    w, wkey = wload(BLK_XBC + jb)
                    w3 = w[:].rearrange("p (k c) -> p k c", k=8)
                    b = 6 + (jb % 2)
                    i = jb % 2
                    for k in range(8):
                        S.add('pe', lambda e, b=b, k=k, w3=w3: e.matmul(
                            bank(b)[:, 0:NG], lhsT=w3[:, k, :], rhs=actT[:, k, :], start=(k == 0), stop=(k == 7)),
                            reads=[wkey] + ak, writes=bkeys(b))
                    S.add('act', lambda e, b=b, i=i: e.activation(out=cbuf[i][:, 3:3 + NG], in_=bank(b)[:, 0:NG],
                                                                  func=AF.Copy),
                          reads=bkeys(b), writes=[('cbuf', i)] + bkeys(b))
                    S.add('pool', lambda e, i=i, jb=jb: e.tensor_copy(out=cbuf[i][:, 0:3], in_=halo[:, jb, :]),
                          reads=['halo'], writes=[('cbuf', i)])
                    S.add('pool', lambda e, i=i, jb=jb: e.tensor_copy(out=halo[:, jb, :], in_=cbuf[i][:, NG:NG + 3]),
                          reads=[('cbuf', i)], writes=['halo'])
                    for kk in range(4):
                        wc = pp[:, 16 + jb * 4 + kk:16 + jb * 4 + kk + 1]
                        if kk == 0:
                            S.add('dve', lambda e, i=i, wc=wc: e.tensor_scalar(
                                out=cacc[i][:], in0=cbuf[i][:, 0:NG], scalar1=wc, scalar2=None, op0=ALU.mult),
                                reads=[('cbuf', i), 'pp'], writes=[('cacc', i)])
                        else:
                            S.add('dve', lambda e, i=i, wc=wc, kk=kk: e.scalar_tensor_tensor(
                                out=cacc[i][:], in0=cbuf[i][:, kk:kk + NG], scalar=wc, in1=cacc[i][:],
                                op0=ALU.mult, op1=ALU.add),
                                reads=[('cbuf', i), 'pp', ('cacc', i)], writes=[('cacc', i)])
                    bias = pp[:, 64 + jb:65 + jb]
                    if jb < 8:
                        S.add('act', lambda e, i=i, bias=bias: e.activation(out=sfm[i][:], in_=cacc[i][:], func=AF.Silu,
                                                                            bias=bias),
                              reads=[('cacc', i), 'pp'], writes=[('sfm', i)])
                        for j in range(3):
                            S.add('pe', lambda e, i=i, j=j, jb=jb: e.transpose(
                                out=bank(2 * j + jb // 4)[:, (jb % 4) * 128:(jb % 4 + 1) * 128],
                                in_=sfm[i][:, j * 128:(j + 1) * 128], identity=ident_f[:]),
                                reads=[('sfm', i), 'ident_f'], writes=bkeys(2 * j + jb // 4))
                    else:
                        dst = BT if jb < 10 else CT
                        dk = 'BT' if jb < 10 else 'CT'
                        gg = jb % 2
                        S.add('act', lambda e, i=i, bias=bias, dst=dst, gg=gg: e.activation(
                            out=dst[:, gg, :], in_=cacc[i][:], func=AF.Silu, bias=bias),
                            reads=[('cacc', i), 'pp'], writes=[(dk, gg)])
                for j in range(3):
                    S.add('act', lambda e, j=j: e.activation(out=Xtok[:, j, :], in_=bank(2 * j, 2), func=AF.Copy),
                          reads=bkeys(2 * j, 2), writes=[('Xtok', j)] + bkeys(2 * j, 2))
                pbv = bank(7).bitcast(BF16)
                for gg in range(2):
                    for j in range(3):
                        S.add('pe', lambda e, j=j, gg=gg, pbv=pbv: e.transpose(
                            out=pbv[:, j * 256 + gg * 128:j * 256 + (gg + 1) * 128],
                            in_=BT[:, gg, j * 128:(j + 1) * 128], identity=ident_b[:]),
                            reads=[('BT', gg), 'ident_b'], writes=bkeys(7))
                S.add('dve', lambda e, pbv=pbv: e.tensor_copy(
                    out=Btok[:], in_=pbv[:, 0:768].rearrange("p (j c) -> p j c", j=3)),
                    reads=bkeys(7), writes=['Btok'] + bkeys(7))
                if g == 0:
                    dbgdump('zs', R1[:, 0:3 * D], [128, 3 * D], F32, [('zs', jj) for jj in range(3)])
                    dbgdump('Xtok', R1[:, 3 * D:6 * D], [128, 3 * D], F32, [('Xtok', jj) for jj in range(3)])
                    dbgdump('dtt', dtt[:].rearrange("p j h -> p (j h)"), [128, 48], F32, ['dtt'])
                    dbgdump('att', att[:].rearrange("p j h -> p (j h)"), [128, 48], F32, ['att'])
                    dbgdump('BT', BT[:].rearrange("p g c -> p (g c)"), [128, 2 * NG], BF16, [('BT', 0), ('BT', 1)])
                    dbgdump('CT', CT[:].rearrange("p g c -> p (g c)"), [128, 2 * NG], BF16, [('CT', 0), ('CT', 1)])
                    dbgdump('Btok', Btok[:].rearrange("p j c -> p (j c)"), [128, 768], BF16, ['Btok'])
                for j in range(3):
                    tc0 = j * 128
                    a_j = att[:, j, :]
                    for ci, lt in enumerate((UTf, SGf, ONESf)):
                        S.add('pe', lambda e, ci=ci, lt=lt, a_j=a_j: e.matmul(
                            bank(0)[:, ci * 16:(ci + 1) * 16], lhsT=lt[:], rhs=a_j, start=True, stop=True),
                            reads=['att', 'UTf', 'SGf', 'ONESf'], writes=bkeys(0))
                    S.add('act', lambda e: e.activation(out=ex48[:], in_=bank(0)[:, 0:48], func=AF.Exp),
                          reads=bkeys(0), writes=['ex48'] + bkeys(0))
                    S.add('dve', lambda e, j=j: e.tensor_tensor(out=dtd[:], in0=dtt[:, j, :], in1=ex48[:, 16:32],
                                                                op=ALU.mult),
                          reads=['dtt', 'ex48'], writes=['dtd'])
                    S.add('dve', lambda e, j=j: e.tensor_tensor(out=v3(Xdt[:]), in0=v3(Xtok[:, j, :]),
                                                                in1=bc3(dtt[:, j, :], 64), op=ALU.mult),
                          reads=[('Xtok', j), 'dtt'], writes=['Xdt'])
                    S.add('dve', lambda e, j=j: e.tensor_tensor(out=v3(Xdec[:]), in0=v3(Xtok[:, j, :]),
                                                                in1=bc3(dtd[:], 64), op=ALU.mult),
                          reads=[('Xtok', j), 'dtd'], writes=['Xdec'])
                    S.add('dve', lambda e, a_j=a_j: e.tensor_tensor(
                        out=RH[:], in0=UTf[:].unsqueeze(1).broadcast_to([128, 16, 128]), in1=bc3(a_j, 128),
                        op=ALU.mult),
                        reads=['att', 'UTf'], writes=['RH'])
                    for gg in range(2):
                        S.add('pe', lambda e, gg=gg, tc0=tc0: e.matmul(
                            bank(0)[:, 128 + gg * 128:256 + gg * 128], lhsT=BT[:, gg, tc0:tc0 + 128],
                            rhs=CT[:, gg, tc0:tc0 + 128], start=True, stop=True),
                            reads=[('BT', gg), ('CT', gg)], writes=bkeys(0))
                    S.add('dve', lambda e: e.tensor_tensor(
                        out=CBm[:], in0=bank(0)[:, 128:384].rearrange("p (g c) -> p g c", g=2),
                        in1=UTf[:].unsqueeze(1).broadcast_to([128, 2, 128]), op=ALU.mult),
                        reads=bkeys(0) + ['UTf'], writes=['CBm'] + bkeys(0))
                    for hq in range(4):
                        i = hq % 2
                        gg = hq // 2
                        S.add('pe', lambda e, i=i, hq=hq: e.matmul(
                            bank(1 + i), lhsT=SGf[:], rhs=RH[:, 4 * hq:4 * hq + 4, :].rearrange("p h c -> p (h c)"),
                            start=True, stop=True),
                            reads=['RH', 'SGf'], writes=bkeys(1 + i))
                        S.add('act', lambda e, i=i: e.activation(out=Eb[i][:], in_=bank(1 + i), func=AF.Exp),
                              reads=bkeys(1 + i), writes=[('Eb', i)] + bkeys(1 + i))
                        S.add('dve', lambda e, i=i, gg=gg: e.tensor_tensor(
                            out=MTb[i][:], in0=Eb[i][:].rearrange("p (h c) -> p h c", h=4),
                            in1=CBm[:, gg:gg + 1, :].broadcast_to([128, 4, 128]), op=ALU.mult),
                            reads=[('Eb', i), 'CBm'], writes=[('MTb', i)])
                        for hh in range(4):
                            h = 4 * hq + hh
                            S.add('pe', lambda e, i=i, hh=hh, h=h: e.matmul(
                                bank(3 + h // 8)[:, (h % 8) * 64:(h % 8 + 1) * 64], lhsT=MTb[i][:, hh, :],
                                rhs=Xdt[:, h * 64:(h + 1) * 64], start=True, stop=True),
                                reads=[('MTb', i), 'Xdt'], writes=bkeys(3 + h // 8))
                    for gg in range(2):
                        S.add('pe', lambda e, gg=gg, tc0=tc0: e.matmul(
                            bank(5 + gg), lhsT=CT[:, gg, tc0:tc0 + 128], rhs=state_b[:, gg * 512:(gg + 1) * 512],
                            start=True, stop=True),
                            reads=[('CT', gg), 'state_b'], writes=bkeys(5 + gg))
                    S.add('dve', lambda e: e.tensor_tensor(out=v3(t1[:]), in0=v3(bank(5, 2)), in1=bc3(ex48[:, 0:16], 64),
                                                           op=ALU.mult),
                          reads=bkeys(5, 2) + ['ex48'], writes=['t1'] + bkeys(5, 2))
                    S.add('dve', lambda e: e.tensor_tensor(out=t1[:], in0=bank(3, 2), in1=t1[:], op=ALU.add),
                          reads=bkeys(3, 2) + ['t1'], writes=['t1'] + bkeys(3, 2))
                    for gg in range(2):
                        S.add('pe', lambda e, gg=gg, j=j: e.matmul(
                            bank(5 + gg), lhsT=Btok[:, j, gg * 128:(gg + 1) * 128], rhs=Xdec[:, gg * 512:(gg + 1) * 512],
                            start=True, stop=True),
                            reads=['Btok', 'Xdec'], writes=bkeys(5 + gg))
                    S.add('dve', lambda e, j=j: e.tensor_tensor(out=v3(Xtok[:, j, :]), in0=v3(Xtok[:, j, :]),
                                                                in1=bc3(smallbc[:, 32:48], 64), op=ALU.mult),
                          reads=[('Xtok', j), 'smallbc'], writes=[('Xtok', j)])
                    S.add('dve', lambda e, j=j: e.tensor_tensor(out=t1[:], in0=t1[:], in1=Xtok[:, j, :], op=ALU.add),
                          reads=[('Xtok', j), 't1'], writes=['t1'])
                    S.add('dve', lambda e, j=j: e.tensor_tensor(out=t1[:], in0=t1[:], in1=zs[:, j, :], op=ALU.mult),
                          reads=[('zs', j), 't1'], writes=['t1'])
                    S.add('dve', lambda e: e.tensor_tensor(out=v3(state[:]), in0=v3(state[:]), in1=bc3(ex48[:, 32:48], 64),
                                                           op=ALU.mult),
                          reads=['state', 'ex48'], writes=['state'])
                    S.add('dve', lambda e: e.tensor_tensor(out=state[:], in0=bank(5, 2), in1=state[:], op=ALU.add),
                          reads=bkeys(5, 2) + ['state'], writes=['state'] + bkeys(5, 2))
                    S.add('act', lambda e: e.activation(out=state_b[:], in_=state[:], func=AF.Copy),
                          reads=['state'], writes=['state_b'])
                    if g == 0:
                        dbgdump('yg%d' % j, t1[:], [128, D], F32, ['t1'])
                        dbgdump('ex48_%d' % j, ex48[:], [128, 48], F32, ['ex48'])
                    rc, rk = rstd3(t1[:], ['t1'], 4, 'st3b')
                    norm_transpose(j, t1[:], ['t1'], rc, rk, 0, ygnT, 'ygnT')
                for k in range(8):
                    i = k % 2
                    S.add('act', lambda e, i=i, k=k: e.activation(out=sqb[i][:], in_=ysbT[:, k, gc0:gc0 + NG],
                                                                  func=AF.Square),
                          reads=[('ysbT', k, q) for q in range(9)], writes=[('sqb', i)])
                    S.add('pe', lambda e, i=i, k=k: e.matmul(bank(6)[:, 0:NG], lhsT=ones_b[:], rhs=sqb[i][:],
                                                             start=(k == 0), stop=(k == 7)),
                          reads=[('sqb', i), 'ones_b'], writes=bkeys(6))
                S.add('act', lambda e: e.activation(out=lnr[:], in_=bank(6)[:, 0:NG], func=AF.Ln, scale=1.0 / D, bias=EPS),
                      reads=bkeys(6), writes=['lnr'] + bkeys(6))
                S.add('act', lambda e: e.activation(out=r_bc[:], in_=lnr[:], func=AF.Exp, scale=-0.5),
                      reads=['lnr'], writes=['r_bc'])
                for k in range(8):
                    S.add('dve', lambda e, k=k: e.scalar_tensor_tensor(
                        out=ysbn[:, k, :], in0=ysbT[:, k, gc0:gc0 + NG], scalar=pp[:, 8 + k:9 + k], in1=r_bc[:],
                        op0=ALU.mult, op1=ALU.mult),
                        reads=[('ysbT', k, q) for q in range(9)] + ['pp', 'r_bc'], writes=[('ysbn', k)])
                for kc in range(16):
                    w, wkey = wload(BLK_OUT + kc)
                    src = ygnT if kc < 8 else ysbn
                    sk = [('ygnT', jj) for jj in range(3)] if kc < 8 else [('ysbn', kc - 8)]
                    for j in range(3):
                        for hf in range(2):
                            S.add('pe', lambda e, j=j, hf=hf, w=w, src=src, kc=kc: e.matmul(
                                bank(2 * j + hf), lhsT=src[:, kc % 8, j * 128:(j + 1) * 128],
                                rhs=w[:, hf * 512:(hf + 1) * 512], start=(kc == 0), stop=(kc == 15)),
                                reads=[wkey] + sk, writes=bkeys(2 * j + hf))
                for j in range(3):
                    rc, rk = rstd3(bank(2 * j, 2), bkeys(2 * j, 2), 8, 'st3c')
                    S.add('dve', lambda e, j=j, rc=rc: e.scalar_tensor_tensor(
                        out=t1[:], in0=bank(2 * j, 2), scalar=rc, in1=gpost_bc[:], op0=ALU.mult, op1=ALU.mult),
                        reads=bkeys(2 * j, 2) + [rk, 'gpost_bc'], writes=['t1'] + bkeys(2 * j, 2))
                    S.add('dve', lambda e, j=j: e.tensor_tensor(out=hbuf[:, j, :], in0=hbuf[:, j, :], in1=t1[:], op=ALU.add),
                          reads=['t1', ('hbuf', j)], writes=[('hbuf', j)])
                if g == 0:
                    dbgdump('ygnT', ygnT[:].rearrange("p k c -> p (k c)"), [128, 8 * NG], BF16, [('ygnT', jj) for jj in range(3)])
                    dbgdump('ysbn', ysbn[:].rearrange("p k c -> p (k c)"), [128, 8 * NG], BF16, [('ysbn', k) for k in range(8)])
                    dbgdump('r_bc', r_bc[:], [128, NG], F32, ['r_bc'])
                    dbgdump('h1', hbuf[:].rearrange("p j c -> p (j c)"), [128, 3 * D], F32, hk)
                    dbgdump('state', state[:], [128, D], F32, ['state'])
                for j in range(3):
                    rc, rk = rstd3(hbuf[:, j, :], [('hbuf', j)], 12, 'st3d')
                    norm_transpose(j, hbuf[:, j, :], [('hbuf', j)], rc, rk, 84, actT, 'actT')
                for fc in range(NFC):
                    wg, wgk = wload(BLK_GATE + fc)
                    wu, wuk = wload(BLK_UP + fc)
                    wg3 = wg[:].rearrange("p (k c) -> p k c", k=8)
                    wu3 = wu[:].rearrange("p (k c) -> p k c", k=8)
                    i = fc % 2
                    bg = i
                    bu = 2 + i
                    for k in range(8):
                        S.add('pe', lambda e, bg=bg, k=k, wg3=wg3: e.matmul(
                            bank(bg)[:, 0:NG], lhsT=wg3[:, k, :], rhs=actT[:, k, :], start=(k == 0), stop=(k == 7)),
                            reads=[wgk] + ak, writes=bkeys(bg))
                    for k in range(8):
                        S.add('pe', lambda e, bu=bu, k=k, wu3=wu3: e.matmul(
                            bank(bu)[:, 0:NG], lhsT=wu3[:, k, :], rhs=actT[:, k, :], start=(k == 0), stop=(k == 7)),
                            reads=[wuk] + ak, writes=bkeys(bu))
                    S.add('act', lambda e, bg=bg, i=i: e.activation(out=gcb[i][:, 2:2 + NG], in_=bank(bg)[:, 0:NG],
                                                                    func=AF.Copy),
                          reads=bkeys(bg), writes=[('cbuf', i)] + bkeys(bg))
                    S.add('pool', lambda e, i=i, fc=fc: e.tensor_copy(out=gcb[i][:, 0:2], in_=ghalo[:, fc, :]),
                          reads=['ghalo'], writes=[('cbuf', i)])
                    S.add('pool', lambda e, i=i, fc=fc: e.tensor_copy(out=ghalo[:, fc, :], in_=gcb[i][:, NG:NG + 2]),
                          reads=[('cbuf', i)], writes=['ghalo'])
                    for kk in range(3):
                        wc = ffnp[:, fc * 3 + kk:fc * 3 + kk + 1]
                        if kk == 0:
                            S.add('dve', lambda e, i=i, wc=wc: e.tensor_scalar(
                                out=cacc[i][:], in0=gcb[i][:, 0:NG], scalar1=wc, scalar2=None, op0=ALU.mult),
                                reads=[('cbuf', i), 'ffnp'], writes=[('cacc', i)])
                        else:
                            S.add('dve', lambda e, i=i, wc=wc, kk=kk: e.scalar_tensor_tensor(
                                out=cacc[i][:], in0=gcb[i][:, kk:kk + NG], scalar=wc, in1=cacc[i][:],
                                op0=ALU.mult, op1=ALU.add),
                                reads=[('cbuf', i), 'ffnp', ('cacc', i)], writes=[('cacc', i)])
                    S.add('act', lambda e, i=i, fc=fc: e.activation(out=glb[i][:], in_=cacc[i][:], func=AF.Gelu_apprx_tanh,
                                                                    bias=ffnp[:, 66 + fc:67 + fc]),
                          reads=[('cacc', i), 'ffnp'], writes=[('sfm', i)])
                    S.add('dve', lambda e, i=i, fc=fc, bu=bu: e.tensor_tensor(
                        out=aT[:, fc, :], in0=bank(bu)[:, 0:NG], in1=glb[i][:], op=ALU.mult),
                        reads=bkeys(bu) + [('sfm', i)],
                        writes=[('zs', jj) for jj in range(3)] + [('Xtok', jj) for jj in range(3)] + bkeys(bu))
                aTk = [('zs', jj) for jj in range(3)] + [('Xtok', jj) for jj in range(3)]
                for fc in range(NFC):
                    w, wkey = wload(BLK_DOWN + fc)
                    for j in range(3):
                        for hf in range(2):
                            S.add('pe', lambda e, j=j, hf=hf, w=w, fc=fc: e.matmul(
                                bank(2 * j + hf), lhsT=aT[:, fc, j * 128:(j + 1) * 128],
                                rhs=w[:, hf * 512:(hf + 1) * 512], start=(fc == 0), stop=(fc == NFC - 1)),
                                reads=[wkey] + aTk, writes=bkeys(2 * j + hf))
                for j in range(3):
                    t = t0 + j
                    rc, rk = rstd3(bank(2 * j, 2), bkeys(2 * j, 2), 0, 'st3a')
                    S.add('dve', lambda e, j=j, rc=rc: e.scalar_tensor_tensor(
                        out=t1[:], in0=bank(2 * j, 2), scalar=rc, in1=gfpost_bc[:], op0=ALU.mult, op1=ALU.mult),
                        reads=bkeys(2 * j, 2) + [rk, 'gfpost_bc'], writes=['t1'] + bkeys(2 * j, 2))
                    S.add('dve', lambda e, j=j: e.tensor_tensor(out=hbuf[:, j, :], in0=hbuf[:, j, :], in1=t1[:], op=ALU.add),
                          reads=['t1', ('hbuf', j)], writes=[('hbuf', j)])
                    if t >= 1:
                        S.add('sp', lambda e, j=j, t=t: e.dma_start(out=out_d[(t - 1) * 128:t * 128, :], in_=hbuf[:, j, :]),
                              reads=[('hbuf', j)], dma=True)

            for g in range(11 if stage >= 4 else 1):
                do_group(g)

        tail = [op for q in ENGS for op in S.dma_ops[q][-NDSEM:]]
        S.add('sp', lambda e: e.nop(), extra=tail)
        es2.close()
        S.emit(nc, es)
    return nc


def _host_layout(inputs):
    f = np.float32
    w_in = np.asarray(inputs["w_in"], f)[0]
    w_out = np.asarray(inputs["w_out"], f)[0]
    w_up = np.asarray(inputs["w_up"], f)[0]
    w_down = np.asarray(inputs["w_down"], f)[0]

    def kblocks(w, c0, nb):
        sub = w[:, c0:c0 + nb * 128].reshape(8, 128, nb, 128)
        return np.ascontiguousarray(sub.transpose(2, 1, 0, 3)).reshape(nb * 128, 1024)

    blocks = [kblocks(w_in, OFF_Z, 8), kblocks(w_in, OFF_XBC, 12), kblocks(w_in, OFF_Q, 8),
              kblocks(w_in, OFF_K, 8), kblocks(w_in, OFF_V, 8),
              w_out.reshape(16 * 128, 1024),
              kblocks(w_up, 0, 22), kblocks(w_up, DFF, 22),
              w_down.reshape(22 * 128, 1024)]
    wblk = np.ascontiguousarray(np.concatenate(blocks, axis=0))
    assert wblk.shape == (NBLK * 128, 1024)
    wdt = np.ascontiguousarray(w_in[:, OFF_DT:OFF_DT + 16].reshape(8, 128, 16).transpose(1, 0, 2)).reshape(128, 128)
    vecs = np.zeros((8, D), f)
    vecs[0] = np.asarray(inputs["mix_pre_g"], f)[0]
    vecs[1] = np.asarray(inputs["mix_post_g"], f)[0]
    vecs[2] = np.asarray(inputs["ffn_pre_g"], f)[0]
    vecs[3] = np.asarray(inputs["ffn_post_g"], f)[0]
    pp = np.zeros((128, 128), f)
    pp[:, 0:8] = np.asarray(inputs["ssd_norm_g"], f)[0].reshape(8, 128).T
    pp[:, 8:16] = np.asarray(inputs["sb_norm_g"], f)[0].reshape(8, 128).T
    cw = np.asarray(inputs["ssd_conv_w"], f)[0]
    pp[:, 16:64] = cw.reshape(4, 12, 128).transpose(2, 1, 0).reshape(128, 48)
    pp[:, 64:76] = np.asarray(inputs["ssd_conv_b"], f)[0].reshape(12, 128).T
    pp[:, 76:84] = np.asarray(inputs["mix_pre_g"], f)[0].reshape(8, 128).T
    pp[:, 84:92] = np.asarray(inputs["ffn_pre_g"], f)[0].reshape(8, 128).T
    fw = np.asarray(inputs["ffn_conv_w"], f)[0]
    ffn = np.zeros((128, 128), f)
    ffn[:, 0:66] = fw.reshape(3, 22, 128).transpose(2, 1, 0).reshape(128, 66)
    ffn[:, 66:88] = np.asarray(inputs["ffn_conv_b"], f)[0].reshape(22, 128).T
    small = np.zeros((1, 64), f)
    small[0, 0:16] = np.asarray(inputs["ssd_dt_bias"], f)[0]
    small[0, 16:32] = np.asarray(inputs["ssd_a_log"], f)[0]
    small[0, 32:48] = np.asarray(inputs["ssd_d"], f)[0]
    return dict(wblk=wblk, wdt=wdt, vecs=vecs, ppd=pp, ffnpd=ffn, small=small,
                meta=np.ascontiguousarray(np.asarray(inputs["meta_tokens"], f)))


def kernel(**inputs):
    x = np.asarray(inputs["x"], np.float32)
    shared = _host_layout(inputs)
    nc = build_nc()
    in_maps = []
    for b in range(8):
        m = dict(shared)
        m["x"] = np.ascontiguousarray(x[b])
        in_maps.append(m)
    res = run_bass_kernel_spmd(nc, in_maps, core_ids=list(range(8)))
    return np.stack([r["out"] for r in res.results], axis=0)
```

```python
import numpy as np
from contextlib import ExitStack
import concourse.bass as bass
import concourse.mybir as mybir
from concourse.bass_utils import run_bass_kernel_spmd

F32 = mybir.dt.float32
BF16 = mybir.dt.bfloat16
AF = mybir.ActivationFunctionType
ALU = mybir.AluOpType

D = 1024
SEQ = 4096
NMETA = 16
NT = 33
LP = NT * 128
PAD = 112
H = 16
DFF = 2816
NFC = 22
EPS = 1e-6
OFF_Z, OFF_XBC, OFF_DT, OFF_Q, OFF_K, OFF_V = 0, 1024, 2560, 2576, 3600, 4624

BLK_Z = 0
BLK_XBC = 8
BLK_Q = 20
BLK_K = 28
BLK_V = 36
BLK_OUT = 44
BLK_GATE = 60
BLK_UP = 82
BLK_DOWN = 104
NBLK = 126

ENGS = ['pe', 'act', 'dve', 'pool', 'sp']
NDSEM = 8


class _Op:
    __slots__ = ('eng', 'fn', 'deps', 'is_dma', 'needed', 'val', 'sem', 'idx')


class Sched:
    def __init__(self):
        self.ops = {e: [] for e in ENGS}
        self.last_w = {}
        self.readers = {}
        self.seen_c = {e: {p: -1 for p in ENGS} for e in ENGS}
        self.seen_d = {e: set() for e in ENGS}
        self.ndma = {e: 0 for e in ENGS}
        self.dma_ops = {e: [] for e in ENGS}

    def add(self, eng, fn, reads=(), writes=(), dma=False, extra=()):
        op = _Op()
        op.eng = eng
        op.fn = fn
        op.is_dma = dma
        op.needed = False
        op.val = None
        op.sem = None
        op.idx = len(self.ops[eng])
        deps = list(extra)
        for k in reads:
            w = self.last_w.get(k)
            if w is not None:
                deps.append(w)
        for k in writes:
            w = self.last_w.get(k)
            if w is not None:
                deps.append(w)
            deps.extend(self.readers.get(k, ()))
        if dma:
            n = self.ndma[eng]
            if n >= NDSEM:
                deps.append(self.dma_ops[eng][n - NDSEM])
            op.sem = ('d', eng, n % NDSEM)
            op.val = 16 * (n // NDSEM + 1)
            self.ndma[eng] = n + 1
            self.dma_ops[eng].append(op)
        cdeps = {}
        ddeps = []
        for d in deps:
            if d is op:
                continue
            if d.is_dma:
                key = (d.eng, d.sem, d.val)
                if key in self.seen_d[eng]:
                    continue
                self.seen_d[eng].add(key)
                ddeps.append(d)
            else:
                if d.eng == eng and eng == 'pe':
                    continue
                if d.idx <= self.seen_c[eng][d.eng]:
                    continue
                if d.eng not in cdeps or cdeps[d.eng].idx < d.idx:
                    cdeps[d.eng] = d
        for p, d in cdeps.items():
            self.seen_c[eng][p] = d.idx
            d.needed = True
        op.deps = list(cdeps.values()) + ddeps
        self.ops[eng].append(op)
        for k in writes:
            self.last_w[k] = op
            self.readers[k] = []
        for k in reads:
            if k in writes:
                continue
            self.readers.setdefault(k, []).append(op)
        return op

    def emit(self, nc, es):
        sem_c = {e: es.enter_context(nc.semaphore('sc_' + e)) for e in ENGS}
        sem_d = {}
        for e in ENGS:
            for i in range(min(NDSEM, self.ndma[e])):
                sem_d[('d', e, i)] = es.enter_context(nc.semaphore('sd_%s_%d' % (e, i)))
        for e in ENGS:
            c = 0
            for op in self.ops[e]:
                if op.is_dma:
                    continue
                if op.needed:
                    c += 1
                    op.val = c
        block = es.enter_context(nc.Block())
        secs = {'pe': block.tensor, 'act': block.scalar, 'dve': block.vector,
                'pool': block.gpsimd, 'sp': block.sync}

        def mk(e):
            def body(eng):
                for op in self.ops[e]:
                    for d in op.deps:
                        if d.is_dma:
                            eng.wait_ge(sem_d[d.sem], d.val)
                        else:
                            eng.wait_ge(sem_c[d.eng], d.val)
                    inst = op.fn(eng)
                    if op.is_dma:
                        inst.then_inc(sem_d[op.sem], 16)
                    elif op.needed:
                        inst.then_inc(sem_c[e], 1)
            return body

        for e in ENGS:
            if self.ops[e]:
                secs[e](mk(e))


def build_nc(stage=99, debug=False):
    nc = bass.Bass("TRN2", target_bir_lowering=False)
    es = ExitStack()

    def din(name, shape):
        return nc.dram_tensor(name, list(shape), F32, kind="ExternalInput").ap()

    x_d = din("x", [SEQ, D])
    meta_d = din("meta", [NMETA, D])
    wblk_d = din("wblk", [NBLK * 128, 1024])
    wdt_d = din("wdt", [128, 8 * 16])
    vecs_d = din("vecs", [8, D])
    pp_d = din("ppd", [128, 128])
    ffnp_d = din("ffnpd", [128, 128])
    small_d = din("small", [1, 64])
    out_d = nc.dram_tensor("out", [SEQ, D], F32, kind="ExternalOutput").ap()
    wscr = nc.dram_tensor("wscr", [NBLK * 128, 1024], BF16, kind="Internal").ap()
    dbg = {}
    if debug:
        dbg['xnT'] = nc.dram_tensor("dbg_xnT", [128, 8 * LP], BF16, kind="ExternalOutput").ap()
        dbg['ysbT'] = nc.dram_tensor("dbg_ysbT", [128, 8 * LP], BF16, kind="ExternalOutput").ap()

    S = Sched()
    with es:
        es2 = ExitStack()

        def sb(name, shape, dt=F32):
            return es.enter_context(nc.sbuf_tensor(name, list(shape), dt))

        def sb2(name, shape, dt=F32):
            return es2.enter_context(nc.sbuf_tensor(name, list(shape), dt))

        ps = es.enter_context(nc.psum_tensor("ps", [128, 4096], F32))

        def bank(b, n=1):
            return ps[:, 512 * b:512 * (b + n)]

        def bkeys(b, n=1):
            return [('ps', b + i) for i in range(n)]

        def dbgdump(name, ap, shape, dt, reads):
            if not debug:
                return
            t = nc.dram_tensor("dbg_" + name, list(shape), dt, kind="ExternalOutput").ap()
            S.add('sp', lambda e: e.dma_start(out=t, in_=ap), reads=reads, dma=True)

        ident_b = sb("ident_b", [128, 128], BF16)
        tri_b = sb("tri_b", [128, 128], BF16)
        tric_b = sb("tric_b", [128, 128], BF16)
        mdiag_b = sb("mdiag_b", [128, 128], BF16)
        padm = sb("padm", [128, 1], F32)
        mhalf = sb("mhalf", [128, 1], F32)
        pp = sb("pp", [128, 128], F32)
        junk = sb("junk", [128, D], BF16)

        def cmask(t, pattern_mult, chan_mult, op, base=0):
            S.add('pool', lambda e: e.memset(t[:], 1.0), writes=[t.name])
            S.add('pool', lambda e: e.affine_select(
                out=t[:], in_=t[:], pattern=[[pattern_mult, t.shape[1]]], compare_op=op,
                fill=0.0, base=base, channel_multiplier=chan_mult), reads=[t.name], writes=[t.name])

        cmask(ident_b, -1, 1, ALU.is_equal)
        cmask(tri_b, -1, 1, ALU.is_ge)
        cmask(tric_b, 1, -1, ALU.is_gt)
        cmask(mdiag_b, 1, -1, ALU.is_gt)
        cmask(padm, 0, 1, ALU.is_ge, base=-PAD)
        S.add('pool', lambda e: e.memset(mhalf[:], -0.5), writes=['mhalf'])
        S.add('sp', lambda e: e.dma_start(out=pp[:], in_=pp_d[:, :]), writes=['pp'], dma=True)

        def prep_blocks(b0, nb):
            S.add('pool', lambda e: e.dma_start(out=wscr[b0 * 128:(b0 + nb) * 128, :],
                                                in_=wblk_d[b0 * 128:(b0 + nb) * 128, :]),
                  writes=[('wscr', b) for b in range(b0, b0 + nb)], dma=True)

        for b0 in (BLK_Q, BLK_K, BLK_V):
            prep_blocks(b0, 8)
        for b0 in range(0, 20, 4):
            prep_blocks(b0, 4)
        for b0 in range(BLK_OUT, NBLK, 4):
            prep_blocks(b0, min(4, NBLK - b0))

        ysbT = sb("ysbT", [128, 8, LP], BF16)
        xnT = sb2("xnT", [128, 8, LP], BF16)
        gpre_bc = sb2("gpre_bc", [128, D], F32)

        S.add('sp', lambda e: e.dma_start(out=gpre_bc[:], in_=vecs_d[0:1, :].partition_broadcast(128)),
              writes=['gpre_bc'], dma=True)
        xt = [sb2("xt%d" % i, [128, D], F32) for i in range(2)]
        xnb = [sb2("xnb%d" % i, [128, D], BF16) for i in range(2)]
        st1 = sb2("st1", [128, NT * 4], F32)

        def load_x_tile(t, buf, key):
            if t == 0:
                S.add('pool', lambda e: e.memset(buf[:], 0.0), writes=[key])
                S.add('sp', lambda e: e.dma_start(out=buf[PAD:128, :], in_=meta_d[:, :]),
                      writes=[key], dma=True)
            else:
                S.add('sp', lambda e: e.dma_start(out=buf[:], in_=x_d[(t - 1) * 128:t * 128, :]),
                      writes=[key], dma=True)

        def rstd_from(src, srckey, col, stt, sttname, junkbuf, junkkey):
            S.add('act', lambda e: e.activation(out=junkbuf[:], in_=src, func=AF.Square,
                                                accum_out=stt[:, col:col + 1]),
                  reads=[srckey], writes=[(sttname, col)])
            S.add('dve', lambda e: e.tensor_scalar(out=stt[:, col + 1:col + 2], in0=stt[:, col:col + 1],
                                                   scalar1=1.0 / D, scalar2=EPS, op0=ALU.mult, op1=ALU.add),
                  reads=[(sttname, col)], writes=[(sttname, col + 1)])
            S.add('pool', lambda e: e.tensor_tensor(out=stt[:, col + 2:col + 3], in0=stt[:, col + 1:col + 2],
                                                    in1=mhalf[:], op=ALU.pow),
                  reads=[(sttname, col + 1), 'mhalf'], writes=[(sttname, col + 2)])

        for t in range(NT):
            i = t % 2
            load_x_tile(t, xt[i], ('xt', i))
            rstd_from(xt[i][:], ('xt', i), 4 * t, st1, 'st1', junk, 'junk')
            S.add('dve', lambda e, i=i, t=t: e.scalar_tensor_tensor(
                out=xnb[i][:], in0=xt[i][:], scalar=st1[:, 4 * t + 2:4 * t + 3], in1=gpre_bc[:],
                op0=ALU.mult, op1=ALU.mult),
                reads=[('xt', i), ('st1', 4 * t + 2), 'gpre_bc'], writes=[('xnb', i)])
            pb = 4 + (t % 2)
            pbv = bank(pb).bitcast(BF16)
            for k in range(8):
                S.add('pe', lambda e, i=i, k=k, pbv=pbv: e.transpose(
                    out=pbv[:, k * 128:(k + 1) * 128], in_=xnb[i][:, k * 128:(k + 1) * 128], identity=ident_b[:]),
                    reads=[('xnb', i), 'ident_b'], writes=bkeys(pb))
            S.add('act', lambda e, t=t, pbv=pbv: e.activation(
                out=xnT[:, :, t * 128:(t + 1) * 128], in_=pbv.rearrange("p (k c) -> p k c", k=8), func=AF.Copy),
                reads=bkeys(pb), writes=[('xnT', t)] + bkeys(pb))

        if debug:
            S.add('sp', lambda e: e.dma_start(out=dbg['xnT'], in_=xnT[:].rearrange("p k c -> p (k c)")),
                  reads=[('xnT', t) for t in range(NT)], dma=True)

        if stage >= 2:
            wq = sb2("wq", [128, 8, 128], BF16)
            wk = sb2("wk", [128, 8, 128], BF16)
            wv = sb2("wv", [128, 8, 128], BF16)
            qT = sb2("qT", [128, LP], BF16)
            kT = sb2("kT", [128, LP], BF16)
            vv = sb2("vv", [128, NT, 128], BF16)
            eb = [sb2("eb%d" % i, [128, 2, 512], BF16) for i in range(4)]
            spb = [sb2("spb%d" % i, [128, 2, 512], BF16) for i in range(3)]
            gb = [sb2("gb%d" % i, [128, 2, 512], BF16) for i in range(2)]
            wb = [sb2("wb%d" % i, [128, 2, 512], BF16) for i in range(2)]
            allx = [('xnT', t) for t in range(NT)]

            def load_wblk(dst, key, blk):
                S.add('sp', lambda e: e.dma_start(
                    out=dst[:].rearrange("p k c -> p (k c)"), in_=wscr[blk * 128:(blk + 1) * 128, :]),
                    reads=[('wscr', blk)], writes=[key], dma=True)

            gstep = 0
            for hp in range(8):
                load_wblk(wq, 'wq', BLK_Q + hp)
                load_wblk(wk, 'wk', BLK_K + hp)
                load_wblk(wv, 'wv', BLK_V + hp)
                pcnt = 0
                for (dst, dkey, wt, wkey, scale) in ((qT, 'qT', wq, 'wq', 0.125), (kT, 'kT', wk, 'wk', 1.0)):
                    for tb in range(9):
                        c0 = tb * 512
                        n = min(512, LP - c0)
                        b = pcnt % 4
                        pcnt += 1
                        for k in range(8):
                            S.add('pe', lambda e, b=b, n=n, wt=wt, k=k, c0=c0: e.matmul(
                                bank(b)[:, 0:n], lhsT=wt[:, k, :], rhs=xnT[:, k, c0:c0 + n],
                                start=(k == 0), stop=(k == 7)),
                                reads=[wkey] + allx[c0 // 128:(c0 + n) // 128], writes=bkeys(b))
                        S.add('act', lambda e, b=b, n=n, dst=dst, c0=c0, scale=scale: e.activation(
                            out=dst[:, c0:c0 + n], in_=bank(b)[:, 0:n], func=AF.Copy, scale=scale),
                            reads=bkeys(b), writes=[(dkey, j) for j in range(c0 // 128, (c0 + n) // 128)] + bkeys(b))
                for t0 in range(0, NT, 4):
                    nt = min(4, NT - t0)
                    b = pcnt % 4
                    pcnt += 1
                    for j in range(nt):
                        t = t0 + j
                        for k in range(8):
                            S.add('pe', lambda e, b=b, j=j, t=t, k=k: e.matmul(
                                bank(b)[:, j * 128:(j + 1) * 128], lhsT=xnT[:, k, t * 128:(t + 1) * 128],
                                rhs=wv[:, k, :], start=(k == 0), stop=(k == 7)),
                                reads=['wv', ('xnT', t)], writes=bkeys(b))
                    S.add('dve', lambda e, b=b, nt=nt, t0=t0: e.tensor_copy(
                        out=vv[:, t0:t0 + nt, :], in_=bank(b)[:, 0:nt * 128].rearrange("p (t c) -> p t c", t=nt)),
                        reads=bkeys(b), writes=[('vv', t) for t in range(t0, t0 + nt)] + bkeys(b))

                steps = []
                for qg in range(9):
                    qb0 = qg * 4
                    nq = min(4, NT - qb0)
                    qb1 = qb0 + nq - 1
                    for kb in range(qb1, -1, -1):
                        steps.append(dict(qg=qg, kb=kb, qb0=qb0, qb1=qb1, NQ=nq * 128, qc0=qb0 * 128,
                                          ob=6 + (qg % 2), co=max(0, kb - qb0) * 128,
                                          first=(kb == qb1), last=(kb == 0)))
                cb = 4
                c3 = bank(cb, 2).rearrange("p (h c) -> p h c", h=2)

                def fZ(st, gs):
                    co, NQ, kb, qc0 = st['co'], st['NQ'], st['kb'], st['qc0']
                    zs = (gs % 2) * 2
                    z3 = bank(zs, 2).rearrange("p (h c) -> p h c", h=2)
                    for h in range(2):
                        r0 = 64 * h
                        S.add('pe', lambda e, h=h, r0=r0, z3=z3: e.matmul(
                            z3[:, h, co:NQ], lhsT=kT[r0:r0 + 64, kb * 128:(kb + 1) * 128],
                            rhs=qT[r0:r0 + 64, qc0 + co:qc0 + NQ], start=True, stop=True),
                            reads=[('kT', kb)] + [('qT', j) for j in range(st['qb0'] + co // 128, st['qb1'] + 1)],
                            writes=bkeys(zs + h))

                def fE(st, gs):
                    co, NQ, kb = st['co'], st['NQ'], st['kb']
                    zs = (gs % 2) * 2
                    i = gs % 4
                    z3 = bank(zs, 2).rearrange("p (h c) -> p h c", h=2)
                    S.add('act', lambda e: e.activation(
                        out=eb[i][:, :, co:NQ], in_=z3[:, :, co:NQ], func=AF.Exp),
                        reads=bkeys(zs, 2), writes=[('eb', i)] + bkeys(zs, 2))
                    if kb >= st['qb0']:
                        S.add('dve', lambda e: e.tensor_tensor(
                            out=eb[i][:, :, co:co + 128], in0=eb[i][:, :, co:co + 128],
                            in1=mdiag_b[:].unsqueeze(1).broadcast_to([128, 2, 128]), op=ALU.mult),
                            reads=[('eb', i), 'mdiag_b'], writes=[('eb', i)])
                    if kb == 0:
                        S.add('dve', lambda e: e.tensor_scalar(
                            out=eb[i][:, :, co:NQ], in0=eb[i][:, :, co:NQ], scalar1=padm[:, 0:1],
                            scalar2=None, op0=ALU.mult),
                            reads=[('eb', i), 'padm'], writes=[('eb', i)])

                def fL(st, gs):
                    co, NQ = st['co'], st['NQ']
                    i = gs % 3
                    ie = gs % 4
                    S.add('act', lambda e: e.activation(
                        out=spb[i][:, :, co:NQ], in_=eb[ie][:, :, co:NQ], func=AF.Ln, bias=1.0),
                        reads=[('eb', ie)], writes=[('spb', i)])

                def fT(st, gs):
                    co, NQ = st['co'], st['NQ']
                    i = gs % 3
                    for h in range(2):
                        S.add('pe', lambda e, h=h: e.matmul(
                            c3[:, h, co:NQ], lhsT=tri_b[:], rhs=spb[i][:, h, co:NQ],
                            start=st['first'], stop=False, skip_group_check=True),
                            reads=[('spb', i), 'tri_b'], writes=bkeys(cb + h))

                def fG(st, gs):
                    co, NQ = st['co'], st['NQ']
                    i = gs % 2
                    S.add('act', lambda e: e.activation(
                        out=gb[i][:, :, co:NQ], in_=c3[:, :, co:NQ], func=AF.Exp, scale=-1.0),
                        reads=bkeys(cb, 2), writes=[('gb', i)] + bkeys(cb, 2))

                def fT2(st, gs):
                    co, NQ = st['co'], st['NQ']
                    i = gs % 3
                    if st['last']:
                        return
                    for h in range(2):
                        S.add('pe', lambda e, h=h: e.matmul(
                            c3[:, h, co:NQ], lhsT=tric_b[:], rhs=spb[i][:, h, co:NQ],
                            start=False, stop=False, skip_group_check=True),
                            reads=[('spb', i), 'tric_b'], writes=bkeys(cb + h))

                def fW(st, gs):
                    co, NQ = st['co'], st['NQ']
                    i = gs % 2
                    S.add('dve', lambda e: e.tensor_tensor(
                        out=wb[i][:, :, co:NQ], in0=eb[gs % 4][:, :, co:NQ], in1=gb[i][:, :, co:NQ], op=ALU.mult),
                        reads=[('eb', gs % 4), ('gb', i)], writes=[('wb', i)])

                def fV(st, gs, hp=hp):
                    co, NQ, kb, ob, qc0 = st['co'], st['NQ'], st['kb'], st['ob'], st['qc0']
                    i = gs % 2
                    for h in range(2):
                        r0 = 64 * h
                        S.add('pe', lambda e, h=h, r0=r0: e.matmul(
                            bank(ob)[r0:r0 + 64, co:NQ], lhsT=vv[:, kb, r0:r0 + 64], rhs=wb[i][:, h, co:NQ],
                            start=st['first'], stop=False, skip_group_check=True),
                            reads=[('wb', i), ('vv', kb)], writes=bkeys(ob))
                    if st['last']:
                        S.add('dve', lambda e: e.tensor_copy(
                            out=ysbT[:, hp, qc0:qc0 + NQ], in_=bank(ob)[:, 0:NQ]),
                            reads=bkeys(ob), writes=[('ysbT', hp, st['qg'])] + bkeys(ob))

                n = len(steps)
                for i in range(-2, n + 2):
                    if 0 <= i - 1 < n:
                        fT(steps[i - 1], gstep + i - 1)
                    if 0 <= i + 2 < n:
                        fZ(steps[i + 2], gstep + i + 2)
                    if 0 <= i - 2 < n:
                        fV(steps[i - 2], gstep + i - 2)
                    if 0 <= i + 1 < n:
                        fE(steps[i + 1], gstep + i + 1)
                    if 0 <= i < n:
                        fL(steps[i], gstep + i)
                    if 0 <= i - 1 < n:
                        fG(steps[i - 1], gstep + i - 1)
                        fT2(steps[i - 1], gstep + i - 1)
                        fW(steps[i - 1], gstep + i - 1)
                gstep += n

            if debug:
                S.add('sp', lambda e: e.dma_start(out=dbg['ysbT'], in_=ysbT[:].rearrange("p k c -> p (k c)")),
                      reads=[('ysbT', hp, qg) for hp in range(8) for qg in range(9)], dma=True)

        def barrier():
            tails = {e: [op for op in S.ops[e] if not op.is_dma][-1:] for e in ENGS}
            dts = [op for q in ENGS for op in S.dma_ops[q][-NDSEM:]]
            for e in ENGS:
                ex = [o for p in ENGS if p != e for o in tails[p]] + dts
                S.add(e, lambda eng: eng.nop(), extra=ex)

        if stage >= 3:
            barrier()
            es2.close()
            NG = 384
            ident_f = sb("ident_f", [128, 128], F32)
            UTf = sb("UTf", [128, 128], F32)
            SGf = sb("SGf", [128, 128], F32)
            ONESf = sb("ONESf", [128, 128], F32)
            ones_b = sb("ones_b", [128, 128], BF16)
            cmask(ident_f, -1, 1, ALU.is_equal)
            cmask(UTf, 1, -1, ALU.is_ge)
            cmask(SGf, -1, 1, ALU.is_gt)
            S.add('pool', lambda e: e.memset(ONESf[:], 1.0), writes=['ONESf'])
            S.add('pool', lambda e: e.memset(ones_b[:], 1.0), writes=['ones_b'])
            ffnp = sb("ffnp", [128, 128], F32)
            S.add('sp', lambda e: e.dma_start(out=ffnp[:], in_=ffnp_d[:, :]), writes=['ffnp'], dma=True)
            smallbc = sb("smallbc", [128, 64], F32)
            S.add('sp', lambda e: e.dma_start(out=smallbc[:], in_=small_d[0:1, :].partition_broadcast(128)),
                  writes=['smallbc'], dma=True)
            Abc = sb("Abc", [128, 16], F32)
            S.add('act', lambda e: e.activation(out=Abc[:], in_=smallbc[:, 16:32], func=AF.Exp),
                  reads=['smallbc'], writes=['Abc'])
            S.add('dve', lambda e: e.tensor_scalar(out=Abc[:], in0=Abc[:], scalar1=-1.0, scalar2=None, op0=ALU.mult),
                  reads=['Abc'], writes=['Abc'])
            wdt_b = sb("wdt_b", [128, 8, 16], BF16)
            S.add('pool', lambda e: e.dma_start(out=wdt_b[:].rearrange("p k c -> p (k c)"), in_=wdt_d[:, :]),
                  writes=['wdt_b'], dma=True)
            gpost_bc = sb("gpost_bc", [128, D], F32)
            gfpost_bc = sb("gfpost_bc", [128, D], F32)
            S.add('sp', lambda e: e.dma_start(out=gpost_bc[:], in_=vecs_d[1:2, :].partition_broadcast(128)),
                  writes=['gpost_bc'], dma=True)
            S.add('sp', lambda e: e.dma_start(out=gfpost_bc[:], in_=vecs_d[3:4, :].partition_broadcast(128)),
                  writes=['gfpost_bc'], dma=True)

            NWB = 4
            wbuf = [sb("wbuf%d" % i, [128, 1024], BF16) for i in range(NWB)]
            wcnt = [0]

            def wload(blk):
                i = wcnt[0] % NWB
                wcnt[0] += 1
                S.add('sp', lambda e: e.dma_start(out=wbuf[i][:], in_=wscr[blk * 128:(blk + 1) * 128, :]),
                      reads=[('wscr', blk)], writes=[('wbuf', i)], dma=True)
                return wbuf[i], ('wbuf', i)

            state = sb("state", [128, 1024], F32)
            state_b = sb("state_b", [128, 1024], BF16)
            S.add('pool', lambda e: e.memset(state[:], 0.0), writes=['state'])
            S.add('pool', lambda e: e.memset(state_b[:], 0.0), writes=['state_b'])
            halo = sb("halo", [128, 12, 3], F32)
            ghalo = sb("ghalo", [128, NFC, 2], F32)
            S.add('pool', lambda e: e.memset(halo[:], 0.0), writes=['halo'])
            S.add('pool', lambda e: e.memset(ghalo[:], 0.0), writes=['ghalo'])

            hbuf = sb("hbuf", [128, 3, D], F32)
            actT = sb("actT", [128, 8, NG], BF16)
            R1 = sb("R1", [128, 6 * D], F32)
            zs = R1[:, 0:3 * D].rearrange("p (j c) -> p j c", j=3)
            Xtok = R1[:, 3 * D:6 * D].rearrange("p (j c) -> p j c", j=3)
            aT = R1[:, 0:NFC * NG // 2].bitcast(BF16).rearrange("p (f c) -> p f c", f=NFC)
            BT = sb("BT", [128, 2, NG], BF16)
            CT = sb("CT", [128, 2, NG], BF16)
            Btok = sb("Btok", [128, 3, 256], BF16)
            cbuf = [sb("cbuf%d" % i, [128, NG + 3], F32) for i in range(4)]
            cacc = [sb("cacc%d" % i, [128, NG], F32) for i in range(4)]
            sfm = [sb("sfm%d" % i, [128, NG], F32) for i in range(4)]
            t1 = sb("t1", [128, D], F32)
            RH = sb("RH", [128, 16, 128], F32)
            Xdt = sb("Xdt", [128, D], BF16)
            Xdec = sb("Xdec", [128, D], BF16)
            Eb = [sb("Eb%d" % i, [128, 512], F32) for i in range(2)]
            MTb = [sb("MTb%d" % i, [128, 4, 128], BF16) for i in range(2)]
            CBm = sb("CBm", [128, 2, 128], F32)
            tokbfs = [sb("tokbf%d" % i, [128, D], BF16) for i in range(3)]
            ygnT = sb("ygnT", [128, 8, NG], BF16)
            ysbn = sb("ysbn", [128, 8, NG], BF16)
            r_bc = sb("r_bc", [128, NG], F32)
            lnr = r_bc
            sqb = [sb("sqb%d" % i, [128, NG], BF16) for i in range(2)]
            dtt = sb("dtt", [128, 3, 16], F32)
            att = sb("att", [128, 3, 16], F32)
            dte = sb("dte", [128, 3, 16], F32)
            ex48 = sb("ex48", [128, 48], F32)
            dtd = sb("dtd", [128, 16], F32)
            st3 = sb("st3", [128, 64], F32)
            gcb = cbuf
            RHf = RH[:].rearrange("p h c -> p (h c)")
            tmps = [(t1[:], ['t1']), (RHf[:, 0:D], ['RH', ('tmpf', 1)]), (RHf[:, D:2 * D], ['RH', ('tmpf', 2)])]
            glb = sfm

            def bc3(ap2, n):
                return ap2.unsqueeze(2).broadcast_to([128, 16, n])

            def v3(ap2):
                return ap2.rearrange("p (h d) -> p h d", h=16)

            def run_rr(gens):
                gens = list(gens)
                while gens:
                    nxt = []
                    for gq in gens:
                        try:
                            next(gq)
                            nxt.append(gq)
                        except StopIteration:
                            pass
                    gens = nxt

            def rinfo(slot):
                return st3[:, 4 * slot + 2:4 * slot + 3], (('st3', slot), 2)

            def g_rstd3(src, srcreads, slot):
                col = 4 * slot
                key = ('st3', slot)
                S.add('act', lambda e: e.activation(out=junk[:], in_=src, func=AF.Square,
                                                    accum_out=st3[:, col:col + 1]),
                      reads=srcreads, writes=[(key, 0)])
                yield
                S.add('dve', lambda e: e.tensor_scalar(out=st3[:, col + 1:col + 2], in0=st3[:, col:col + 1],
                                                       scalar1=1.0 / D, scalar2=EPS, op0=ALU.mult, op1=ALU.add),
                      reads=[(key, 0)], writes=[(key, 1)])
                yield
                S.add('pool', lambda e: e.tensor_tensor(out=st3[:, col + 2:col + 3], in0=st3[:, col + 1:col + 2],
                                                        in1=mhalf[:], op=ALU.pow),
                      reads=[(key, 1), 'mhalf'], writes=[(key, 2)])
                yield

            def rstd3(src, srcreads, slot):
                for _ in g_rstd3(src, srcreads, slot):
                    pass
                return rinfo(slot)

            def g_norm_transpose(j, src, srckeys, slot, dstT, dkey, pb=7):
                rcol, rkey = rinfo(slot)
                tb = tokbfs[j % 3]
                tk = ('tokbf', j % 3)
                S.add('dve', lambda e: e.tensor_scalar(out=tb[:], in0=src, scalar1=rcol, scalar2=None,
                                                       op0=ALU.mult),
                      reads=srckeys + [rkey], writes=[tk])
                yield
                pbv = bank(pb).bitcast(BF16)
                for k in range(8):
                    S.add('pe', lambda e, k=k: e.transpose(
                        out=pbv[:, k * 128:(k + 1) * 128], in_=tb[:, k * 128:(k + 1) * 128], identity=ident_b[:]),
                        reads=[tk, 'ident_b'], writes=bkeys(pb))
                yield
                S.add('act', lambda e: e.activation(
                    out=dstT[:, :, j * 128:(j + 1) * 128], in_=pbv.rearrange("p (k c) -> p k c", k=8), func=AF.Copy),
                    reads=bkeys(pb), writes=[(dkey, j)] + bkeys(pb))
                yield

            def norm_transpose(j, src, srckeys, slot, dstT, dkey, pb=7):
                for _ in g_norm_transpose(j, src, srckeys, slot, dstT, dkey, pb):
                    pass

            def apply_gain(dstT, dkey, gcol):
                for k in range(8):
                    S.add('dve', lambda e, k=k: e.tensor_scalar(
                        out=dstT[:, k, :], in0=dstT[:, k, :], scalar1=pp[:, gcol + k:gcol + k + 1], scalar2=None,
                        op0=ALU.mult),
                        reads=[(dkey, jj) for jj in range(3)] + ['pp'], writes=[(dkey, jj) for jj in range(3)])

            def do_group(g):
                t0 = 3 * g
                gc0 = t0 * 128
                hk = [('hbuf', j) for j in range(3)]
                def chain_a(j):
                    load_x_tile(t0 + j, hbuf[:, j, :], ('hbuf', j))
                    yield
                    yield from g_rstd3(hbuf[:, j, :], [('hbuf', j)], j)
                    yield from g_norm_transpose(j, hbuf[:, j, :], [('hbuf', j)], j, actT, 'actT', pb=7 - j)

                run_rr([chain_a(j) for j in range(3)])
                apply_gain(actT, 'actT', 76)
                ak = [('actT', j) for j in range(3)]
                if g == 0:
                    dbgdump('actT', actT[:].rearrange("p k c -> p (k c)"), [128, 8 * NG], BF16, ak)
                for cbk in range(8):
                    w, wkey = wload(BLK_Z + cbk)
                    w3 = w[:].rearrange("p (k c) -> p k c", k=8)
                    for j in range(3):
                        b = 2 * j + cbk // 4
                        for k in range(8):
                            S.add('pe', lambda e, b=b, j=j, k=k, w3=w3, cbk=cbk: e.matmul(
                                bank(b)[:, (cbk % 4) * 128:(cbk % 4 + 1) * 128], lhsT=actT[:, k, j * 128:(j + 1) * 128],
                                rhs=w3[:, k, :], start=(k == 0), stop=(k == 7)),
                                reads=[wkey, ('actT', j)], writes=bkeys(b))
                for j in range(3):
                    S.add('act', lambda e, j=j: e.activation(out=zs[:, j, :], in_=bank(2 * j, 2), func=AF.Silu),
                          reads=bkeys(2 * j, 2), writes=[('zs', j)] + bkeys(2 * j, 2))
                for j in range(3):
                    for k in range(8):
                        S.add('pe', lambda e, j=j, k=k: e.matmul(
                            bank(6)[:, j * 16:(j + 1) * 16], lhsT=actT[:, k, j * 128:(j + 1) * 128],
                            rhs=wdt_b[:, k, :], start=(k == 0), stop=(k == 7)),
                            reads=['wdt_b', ('actT', j)], writes=bkeys(6))
                S.add('dve', lambda e: e.tensor_tensor(
                    out=dte[:], in0=bank(6)[:, 0:48].rearrange("p (j h) -> p j h", j=3),
                    in1=smallbc[:, 0:16].unsqueeze(1).broadcast_to([128, 3, 16]), op=ALU.add),
                    reads=bkeys(6) + ['smallbc'], writes=['dte'] + bkeys(6))
                S.add('act', lambda e: e.activation(out=dte[:], in_=dte[:], func=AF.Exp), reads=['dte'], writes=['dte'])
                S.add('act', lambda e: e.activation(out=dtt[:], in_=dte[:], func=AF.Ln, bias=1.0),
                      reads=['dte'], writes=['dtt'])
                if g == 0:
                    S.add('dve', lambda e: e.tensor_scalar(out=dtt[:, 0, :], in0=dtt[:, 0, :], scalar1=padm[:, 0:1],
                                                           scalar2=None, op0=ALU.mult),
                          reads=['dtt', 'padm'], writes=['dtt'])
                S.add('dve', lambda e: e.tensor_tensor(
                    out=att[:], in0=dtt[:], in1=Abc[:].unsqueeze(1).broadcast_to([128, 3, 16]), op=ALU.mult),
                    reads=['dtt', 'Abc'], writes=['att'])
                pend = []

                def emit_tr(jb, i):
                    for j in range(3):
                        S.add('pe', lambda e, i=i, j=j, jb=jb: e.transpose(
                            out=bank(2 * j + jb // 4)[:, (jb % 4) * 128:(jb % 4 + 1) * 128],
                            in_=sfm[i][:, j * 128:(j + 1) * 128], identity=ident_f[:]),
                            reads=[('sfm', i), 'ident_f'], writes=bkeys(2 * j + jb // 4))

                for jb in range(12):
                    w, wkey = wload(BLK_XBC + jb)
                    w3 = w[:].rearrange("p (k c) -> p k c", k=8)
                    b = 6 + (jb % 2)
                    i = jb % 4
                    for k in range(8):
                        S.add('pe', lambda e, b=b, k=k, w3=w3: e.matmul(
                            bank(b)[:, 0:NG], lhsT=w3[:, k, :], rhs=actT[:, k, :], start=(k == 0), stop=(k == 7)),
                            reads=[wkey] + ak, writes=bkeys(b))
                    if len(pend) >= 2:
                        emit_tr(*pend.pop(0))
                    S.add('act', lambda e, b=b, i=i: e.activation(out=cbuf[i][:, 3:3 + NG], in_=bank(b)[:, 0:NG],
                                                                  func=AF.Copy),
                          reads=bkeys(b), writes=[('cbuf', i)] + bkeys(b))
                    S.add('pool', lambda e, i=i, jb=jb: e.tensor_copy(out=cbuf[i][:, 0:3], in_=halo[:, jb, :]),
                          reads=['halo'], writes=[('cbuf', i)])
                    S.add('pool', lambda e, i=i, jb=jb: e.tensor_copy(out=halo[:, jb, :], in_=cbuf[i][:, NG:NG + 3]),
                          reads=[('cbuf', i)], writes=['halo'])
                    for kk in range(4):
                        wc = pp[:, 16 + jb * 4 + kk:16 + jb * 4 + kk + 1]
                        if kk == 0:
                            S.add('dve', lambda e, i=i, wc=wc: e.tensor_scalar(
                                out=cacc[i][:], in0=cbuf[i][:, 0:NG], scalar1=wc, scalar2=None, op0=ALU.mult),
                                reads=[('cbuf', i), 'pp'], writes=[('cacc', i)])
                        else:
                            S.add('dve', lambda e, i=i, wc=wc, kk=kk: e.scalar_tensor_tensor(
                                out=cacc[i][:], in0=cbuf[i][:, kk:kk + NG], scalar=wc, in1=cacc[i][:],
                                op0=ALU.mult, op1=ALU.add),
                                reads=[('cbuf', i), 'pp', ('cacc', i)], writes=[('cacc', i)])
                    bias = pp[:, 64 + jb:65 + jb]
                    if jb < 8:
                        S.add('act', lambda e, i=i, bias=bias: e.activation(out=sfm[i][:], in_=cacc[i][:], func=AF.Silu,
                                                                            bias=bias),
                              reads=[('cacc', i), 'pp'], writes=[('sfm', i)])
                        pend.append((jb, i))
                    else:
                        dst = BT if jb < 10 else CT
                        dk = 'BT' if jb < 10 else 'CT'
                        gg = jb % 2
                        S.add('act', lambda e, i=i, bias=bias, dst=dst, gg=gg: e.activation(
                            out=dst[:, gg, :], in_=cacc[i][:], func=AF.Silu, bias=bias),
                            reads=[('cacc', i), 'pp'], writes=[(dk, gg)])
                while pend:
                    emit_tr(*pend.pop(0))
                for j in range(3):
                    S.add('act', lambda e, j=j: e.activation(out=Xtok[:, j, :], in_=bank(2 * j, 2), func=AF.Copy),
                          reads=bkeys(2 * j, 2), writes=[('Xtok', j)] + bkeys(2 * j, 2))
                pbv = bank(7).bitcast(BF16)
                for gg in range(2):
                    for j in range(3):
                        S.add('pe', lambda e, j=j, gg=gg, pbv=pbv: e.transpose(
                            out=pbv[:, j * 256 + gg * 128:j * 256 + (gg + 1) * 128],
                            in_=BT[:, gg, j * 128:(j + 1) * 128], identity=ident_b[:]),
                            reads=[('BT', gg), 'ident_b'], writes=bkeys(7))
                S.add('dve', lambda e, pbv=pbv: e.tensor_copy(
                    out=Btok[:], in_=pbv[:, 0:768].rearrange("p (j c) -> p j c", j=3)),
                    reads=bkeys(7), writes=['Btok'] + bkeys(7))
                if g == 0:
                    dbgdump('zs', R1[:, 0:3 * D], [128, 3 * D], F32, [('zs', jj) for jj in range(3)])
                    dbgdump('Xtok', R1[:, 3 * D:6 * D], [128, 3 * D], F32, [('Xtok', jj) for jj in range(3)])
                    dbgdump('dtt', dtt[:].rearrange("p j h -> p (j h)"), [128, 48], F32, ['dtt'])
                    dbgdump('att', att[:].rearrange("p j h -> p (j h)"), [128, 48], F32, ['att'])
                    dbgdump('BT', BT[:].rearrange("p g c -> p (g c)"), [128, 2 * NG], BF16, [('BT', 0), ('BT', 1)])
                    dbgdump('CT', CT[:].rearrange("p g c -> p (g c)"), [128, 2 * NG], BF16, [('CT', 0), ('CT', 1)])
                    dbgdump('Btok', Btok[:].rearrange("p j c -> p (j c)"), [128, 768], BF16, ['Btok'])
                for j in range(3):
                    tc0 = j * 128
                    a_j = att[:, j, :]
                    for ci, lt in enumerate((UTf, SGf, ONESf)):
                        S.add('pe', lambda e, ci=ci, lt=lt, a_j=a_j: e.matmul(
                            bank(0)[:, ci * 16:(ci + 1) * 16], lhsT=lt[:], rhs=a_j, start=True, stop=True),
                            reads=['att', 'UTf', 'SGf', 'ONESf'], writes=bkeys(0))
                    S.add('act', lambda e: e.activation(out=ex48[:], in_=bank(0)[:, 0:48], func=AF.Exp),
                          reads=bkeys(0), writes=['ex48'] + bkeys(0))
                    S.add('dve', lambda e, a_j=a_j: e.tensor_tensor(
                        out=RH[:], in0=UTf[:].unsqueeze(1).broadcast_to([128, 16, 128]), in1=bc3(a_j, 128),
                        op=ALU.mult),
                        reads=['att', 'UTf'], writes=['RH'])
                    for gg in range(2):
                        S.add('pe', lambda e, gg=gg, tc0=tc0: e.matmul(
                            bank(0)[:, 128 + gg * 128:256 + gg * 128], lhsT=BT[:, gg, tc0:tc0 + 128],
                            rhs=CT[:, gg, tc0:tc0 + 128], start=True, stop=True),
                            reads=[('BT', gg), ('CT', gg)], writes=bkeys(0))
                    S.add('dve', lambda e: e.tensor_tensor(
                        out=CBm[:], in0=bank(0)[:, 128:384].rearrange("p (g c) -> p g c", g=2),
                        in1=UTf[:].unsqueeze(1).broadcast_to([128, 2, 128]), op=ALU.mult),
                        reads=bkeys(0) + ['UTf'], writes=['CBm'] + bkeys(0))
                    for gg in range(2):
                        S.add('pe', lambda e, gg=gg, tc0=tc0: e.matmul(
                            bank(5 + gg), lhsT=CT[:, gg, tc0:tc0 + 128], rhs=state_b[:, gg * 512:(gg + 1) * 512],
                            start=True, stop=True),
                            reads=[('CT', gg), 'state_b'], writes=bkeys(5 + gg))
                    S.add('dve', lambda e, j=j: e.tensor_tensor(out=dtd[:], in0=dtt[:, j, :], in1=ex48[:, 16:32],
                                                                op=ALU.mult),
                          reads=['dtt', 'ex48'], writes=['dtd'])
                    S.add('dve', lambda e, j=j: e.tensor_tensor(out=v3(Xdt[:]), in0=v3(Xtok[:, j, :]),
                                                                in1=bc3(dtt[:, j, :], 64), op=ALU.mult),
                          reads=[('Xtok', j), 'dtt'], writes=['Xdt'])
                    S.add('dve', lambda e, j=j: e.tensor_tensor(out=v3(Xdec[:]), in0=v3(Xtok[:, j, :]),
                                                                in1=bc3(dtd[:], 64), op=ALU.mult),
                          reads=[('Xtok', j), 'dtd'], writes=['Xdec'])
                    def fD(hq):
                        i = hq % 2
                        S.add('pe', lambda e, i=i, hq=hq: e.matmul(
                            bank(1 + i), lhsT=SGf[:], rhs=RH[:, 4 * hq:4 * hq + 4, :].rearrange("p h c -> p (h c)"),
                            start=True, stop=True),
                            reads=['RH', 'SGf'], writes=bkeys(1 + i))
                        S.add('act', lambda e, i=i: e.activation(out=Eb[i][:], in_=bank(1 + i), func=AF.Exp),
                              reads=bkeys(1 + i), writes=[('Eb', i)] + bkeys(1 + i))

                    def fY(hq):
                        i = hq % 2
                        gg = hq // 2
                        S.add('dve', lambda e, i=i, gg=gg: e.tensor_tensor(
                            out=MTb[i][:], in0=Eb[i][:].rearrange("p (h c) -> p h c", h=4),
                            in1=CBm[:, gg:gg + 1, :].broadcast_to([128, 4, 128]), op=ALU.mult),
                            reads=[('Eb', i), 'CBm'], writes=[('MTb', i)])
                        for hh in range(4):
                            h = 4 * hq + hh
                            S.add('pe', lambda e, i=i, hh=hh, h=h: e.matmul(
                                bank(3 + h // 8)[:, (h % 8) * 64:(h % 8 + 1) * 64], lhsT=MTb[i][:, hh, :],
                                rhs=Xdt[:, h * 64:(h + 1) * 64], start=True, stop=True),
                                reads=[('MTb', i), 'Xdt'], writes=bkeys(3 + h // 8))

                    fD(0)
                    fD(1)
                    fY(0)
                    fD(2)
                    fY(1)
                    fD(3)
                    fY(2)
                    fY(3)
                    S.add('dve', lambda e: e.tensor_tensor(out=v3(t1[:]), in0=v3(bank(5, 2)), in1=bc3(ex48[:, 0:16], 64),
                                                           op=ALU.mult),
                          reads=bkeys(5, 2) + ['ex48'], writes=['t1'] + bkeys(5, 2))
                    S.add('dve', lambda e: e.tensor_tensor(out=t1[:], in0=bank(3, 2), in1=t1[:], op=ALU.add),
                          reads=bkeys(3, 2) + ['t1'], writes=['t1'] + bkeys(3, 2))
                    for gg in range(2):
                        S.add('pe', lambda e, gg=gg, j=j: e.matmul(
                            bank(5 + gg), lhsT=Btok[:, j, gg * 128:(gg + 1) * 128], rhs=Xdec[:, gg * 512:(gg + 1) * 512],
                            start=True, stop=True),
                            reads=['Btok', 'Xdec'], writes=bkeys(5 + gg))
                    S.add('dve', lambda e, j=j: e.tensor_tensor(out=v3(Xtok[:, j, :]), in0=v3(Xtok[:, j, :]),
                                                                in1=bc3(smallbc[:, 32:48], 64), op=ALU.mult),
                          reads=[('Xtok', j), 'smallbc'], writes=[('Xtok', j)])
                    S.add('dve', lambda e, j=j: e.tensor_tensor(out=t1[:], in0=t1[:], in1=Xtok[:, j, :], op=ALU.add),
                          reads=[('Xtok', j), 't1'], writes=['t1'])
                    S.add('dve', lambda e, j=j: e.tensor_tensor(out=t1[:], in0=t1[:], in1=zs[:, j, :], op=ALU.mult),
                          reads=[('zs', j), 't1'], writes=['t1'])
                    S.add('dve', lambda e: e.tensor_tensor(out=v3(state[:]), in0=v3(state[:]), in1=bc3(ex48[:, 32:48], 64),
                                                           op=ALU.mult),
                          reads=['state', 'ex48'], writes=['state'])
                    S.add('dve', lambda e: e.tensor_tensor(out=state[:], in0=bank(5, 2), in1=state[:], op=ALU.add),
                          reads=bkeys(5, 2) + ['state'], writes=['state'] + bkeys(5, 2))
                    S.add('act', lambda e: e.activation(out=state_b[:], in_=state[:], func=AF.Copy),
                          reads=['state'], writes=['state_b'])
                    if g == 0:
                        dbgdump('yg%d' % j, t1[:], [128, D], F32, ['t1'])
                        dbgdump('ex48_%d' % j, ex48[:], [128, 48], F32, ['ex48'])
                    rstd3(t1[:], ['t1'], 3 + j)
                    norm_transpose(j, t1[:], ['t1'], 3 + j, ygnT, 'ygnT')
                apply_gain(ygnT, 'ygnT', 0)
                for k in range(8):
                    i = k % 2
                    S.add('act', lambda e, i=i, k=k: e.activation(out=sqb[i][:], in_=ysbT[:, k, gc0:gc0 + NG],
                                                                  func=AF.Square),
                          reads=[('ysbT', k, q) for q in range(9)], writes=[('sqb', i)])
                    S.add('pe', lambda e, i=i, k=k: e.matmul(bank(6)[:, 0:NG], lhsT=ones_b[:], rhs=sqb[i][:],
                                                             start=(k == 0), stop=(k == 7)),
                          reads=[('sqb', i), 'ones_b'], writes=bkeys(6))
                S.add('act', lambda e: e.activation(out=lnr[:], in_=bank(6)[:, 0:NG], func=AF.Ln, scale=1.0 / D, bias=EPS),
                      reads=bkeys(6), writes=['lnr', 'r_bc'] + bkeys(6))
                S.add('act', lambda e: e.activation(out=r_bc[:], in_=lnr[:], func=AF.Exp, scale=-0.5),
                      reads=['lnr', 'r_bc'], writes=['r_bc', 'lnr'])
                for k in range(8):
                    S.add('dve', lambda e, k=k: e.scalar_tensor_tensor(
                        out=ysbn[:, k, :], in0=ysbT[:, k, gc0:gc0 + NG], scalar=pp[:, 8 + k:9 + k], in1=r_bc[:],
                        op0=ALU.mult, op1=ALU.mult),
                        reads=[('ysbT', k, q) for q in range(9)] + ['pp', 'r_bc'], writes=[('ysbn', k)])
                for kc in range(16):
                    w, wkey = wload(BLK_OUT + kc)
                    src = ygnT if kc < 8 else ysbn
                    sk = [('ygnT', jj) for jj in range(3)] if kc < 8 else [('ysbn', kc - 8)]
                    for j in range(3):
                        for hf in range(2):
                            S.add('pe', lambda e, j=j, hf=hf, w=w, src=src, kc=kc: e.matmul(
                                bank(2 * j + hf), lhsT=src[:, kc % 8, j * 128:(j + 1) * 128],
                                rhs=w[:, hf * 512:(hf + 1) * 512], start=(kc == 0), stop=(kc == 15)),
                                reads=[wkey] + sk, writes=bkeys(2 * j + hf))
                def chain_post(j, gbc, gkey, slot0, final):
                    tmp, tkk = tmps[j]
                    S.add('dve', lambda e: e.tensor_tensor(
                        out=tmp, in0=bank(2 * j, 2), in1=gbc[:], op=ALU.mult),
                        reads=bkeys(2 * j, 2) + [gkey], writes=tkk + bkeys(2 * j, 2))
                    yield
                    yield from g_rstd3(bank(2 * j, 2), bkeys(2 * j, 2), slot0 + j)
                    rc, rk = rinfo(slot0 + j)
                    S.add('dve', lambda e: e.scalar_tensor_tensor(
                        out=hbuf[:, j, :], in0=tmp, scalar=rc, in1=hbuf[:, j, :], op0=ALU.mult, op1=ALU.add),
                        reads=tkk + [rk, ('hbuf', j)], writes=[('hbuf', j)])
                    yield
                    t = t0 + j
                    if final and t >= 1:
                        S.add('sp', lambda e: e.dma_start(out=out_d[(t - 1) * 128:t * 128, :], in_=hbuf[:, j, :]),
                              reads=[('hbuf', j)], dma=True)

                run_rr([chain_post(j, gpost_bc, 'gpost_bc', 6, False) for j in range(3)])
                if g == 0:
                    dbgdump('ygnT', ygnT[:].rearrange("p k c -> p (k c)"), [128, 8 * NG], BF16, [('ygnT', jj) for jj in range(3)])
                    dbgdump('ysbn', ysbn[:].rearrange("p k c -> p (k c)"), [128, 8 * NG], BF16, [('ysbn', k) for k in range(8)])
                    dbgdump('r_bc', r_bc[:], [128, NG], F32, ['r_bc'])
                    dbgdump('h1', hbuf[:].rearrange("p j c -> p (j c)"), [128, 3 * D], F32, hk)
                    dbgdump('state', state[:], [128, D], F32, ['state'])
                def chain_e(j):
                    yield from g_rstd3(hbuf[:, j, :], [('hbuf', j)], 9 + j)
                    yield from g_norm_transpose(j, hbuf[:, j, :], [('hbuf', j)], 9 + j, actT, 'actT', pb=7 - j)

                run_rr([chain_e(j) for j in range(3)])
                apply_gain(actT, 'actT', 84)
                for fc in range(NFC):
                    wg, wgk = wload(BLK_GATE + fc)
                    wu, wuk = wload(BLK_UP + fc)
                    wg3 = wg[:].rearrange("p (k c) -> p k c", k=8)
                    wu3 = wu[:].rearrange("p (k c) -> p k c", k=8)
                    i = fc % 4
                    bg = fc % 3
                    bu = 3 + fc % 3
                    for k in range(8):
                        S.add('pe', lambda e, bg=bg, k=k, wg3=wg3: e.matmul(
                            bank(bg)[:, 0:NG], lhsT=wg3[:, k, :], rhs=actT[:, k, :], start=(k == 0), stop=(k == 7)),
                            reads=[wgk] + ak, writes=bkeys(bg))
                    for k in range(8):
                        S.add('pe', lambda e, bu=bu, k=k, wu3=wu3: e.matmul(
                            bank(bu)[:, 0:NG], lhsT=wu3[:, k, :], rhs=actT[:, k, :], start=(k == 0), stop=(k == 7)),
                            reads=[wuk] + ak, writes=bkeys(bu))
                    S.add('act', lambda e, bg=bg, i=i: e.activation(out=gcb[i][:, 2:2 + NG], in_=bank(bg)[:, 0:NG],
                                                                    func=AF.Copy),
                          reads=bkeys(bg), writes=[('cbuf', i)] + bkeys(bg))
                    S.add('pool', lambda e, i=i, fc=fc: e.tensor_copy(out=gcb[i][:, 0:2], in_=ghalo[:, fc, :]),
                          reads=['ghalo'], writes=[('cbuf', i)])
                    S.add('pool', lambda e, i=i, fc=fc: e.tensor_copy(out=ghalo[:, fc, :], in_=gcb[i][:, NG:NG + 2]),
                          reads=[('cbuf', i)], writes=['ghalo'])
                    for kk in range(3):
                        wc = ffnp[:, fc * 3 + kk:fc * 3 + kk + 1]
                        if kk == 0:
                            S.add('dve', lambda e, i=i, wc=wc: e.tensor_scalar(
                                out=cacc[i][:], in0=gcb[i][:, 0:NG], scalar1=wc, scalar2=None, op0=ALU.mult),
                                reads=[('cbuf', i), 'ffnp'], writes=[('cacc', i)])
                        else:
                            S.add('dve', lambda e, i=i, wc=wc, kk=kk: e.scalar_tensor_tensor(
                                out=cacc[i][:], in0=gcb[i][:, kk:kk + NG], scalar=wc, in1=cacc[i][:],
                                op0=ALU.mult, op1=ALU.add),
                                reads=[('cbuf', i), 'ffnp', ('cacc', i)], writes=[('cacc', i)])
                    S.add('act', lambda e, i=i, fc=fc: e.activation(out=glb[i][:], in_=cacc[i][:], func=AF.Gelu_apprx_tanh,
                                                                    bias=ffnp[:, 66 + fc:67 + fc]),
                          reads=[('cacc', i), 'ffnp'], writes=[('sfm', i)])
                    S.add('dve', lambda e, i=i, fc=fc, bu=bu: e.tensor_tensor(
                        out=aT[:, fc, :], in0=bank(bu)[:, 0:NG], in1=glb[i][:], op=ALU.mult),
                        reads=bkeys(bu) + [('sfm', i)],
                        writes=[('zs', jj) for jj in range(3)] + [('Xtok', jj) for jj in range(3)] + bkeys(bu))
                aTk = [('zs', jj) for jj in range(3)] + [('Xtok', jj) for jj in range(3)]
                for fc in range(NFC):
                    w, wkey = wload(BLK_DOWN + fc)
                    for j in range(3):
                        for hf in range(2):
                            S.add('pe', lambda e, j=j, hf=hf, w=w, fc=fc: e.matmul(
                                bank(2 * j + hf), lhsT=aT[:, fc, j * 128:(j + 1) * 128],
                                rhs=w[:, hf * 512:(hf + 1) * 512], start=(fc == 0), stop=(fc == NFC - 1)),
                                reads=[wkey] + aTk, writes=bkeys(2 * j + hf))
                run_rr([chain_post(j, gfpost_bc, 'gfpost_bc', 12, True) for j in range(3)])

            for g in range(11 if stage >= 4 else 1):
                do_group(g)

        tail = [op for q in ENGS for op in S.dma_ops[q][-NDSEM:]]
        S.add('sp', lambda e: e.nop(), extra=tail)
        es2.close()
        S.emit(nc, es)
    return nc


def _host_layout(inputs):
    f = np.float32
    w_in = np.asarray(inputs["w_in"], f)[0]
    w_out = np.asarray(inputs["w_out"], f)[0]
    w_up = np.asarray(inputs["w_up"], f)[0]
    w_down = np.asarray(inputs["w_down"], f)[0]

    def kblocks(w, c0, nb):
        sub = w[:, c0:c0 + nb * 128].reshape(8, 128, nb, 128)
        return np.ascontiguousarray(sub.transpose(2, 1, 0, 3)).reshape(nb * 128, 1024)

    blocks = [kblocks(w_in, OFF_Z, 8), kblocks(w_in, OFF_XBC, 12), kblocks(w_in, OFF_Q, 8),
              kblocks(w_in, OFF_K, 8), kblocks(w_in, OFF_V, 8),
              w_out.reshape(16 * 128, 1024),
              kblocks(w_up, 0, 22), kblocks(w_up, DFF, 22),
              w_down.reshape(22 * 128, 1024)]
    wblk = np.ascontiguousarray(np.concatenate(blocks, axis=0))
    assert wblk.shape == (NBLK * 128, 1024)
    wdt = np.ascontiguousarray(w_in[:, OFF_DT:OFF_DT + 16].reshape(8, 128, 16).transpose(1, 0, 2)).reshape(128, 128)
    vecs = np.zeros((8, D), f)
    vecs[0] = np.asarray(inputs["mix_pre_g"], f)[0]
    vecs[1] = np.asarray(inputs["mix_post_g"], f)[0]
    vecs[2] = np.asarray(inputs["ffn_pre_g"], f)[0]
    vecs[3] = np.asarray(inputs["ffn_post_g"], f)[0]
    pp = np.zeros((128, 128), f)
    pp[:, 0:8] = np.asarray(inputs["ssd_norm_g"], f)[0].reshape(8, 128).T
    pp[:, 8:16] = np.asarray(inputs["sb_norm_g"], f)[0].reshape(8, 128).T
    cw = np.asarray(inputs["ssd_conv_w"], f)[0]
    pp[:, 16:64] = cw.reshape(4, 12, 128).transpose(2, 1, 0).reshape(128, 48)
    pp[:, 64:76] = np.asarray(inputs["ssd_conv_b"], f)[0].reshape(12, 128).T
    pp[:, 76:84] = np.asarray(inputs["mix_pre_g"], f)[0].reshape(8, 128).T
    pp[:, 84:92] = np.asarray(inputs["ffn_pre_g"], f)[0].reshape(8, 128).T
    fw = np.asarray(inputs["ffn_conv_w"], f)[0]
    ffn = np.zeros((128, 128), f)
    ffn[:, 0:66] = fw.reshape(3, 22, 128).transpose(2, 1, 0).reshape(128, 66)
    ffn[:, 66:88] = np.asarray(inputs["ffn_conv_b"], f)[0].reshape(22, 128).T
    small = np.zeros((1, 64), f)
    small[0, 0:16] = np.asarray(inputs["ssd_dt_bias"], f)[0]
    small[0, 16:32] = np.asarray(inputs["ssd_a_log"], f)[0]
    small[0, 32:48] = np.asarray(inputs["ssd_d"], f)[0]
    return dict(wblk=wblk, wdt=wdt, vecs=vecs, ppd=pp, ffnpd=ffn, small=small,
                meta=np.ascontiguousarray(np.asarray(inputs["meta_tokens"], f)))


def kernel(**inputs):
    x = np.asarray(inputs["x"], np.float32)
    shared = _host_layout(inputs)
    nc = build_nc()
    in_maps = []
    for b in range(8):
        m = dict(shared)
        m["x"] = np.ascontiguousarray(x[b])
        in_maps.append(m)
    res = run_bass_kernel_spmd(nc, in_maps, core_ids=list(range(8)))
    return np.stack([r["out"] for r in res.results], axis=0)
```

```python
import numpy as np
from contextlib import ExitStack
import concourse.bass as bass
import concourse.mybir as mybir
from concourse.bass_utils import run_bass_kernel_spmd

F32 = mybir.dt.float32
BF16 = mybir.dt.bfloat16
AF = mybir.ActivationFunctionType
ALU = mybir.AluOpType

D = 1024
SEQ = 4096
NMETA = 16
NT = 33
LP = NT * 128
PAD = 112
H = 16
DFF = 2816
NFC = 22
EPS = 1e-6
OFF_Z, OFF_XBC, OFF_DT, OFF_Q, OFF_K, OFF_V = 0, 1024, 2560, 2576, 3600, 4624

BLK_Z = 0
BLK_XBC = 8
BLK_Q = 20
BLK_K = 28
BLK_V = 36
BLK_OUT = 44
BLK_GATE = 60
BLK_UP = 82
BLK_DOWN = 104
NBLK = 126

ENGS = ['pe', 'act', 'dve', 'pool', 'sp']
NDSEM = 8


class _Op:
    __slots__ = ('eng', 'fn', 'deps', 'is_dma', 'needed', 'val', 'sem', 'idx')


class Sched:
    def __init__(self):
        self.ops = {e: [] for e in ENGS}
        self.last_w = {}
        self.readers = {}
        self.seen_c = {e: {p: -1 for p in ENGS} for e in ENGS}
        self.seen_d = {e: set() for e in ENGS}
        self.ndma = {e: 0 for e in ENGS}
        self.dma_ops = {e: [] for e in ENGS}

    def add(self, eng, fn, reads=(), writes=(), dma=False, extra=()):
        op = _Op()
        op.eng = eng
        op.fn = fn
        op.is_dma = dma
        op.needed = False
        op.val = None
        op.sem = None
        op.idx = len(self.ops[eng])
        deps = list(extra)
        for k in reads:
            w = self.last_w.get(k)
            if w is not None:
                deps.append(w)
        for k in writes:
            w = self.last_w.get(k)
            if w is not None:
                deps.append(w)
            deps.extend(self.readers.get(k, ()))
        if dma:
            n = self.ndma[eng]
            if n >= NDSEM:
                deps.append(self.dma_ops[eng][n - NDSEM])
            op.sem = ('d', eng, n % NDSEM)
            op.val = 16 * (n // NDSEM + 1)
            self.ndma[eng] = n + 1
            self.dma_ops[eng].append(op)
        cdeps = {}
        ddeps = []
        for d in deps:
            if d is op:
                continue
            if d.is_dma:
                key = (d.eng, d.sem, d.val)
                if key in self.seen_d[eng]:
                    continue
                self.seen_d[eng].add(key)
                ddeps.append(d)
            else:
                if d.eng == eng and eng == 'pe':
                    continue
                if d.idx <= self.seen_c[eng][d.eng]:
                    continue
                if d.eng not in cdeps or cdeps[d.eng].idx < d.idx:
                    cdeps[d.eng] = d
        for p, d in cdeps.items():
            self.seen_c[eng][p] = d.idx
            d.needed = True
        op.deps = list(cdeps.values()) + ddeps
        self.ops[eng].append(op)
        for k in writes:
            self.last_w[k] = op
            self.readers[k] = []
        for k in reads:
            if k in writes:
                continue
            self.readers.setdefault(k, []).append(op)
        return op

    def emit(self, nc, es):
        sem_c = {e: es.enter_context(nc.semaphore('sc_' + e)) for e in ENGS}
        sem_d = {}
        for e in ENGS:
            for i in range(min(NDSEM, self.ndma[e])):
                sem_d[('d', e, i)] = es.enter_context(nc.semaphore('sd_%s_%d' % (e, i)))
        for e in ENGS:
            c = 0
            for op in self.ops[e]:
                if op.is_dma:
                    continue
                if op.needed:
                    c += 1
                    op.val = c
        block = es.enter_context(nc.Block())
        secs = {'pe': block.tensor, 'act': block.scalar, 'dve': block.vector,
                'pool': block.gpsimd, 'sp': block.sync}

        def mk(e):
            def body(eng):
                for op in self.ops[e]:
                    for d in op.deps:
                        if d.is_dma:
                            eng.wait_ge(sem_d[d.sem], d.val)
                        else:
                            eng.wait_ge(sem_c[d.eng], d.val)
                    inst = op.fn(eng)
                    if op.is_dma:
                        inst.then_inc(sem_d[op.sem], 16)
                    elif op.needed:
                        inst.then_inc(sem_c[e], 1)
            return body

        for e in ENGS:
            if self.ops[e]:
                secs[e](mk(e))


def build_nc(stage=99, debug=False):
    nc = bass.Bass("TRN2", target_bir_lowering=False)
    es = ExitStack()

    def din(name, shape):
        return nc.dram_tensor(name, list(shape), F32, kind="ExternalInput").ap()

    x_d = din("x", [SEQ, D])
    meta_d = din("meta", [NMETA, D])
    wblk_d = din("wblk", [NBLK * 128, 1024])
    wdt_d = din("wdt", [128, 8 * 16])
    vecs_d = din("vecs", [8, D])
    pp_d = din("ppd", [128, 128])
    ffnp_d = din("ffnpd", [128, 128])
    small_d = din("small", [1, 64])
    out_d = nc.dram_tensor("out", [SEQ, D], F32, kind="ExternalOutput").ap()
    wscr = nc.dram_tensor("wscr", [NBLK * 128, 1024], BF16, kind="Internal").ap()
    dbg = {}
    if debug:
        dbg['xnT'] = nc.dram_tensor("dbg_xnT", [128, 8 * LP], BF16, kind="ExternalOutput").ap()
        dbg['ysbT'] = nc.dram_tensor("dbg_ysbT", [128, 8 * LP], BF16, kind="ExternalOutput").ap()

    S = Sched()
    with es:
        es2 = ExitStack()

        def sb(name, shape, dt=F32):
            return es.enter_context(nc.sbuf_tensor(name, list(shape), dt))

        def sb2(name, shape, dt=F32):
            return es2.enter_context(nc.sbuf_tensor(name, list(shape), dt))

        ps = es.enter_context(nc.psum_tensor("ps", [128, 4096], F32))

        def bank(b, n=1):
            return ps[:, 512 * b:512 * (b + n)]

        def bkeys(b, n=1):
            return [('ps', b + i) for i in range(n)]

        def dbgdump(name, ap, shape, dt, reads):
            if not debug:
                return
            t = nc.dram_tensor("dbg_" + name, list(shape), dt, kind="ExternalOutput").ap()
            S.add('sp', lambda e: e.dma_start(out=t, in_=ap), reads=reads, dma=True)

        ident_b = sb("ident_b", [128, 128], BF16)
        tri_b = sb("tri_b", [128, 128], BF16)
        tric_b = sb("tric_b", [128, 128], BF16)
        mdiag_b = sb("mdiag_b", [128, 128], BF16)
        padm = sb("padm", [128, 1], F32)
        mhalf = sb("mhalf", [128, 1], F32)
        pp = sb("pp", [128, 128], F32)
        junk = sb("junk", [128, D], BF16)

        def cmask(t, pattern_mult, chan_mult, op, base=0):
            S.add('pool', lambda e: e.memset(t[:], 1.0), writes=[t.name])
            S.add('pool', lambda e: e.affine_select(
                out=t[:], in_=t[:], pattern=[[pattern_mult, t.shape[1]]], compare_op=op,
                fill=0.0, base=base, channel_multiplier=chan_mult), reads=[t.name], writes=[t.name])

        cmask(ident_b, -1, 1, ALU.is_equal)
        cmask(tri_b, -1, 1, ALU.is_ge)
        cmask(tric_b, 1, -1, ALU.is_gt)
        cmask(mdiag_b, 1, -1, ALU.is_gt)
        cmask(padm, 0, 1, ALU.is_ge, base=-PAD)
        S.add('pool', lambda e: e.memset(mhalf[:], -0.5), writes=['mhalf'])
        S.add('sp', lambda e: e.dma_start(out=pp[:], in_=pp_d[:, :]), writes=['pp'], dma=True)

        def prep_blocks(b0, nb):
            S.add('pool', lambda e: e.dma_start(out=wscr[b0 * 128:(b0 + nb) * 128, :],
                                                in_=wblk_d[b0 * 128:(b0 + nb) * 128, :]),
                  writes=[('wscr', b) for b in range(b0, b0 + nb)], dma=True)

        for b0 in (BLK_Q, BLK_K, BLK_V):
            prep_blocks(b0, 8)
        for b0 in range(0, 20, 4):
            prep_blocks(b0, 4)
        for b0 in range(BLK_OUT, NBLK, 4):
            prep_blocks(b0, min(4, NBLK - b0))

        ysbT = sb("ysbT", [128, 8, LP], BF16)
        xnT = sb2("xnT", [128, 8, LP], BF16)
        gpre_bc = sb2("gpre_bc", [128, D], F32)

        S.add('sp', lambda e: e.dma_start(out=gpre_bc[:], in_=vecs_d[0:1, :].partition_broadcast(128)),
              writes=['gpre_bc'], dma=True)
        xt = [sb2("xt%d" % i, [128, D], F32) for i in range(2)]
        xnb = [sb2("xnb%d" % i, [128, D], BF16) for i in range(2)]
        st1 = sb2("st1", [128, NT * 4], F32)

        def load_x_tile(t, buf, key):
            if t == 0:
                S.add('pool', lambda e: e.memset(buf[:], 0.0), writes=[key])
                S.add('sp', lambda e: e.dma_start(out=buf[PAD:128, :], in_=meta_d[:, :]),
                      writes=[key], dma=True)
            else:
                S.add('sp', lambda e: e.dma_start(out=buf[:], in_=x_d[(t - 1) * 128:t * 128, :]),
                      writes=[key], dma=True)

        def rstd_from(src, srckey, col, stt, sttname, junkbuf, junkkey):
            S.add('act', lambda e: e.activation(out=junkbuf[:], in_=src, func=AF.Square,
                                                accum_out=stt[:, col:col + 1]),
                  reads=[srckey], writes=[(sttname, col)])
            S.add('dve', lambda e: e.tensor_scalar(out=stt[:, col + 1:col + 2], in0=stt[:, col:col + 1],
                                                   scalar1=1.0 / D, scalar2=EPS, op0=ALU.mult, op1=ALU.add),
                  reads=[(sttname, col)], writes=[(sttname, col + 1)])
            S.add('pool', lambda e: e.tensor_tensor(out=stt[:, col + 2:col + 3], in0=stt[:, col + 1:col + 2],
                                                    in1=mhalf[:], op=ALU.pow),
                  reads=[(sttname, col + 1), 'mhalf'], writes=[(sttname, col + 2)])

        for t in range(NT):
            i = t % 2
            load_x_tile(t, xt[i], ('xt', i))
            rstd_from(xt[i][:], ('xt', i), 4 * t, st1, 'st1', junk, 'junk')
            S.add('dve', lambda e, i=i, t=t: e.scalar_tensor_tensor(
                out=xnb[i][:], in0=xt[i][:], scalar=st1[:, 4 * t + 2:4 * t + 3], in1=gpre_bc[:],
                op0=ALU.mult, op1=ALU.mult),
                reads=[('xt', i), ('st1', 4 * t + 2), 'gpre_bc'], writes=[('xnb', i)])
            pb = 4 + (t % 2)
            pbv = bank(pb).bitcast(BF16)
            for k in range(8):
                S.add('pe', lambda e, i=i, k=k, pbv=pbv: e.transpose(
                    out=pbv[:, k * 128:(k + 1) * 128], in_=xnb[i][:, k * 128:(k + 1) * 128], identity=ident_b[:]),
                    reads=[('xnb', i), 'ident_b'], writes=bkeys(pb))
            S.add('act', lambda e, t=t, pbv=pbv: e.activation(
                out=xnT[:, :, t * 128:(t + 1) * 128], in_=pbv.rearrange("p (k c) -> p k c", k=8), func=AF.Copy),
                reads=bkeys(pb), writes=[('xnT', t)] + bkeys(pb))

        if debug:
            S.add('sp', lambda e: e.dma_start(out=dbg['xnT'], in_=xnT[:].rearrange("p k c -> p (k c)")),
                  reads=[('xnT', t) for t in range(NT)], dma=True)

        if stage >= 2:
            wq = sb2("wq", [128, 8, 128], BF16)
            wk = sb2("wk", [128, 8, 128], BF16)
            wv = sb2("wv", [128, 8, 128], BF16)
            qT = sb2("qT", [128, LP], BF16)
            kT = sb2("kT", [128, LP], BF16)
            vv = sb2("vv", [128, NT, 128], BF16)
            eb = [sb2("eb%d" % i, [128, 2, 512], BF16) for i in range(4)]
            spb = [sb2("spb%d" % i, [128, 2, 512], BF16) for i in range(3)]
            gb = [sb2("gb%d" % i, [128, 2, 512], BF16) for i in range(2)]
            wb = [sb2("wb%d" % i, [128, 2, 512], BF16) for i in range(2)]
            allx = [('xnT', t) for t in range(NT)]

            def load_wblk(dst, key, blk):
                S.add('sp', lambda e: e.dma_start(
                    out=dst[:].rearrange("p k c -> p (k c)"), in_=wscr[blk * 128:(blk + 1) * 128, :]),
                    reads=[('wscr', blk)], writes=[key], dma=True)

            gstep = 0
            for hp in range(8):
                load_wblk(wq, 'wq', BLK_Q + hp)
                load_wblk(wk, 'wk', BLK_K + hp)
                load_wblk(wv, 'wv', BLK_V + hp)
                def blk_tiles(gq):
                    return [0] if gq == 0 else list(range(4 * gq - 3, 4 * gq + 1))

                def unit_qk(gq, which, b):
                    tiles = blk_tiles(gq)
                    c0 = tiles[0] * 128
                    n = len(tiles) * 128
                    dst, dkey, wt, wkey, scale = ((qT, 'qT', wq, 'wq', 0.125) if which == 'q'
                                                  else (kT, 'kT', wk, 'wk', 1.0))
                    for k in range(8):
                        S.add('pe', lambda e, k=k: e.matmul(
                            bank(b)[:, 0:n], lhsT=wt[:, k, :], rhs=xnT[:, k, c0:c0 + n],
                            start=(k == 0), stop=(k == 7)),
                            reads=[wkey] + [('xnT', t) for t in tiles], writes=bkeys(b))
                    S.add('dve', lambda e: e.tensor_scalar(
                        out=dst[:, c0:c0 + n], in0=bank(b)[:, 0:n], scalar1=scale, scalar2=None, op0=ALU.mult),
                        reads=bkeys(b), writes=[(dkey, t) for t in tiles] + bkeys(b))

                def unit_v(tiles, b):
                    for j, t in enumerate(tiles):
                        for k in range(8):
                            S.add('pe', lambda e, j=j, t=t, k=k: e.matmul(
                                bank(b)[:, j * 128:(j + 1) * 128], lhsT=xnT[:, k, t * 128:(t + 1) * 128],
                                rhs=wv[:, k, :], start=(k == 0), stop=(k == 7)),
                                reads=['wv', ('xnT', t)], writes=bkeys(b))
                    nt_ = len(tiles)
                    t0_ = tiles[0]
                    S.add('dve', lambda e: e.tensor_copy(
                        out=vv[:, t0_:t0_ + nt_, :], in_=bank(b)[:, 0:nt_ * 128].rearrange("p (t c) -> p t c", t=nt_)),
                        reads=bkeys(b), writes=[('vv', t) for t in tiles] + bkeys(b))

                def block_units(gq, b):
                    tiles = blk_tiles(gq)
                    us = [lambda: unit_qk(gq, 'q', b), lambda: unit_qk(gq, 'k', b)]
                    for h0 in range(0, len(tiles), 2):
                        tl = tiles[h0:h0 + 2]
                        us.append(lambda tl=tl: unit_v(tl, b))
                    return us

                pb_ = 0
                for gq in (0, 1):
                    tiles = blk_tiles(gq)
                    unit_qk(gq, 'q', pb_ % 4); pb_ += 1
                    unit_qk(gq, 'k', pb_ % 4); pb_ += 1
                    for h0 in range(0, len(tiles), 2):
                        unit_v(tiles[h0:h0 + 2], pb_ % 4); pb_ += 1

                steps = []
                for qg in range(9):
                    qb0 = 0 if qg == 0 else 4 * qg - 3
                    nq = 1 if qg == 0 else 4
                    qb1 = qb0 + nq - 1
                    for kb in range(qb1, -1, -1):
                        steps.append(dict(qg=qg, kb=kb, lidx=qb1 - kb, qb0=qb0, qb1=qb1, NQ=nq * 128, qc0=qb0 * 128,
                                          ob=6 + (qg % 2), co=max(0, kb - qb0) * 128,
                                          first=(kb == qb1), last=(kb == 0)))
                cb = 4
                c3 = bank(cb, 2).rearrange("p (h c) -> p h c", h=2)

                def fZ(st, gs):
                    co, NQ, kb, qc0 = st['co'], st['NQ'], st['kb'], st['qc0']
                    zs = (gs % 2) * 2
                    z3 = bank(zs, 2).rearrange("p (h c) -> p h c", h=2)
                    for h in range(2):
                        r0 = 64 * h
                        S.add('pe', lambda e, h=h, r0=r0, z3=z3: e.matmul(
                            z3[:, h, co:NQ], lhsT=kT[r0:r0 + 64, kb * 128:(kb + 1) * 128],
                            rhs=qT[r0:r0 + 64, qc0 + co:qc0 + NQ], start=True, stop=True),
                            reads=[('kT', kb)] + [('qT', j) for j in range(st['qb0'] + co // 128, st['qb1'] + 1)],
                            writes=bkeys(zs + h))

                def fE(st, gs):
                    co, NQ, kb = st['co'], st['NQ'], st['kb']
                    zs = (gs % 2) * 2
                    i = gs % 4
                    z3 = bank(zs, 2).rearrange("p (h c) -> p h c", h=2)
                    S.add('act', lambda e: e.activation(
                        out=eb[i][:, :, co:NQ], in_=z3[:, :, co:NQ], func=AF.Exp),
                        reads=bkeys(zs, 2), writes=[('eb', i)] + bkeys(zs, 2))
                    if kb >= st['qb0']:
                        S.add('dve', lambda e: e.tensor_tensor(
                            out=eb[i][:, :, co:co + 128], in0=eb[i][:, :, co:co + 128],
                            in1=mdiag_b[:].unsqueeze(1).broadcast_to([128, 2, 128]), op=ALU.mult),
                            reads=[('eb', i), 'mdiag_b'], writes=[('eb', i)])
                    if kb == 0:
                        S.add('dve', lambda e: e.tensor_scalar(
                            out=eb[i][:, :, co:NQ], in0=eb[i][:, :, co:NQ], scalar1=padm[:, 0:1],
                            scalar2=None, op0=ALU.mult),
                            reads=[('eb', i), 'padm'], writes=[('eb', i)])

                def fL(st, gs):
                    co, NQ = st['co'], st['NQ']
                    i = gs % 3
                    ie = gs % 4
                    S.add('act', lambda e: e.activation(
                        out=spb[i][:, :, co:NQ], in_=eb[ie][:, :, co:NQ], func=AF.Ln, bias=1.0),
                        reads=[('eb', ie)], writes=[('spb', i)])

                def fT(st, gs):
                    co, NQ = st['co'], st['NQ']
                    i = gs % 3
                    for h in range(2):
                        S.add('pe', lambda e, h=h: e.matmul(
                            c3[:, h, co:NQ], lhsT=tri_b[:], rhs=spb[i][:, h, co:NQ],
                            start=st['first'], stop=False, skip_group_check=True),
                            reads=[('spb', i), 'tri_b'], writes=bkeys(cb + h))

                def fG(st, gs):
                    co, NQ = st['co'], st['NQ']
                    i = gs % 2
                    S.add('act', lambda e: e.activation(
                        out=gb[i][:, :, co:NQ], in_=c3[:, :, co:NQ], func=AF.Exp, scale=-1.0),
                        reads=bkeys(cb, 2), writes=[('gb', i)] + bkeys(cb, 2))

                def fT2(st, gs):
                    co, NQ = st['co'], st['NQ']
                    i = gs % 3
                    if st['last']:
                        return
                    for h in range(2):
                        S.add('pe', lambda e, h=h: e.matmul(
                            c3[:, h, co:NQ], lhsT=tric_b[:], rhs=spb[i][:, h, co:NQ],
                            start=False, stop=False, skip_group_check=True),
                            reads=[('spb', i), 'tric_b'], writes=bkeys(cb + h))

                def fW(st, gs):
                    co, NQ = st['co'], st['NQ']
                    i = gs % 2
                    S.add('dve', lambda e: e.tensor_tensor(
                        out=wb[i][:, :, co:NQ], in0=eb[gs % 4][:, :, co:NQ], in1=gb[i][:, :, co:NQ], op=ALU.mult),
                        reads=[('eb', gs % 4), ('gb', i)], writes=[('wb', i)])

                def fV(st, gs, hp=hp):
                    co, NQ, kb, ob, qc0 = st['co'], st['NQ'], st['kb'], st['ob'], st['qc0']
                    i = gs % 2
                    for h in range(2):
                        r0 = 64 * h
                        S.add('pe', lambda e, h=h, r0=r0: e.matmul(
                            bank(ob)[r0:r0 + 64, co:NQ], lhsT=vv[:, kb, r0:r0 + 64], rhs=wb[i][:, h, co:NQ],
                            start=st['first'], stop=False, skip_group_check=True),
                            reads=[('wb', i), ('vv', kb)], writes=bkeys(ob))
                    if st['last']:
                        S.add('dve', lambda e: e.tensor_copy(
                            out=ysbT[:, hp, qc0:qc0 + NQ], in_=bank(ob)[:, 0:NQ]),
                            reads=bkeys(ob), writes=[('ysbT', hp, st['qg'])] + bkeys(ob))

                n = len(steps)
                for i in range(-2, n + 2):
                    if 0 <= i - 1 < n:
                        fT(steps[i - 1], gstep + i - 1)
                    if 0 <= i + 2 < n:
                        fZ(steps[i + 2], gstep + i + 2)
                    if 0 <= i - 2 < n:
                        fV(steps[i - 2], gstep + i - 2)
                    if 0 <= i < n:
                        st_ = steps[i]
                        if 1 <= st_['qg'] <= 7 and 1 <= st_['lidx'] <= 4:
                            us = block_units(st_['qg'] + 1, 6 + ((st_['qg'] + 1) % 2))
                            us[st_['lidx'] - 1]()
                    if 0 <= i + 1 < n:
                        fE(steps[i + 1], gstep + i + 1)
                    if 0 <= i < n:
                        fL(steps[i], gstep + i)
                    if 0 <= i - 1 < n:
                        fG(steps[i - 1], gstep + i - 1)
                        fT2(steps[i - 1], gstep + i - 1)
                        fW(steps[i - 1], gstep + i - 1)
                gstep += n

            if debug:
                S.add('sp', lambda e: e.dma_start(out=dbg['ysbT'], in_=ysbT[:].rearrange("p k c -> p (k c)")),
                      reads=[('ysbT', hp, qg) for hp in range(8) for qg in range(9)], dma=True)

        def barrier():
            tails = {e: [op for op in S.ops[e] if not op.is_dma][-1:] for e in ENGS}
            dts = [op for q in ENGS for op in S.dma_ops[q][-NDSEM:]]
            for e in ENGS:
                ex = [o for p in ENGS if p != e for o in tails[p]] + dts
                S.add(e, lambda eng: eng.nop(), extra=ex)

        if stage >= 3:
            barrier()
            es2.close()
            NG = 384
            ident_f = sb("ident_f", [128, 128], F32)
            UTf = sb("UTf", [128, 128], F32)
            SGf = sb("SGf", [128, 128], F32)
            ONESf = sb("ONESf", [128, 128], F32)
            ones_b = sb("ones_b", [128, 128], BF16)
            cmask(ident_f, -1, 1, ALU.is_equal)
            cmask(UTf, 1, -1, ALU.is_ge)
            cmask(SGf, -1, 1, ALU.is_gt)
            S.add('pool', lambda e: e.memset(ONESf[:], 1.0), writes=['ONESf'])
            S.add('pool', lambda e: e.memset(ones_b[:], 1.0), writes=['ones_b'])
            ffnp = sb("ffnp", [128, 128], F32)
            S.add('sp', lambda e: e.dma_start(out=ffnp[:], in_=ffnp_d[:, :]), writes=['ffnp'], dma=True)
            smallbc = sb("smallbc", [128, 64], F32)
            S.add('sp', lambda e: e.dma_start(out=smallbc[:], in_=small_d[0:1, :].partition_broadcast(128)),
                  writes=['smallbc'], dma=True)
            Abc = sb("Abc", [128, 16], F32)
            S.add('act', lambda e: e.activation(out=Abc[:], in_=smallbc[:, 16:32], func=AF.Exp),
                  reads=['smallbc'], writes=['Abc'])
            S.add('dve', lambda e: e.tensor_scalar(out=Abc[:], in0=Abc[:], scalar1=-1.0, scalar2=None, op0=ALU.mult),
                  reads=['Abc'], writes=['Abc'])
            wdt_b = sb("wdt_b", [128, 8, 16], BF16)
            S.add('pool', lambda e: e.dma_start(out=wdt_b[:].rearrange("p k c -> p (k c)"), in_=wdt_d[:, :]),
                  writes=['wdt_b'], dma=True)
            gpost_bc = sb("gpost_bc", [128, D], F32)
            gfpost_bc = sb("gfpost_bc", [128, D], F32)
            S.add('sp', lambda e: e.dma_start(out=gpost_bc[:], in_=vecs_d[1:2, :].partition_broadcast(128)),
                  writes=['gpost_bc'], dma=True)
            S.add('sp', lambda e: e.dma_start(out=gfpost_bc[:], in_=vecs_d[3:4, :].partition_broadcast(128)),
                  writes=['gfpost_bc'], dma=True)

            NWB = 4
            wbuf = [sb("wbuf%d" % i, [128, 1024], BF16) for i in range(NWB)]
            wcnt = [0]

            def wload(blk):
                i = wcnt[0] % NWB
                wcnt[0] += 1
                S.add('sp', lambda e: e.dma_start(out=wbuf[i][:], in_=wscr[blk * 128:(blk + 1) * 128, :]),
                      reads=[('wscr', blk)], writes=[('wbuf', i)], dma=True)
                return wbuf[i], ('wbuf', i)

            state = sb("state", [128, 1024], F32)
            state_b = sb("state_b", [128, 1024], BF16)
            S.add('pool', lambda e: e.memset(state[:], 0.0), writes=['state'])
            S.add('pool', lambda e: e.memset(state_b[:], 0.0), writes=['state_b'])
            halo = sb("halo", [128, 12, 3], F32)
            ghalo = sb("ghalo", [128, NFC, 2], F32)
            S.add('pool', lambda e: e.memset(halo[:], 0.0), writes=['halo'])
            S.add('pool', lambda e: e.memset(ghalo[:], 0.0), writes=['ghalo'])

            hbuf = sb("hbuf", [128, 3, D], F32)
            actT = sb("actT", [128, 8, NG], BF16)
            R1 = sb("R1", [128, 6 * D], F32)
            zs = R1[:, 0:3 * D].rearrange("p (j c) -> p j c", j=3)
            Xtok = R1[:, 3 * D:6 * D].rearrange("p (j c) -> p j c", j=3)
            aT = R1[:, 0:NFC * NG // 2].bitcast(BF16).rearrange("p (f c) -> p f c", f=NFC)
            BT = sb("BT", [128, 2, NG], BF16)
            CT = sb("CT", [128, 2, NG], BF16)
            Btok = sb("Btok", [128, 3, 256], BF16)
            cbuf = [sb("cbuf%d" % i, [128, NG + 3], F32) for i in range(4)]
            cacc = [sb("cacc%d" % i, [128, NG], F32) for i in range(4)]
            sfm = [sb("sfm%d" % i, [128, NG], F32) for i in range(4)]
            t1 = sb("t1", [128, D], F32)
            RH = sb("RH", [128, 16, 128], F32)
            Xdt = sb("Xdt", [128, D], BF16)
            Xdec = sb("Xdec", [128, D], BF16)
            Eb = [sb("Eb%d" % i, [128, 512], F32) for i in range(2)]
            MTb = [sb("MTb%d" % i, [128, 4, 128], BF16) for i in range(2)]
            CBm = sb("CBm", [128, 2, 128], F32)
            tokbfs = [sb("tokbf%d" % i, [128, D], BF16) for i in range(3)]
            ygnT = sb("ygnT", [128, 8, NG], BF16)
            ysbn = sb("ysbn", [128, 8, NG], BF16)
            r_bc = sb("r_bc", [128, NG], F32)
            lnr = r_bc
            sqb = [sb("sqb%d" % i, [128, NG], BF16) for i in range(2)]
            dtt = sb("dtt", [128, 3, 16], F32)
            att = sb("att", [128, 3, 16], F32)
            dte = sb("dte", [128, 3, 16], F32)
            ex48 = sb("ex48", [128, 48], F32)
            dtd = sb("dtd", [128, 16], F32)
            st3 = sb("st3", [128, 64], F32)
            gcb = cbuf
            RHf = RH[:].rearrange("p h c -> p (h c)")
            tmps = [(t1[:], ['t1']), (RHf[:, 0:D], ['RH', ('tmpf', 1)]), (RHf[:, D:2 * D], ['RH', ('tmpf', 2)])]
            glb = sfm

            def bc3(ap2, n):
                return ap2.unsqueeze(2).broadcast_to([128, 16, n])

            def v3(ap2):
                return ap2.rearrange("p (h d) -> p h d", h=16)

            def run_rr(gens):
                gens = list(gens)
                while gens:
                    nxt = []
                    for gq in gens:
                        try:
                            next(gq)
                            nxt.append(gq)
                        except StopIteration:
                            pass
                    gens = nxt

            def rinfo(slot):
                return st3[:, 4 * slot + 2:4 * slot + 3], (('st3', slot), 2)

            def g_rstd3(src, srcreads, slot):
                col = 4 * slot
                key = ('st3', slot)
                S.add('act', lambda e: e.activation(out=junk[:], in_=src, func=AF.Square,
                                                    accum_out=st3[:, col:col + 1]),
                      reads=srcreads, writes=[(key, 0)])
                yield
                S.add('dve', lambda e: e.tensor_scalar(out=st3[:, col + 1:col + 2], in0=st3[:, col:col + 1],
                                                       scalar1=1.0 / D, scalar2=EPS, op0=ALU.mult, op1=ALU.add),
                      reads=[(key, 0)], writes=[(key, 1)])
                yield
                S.add('pool', lambda e: e.tensor_tensor(out=st3[:, col + 2:col + 3], in0=st3[:, col + 1:col + 2],
                                                        in1=mhalf[:], op=ALU.pow),
                      reads=[(key, 1), 'mhalf'], writes=[(key, 2)])
                yield

            def rstd3(src, srcreads, slot):
                for _ in g_rstd3(src, srcreads, slot):
                    pass
                return rinfo(slot)

            def g_norm_transpose(j, src, srckeys, slot, dstT, dkey, pb=7):
                rcol, rkey = rinfo(slot)
                tb = tokbfs[j % 3]
                tk = ('tokbf', j % 3)
                S.add('dve', lambda e: e.tensor_scalar(out=tb[:], in0=src, scalar1=rcol, scalar2=None,
                                                       op0=ALU.mult),
                      reads=srckeys + [rkey], writes=[tk])
                yield
                pbv = bank(pb).bitcast(BF16)
                for k in range(8):
                    S.add('pe', lambda e, k=k: e.transpose(
                        out=pbv[:, k * 128:(k + 1) * 128], in_=tb[:, k * 128:(k + 1) * 128], identity=ident_b[:]),
                        reads=[tk, 'ident_b'], writes=bkeys(pb))
                yield
                S.add('act', lambda e: e.activation(
                    out=dstT[:, :, j * 128:(j + 1) * 128], in_=pbv.rearrange("p (k c) -> p k c", k=8), func=AF.Copy),
                    reads=bkeys(pb), writes=[(dkey, j)] + bkeys(pb))
                yield

            def norm_transpose(j, src, srckeys, slot, dstT, dkey, pb=7):
                for _ in g_norm_transpose(j, src, srckeys, slot, dstT, dkey, pb):
                    pass

            def apply_gain(dstT, dkey, gcol):
                for k in range(8):
                    S.add('dve', lambda e, k=k: e.tensor_scalar(
                        out=dstT[:, k, :], in0=dstT[:, k, :], scalar1=pp[:, gcol + k:gcol + k + 1], scalar2=None,
                        op0=ALU.mult),
                        reads=[(dkey, jj) for jj in range(3)] + ['pp'], writes=[(dkey, jj) for jj in range(3)])

            def do_group(g):
                t0 = 3 * g
                gc0 = t0 * 128
                hk = [('hbuf', j) for j in range(3)]
                def chain_a(j):
                    load_x_tile(t0 + j, hbuf[:, j, :], ('hbuf', j))
                    yield
                    yield from g_rstd3(hbuf[:, j, :], [('hbuf', j)], j)
                    yield from g_norm_transpose(j, hbuf[:, j, :], [('hbuf', j)], j, actT, 'actT', pb=7 - j)

                run_rr([chain_a(j) for j in range(3)])
                apply_gain(actT, 'actT', 76)
                ak = [('actT', j) for j in range(3)]
                if g == 0:
                    dbgdump('actT', actT[:].rearrange("p k c -> p (k c)"), [128, 8 * NG], BF16, ak)
                for cbk in range(8):
                    w, wkey = wload(BLK_Z + cbk)
                    w3 = w[:].rearrange("p (k c) -> p k c", k=8)
                    for j in range(3):
                        b = 2 * j + cbk // 4
                        for k in range(8):
                            S.add('pe', lambda e, b=b, j=j, k=k, w3=w3, cbk=cbk: e.matmul(
                                bank(b)[:, (cbk % 4) * 128:(cbk % 4 + 1) * 128], lhsT=actT[:, k, j * 128:(j + 1) * 128],
                                rhs=w3[:, k, :], start=(k == 0), stop=(k == 7)),
                                reads=[wkey, ('actT', j)], writes=bkeys(b))
                for j in range(3):
                    S.add('act', lambda e, j=j: e.activation(out=zs[:, j, :], in_=bank(2 * j, 2), func=AF.Silu),
                          reads=bkeys(2 * j, 2), writes=[('zs', j)] + bkeys(2 * j, 2))
                for j in range(3):
                    for k in range(8):
                        S.add('pe', lambda e, j=j, k=k: e.matmul(
                            bank(6)[:, j * 16:(j + 1) * 16], lhsT=actT[:, k, j * 128:(j + 1) * 128],
                            rhs=wdt_b[:, k, :], start=(k == 0), stop=(k == 7)),
                            reads=['wdt_b', ('actT', j)], writes=bkeys(6))
                S.add('dve', lambda e: e.tensor_tensor(
                    out=dte[:], in0=bank(6)[:, 0:48].rearrange("p (j h) -> p j h", j=3),
                    in1=smallbc[:, 0:16].unsqueeze(1).broadcast_to([128, 3, 16]), op=ALU.add),
                    reads=bkeys(6) + ['smallbc'], writes=['dte'] + bkeys(6))
                S.add('act', lambda e: e.activation(out=dte[:], in_=dte[:], func=AF.Exp), reads=['dte'], writes=['dte'])
                S.add('act', lambda e: e.activation(out=dtt[:], in_=dte[:], func=AF.Ln, bias=1.0),
                      reads=['dte'], writes=['dtt'])
                if g == 0:
                    S.add('dve', lambda e: e.tensor_scalar(out=dtt[:, 0, :], in0=dtt[:, 0, :], scalar1=padm[:, 0:1],
                                                           scalar2=None, op0=ALU.mult),
                          reads=['dtt', 'padm'], writes=['dtt'])
                S.add('dve', lambda e: e.tensor_tensor(
                    out=att[:], in0=dtt[:], in1=Abc[:].unsqueeze(1).broadcast_to([128, 3, 16]), op=ALU.mult),
                    reads=['dtt', 'Abc'], writes=['att'])
                pend = []

                def emit_tr(jb, i):
                    for j in range(3):
                        S.add('pe', lambda e, i=i, j=j, jb=jb: e.transpose(
                            out=bank(2 * j + jb // 4)[:, (jb % 4) * 128:(jb % 4 + 1) * 128],
                            in_=sfm[i][:, j * 128:(j + 1) * 128], identity=ident_f[:]),
                            reads=[('sfm', i), 'ident_f'], writes=bkeys(2 * j + jb // 4))

                for jb in range(12):
                    w, wkey = wload(BLK_XBC + jb)
                    w3 = w[:].rearrange("p (k c) -> p k c", k=8)
                    b = 6 + (jb % 2)
                    i = jb % 4
                    for k in range(8):
                        S.add('pe', lambda e, b=b, k=k, w3=w3: e.matmul(
                            bank(b)[:, 0:NG], lhsT=w3[:, k, :], rhs=actT[:, k, :], start=(k == 0), stop=(k == 7)),
                            reads=[wkey] + ak, writes=bkeys(b))
                    if len(pend) >= 2:
                        emit_tr(*pend.pop(0))
                    S.add('act', lambda e, b=b, i=i: e.activation(out=cbuf[i][:, 3:3 + NG], in_=bank(b)[:, 0:NG],
                                                                  func=AF.Copy),
                          reads=bkeys(b), writes=[('cbuf', i)] + bkeys(b))
                    S.add('pool', lambda e, i=i, jb=jb: e.tensor_copy(out=cbuf[i][:, 0:3], in_=halo[:, jb, :]),
                          reads=['halo'], writes=[('cbuf', i)])
                    S.add('pool', lambda e, i=i, jb=jb: e.tensor_copy(out=halo[:, jb, :], in_=cbuf[i][:, NG:NG + 3]),
                          reads=[('cbuf', i)], writes=['halo'])
                    for kk in range(4):
                        wc = pp[:, 16 + jb * 4 + kk:16 + jb * 4 + kk + 1]
                        if kk == 0:
                            S.add('dve', lambda e, i=i, wc=wc: e.tensor_scalar(
                                out=cacc[i][:], in0=cbuf[i][:, 0:NG], scalar1=wc, scalar2=None, op0=ALU.mult),
                                reads=[('cbuf', i), 'pp'], writes=[('cacc', i)])
                        else:
                            S.add('dve', lambda e, i=i, wc=wc, kk=kk: e.scalar_tensor_tensor(
                                out=cacc[i][:], in0=cbuf[i][:, kk:kk + NG], scalar=wc, in1=cacc[i][:],
                                op0=ALU.mult, op1=ALU.add),
                                reads=[('cbuf', i), 'pp', ('cacc', i)], writes=[('cacc', i)])
                    bias = pp[:, 64 + jb:65 + jb]
                    if jb < 8:
                        S.add('act', lambda e, i=i, bias=bias: e.activation(out=sfm[i][:], in_=cacc[i][:], func=AF.Silu,
                                                                            bias=bias),
                              reads=[('cacc', i), 'pp'], writes=[('sfm', i)])
                        pend.append((jb, i))
                    else:
                        dst = BT if jb < 10 else CT
                        dk = 'BT' if jb < 10 else 'CT'
                        gg = jb % 2
                        S.add('act', lambda e, i=i, bias=bias, dst=dst, gg=gg: e.activation(
                            out=dst[:, gg, :], in_=cacc[i][:], func=AF.Silu, bias=bias),
                            reads=[('cacc', i), 'pp'], writes=[(dk, gg)])
                while pend:
                    emit_tr(*pend.pop(0))
                for j in range(3):
                    S.add('act', lambda e, j=j: e.activation(out=Xtok[:, j, :], in_=bank(2 * j, 2), func=AF.Copy),
                          reads=bkeys(2 * j, 2), writes=[('Xtok', j)] + bkeys(2 * j, 2))
                pbv = bank(7).bitcast(BF16)
                for gg in range(2):
                    for j in range(3):
                        S.add('pe', lambda e, j=j, gg=gg, pbv=pbv: e.transpose(
                            out=pbv[:, j * 256 + gg * 128:j * 256 + (gg + 1) * 128],
                            in_=BT[:, gg, j * 128:(j + 1) * 128], identity=ident_b[:]),
                            reads=[('BT', gg), 'ident_b'], writes=bkeys(7))
                S.add('dve', lambda e, pbv=pbv: e.tensor_copy(
                    out=Btok[:], in_=pbv[:, 0:768].rearrange("p (j c) -> p j c", j=3)),
                    reads=bkeys(7), writes=['Btok'] + bkeys(7))
                if g == 0:
                    dbgdump('zs', R1[:, 0:3 * D], [128, 3 * D], F32, [('zs', jj) for jj in range(3)])
                    dbgdump('Xtok', R1[:, 3 * D:6 * D], [128, 3 * D], F32, [('Xtok', jj) for jj in range(3)])
                    dbgdump('dtt', dtt[:].rearrange("p j h -> p (j h)"), [128, 48], F32, ['dtt'])
                    dbgdump('att', att[:].rearrange("p j h -> p (j h)"), [128, 48], F32, ['att'])
                    dbgdump('BT', BT[:].rearrange("p g c -> p (g c)"), [128, 2 * NG], BF16, [('BT', 0), ('BT', 1)])
                    dbgdump('CT', CT[:].rearrange("p g c -> p (g c)"), [128, 2 * NG], BF16, [('CT', 0), ('CT', 1)])
                    dbgdump('Btok', Btok[:].rearrange("p j c -> p (j c)"), [128, 768], BF16, ['Btok'])
                for j in range(3):
                    tc0 = j * 128
                    a_j = att[:, j, :]
                    for ci, lt in enumerate((UTf, SGf, ONESf)):
                        S.add('pe', lambda e, ci=ci, lt=lt, a_j=a_j: e.matmul(
                            bank(0)[:, ci * 16:(ci + 1) * 16], lhsT=lt[:], rhs=a_j, start=True, stop=True),
                            reads=['att', 'UTf', 'SGf', 'ONESf'], writes=bkeys(0))
                    S.add('act', lambda e: e.activation(out=ex48[:], in_=bank(0)[:, 0:48], func=AF.Exp),
                          reads=bkeys(0), writes=['ex48'] + bkeys(0))
                    S.add('dve', lambda e, a_j=a_j: e.tensor_tensor(
                        out=RH[:], in0=UTf[:].unsqueeze(1).broadcast_to([128, 16, 128]), in1=bc3(a_j, 128),
                        op=ALU.mult),
                        reads=['att', 'UTf'], writes=['RH'])
                    for gg in range(2):
                        S.add('pe', lambda e, gg=gg, tc0=tc0: e.matmul(
                            bank(0)[:, 128 + gg * 128:256 + gg * 128], lhsT=BT[:, gg, tc0:tc0 + 128],
                            rhs=CT[:, gg, tc0:tc0 + 128], start=True, stop=True),
                            reads=[('BT', gg), ('CT', gg)], writes=bkeys(0))
                    S.add('dve', lambda e: e.tensor_tensor(
                        out=CBm[:], in0=bank(0)[:, 128:384].rearrange("p (g c) -> p g c", g=2),
                        in1=UTf[:].unsqueeze(1).broadcast_to([128, 2, 128]), op=ALU.mult),
                        reads=bkeys(0) + ['UTf'], writes=['CBm'] + bkeys(0))
                    for gg in range(2):
                        S.add('pe', lambda e, gg=gg, tc0=tc0: e.matmul(
                            bank(5 + gg), lhsT=CT[:, gg, tc0:tc0 + 128], rhs=state_b[:, gg * 512:(gg + 1) * 512],
                            start=True, stop=True),
                            reads=[('CT', gg), 'state_b'], writes=bkeys(5 + gg))
                    S.add('dve', lambda e, j=j: e.tensor_tensor(out=dtd[:], in0=dtt[:, j, :], in1=ex48[:, 16:32],
                                                                op=ALU.mult),
                          reads=['dtt', 'ex48'], writes=['dtd'])
                    S.add('dve', lambda e, j=j: e.tensor_tensor(out=v3(Xdt[:]), in0=v3(Xtok[:, j, :]),
                                                                in1=bc3(dtt[:, j, :], 64), op=ALU.mult),
                          reads=[('Xtok', j), 'dtt'], writes=['Xdt'])
                    S.add('dve', lambda e, j=j: e.tensor_tensor(out=v3(Xdec[:]), in0=v3(Xtok[:, j, :]),
                                                                in1=bc3(dtd[:], 64), op=ALU.mult),
                          reads=[('Xtok', j), 'dtd'], writes=['Xdec'])
                    def fD(hq):
                        i = hq % 2
                        S.add('pe', lambda e, i=i, hq=hq: e.matmul(
                            bank(1 + i), lhsT=SGf[:], rhs=RH[:, 4 * hq:4 * hq + 4, :].rearrange("p h c -> p (h c)"),
                            start=True, stop=True),
                            reads=['RH', 'SGf'], writes=bkeys(1 + i))
                        S.add('act', lambda e, i=i: e.activation(out=Eb[i][:], in_=bank(1 + i), func=AF.Exp),
                              reads=bkeys(1 + i), writes=[('Eb', i)] + bkeys(1 + i))

                    def fY(hq):
                        i = hq % 2
                        gg = hq // 2
                        S.add('dve', lambda e, i=i, gg=gg: e.tensor_tensor(
                            out=MTb[i][:], in0=Eb[i][:].rearrange("p (h c) -> p h c", h=4),
                            in1=CBm[:, gg:gg + 1, :].broadcast_to([128, 4, 128]), op=ALU.mult),
                            reads=[('Eb', i), 'CBm'], writes=[('MTb', i)])
                        for hh in range(4):
                            h = 4 * hq + hh
                            S.add('pe', lambda e, i=i, hh=hh, h=h: e.matmul(
                                bank(3 + h // 8)[:, (h % 8) * 64:(h % 8 + 1) * 64], lhsT=MTb[i][:, hh, :],
                                rhs=Xdt[:, h * 64:(h + 1) * 64], start=True, stop=True),
                                reads=[('MTb', i), 'Xdt'], writes=bkeys(3 + h // 8))

                    fD(0)
                    fD(1)
                    fY(0)
                    fD(2)
                    fY(1)
                    fD(3)
                    fY(2)
                    fY(3)
                    S.add('dve', lambda e: e.tensor_tensor(out=v3(t1[:]), in0=v3(bank(5, 2)), in1=bc3(ex48[:, 0:16], 64),
                                                           op=ALU.mult),
                          reads=bkeys(5, 2) + ['ex48'], writes=['t1'] + bkeys(5, 2))
                    S.add('dve', lambda e: e.tensor_tensor(out=t1[:], in0=bank(3, 2), in1=t1[:], op=ALU.add),
                          reads=bkeys(3, 2) + ['t1'], writes=['t1'] + bkeys(3, 2))
                    for gg in range(2):
                        S.add('pe', lambda e, gg=gg, j=j: e.matmul(
                            bank(5 + gg), lhsT=Btok[:, j, gg * 128:(gg + 1) * 128], rhs=Xdec[:, gg * 512:(gg + 1) * 512],
                            start=True, stop=True),
                            reads=['Btok', 'Xdec'], writes=bkeys(5 + gg))
                    S.add('dve', lambda e, j=j: e.tensor_tensor(out=v3(Xtok[:, j, :]), in0=v3(Xtok[:, j, :]),
                                                                in1=bc3(smallbc[:, 32:48], 64), op=ALU.mult),
                          reads=[('Xtok', j), 'smallbc'], writes=[('Xtok', j)])
                    S.add('dve', lambda e, j=j: e.tensor_tensor(out=t1[:], in0=t1[:], in1=Xtok[:, j, :], op=ALU.add),
                          reads=[('Xtok', j), 't1'], writes=['t1'])
                    S.add('dve', lambda e, j=j: e.tensor_tensor(out=t1[:], in0=t1[:], in1=zs[:, j, :], op=ALU.mult),
                          reads=[('zs', j), 't1'], writes=['t1'])
                    S.add('dve', lambda e: e.tensor_tensor(out=v3(state[:]), in0=v3(state[:]), in1=bc3(ex48[:, 32:48], 64),
                                                           op=ALU.mult),
                          reads=['state', 'ex48'], writes=['state'])
                    S.add('dve', lambda e: e.tensor_tensor(out=state[:], in0=bank(5, 2), in1=state[:], op=ALU.add),
                          reads=bkeys(5, 2) + ['state'], writes=['state'] + bkeys(5, 2))
                    S.add('act', lambda e: e.activation(out=state_b[:], in_=state[:], func=AF.Copy),
                          reads=['state'], writes=['state_b'])
                    if g == 0:
                        dbgdump('yg%d' % j, t1[:], [128, D], F32, ['t1'])
                        dbgdump('ex48_%d' % j, ex48[:], [128, 48], F32, ['ex48'])
                    rstd3(t1[:], ['t1'], 3 + j)
                    norm_transpose(j, t1[:], ['t1'], 3 + j, ygnT, 'ygnT')
                apply_gain(ygnT, 'ygnT', 0)
                for k in range(8):
                    i = k % 2
                    S.add('act', lambda e, i=i, k=k: e.activation(out=sqb[i][:], in_=ysbT[:, k, gc0:gc0 + NG],
                                                                  func=AF.Square),
                          reads=[('ysbT', k, q) for q in range(9)], writes=[('sqb', i)])
                    S.add('pe', lambda e, i=i, k=k: e.matmul(bank(6)[:, 0:NG], lhsT=ones_b[:], rhs=sqb[i][:],
                                                             start=(k == 0), stop=(k == 7)),
                          reads=[('sqb', i), 'ones_b'], writes=bkeys(6))
                S.add('act', lambda e: e.activation(out=lnr[:], in_=bank(6)[:, 0:NG], func=AF.Ln, scale=1.0 / D, bias=EPS),
                      reads=bkeys(6), writes=['lnr', 'r_bc'] + bkeys(6))
                S.add('act', lambda e: e.activation(out=r_bc[:], in_=lnr[:], func=AF.Exp, scale=-0.5),
                      reads=['lnr', 'r_bc'], writes=['r_bc', 'lnr'])
                for k in range(8):
                    S.add('dve', lambda e, k=k: e.scalar_tensor_tensor(
                        out=ysbn[:, k, :], in0=ysbT[:, k, gc0:gc0 + NG], scalar=pp[:, 8 + k:9 + k], in1=r_bc[:],
                        op0=ALU.mult, op1=ALU.mult),
                        reads=[('ysbT', k, q) for q in range(9)] + ['pp', 'r_bc'], writes=[('ysbn', k)])
                for kc in range(16):
                    w, wkey = wload(BLK_OUT + kc)
                    src = ygnT if kc < 8 else ysbn
                    sk = [('ygnT', jj) for jj in range(3)] if kc < 8 else [('ysbn', kc - 8)]
                    for j in range(3):
                        for hf in range(2):
                            S.add('pe', lambda e, j=j, hf=hf, w=w, src=src, kc=kc: e.matmul(
                                bank(2 * j + hf), lhsT=src[:, kc % 8, j * 128:(j + 1) * 128],
                                rhs=w[:, hf * 512:(hf + 1) * 512], start=(kc == 0), stop=(kc == 15)),
                                reads=[wkey] + sk, writes=bkeys(2 * j + hf))
                def chain_post(j, gbc, gkey, slot0, final):
                    tmp, tkk = tmps[j]
                    S.add('dve', lambda e: e.tensor_tensor(
                        out=tmp, in0=bank(2 * j, 2), in1=gbc[:], op=ALU.mult),
                        reads=bkeys(2 * j, 2) + [gkey], writes=tkk + bkeys(2 * j, 2))
                    yield
                    yield from g_rstd3(bank(2 * j, 2), bkeys(2 * j, 2), slot0 + j)
                    rc, rk = rinfo(slot0 + j)
                    S.add('dve', lambda e: e.scalar_tensor_tensor(
                        out=hbuf[:, j, :], in0=tmp, scalar=rc, in1=hbuf[:, j, :], op0=ALU.mult, op1=ALU.add),
                        reads=tkk + [rk, ('hbuf', j)], writes=[('hbuf', j)])
                    yield
                    t = t0 + j
                    if final and t >= 1:
                        S.add('sp', lambda e: e.dma_start(out=out_d[(t - 1) * 128:t * 128, :], in_=hbuf[:, j, :]),
                              reads=[('hbuf', j)], dma=True)

                run_rr([chain_post(j, gpost_bc, 'gpost_bc', 6, False) for j in range(3)])
                if g == 0:
                    dbgdump('ygnT', ygnT[:].rearrange("p k c -> p (k c)"), [128, 8 * NG], BF16, [('ygnT', jj) for jj in range(3)])
                    dbgdump('ysbn', ysbn[:].rearrange("p k c -> p (k c)"), [128, 8 * NG], BF16, [('ysbn', k) for k in range(8)])
                    dbgdump('r_bc', r_bc[:], [128, NG], F32, ['r_bc'])
                    dbgdump('h1', hbuf[:].rearrange("p j c -> p (j c)"), [128, 3 * D], F32, hk)
                    dbgdump('state', state[:], [128, D], F32, ['state'])
                def chain_e(j):
                    yield from g_rstd3(hbuf[:, j, :], [('hbuf', j)], 9 + j)
                    yield from g_norm_transpose(j, hbuf[:, j, :], [('hbuf', j)], 9 + j, actT, 'actT', pb=7 - j)

                run_rr([chain_e(j) for j in range(3)])
                apply_gain(actT, 'actT', 84)
                for fc in range(NFC):
                    wg, wgk = wload(BLK_GATE + fc)
                    wu, wuk = wload(BLK_UP + fc)
                    wg3 = wg[:].rearrange("p (k c) -> p k c", k=8)
                    wu3 = wu[:].rearrange("p (k c) -> p k c", k=8)
                    i = fc % 4
                    bg = fc % 3
                    bu = 3 + fc % 3
                    for k in range(8):
                        S.add('pe', lambda e, bg=bg, k=k, wg3=wg3: e.matmul(
                            bank(bg)[:, 0:NG], lhsT=wg3[:, k, :], rhs=actT[:, k, :], start=(k == 0), stop=(k == 7)),
                            reads=[wgk] + ak, writes=bkeys(bg))
                    for k in range(8):
                        S.add('pe', lambda e, bu=bu, k=k, wu3=wu3: e.matmul(
                            bank(bu)[:, 0:NG], lhsT=wu3[:, k, :], rhs=actT[:, k, :], start=(k == 0), stop=(k == 7)),
                            reads=[wuk] + ak, writes=bkeys(bu))
                    S.add('act', lambda e, bg=bg, i=i: e.activation(out=gcb[i][:, 2:2 + NG], in_=bank(bg)[:, 0:NG],
                                                                    func=AF.Copy),
                          reads=bkeys(bg), writes=[('cbuf', i)] + bkeys(bg))
                    S.add('pool', lambda e, i=i, fc=fc: e.tensor_copy(out=gcb[i][:, 0:2], in_=ghalo[:, fc, :]),
                          reads=['ghalo'], writes=[('cbuf', i)])
                    S.add('pool', lambda e, i=i, fc=fc: e.tensor_copy(out=ghalo[:, fc, :], in_=gcb[i][:, NG:NG + 2]),
                          reads=[('cbuf', i)], writes=['ghalo'])
                    for kk in range(3):
                        wc = ffnp[:, fc * 3 + kk:fc * 3 + kk + 1]
                        if kk == 0:
                            S.add('dve', lambda e, i=i, wc=wc: e.tensor_scalar(
                                out=cacc[i][:], in0=gcb[i][:, 0:NG], scalar1=wc, scalar2=None, op0=ALU.mult),
                                reads=[('cbuf', i), 'ffnp'], writes=[('cacc', i)])
                        else:
                            S.add('dve', lambda e, i=i, wc=wc, kk=kk: e.scalar_tensor_tensor(
                                out=cacc[i][:], in0=gcb[i][:, kk:kk + NG], scalar=wc, in1=cacc[i][:],
                                op0=ALU.mult, op1=ALU.add),
                                reads=[('cbuf', i), 'ffnp', ('cacc', i)], writes=[('cacc', i)])
                    S.add('act', lambda e, i=i, fc=fc: e.activation(out=glb[i][:], in_=cacc[i][:], func=AF.Gelu_apprx_tanh,
                                                                    bias=ffnp[:, 66 + fc:67 + fc]),
                          reads=[('cacc', i), 'ffnp'], writes=[('sfm', i)])
                    S.add('dve', lambda e, i=i, fc=fc, bu=bu: e.tensor_tensor(
                        out=aT[:, fc, :], in0=bank(bu)[:, 0:NG], in1=glb[i][:], op=ALU.mult),
                        reads=bkeys(bu) + [('sfm', i)],
                        writes=[('zs', jj) for jj in range(3)] + [('Xtok', jj) for jj in range(3)] + bkeys(bu))
                aTk = [('zs', jj) for jj in range(3)] + [('Xtok', jj) for jj in range(3)]
                for fc in range(NFC):
                    w, wkey = wload(BLK_DOWN + fc)
                    for j in range(3):
                        for hf in range(2):
                            S.add('pe', lambda e, j=j, hf=hf, w=w, fc=fc: e.matmul(
                                bank(2 * j + hf), lhsT=aT[:, fc, j * 128:(j + 1) * 128],
                                rhs=w[:, hf * 512:(hf + 1) * 512], start=(fc == 0), stop=(fc == NFC - 1)),
                                reads=[wkey] + aTk, writes=bkeys(2 * j + hf))
                run_rr([chain_post(j, gfpost_bc, 'gfpost_bc', 12, True) for j in range(3)])

            for g in range(11 if stage >= 4 else 1):
                do_group(g)

        tail = [op for q in ENGS for op in S.dma_ops[q][-NDSEM:]]
        S.add('sp', lambda e: e.nop(), extra=tail)
        es2.close()
        S.emit(nc, es)
    return nc


def _host_layout(inputs):
    f = np.float32
    w_in = np.asarray(inputs["w_in"], f)[0]
    w_out = np.asarray(inputs["w_out"], f)[0]
    w_up = np.asarray(inputs["w_up"], f)[0]
    w_down = np.asarray(inputs["w_down"], f)[0]

    def kblocks(w, c0, nb):
        sub = w[:, c0:c0 + nb * 128].reshape(8, 128, nb, 128)
        return np.ascontiguousarray(sub.transpose(2, 1, 0, 3)).reshape(nb * 128, 1024)

    blocks = [kblocks(w_in, OFF_Z, 8), kblocks(w_in, OFF_XBC, 12), kblocks(w_in, OFF_Q, 8),
              kblocks(w_in, OFF_K, 8), kblocks(w_in, OFF_V, 8),
              w_out.reshape(16 * 128, 1024),
              kblocks(w_up, 0, 22), kblocks(w_up, DFF, 22),
              w_down.reshape(22 * 128, 1024)]
    wblk = np.ascontiguousarray(np.concatenate(blocks, axis=0))
    assert wblk.shape == (NBLK * 128, 1024)
    wdt = np.ascontiguousarray(w_in[:, OFF_DT:OFF_DT + 16].reshape(8, 128, 16).transpose(1, 0, 2)).reshape(128, 128)
    vecs = np.zeros((8, D), f)
    vecs[0] = np.asarray(inputs["mix_pre_g"], f)[0]
    vecs[1] = np.asarray(inputs["mix_post_g"], f)[0]
    vecs[2] = np.asarray(inputs["ffn_pre_g"], f)[0]
    vecs[3] = np.asarray(inputs["ffn_post_g"], f)[0]
    pp = np.zeros((128, 128), f)
    pp[:, 0:8] = np.asarray(inputs["ssd_norm_g"], f)[0].reshape(8, 128).T
    pp[:, 8:16] = np.asarray(inputs["sb_norm_g"], f)[0].reshape(8, 128).T
    cw = np.asarray(inputs["ssd_conv_w"], f)[0]
    pp[:, 16:64] = cw.reshape(4, 12, 128).transpose(2, 1, 0).reshape(128, 48)
    pp[:, 64:76] = np.asarray(inputs["ssd_conv_b"], f)[0].reshape(12, 128).T
    pp[:, 76:84] = np.asarray(inputs["mix_pre_g"], f)[0].reshape(8, 128).T
    pp[:, 84:92] = np.asarray(inputs["ffn_pre_g"], f)[0].reshape(8, 128).T
    fw = np.asarray(inputs["ffn_conv_w"], f)[0]
    ffn = np.zeros((128, 128), f)
    ffn[:, 0:66] = fw.reshape(3, 22, 128).transpose(2, 1, 0).reshape(128, 66)
    ffn[:, 66:88] = np.asarray(inputs["ffn_conv_b"], f)[0].reshape(22, 128).T
    small = np.zeros((1, 64), f)
    small[0, 0:16] = np.asarray(inputs["ssd_dt_bias"], f)[0]
    small[0, 16:32] = np.asarray(inputs["ssd_a_log"], f)[0]
    small[0, 32:48] = np.asarray(inputs["ssd_d"], f)[0]
    return dict(wblk=wblk, wdt=wdt, vecs=vecs, ppd=pp, ffnpd=ffn, small=small,
                meta=np.ascontiguousarray(np.asarray(inputs["meta_tokens"], f)))


def kernel(**inputs):
    x = np.asarray(inputs["x"], np.float32)
    shared = _host_layout(inputs)
    nc = build_nc()
    in_maps = []
    for b in range(8):
        m = dict(shared)
        m["x"] = np.ascontiguousarray(x[b])
        in_maps.append(m)
    res = run_bass_kernel_spmd(nc, in_maps, core_ids=list(range(8)))
    return np.stack([r["out"] for r in res.results], axis=0)
```

```python
import numpy as np
from contextlib import ExitStack
import concourse.bass as bass
import concourse.mybir as mybir
from concourse.bass_utils import run_bass_kernel_spmd

F32 = mybir.dt.float32
BF16 = mybir.dt.bfloat16
AF = mybir.ActivationFunctionType
ALU = mybir.AluOpType

D = 1024
SEQ = 4096
NMETA = 16
NT = 33
LP = NT * 128
PAD = 112
H = 16
DFF = 2816
NFC = 22
EPS = 1e-6
OFF_Z, OFF_XBC, OFF_DT, OFF_Q, OFF_K, OFF_V = 0, 1024, 2560, 2576, 3600, 4624

BLK_Z = 0
BLK_XBC = 8
BLK_Q = 20
BLK_K = 28
BLK_V = 36
BLK_OUT = 44
BLK_GATE = 60
BLK_UP = 82
BLK_DOWN = 104
NBLK = 126

ENGS = ['pe', 'act', 'dve', 'pool', 'sp']
NDSEM = 8


class _Op:
    __slots__ = ('eng', 'fn', 'deps', 'is_dma', 'needed', 'val', 'sem', 'idx')


class Sched:
    def __init__(self):
        self.ops = {e: [] for e in ENGS}
        self.last_w = {}
        self.readers = {}
        self.seen_c = {e: {p: -1 for p in ENGS} for e in ENGS}
        self.seen_d = {e: set() for e in ENGS}
        self.ndma = {e: 0 for e in ENGS}
        self.dma_ops = {e: [] for e in ENGS}

    def add(self, eng, fn, reads=(), writes=(), dma=False, extra=()):
        op = _Op()
        op.eng = eng
        op.fn = fn
        op.is_dma = dma
        op.needed = False
        op.val = None
        op.sem = None
        op.idx = len(self.ops[eng])
        deps = list(extra)
        for k in reads:
            w = self.last_w.get(k)
            if w is not None:
                deps.append(w)
        for k in writes:
            w = self.last_w.get(k)
            if w is not None:
                deps.append(w)
            deps.extend(self.readers.get(k, ()))
        if dma:
            n = self.ndma[eng]
            if n >= NDSEM:
                deps.append(self.dma_ops[eng][n - NDSEM])
            op.sem = ('d', eng, n % NDSEM)
            op.val = 16 * (n // NDSEM + 1)
            self.ndma[eng] = n + 1
            self.dma_ops[eng].append(op)
        cdeps = {}
        ddeps = []
        for d in deps:
            if d is op:
                continue
            if d.is_dma:
                key = (d.eng, d.sem, d.val)
                if key in self.seen_d[eng]:
                    continue
                self.seen_d[eng].add(key)
                ddeps.append(d)
            else:
                if d.eng == eng and eng == 'pe':
                    continue
                if d.idx <= self.seen_c[eng][d.eng]:
                    continue
                if d.eng not in cdeps or cdeps[d.eng].idx < d.idx:
                    cdeps[d.eng] = d
        for p, d in cdeps.items():
            self.seen_c[eng][p] = d.idx
            d.needed = True
        op.deps = list(cdeps.values()) + ddeps
        self.ops[eng].append(op)
        for k in writes:
            self.last_w[k] = op
            self.readers[k] = []
        for k in reads:
            if k in writes:
                continue
            self.readers.setdefault(k, []).append(op)
        return op

    def emit(self, nc, es):
        sem_c = {e: es.enter_context(nc.semaphore('sc_' + e)) for e in ENGS}
        sem_d = {}
        for e in ENGS:
            for i in range(min(NDSEM, self.ndma[e])):
                sem_d[('d', e, i)] = es.enter_context(nc.semaphore('sd_%s_%d' % (e, i)))
        for e in ENGS:
            c = 0
            for op in self.ops[e]:
                if op.is_dma:
                    continue
                if op.needed:
                    c += 1
                    op.val = c
        block = es.enter_context(nc.Block())
        secs = {'pe': block.tensor, 'act': block.scalar, 'dve': block.vector,
                'pool': block.gpsimd, 'sp': block.sync}

        def mk(e):
            def body(eng):
                for op in self.ops[e]:
                    for d in op.deps:
                        if d.is_dma:
                            eng.wait_ge(sem_d[d.sem], d.val)
                        else:
                            eng.wait_ge(sem_c[d.eng], d.val)
                    inst = op.fn(eng)
                    if op.is_dma:
                        inst.then_inc(sem_d[op.sem], 16)
                    elif op.needed:
                        inst.then_inc(sem_c[e], 1)
            return body

        for e in ENGS:
            if self.ops[e]:
                secs[e](mk(e))


def build_nc(stage=99, debug=False):
    nc = bass.Bass("TRN2", target_bir_lowering=False)
    es = ExitStack()

    def din(name, shape):
        return nc.dram_tensor(name, list(shape), F32, kind="ExternalInput").ap()

    x_d = din("x", [SEQ, D])
    meta_d = din("meta", [NMETA, D])
    wblk_d = din("wblk", [NBLK * 128, 1024])
    wdt_d = din("wdt", [128, 8 * 16])
    vecs_d = din("vecs", [8, D])
    pp_d = din("ppd", [128, 128])
    ffnp_d = din("ffnpd", [128, 128])
    small_d = din("small", [1, 64])
    out_d = nc.dram_tensor("out", [SEQ, D], F32, kind="ExternalOutput").ap()
    wscr = nc.dram_tensor("wscr", [NBLK * 128, 1024], BF16, kind="Internal").ap()
    dbg = {}
    if debug:
        dbg['xnT'] = nc.dram_tensor("dbg_xnT", [128, 8 * LP], BF16, kind="ExternalOutput").ap()
        dbg['ysbT'] = nc.dram_tensor("dbg_ysbT", [128, 8 * LP], BF16, kind="ExternalOutput").ap()

    S = Sched()
    with es:
        es2 = ExitStack()

        def sb(name, shape, dt=F32):
            return es.enter_context(nc.sbuf_tensor(name, list(shape), dt))

        def sb2(name, shape, dt=F32):
            return es2.enter_context(nc.sbuf_tensor(name, list(shape), dt))

        ps = es.enter_context(nc.psum_tensor("ps", [128, 4096], F32))

        def bank(b, n=1):
            return ps[:, 512 * b:512 * (b + n)]

        def bkeys(b, n=1):
            return [('ps', b + i) for i in range(n)]

        def dbgdump(name, ap, shape, dt, reads):
            if not debug:
                return
            t = nc.dram_tensor("dbg_" + name, list(shape), dt, kind="ExternalOutput").ap()
            S.add('sp', lambda e: e.dma_start(out=t, in_=ap), reads=reads, dma=True)

        ident_b = sb("ident_b", [128, 128], BF16)
        tri_b = sb("tri_b", [128, 128], BF16)
        tric_b = sb("tric_b", [128, 128], BF16)
        mdiag_b = sb("mdiag_b", [128, 128], BF16)
        padm = sb("padm", [128, 1], F32)
        mhalf = sb("mhalf", [128, 1], F32)
        pp = sb("pp", [128, 128], F32)
        junk = sb("junk", [128, D], BF16)

        def cmask(t, pattern_mult, chan_mult, op, base=0):
            S.add('pool', lambda e: e.memset(t[:], 1.0), writes=[t.name])
            S.add('pool', lambda e: e.affine_select(
                out=t[:], in_=t[:], pattern=[[pattern_mult, t.shape[1]]], compare_op=op,
                fill=0.0, base=base, channel_multiplier=chan_mult), reads=[t.name], writes=[t.name])

        cmask(ident_b, -1, 1, ALU.is_equal)
        cmask(tri_b, -1, 1, ALU.is_ge)
        cmask(tric_b, 1, -1, ALU.is_gt)
        cmask(mdiag_b, 1, -1, ALU.is_gt)
        cmask(padm, 0, 1, ALU.is_ge, base=-PAD)
        S.add('pool', lambda e: e.memset(mhalf[:], -0.5), writes=['mhalf'])
        S.add('sp', lambda e: e.dma_start(out=pp[:], in_=pp_d[:, :]), writes=['pp'], dma=True)

        def prep_blocks(b0, nb):
            S.add('pool', lambda e: e.dma_start(out=wscr[b0 * 128:(b0 + nb) * 128, :],
                                                in_=wblk_d[b0 * 128:(b0 + nb) * 128, :]),
                  writes=[('wscr', b) for b in range(b0, b0 + nb)], dma=True)


        ysbT = sb("ysbT", [128, 8, LP], BF16)
        xnT = sb2("xnT", [128, 8, LP], BF16)
        gpre_bc = sb2("gpre_bc", [128, D], F32)

        S.add('sp', lambda e: e.dma_start(out=gpre_bc[:], in_=vecs_d[0:1, :].partition_broadcast(128)),
              writes=['gpre_bc'], dma=True)
        xt = [sb2("xt%d" % i, [128, D], F32) for i in range(2)]
        xnb = [sb2("xnb%d" % i, [128, D], BF16) for i in range(2)]
        st1 = sb2("st1", [128, NT * 4], F32)

        def load_x_tile(t, buf, key):
            if t == 0:
                S.add('pool', lambda e: e.memset(buf[:], 0.0), writes=[key])
                S.add('sp', lambda e: e.dma_start(out=buf[PAD:128, :], in_=meta_d[:, :]),
                      writes=[key], dma=True)
            else:
                S.add('sp', lambda e: e.dma_start(out=buf[:], in_=x_d[(t - 1) * 128:t * 128, :]),
                      writes=[key], dma=True)

        def rstd_from(src, srckey, col, stt, sttname, junkbuf, junkkey):
            S.add('act', lambda e: e.activation(out=junkbuf[:], in_=src, func=AF.Square,
                                                accum_out=stt[:, col:col + 1]),
                  reads=[srckey], writes=[(sttname, col)])
            S.add('dve', lambda e: e.tensor_scalar(out=stt[:, col + 1:col + 2], in0=stt[:, col:col + 1],
                                                   scalar1=1.0 / D, scalar2=EPS, op0=ALU.mult, op1=ALU.add),
                  reads=[(sttname, col)], writes=[(sttname, col + 1)])
            S.add('pool', lambda e: e.tensor_tensor(out=stt[:, col + 2:col + 3], in0=stt[:, col + 1:col + 2],
                                                    in1=mhalf[:], op=ALU.pow),
                  reads=[(sttname, col + 1), 'mhalf'], writes=[(sttname, col + 2)])

        def p1_stage(t, s):
            i = t % 2
            pb = 4 + i
            pbv = bank(pb).bitcast(BF16)
            col = 4 * t
            if s == 0:
                load_x_tile(t, xt[i], ('xt', i))
            elif s == 1:
                S.add('act', lambda e: e.activation(out=junk[:], in_=xt[i][:], func=AF.Square,
                                                    accum_out=st1[:, col:col + 1]),
                      reads=[('xt', i)], writes=[('st1', col)])
            elif s == 2:
                S.add('dve', lambda e: e.tensor_scalar(out=st1[:, col + 1:col + 2], in0=st1[:, col:col + 1],
                                                       scalar1=1.0 / D, scalar2=EPS, op0=ALU.mult, op1=ALU.add),
                      reads=[('st1', col)], writes=[('st1', col + 1)])
            elif s == 3:
                S.add('pool', lambda e: e.tensor_tensor(out=st1[:, col + 2:col + 3], in0=st1[:, col + 1:col + 2],
                                                        in1=mhalf[:], op=ALU.pow),
                      reads=[('st1', col + 1), 'mhalf'], writes=[('st1', col + 2)])
            elif s == 4:
                S.add('dve', lambda e: e.scalar_tensor_tensor(
                    out=xnb[i][:], in0=xt[i][:], scalar=st1[:, col + 2:col + 3], in1=gpre_bc[:],
                    op0=ALU.mult, op1=ALU.mult),
                    reads=[('xt', i), ('st1', col + 2), 'gpre_bc'], writes=[('xnb', i)])
            elif s == 5:
                for k in range(8):
                    S.add('pe', lambda e, k=k: e.transpose(
                        out=pbv[:, k * 128:(k + 1) * 128], in_=xnb[i][:, k * 128:(k + 1) * 128], identity=ident_b[:]),
                        reads=[('xnb', i), 'ident_b'], writes=bkeys(pb))
            elif s == 6:
                S.add('act', lambda e: e.activation(
                    out=xnT[:, :, t * 128:(t + 1) * 128], in_=pbv.rearrange("p (k c) -> p k c", k=8), func=AF.Copy),
                    reads=bkeys(pb), writes=[('xnT', t)] + bkeys(pb))

        for t0_ in range(0, NT, 2):
            ts_ = [t for t in (t0_, t0_ + 1) if t < NT]
            for s in range(7):
                for t in ts_:
                    p1_stage(t, s)
            if t0_ == 4:
                for b0 in (BLK_Q, BLK_K, BLK_V):
                    prep_blocks(b0, 8)
        for b0 in range(0, 20, 4):
            prep_blocks(b0, 4)
        for b0 in range(BLK_OUT, NBLK, 4):
            prep_blocks(b0, min(4, NBLK - b0))

        if debug:
            S.add('sp', lambda e: e.dma_start(out=dbg['xnT'], in_=xnT[:].rearrange("p k c -> p (k c)")),
                  reads=[('xnT', t) for t in range(NT)], dma=True)

        if stage >= 2:
            wq = sb2("wq", [128, 8, 128], BF16)
            wk = sb2("wk", [128, 8, 128], BF16)
            wv = sb2("wv", [128, 8, 128], BF16)
            qT = sb2("qT", [128, LP], BF16)
            kT = sb2("kT", [128, LP], BF16)
            vv = sb2("vv", [128, NT, 128], BF16)
            eb = [sb2("eb%d" % i, [128, 2, 512], BF16) for i in range(4)]
            spb = [sb2("spb%d" % i, [128, 2, 512], BF16) for i in range(3)]
            gb = [sb2("gb%d" % i, [128, 2, 512], BF16) for i in range(2)]
            wb = [sb2("wb%d" % i, [128, 2, 512], BF16) for i in range(2)]
            allx = [('xnT', t) for t in range(NT)]

            def load_wblk(dst, key, blk):
                S.add('sp', lambda e: e.dma_start(
                    out=dst[:].rearrange("p k c -> p (k c)"), in_=wscr[blk * 128:(blk + 1) * 128, :]),
                    reads=[('wscr', blk)], writes=[key], dma=True)

            gstep = 0
            for hp in range(8):
                load_wblk(wq, 'wq', BLK_Q + hp)
                load_wblk(wk, 'wk', BLK_K + hp)
                load_wblk(wv, 'wv', BLK_V + hp)
                def blk_tiles(gq):
                    return [0] if gq == 0 else list(range(4 * gq - 3, 4 * gq + 1))

                def unit_qk(gq, which, b):
                    tiles = blk_tiles(gq)
                    c0 = tiles[0] * 128
                    n = len(tiles) * 128
                    dst, dkey, wt, wkey, scale = ((qT, 'qT', wq, 'wq', 0.125) if which == 'q'
                                                  else (kT, 'kT', wk, 'wk', 1.0))
                    for k in range(8):
                        S.add('pe', lambda e, k=k: e.matmul(
                            bank(b)[:, 0:n], lhsT=wt[:, k, :], rhs=xnT[:, k, c0:c0 + n],
                            start=(k == 0), stop=(k == 7)),
                            reads=[wkey] + [('xnT', t) for t in tiles], writes=bkeys(b))
                    S.add('dve', lambda e: e.tensor_scalar(
                        out=dst[:, c0:c0 + n], in0=bank(b)[:, 0:n], scalar1=scale, scalar2=None, op0=ALU.mult),
                        reads=bkeys(b), writes=[(dkey, t) for t in tiles] + bkeys(b))

                def unit_v(tiles, b):
                    for j, t in enumerate(tiles):
                        for k in range(8):
                            S.add('pe', lambda e, j=j, t=t, k=k: e.matmul(
                                bank(b)[:, j * 128:(j + 1) * 128], lhsT=xnT[:, k, t * 128:(t + 1) * 128],
                                rhs=wv[:, k, :], start=(k == 0), stop=(k == 7)),
                                reads=['wv', ('xnT', t)], writes=bkeys(b))
                    nt_ = len(tiles)
                    t0_ = tiles[0]
                    S.add('dve', lambda e: e.tensor_copy(
                        out=vv[:, t0_:t0_ + nt_, :], in_=bank(b)[:, 0:nt_ * 128].rearrange("p (t c) -> p t c", t=nt_)),
                        reads=bkeys(b), writes=[('vv', t) for t in tiles] + bkeys(b))

                def block_units(gq, b):
                    tiles = blk_tiles(gq)
                    us = [lambda: unit_qk(gq, 'q', b), lambda: unit_qk(gq, 'k', b)]
                    for h0 in range(0, len(tiles), 2):
                        tl = tiles[h0:h0 + 2]
                        us.append(lambda tl=tl: unit_v(tl, b))
                    return us

                pb_ = 0
                for gq in (0, 1):
                    tiles = blk_tiles(gq)
                    unit_qk(gq, 'q', pb_ % 4); pb_ += 1
                    unit_qk(gq, 'k', pb_ % 4); pb_ += 1
                    for h0 in range(0, len(tiles), 2):
                        unit_v(tiles[h0:h0 + 2], pb_ % 4); pb_ += 1

                steps = []
                for qg in range(9):
                    qb0 = 0 if qg == 0 else 4 * qg - 3
                    nq = 1 if qg == 0 else 4
                    qb1 = qb0 + nq - 1
                    for kb in range(qb1, -1, -1):
                        steps.append(dict(qg=qg, kb=kb, lidx=qb1 - kb, qb0=qb0, qb1=qb1, NQ=nq * 128, qc0=qb0 * 128,
                                          ob=6 + (qg % 2), co=max(0, kb - qb0) * 128,
                                          first=(kb == qb1), last=(kb == 0)))
                cb = 4
                c3 = bank(cb, 2).rearrange("p (h c) -> p h c", h=2)

                def fZ(st, gs):
                    co, NQ, kb, qc0 = st['co'], st['NQ'], st['kb'], st['qc0']
                    zs = (gs % 2) * 2
                    z3 = bank(zs, 2).rearrange("p (h c) -> p h c", h=2)
                    for h in range(2):
                        r0 = 64 * h
                        S.add('pe', lambda e, h=h, r0=r0, z3=z3: e.matmul(
                            z3[:, h, co:NQ], lhsT=kT[r0:r0 + 64, kb * 128:(kb + 1) * 128],
                            rhs=qT[r0:r0 + 64, qc0 + co:qc0 + NQ], start=True, stop=True),
                            reads=[('kT', kb)] + [('qT', j) for j in range(st['qb0'] + co // 128, st['qb1'] + 1)],
                            writes=bkeys(zs + h))

                def fE(st, gs):
                    co, NQ, kb = st['co'], st['NQ'], st['kb']
                    zs = (gs % 2) * 2
                    i = gs % 4
                    z3 = bank(zs, 2).rearrange("p (h c) -> p h c", h=2)
                    S.add('act', lambda e: e.activation(
                        out=eb[i][:, :, co:NQ], in_=z3[:, :, co:NQ], func=AF.Exp),
                        reads=bkeys(zs, 2), writes=[('eb', i)] + bkeys(zs, 2))
                    if kb >= st['qb0']:
                        S.add('dve', lambda e: e.tensor_tensor(
                            out=eb[i][:, :, co:co + 128], in0=eb[i][:, :, co:co + 128],
                            in1=mdiag_b[:].unsqueeze(1).broadcast_to([128, 2, 128]), op=ALU.mult),
                            reads=[('eb', i), 'mdiag_b'], writes=[('eb', i)])
                    if kb == 0:
                        S.add('dve', lambda e: e.tensor_scalar(
                            out=eb[i][:, :, co:NQ], in0=eb[i][:, :, co:NQ], scalar1=padm[:, 0:1],
                            scalar2=None, op0=ALU.mult),
                            reads=[('eb', i), 'padm'], writes=[('eb', i)])

                def fL(st, gs):
                    co, NQ = st['co'], st['NQ']
                    i = gs % 3
                    ie = gs % 4
                    S.add('act', lambda e: e.activation(
                        out=spb[i][:, :, co:NQ], in_=eb[ie][:, :, co:NQ], func=AF.Ln, bias=1.0),
                        reads=[('eb', ie)], writes=[('spb', i)])

                def fT(st, gs):
                    co, NQ = st['co'], st['NQ']
                    i = gs % 3
                    for h in range(2):
                        S.add('pe', lambda e, h=h: e.matmul(
                            c3[:, h, co:NQ], lhsT=tri_b[:], rhs=spb[i][:, h, co:NQ],
                            start=st['first'], stop=False, skip_group_check=True),
                            reads=[('spb', i), 'tri_b'], writes=bkeys(cb + h))

                def fG(st, gs):
                    co, NQ = st['co'], st['NQ']
                    i = gs % 2
                    S.add('act', lambda e: e.activation(
                        out=gb[i][:, :, co:NQ], in_=c3[:, :, co:NQ], func=AF.Exp, scale=-1.0),
                        reads=bkeys(cb, 2), writes=[('gb', i)] + bkeys(cb, 2))

                def fT2(st, gs):
                    co, NQ = st['co'], st['NQ']
                    i = gs % 3
                    if st['last']:
                        return
                    for h in range(2):
                        S.add('pe', lambda e, h=h: e.matmul(
                            c3[:, h, co:NQ], lhsT=tric_b[:], rhs=spb[i][:, h, co:NQ],
                            start=False, stop=False, skip_group_check=True),
                            reads=[('spb', i), 'tric_b'], writes=bkeys(cb + h))

                def fW(st, gs):
                    co, NQ = st['co'], st['NQ']
                    i = gs % 2
                    S.add('dve', lambda e: e.tensor_tensor(
                        out=wb[i][:, :, co:NQ], in0=eb[gs % 4][:, :, co:NQ], in1=gb[i][:, :, co:NQ], op=ALU.mult),
                        reads=[('eb', gs % 4), ('gb', i)], writes=[('wb', i)])

                def fV(st, gs, hp=hp):
                    co, NQ, kb, ob, qc0 = st['co'], st['NQ'], st['kb'], st['ob'], st['qc0']
                    i = gs % 2
                    for h in range(2):
                        r0 = 64 * h
                        S.add('pe', lambda e, h=h, r0=r0: e.matmul(
                            bank(ob)[r0:r0 + 64, co:NQ], lhsT=vv[:, kb, r0:r0 + 64], rhs=wb[i][:, h, co:NQ],
                            start=st['first'], stop=False, skip_group_check=True),
                            reads=[('wb', i), ('vv', kb)], writes=bkeys(ob))
                    if st['last']:
                        S.add('dve', lambda e: e.tensor_copy(
                            out=ysbT[:, hp, qc0:qc0 + NQ], in_=bank(ob)[:, 0:NQ]),
                            reads=bkeys(ob), writes=[('ysbT', hp, st['qg'])] + bkeys(ob))

                n = len(steps)
                for i in range(-2, n + 2):
                    if 0 <= i - 1 < n:
                        fT(steps[i - 1], gstep + i - 1)
                    if 0 <= i + 2 < n:
                        fZ(steps[i + 2], gstep + i + 2)
                    if 0 <= i - 2 < n:
                        fV(steps[i - 2], gstep + i - 2)
                    if 0 <= i < n:
                        st_ = steps[i]
                        if 1 <= st_['qg'] <= 7 and 1 <= st_['lidx'] <= 4:
                            us = block_units(st_['qg'] + 1, 6 + ((st_['qg'] + 1) % 2))
                            us[st_['lidx'] - 1]()
                    if 0 <= i + 1 < n:
                        fE(steps[i + 1], gstep + i + 1)
                    if 0 <= i < n:
                        fL(steps[i], gstep + i)
                    if 0 <= i - 1 < n:
                        fG(steps[i - 1], gstep + i - 1)
                        fT2(steps[i - 1], gstep + i - 1)
                        fW(steps[i - 1], gstep + i - 1)
                gstep += n

            if debug:
                S.add('sp', lambda e: e.dma_start(out=dbg['ysbT'], in_=ysbT[:].rearrange("p k c -> p (k c)")),
                      reads=[('ysbT', hp, qg) for hp in range(8) for qg in range(9)], dma=True)

        def barrier():
            tails = {e: [op for op in S.ops[e] if not op.is_dma][-1:] for e in ENGS}
            dts = [op for q in ENGS for op in S.dma_ops[q][-NDSEM:]]
            for e in ENGS:
                ex = [o for p in ENGS if p != e for o in tails[p]] + dts
                S.add(e, lambda eng: eng.nop(), extra=ex)

        if stage >= 3:
            barrier()
            es2.close()
            NG = 384
            ident_f = sb("ident_f", [128, 128], F32)
            UTf = sb("UTf", [128, 128], F32)
            SGf = sb("SGf", [128, 128], F32)
            ONESf = sb("ONESf", [128, 128], F32)
            ones_b = sb("ones_b", [128, 128], BF16)
            cmask(ident_f, -1, 1, ALU.is_equal)
            cmask(UTf, 1, -1, ALU.is_ge)
            cmask(SGf, -1, 1, ALU.is_gt)
            S.add('pool', lambda e: e.memset(ONESf[:], 1.0), writes=['ONESf'])
            S.add('pool', lambda e: e.memset(ones_b[:], 1.0), writes=['ones_b'])
            ffnp = sb("ffnp", [128, 128], F32)
            S.add('sp', lambda e: e.dma_start(out=ffnp[:], in_=ffnp_d[:, :]), writes=['ffnp'], dma=True)
            smallbc = sb("smallbc", [128, 64], F32)
            S.add('sp', lambda e: e.dma_start(out=smallbc[:], in_=small_d[0:1, :].partition_broadcast(128)),
                  writes=['smallbc'], dma=True)
            Abc = sb("Abc", [128, 16], F32)
            S.add('act', lambda e: e.activation(out=Abc[:], in_=smallbc[:, 16:32], func=AF.Exp),
                  reads=['smallbc'], writes=['Abc'])
            S.add('dve', lambda e: e.tensor_scalar(out=Abc[:], in0=Abc[:], scalar1=-1.0, scalar2=None, op0=ALU.mult),
                  reads=['Abc'], writes=['Abc'])
            wdt_b = sb("wdt_b", [128, 8, 16], BF16)
            S.add('pool', lambda e: e.dma_start(out=wdt_b[:].rearrange("p k c -> p (k c)"), in_=wdt_d[:, :]),
                  writes=['wdt_b'], dma=True)
            gpost_bc = sb("gpost_bc", [128, D], F32)
            gfpost_bc = sb("gfpost_bc", [128, D], F32)
            S.add('sp', lambda e: e.dma_start(out=gpost_bc[:], in_=vecs_d[1:2, :].partition_broadcast(128)),
                  writes=['gpost_bc'], dma=True)
            S.add('sp', lambda e: e.dma_start(out=gfpost_bc[:], in_=vecs_d[3:4, :].partition_broadcast(128)),
                  writes=['gfpost_bc'], dma=True)

            NWB = 4
            wbuf = [sb("wbuf%d" % i, [128, 1024], BF16) for i in range(NWB)]
            wcnt = [0]

            def wload(blk):
                i = wcnt[0] % NWB
                wcnt[0] += 1
                S.add('sp', lambda e: e.dma_start(out=wbuf[i][:], in_=wscr[blk * 128:(blk + 1) * 128, :]),
                      reads=[('wscr', blk)], writes=[('wbuf', i)], dma=True)
                return wbuf[i], ('wbuf', i)

            state = sb("state", [128, 1024], F32)
            state_b = sb("state_b", [128, 1024], BF16)
            S.add('pool', lambda e: e.memset(state[:], 0.0), writes=['state'])
            S.add('pool', lambda e: e.memset(state_b[:], 0.0), writes=['state_b'])
            halo = sb("halo", [128, 12, 3], F32)
            ghalo = sb("ghalo", [128, NFC, 2], F32)
            S.add('pool', lambda e: e.memset(halo[:], 0.0), writes=['halo'])
            S.add('pool', lambda e: e.memset(ghalo[:], 0.0), writes=['ghalo'])

            hbuf = sb("hbuf", [128, 3, D], F32)
            actT = sb("actT", [128, 8, NG], BF16)
            R1 = sb("R1", [128, 6 * D], F32)
            zs = R1[:, 0:3 * D].rearrange("p (j c) -> p j c", j=3)
            Xtok = R1[:, 3 * D:6 * D].rearrange("p (j c) -> p j c", j=3)
            aT = R1[:, 0:NFC * NG // 2].bitcast(BF16).rearrange("p (f c) -> p f c", f=NFC)
            BT = sb("BT", [128, 2, NG], BF16)
            CT = sb("CT", [128, 2, NG], BF16)
            Btok = sb("Btok", [128, 3, 256], BF16)
            cbuf = [sb("cbuf%d" % i, [128, NG + 3], F32) for i in range(4)]
            cacc = [sb("cacc%d" % i, [128, NG], F32) for i in range(4)]
            sfm = [sb("sfm%d" % i, [128, NG], F32) for i in range(4)]
            t1 = sb("t1", [128, D], F32)
            RH = sb("RH", [128, 16, 128], F32)
            Xdt = sb("Xdt", [128, D], BF16)
            Xdec = sb("Xdec", [128, D], BF16)
            Eb = [sb("Eb%d" % i, [128, 512], F32) for i in range(2)]
            MTb = [sb("MTb%d" % i, [128, 4, 128], BF16) for i in range(2)]
            CBm = sb("CBm", [128, 2, 128], F32)
            tokbfs = [sb("tokbf%d" % i, [128, D], BF16) for i in range(3)]
            ygnT = sb("ygnT", [128, 8, NG], BF16)
            ysbn = sb("ysbn", [128, 8, NG], BF16)
            r_bc = sb("r_bc", [128, NG], F32)
            lnr = r_bc
            sqb = [sb("sqb%d" % i, [128, NG], BF16) for i in range(2)]
            dtt = sb("dtt", [128, 3, 16], F32)
            att = sb("att", [128, 3, 16], F32)
            dte = sb("dte", [128, 3, 16], F32)
            ex48 = sb("ex48", [128, 48], F32)
            dtd = sb("dtd", [128, 16], F32)
            st3 = sb("st3", [128, 64], F32)
            gcb = cbuf
            RHf = RH[:].rearrange("p h c -> p (h c)")
            tmps = [(t1[:], ['t1']), (RHf[:, 0:D], ['RH', ('tmpf', 1)]), (RHf[:, D:2 * D], ['RH', ('tmpf', 2)])]
            glb = sfm

            def bc3(ap2, n):
                return ap2.unsqueeze(2).broadcast_to([128, 16, n])

            def v3(ap2):
                return ap2.rearrange("p (h d) -> p h d", h=16)

            def run_rr(gens):
                gens = list(gens)
                while gens:
                    nxt = []
                    for gq in gens:
                        try:
                            next(gq)
                            nxt.append(gq)
                        except StopIteration:
                            pass
                    gens = nxt

            def rinfo(slot):
                return st3[:, 4 * slot + 2:4 * slot + 3], (('st3', slot), 2)

            def g_rstd3(src, srcreads, slot):
                col = 4 * slot
                key = ('st3', slot)
                S.add('act', lambda e: e.activation(out=junk[:], in_=src, func=AF.Square,
                                                    accum_out=st3[:, col:col + 1]),
                      reads=srcreads, writes=[(key, 0)])
                yield
                S.add('dve', lambda e: e.tensor_scalar(out=st3[:, col + 1:col + 2], in0=st3[:, col:col + 1],
                                                       scalar1=1.0 / D, scalar2=EPS, op0=ALU.mult, op1=ALU.add),
                      reads=[(key, 0)], writes=[(key, 1)])
                yield
                S.add('pool', lambda e: e.tensor_tensor(out=st3[:, col + 2:col + 3], in0=st3[:, col + 1:col + 2],
                                                        in1=mhalf[:], op=ALU.pow),
                      reads=[(key, 1), 'mhalf'], writes=[(key, 2)])
                yield

            def rstd3(src, srcreads, slot):
                for _ in g_rstd3(src, srcreads, slot):
                    pass
                return rinfo(slot)

            def g_norm_transpose(j, src, srckeys, slot, dstT, dkey, pb=7):
                rcol, rkey = rinfo(slot)
                tb = tokbfs[j % 3]
                tk = ('tokbf', j % 3)
                S.add('dve', lambda e: e.tensor_scalar(out=tb[:], in0=src, scalar1=rcol, scalar2=None,
                                                       op0=ALU.mult),
                      reads=srckeys + [rkey], writes=[tk])
                yield
                pbv = bank(pb).bitcast(BF16)
                for k in range(8):
                    S.add('pe', lambda e, k=k: e.transpose(
                        out=pbv[:, k * 128:(k + 1) * 128], in_=tb[:, k * 128:(k + 1) * 128], identity=ident_b[:]),
                        reads=[tk, 'ident_b'], writes=bkeys(pb))
                yield
                S.add('act', lambda e: e.activation(
                    out=dstT[:, :, j * 128:(j + 1) * 128], in_=pbv.rearrange("p (k c) -> p k c", k=8), func=AF.Copy),
                    reads=bkeys(pb), writes=[(dkey, j)] + bkeys(pb))
                yield

            def norm_transpose(j, src, srckeys, slot, dstT, dkey, pb=7):
                for _ in g_norm_transpose(j, src, srckeys, slot, dstT, dkey, pb):
                    pass

            def apply_gain(dstT, dkey, gcol):
                for k in range(8):
                    S.add('dve', lambda e, k=k: e.tensor_scalar(
                        out=dstT[:, k, :], in0=dstT[:, k, :], scalar1=pp[:, gcol + k:gcol + k + 1], scalar2=None,
                        op0=ALU.mult),
                        reads=[(dkey, jj) for jj in range(3)] + ['pp'], writes=[(dkey, jj) for jj in range(3)])

            def do_group(g):
                t0 = 3 * g
                gc0 = t0 * 128
                hk = [('hbuf', j) for j in range(3)]
                def chain_a(j):
                    load_x_tile(t0 + j, hbuf[:, j, :], ('hbuf', j))
                    yield
                    yield from g_rstd3(hbuf[:, j, :], [('hbuf', j)], j)
                    yield from g_norm_transpose(j, hbuf[:, j, :], [('hbuf', j)], j, actT, 'actT', pb=7 - j)

                run_rr([chain_a(j) for j in range(3)])
                apply_gain(actT, 'actT', 76)
                ak = [('actT', j) for j in range(3)]
                if g == 0:
                    dbgdump('actT', actT[:].rearrange("p k c -> p (k c)"), [128, 8 * NG], BF16, ak)
                for cbk in range(8):
                    w, wkey = wload(BLK_Z + cbk)
                    w3 = w[:].rearrange("p (k c) -> p k c", k=8)
                    for j in range(3):
                        b = 2 * j + cbk // 4
                        for k in range(8):
                            S.add('pe', lambda e, b=b, j=j, k=k, w3=w3, cbk=cbk: e.matmul(
                                bank(b)[:, (cbk % 4) * 128:(cbk % 4 + 1) * 128], lhsT=actT[:, k, j * 128:(j + 1) * 128],
                                rhs=w3[:, k, :], start=(k == 0), stop=(k == 7)),
                                reads=[wkey, ('actT', j)], writes=bkeys(b))
                for j in range(3):
                    S.add('act', lambda e, j=j: e.activation(out=zs[:, j, :], in_=bank(2 * j, 2), func=AF.Silu),
                          reads=bkeys(2 * j, 2), writes=[('zs', j)] + bkeys(2 * j, 2))
                for j in range(3):
                    for k in range(8):
                        S.add('pe', lambda e, j=j, k=k: e.matmul(
                            bank(6)[:, j * 16:(j + 1) * 16], lhsT=actT[:, k, j * 128:(j + 1) * 128],
                            rhs=wdt_b[:, k, :], start=(k == 0), stop=(k == 7)),
                            reads=['wdt_b', ('actT', j)], writes=bkeys(6))
                S.add('dve', lambda e: e.tensor_tensor(
                    out=dte[:], in0=bank(6)[:, 0:48].rearrange("p (j h) -> p j h", j=3),
                    in1=smallbc[:, 0:16].unsqueeze(1).broadcast_to([128, 3, 16]), op=ALU.add),
                    reads=bkeys(6) + ['smallbc'], writes=['dte'] + bkeys(6))
                S.add('act', lambda e: e.activation(out=dte[:], in_=dte[:], func=AF.Exp), reads=['dte'], writes=['dte'])
                S.add('act', lambda e: e.activation(out=dtt[:], in_=dte[:], func=AF.Ln, bias=1.0),
                      reads=['dte'], writes=['dtt'])
                if g == 0:
                    S.add('dve', lambda e: e.tensor_scalar(out=dtt[:, 0, :], in0=dtt[:, 0, :], scalar1=padm[:, 0:1],
                                                           scalar2=None, op0=ALU.mult),
                          reads=['dtt', 'padm'], writes=['dtt'])
                S.add('dve', lambda e: e.tensor_tensor(
                    out=att[:], in0=dtt[:], in1=Abc[:].unsqueeze(1).broadcast_to([128, 3, 16]), op=ALU.mult),
                    reads=['dtt', 'Abc'], writes=['att'])
                pend = []

                def emit_tr(jb, i):
                    for j in range(3):
                        S.add('pe', lambda e, i=i, j=j, jb=jb: e.transpose(
                            out=bank(2 * j + jb // 4)[:, (jb % 4) * 128:(jb % 4 + 1) * 128],
                            in_=sfm[i][:, j * 128:(j + 1) * 128], identity=ident_f[:]),
                            reads=[('sfm', i), 'ident_f'], writes=bkeys(2 * j + jb // 4))

                for jb in range(12):
                    w, wkey = wload(BLK_XBC + jb)
                    w3 = w[:].rearrange("p (k c) -> p k c", k=8)
                    b = 6 + (jb % 2)
                    i = jb % 4
                    for k in range(8):
                        S.add('pe', lambda e, b=b, k=k, w3=w3: e.matmul(
                            bank(b)[:, 0:NG], lhsT=w3[:, k, :], rhs=actT[:, k, :], start=(k == 0), stop=(k == 7)),
                            reads=[wkey] + ak, writes=bkeys(b))
                    if len(pend) >= 2:
                        emit_tr(*pend.pop(0))
                    S.add('act', lambda e, b=b, i=i: e.activation(out=cbuf[i][:, 3:3 + NG], in_=bank(b)[:, 0:NG],
                                                                  func=AF.Copy),
                          reads=bkeys(b), writes=[('cbuf', i)] + bkeys(b))
                    S.add('pool', lambda e, i=i, jb=jb: e.tensor_copy(out=cbuf[i][:, 0:3], in_=halo[:, jb, :]),
                          reads=['halo'], writes=[('cbuf', i)])
                    S.add('pool', lambda e, i=i, jb=jb: e.tensor_copy(out=halo[:, jb, :], in_=cbuf[i][:, NG:NG + 3]),
                          reads=[('cbuf', i)], writes=['halo'])
                    for kk in range(4):
                        wc = pp[:, 16 + jb * 4 + kk:16 + jb * 4 + kk + 1]
                        if kk == 0:
                            S.add('dve', lambda e, i=i, wc=wc: e.tensor_scalar(
                                out=cacc[i][:], in0=cbuf[i][:, 0:NG], scalar1=wc, scalar2=None, op0=ALU.mult),
                                reads=[('cbuf', i), 'pp'], writes=[('cacc', i)])
                        else:
                            S.add('dve', lambda e, i=i, wc=wc, kk=kk: e.scalar_tensor_tensor(
                                out=cacc[i][:], in0=cbuf[i][:, kk:kk + NG], scalar=wc, in1=cacc[i][:],
                                op0=ALU.mult, op1=ALU.add),
                                reads=[('cbuf', i), 'pp', ('cacc', i)], writes=[('cacc', i)])
                    bias = pp[:, 64 + jb:65 + jb]
                    if jb < 8:
                        S.add('act', lambda e, i=i, bias=bias: e.activation(out=sfm[i][:], in_=cacc[i][:], func=AF.Silu,
                                                                            bias=bias),
                              reads=[('cacc', i), 'pp'], writes=[('sfm', i)])
                        pend.append((jb, i))
                    else:
                        dst = BT if jb < 10 else CT
                        dk = 'BT' if jb < 10 else 'CT'
                        gg = jb % 2
                        S.add('act', lambda e, i=i, bias=bias, dst=dst, gg=gg: e.activation(
                            out=dst[:, gg, :], in_=cacc[i][:], func=AF.Silu, bias=bias),
                            reads=[('cacc', i), 'pp'], writes=[(dk, gg)])
                while pend:
                    emit_tr(*pend.pop(0))
                for j in range(3):
                    S.add('act', lambda e, j=j: e.activation(out=Xtok[:, j, :], in_=bank(2 * j, 2), func=AF.Copy),
                          reads=bkeys(2 * j, 2), writes=[('Xtok', j)] + bkeys(2 * j, 2))
                pbv = bank(7).bitcast(BF16)
                for gg in range(2):
                    for j in range(3):
                        S.add('pe', lambda e, j=j, gg=gg, pbv=pbv: e.transpose(
                            out=pbv[:, j * 256 + gg * 128:j * 256 + (gg + 1) * 128],
                            in_=BT[:, gg, j * 128:(j + 1) * 128], identity=ident_b[:]),
                            reads=[('BT', gg), 'ident_b'], writes=bkeys(7))
                S.add('dve', lambda e, pbv=pbv: e.tensor_copy(
                    out=Btok[:], in_=pbv[:, 0:768].rearrange("p (j c) -> p j c", j=3)),
                    reads=bkeys(7), writes=['Btok'] + bkeys(7))
                if g == 0:
                    dbgdump('zs', R1[:, 0:3 * D], [128, 3 * D], F32, [('zs', jj) for jj in range(3)])
                    dbgdump('Xtok', R1[:, 3 * D:6 * D], [128, 3 * D], F32, [('Xtok', jj) for jj in range(3)])
                    dbgdump('dtt', dtt[:].rearrange("p j h -> p (j h)"), [128, 48], F32, ['dtt'])
                    dbgdump('att', att[:].rearrange("p j h -> p (j h)"), [128, 48], F32, ['att'])
                    dbgdump('BT', BT[:].rearrange("p g c -> p (g c)"), [128, 2 * NG], BF16, [('BT', 0), ('BT', 1)])
                    dbgdump('CT', CT[:].rearrange("p g c -> p (g c)"), [128, 2 * NG], BF16, [('CT', 0), ('CT', 1)])
                    dbgdump('Btok', Btok[:].rearrange("p j c -> p (j c)"), [128, 768], BF16, ['Btok'])
                for j in range(3):
                    tc0 = j * 128
                    a_j = att[:, j, :]
                    for ci, lt in enumerate((UTf, SGf, ONESf)):
                        S.add('pe', lambda e, ci=ci, lt=lt, a_j=a_j: e.matmul(
                            bank(0)[:, ci * 16:(ci + 1) * 16], lhsT=lt[:], rhs=a_j, start=True, stop=True),
                            reads=['att', 'UTf', 'SGf', 'ONESf'], writes=bkeys(0))
                    S.add('act', lambda e: e.activation(out=ex48[:], in_=bank(0)[:, 0:48], func=AF.Exp),
                          reads=bkeys(0), writes=['ex48'] + bkeys(0))
                    S.add('dve', lambda e, a_j=a_j: e.tensor_tensor(
                        out=RH[:], in0=UTf[:].unsqueeze(1).broadcast_to([128, 16, 128]), in1=bc3(a_j, 128),
                        op=ALU.mult),
                        reads=['att', 'UTf'], writes=['RH'])
                    for gg in range(2):
                        S.add('pe', lambda e, gg=gg, tc0=tc0: e.matmul(
                            bank(0)[:, 128 + gg * 128:256 + gg * 128], lhsT=BT[:, gg, tc0:tc0 + 128],
                            rhs=CT[:, gg, tc0:tc0 + 128], start=True, stop=True),
                            reads=[('BT', gg), ('CT', gg)], writes=bkeys(0))
                    S.add('dve', lambda e: e.tensor_tensor(
                        out=CBm[:], in0=bank(0)[:, 128:384].rearrange("p (g c) -> p g c", g=2),
                        in1=UTf[:].unsqueeze(1).broadcast_to([128, 2, 128]), op=ALU.mult),
                        reads=bkeys(0) + ['UTf'], writes=['CBm'] + bkeys(0))
                    for gg in range(2):
                        S.add('pe', lambda e, gg=gg, tc0=tc0: e.matmul(
                            bank(5 + gg), lhsT=CT[:, gg, tc0:tc0 + 128], rhs=state_b[:, gg * 512:(gg + 1) * 512],
                            start=True, stop=True),
                            reads=[('CT', gg), 'state_b'], writes=bkeys(5 + gg))
                    S.add('dve', lambda e, j=j: e.tensor_tensor(out=dtd[:], in0=dtt[:, j, :], in1=ex48[:, 16:32],
                                                                op=ALU.mult),
                          reads=['dtt', 'ex48'], writes=['dtd'])
                    S.add('dve', lambda e, j=j: e.tensor_tensor(out=v3(Xdt[:]), in0=v3(Xtok[:, j, :]),
                                                                in1=bc3(dtt[:, j, :], 64), op=ALU.mult),
                          reads=[('Xtok', j), 'dtt'], writes=['Xdt'])
                    S.add('dve', lambda e, j=j: e.tensor_tensor(out=v3(Xdec[:]), in0=v3(Xtok[:, j, :]),
                                                                in1=bc3(dtd[:], 64), op=ALU.mult),
                          reads=[('Xtok', j), 'dtd'], writes=['Xdec'])
                    def fD(hq):
                        i = hq % 2
                        S.add('pe', lambda e, i=i, hq=hq: e.matmul(
                            bank(1 + i), lhsT=SGf[:], rhs=RH[:, 4 * hq:4 * hq + 4, :].rearrange("p h c -> p (h c)"),
                            start=True, stop=True),
                            reads=['RH', 'SGf'], writes=bkeys(1 + i))
                        S.add('act', lambda e, i=i: e.activation(out=Eb[i][:], in_=bank(1 + i), func=AF.Exp),
                              reads=bkeys(1 + i), writes=[('Eb', i)] + bkeys(1 + i))

                    def fY(hq):
                        i = hq % 2
                        gg = hq // 2
                        S.add('dve', lambda e, i=i, gg=gg: e.tensor_tensor(
                            out=MTb[i][:], in0=Eb[i][:].rearrange("p (h c) -> p h c", h=4),
                            in1=CBm[:, gg:gg + 1, :].broadcast_to([128, 4, 128]), op=ALU.mult),
                            reads=[('Eb', i), 'CBm'], writes=[('MTb', i)])
                        for hh in range(4):
                            h = 4 * hq + hh
                            S.add('pe', lambda e, i=i, hh=hh, h=h: e.matmul(
                                bank(3 + h // 8)[:, (h % 8) * 64:(h % 8 + 1) * 64], lhsT=MTb[i][:, hh, :],
                                rhs=Xdt[:, h * 64:(h + 1) * 64], start=True, stop=True),
                                reads=[('MTb', i), 'Xdt'], writes=bkeys(3 + h // 8))

                    fD(0)
                    fD(1)
                    fY(0)
                    fD(2)
                    fY(1)
                    fD(3)
                    fY(2)
                    fY(3)
                    S.add('dve', lambda e: e.tensor_tensor(out=v3(t1[:]), in0=v3(bank(5, 2)), in1=bc3(ex48[:, 0:16], 64),
                                                           op=ALU.mult),
                          reads=bkeys(5, 2) + ['ex48'], writes=['t1'] + bkeys(5, 2))
                    S.add('dve', lambda e: e.tensor_tensor(out=t1[:], in0=bank(3, 2), in1=t1[:], op=ALU.add),
                          reads=bkeys(3, 2) + ['t1'], writes=['t1'] + bkeys(3, 2))
                    for gg in range(2):
                        S.add('pe', lambda e, gg=gg, j=j: e.matmul(
                            bank(5 + gg), lhsT=Btok[:, j, gg * 128:(gg + 1) * 128], rhs=Xdec[:, gg * 512:(gg + 1) * 512],
                            start=True, stop=True),
                            reads=['Btok', 'Xdec'], writes=bkeys(5 + gg))
                    S.add('dve', lambda e, j=j: e.tensor_tensor(out=v3(Xtok[:, j, :]), in0=v3(Xtok[:, j, :]),
                                                                in1=bc3(smallbc[:, 32:48], 64), op=ALU.mult),
                          reads=[('Xtok', j), 'smallbc'], writes=[('Xtok', j)])
                    S.add('dve', lambda e, j=j: e.tensor_tensor(out=t1[:], in0=t1[:], in1=Xtok[:, j, :], op=ALU.add),
                          reads=[('Xtok', j), 't1'], writes=['t1'])
                    S.add('dve', lambda e, j=j: e.tensor_tensor(out=t1[:], in0=t1[:], in1=zs[:, j, :], op=ALU.mult),
                          reads=[('zs', j), 't1'], writes=['t1'])
                    S.add('dve', lambda e: e.tensor_tensor(out=v3(state[:]), in0=v3(state[:]), in1=bc3(ex48[:, 32:48], 64),
                                                           op=ALU.mult),
                          reads=['state', 'ex48'], writes=['state'])
                    S.add('dve', lambda e: e.tensor_tensor(out=state[:], in0=bank(5, 2), in1=state[:], op=ALU.add),
                          reads=bkeys(5, 2) + ['state'], writes=['state'] + bkeys(5, 2))
                    S.add('act', lambda e: e.activation(out=state_b[:], in_=state[:], func=AF.Copy),
                          reads=['state'], writes=['state_b'])
                    if g == 0:
                        dbgdump('yg%d' % j, t1[:], [128, D], F32, ['t1'])
                        dbgdump('ex48_%d' % j, ex48[:], [128, 48], F32, ['ex48'])
                    rstd3(t1[:], ['t1'], 3 + j)
                    norm_transpose(j, t1[:], ['t1'], 3 + j, ygnT, 'ygnT')
                apply_gain(ygnT, 'ygnT', 0)
                for k in range(8):
                    i = k % 2
                    S.add('act', lambda e, i=i, k=k: e.activation(out=sqb[i][:], in_=ysbT[:, k, gc0:gc0 + NG],
                                                                  func=AF.Square),
                          reads=[('ysbT', k, q) for q in range(9)], writes=[('sqb', i)])
                    S.add('pe', lambda e, i=i, k=k: e.matmul(bank(6)[:, 0:NG], lhsT=ones_b[:], rhs=sqb[i][:],
                                                             start=(k == 0), stop=(k == 7)),
                          reads=[('sqb', i), 'ones_b'], writes=bkeys(6))
                S.add('act', lambda e: e.activation(out=lnr[:], in_=bank(6)[:, 0:NG], func=AF.Ln, scale=1.0 / D, bias=EPS),
                      reads=bkeys(6), writes=['lnr', 'r_bc'] + bkeys(6))
                S.add('act', lambda e: e.activation(out=r_bc[:], in_=lnr[:], func=AF.Exp, scale=-0.5),
                      reads=['lnr', 'r_bc'], writes=['r_bc', 'lnr'])
                for k in range(8):
                    S.add('dve', lambda e, k=k: e.scalar_tensor_tensor(
                        out=ysbn[:, k, :], in0=ysbT[:, k, gc0:gc0 + NG], scalar=pp[:, 8 + k:9 + k], in1=r_bc[:],
                        op0=ALU.mult, op1=ALU.mult),
                        reads=[('ysbT', k, q) for q in range(9)] + ['pp', 'r_bc'], writes=[('ysbn', k)])
                for kc in range(16):
                    w, wkey = wload(BLK_OUT + kc)
                    src = ygnT if kc < 8 else ysbn
                    sk = [('ygnT', jj) for jj in range(3)] if kc < 8 else [('ysbn', kc - 8)]
                    for j in range(3):
                        for hf in range(2):
                            S.add('pe', lambda e, j=j, hf=hf, w=w, src=src, kc=kc: e.matmul(
                                bank(2 * j + hf), lhsT=src[:, kc % 8, j * 128:(j + 1) * 128],
                                rhs=w[:, hf * 512:(hf + 1) * 512], start=(kc == 0), stop=(kc == 15)),
                                reads=[wkey] + sk, writes=bkeys(2 * j + hf))
                def chain_post(j, gbc, gkey, slot0, final):
                    tmp, tkk = tmps[j]
                    S.add('dve', lambda e: e.tensor_tensor(
                        out=tmp, in0=bank(2 * j, 2), in1=gbc[:], op=ALU.mult),
                        reads=bkeys(2 * j, 2) + [gkey], writes=tkk + bkeys(2 * j, 2))
                    yield
                    yield from g_rstd3(bank(2 * j, 2), bkeys(2 * j, 2), slot0 + j)
                    rc, rk = rinfo(slot0 + j)
                    S.add('dve', lambda e: e.scalar_tensor_tensor(
                        out=hbuf[:, j, :], in0=tmp, scalar=rc, in1=hbuf[:, j, :], op0=ALU.mult, op1=ALU.add),
                        reads=tkk + [rk, ('hbuf', j)], writes=[('hbuf', j)])
                    yield
                    t = t0 + j
                    if final and t >= 1:
                        S.add('sp', lambda e: e.dma_start(out=out_d[(t - 1) * 128:t * 128, :], in_=hbuf[:, j, :]),
                              reads=[('hbuf', j)], dma=True)

                run_rr([chain_post(j, gpost_bc, 'gpost_bc', 6, False) for j in range(3)])
                if g == 0:
                    dbgdump('ygnT', ygnT[:].rearrange("p k c -> p (k c)"), [128, 8 * NG], BF16, [('ygnT', jj) for jj in range(3)])
                    dbgdump('ysbn', ysbn[:].rearrange("p k c -> p (k c)"), [128, 8 * NG], BF16, [('ysbn', k) for k in range(8)])
                    dbgdump('r_bc', r_bc[:], [128, NG], F32, ['r_bc'])
                    dbgdump('h1', hbuf[:].rearrange("p j c -> p (j c)"), [128, 3 * D], F32, hk)
                    dbgdump('state', state[:], [128, D], F32, ['state'])
                def chain_e(j):
                    yield from g_rstd3(hbuf[:, j, :], [('hbuf', j)], 9 + j)
                    yield from g_norm_transpose(j, hbuf[:, j, :], [('hbuf', j)], 9 + j, actT, 'actT', pb=7 - j)

                run_rr([chain_e(j) for j in range(3)])
                apply_gain(actT, 'actT', 84)
                for fc in range(NFC):
                    wg, wgk = wload(BLK_GATE + fc)
                    wu, wuk = wload(BLK_UP + fc)
                    wg3 = wg[:].rearrange("p (k c) -> p k c", k=8)
                    wu3 = wu[:].rearrange("p (k c) -> p k c", k=8)
                    i = fc % 4
                    bg = fc % 3
                    bu = 3 + fc % 3
                    for k in range(8):
                        S.add('pe', lambda e, bg=bg, k=k, wg3=wg3: e.matmul(
                            bank(bg)[:, 0:NG], lhsT=wg3[:, k, :], rhs=actT[:, k, :], start=(k == 0), stop=(k == 7)),
                            reads=[wgk] + ak, writes=bkeys(bg))
                    for k in range(8):
                        S.add('pe', lambda e, bu=bu, k=k, wu3=wu3: e.matmul(
                            bank(bu)[:, 0:NG], lhsT=wu3[:, k, :], rhs=actT[:, k, :], start=(k == 0), stop=(k == 7)),
                            reads=[wuk] + ak, writes=bkeys(bu))
                    S.add('act', lambda e, bg=bg, i=i: e.activation(out=gcb[i][:, 2:2 + NG], in_=bank(bg)[:, 0:NG],
                                                                    func=AF.Copy),
                          reads=bkeys(bg), writes=[('cbuf', i)] + bkeys(bg))
                    S.add('pool', lambda e, i=i, fc=fc: e.tensor_copy(out=gcb[i][:, 0:2], in_=ghalo[:, fc, :]),
                          reads=['ghalo'], writes=[('cbuf', i)])
                    S.add('pool', lambda e, i=i, fc=fc: e.tensor_copy(out=ghalo[:, fc, :], in_=gcb[i][:, NG:NG + 2]),
                          reads=[('cbuf', i)], writes=['ghalo'])
                    for kk in range(3):
                        wc = ffnp[:, fc * 3 + kk:fc * 3 + kk + 1]
                        if kk == 0:
                            S.add('dve', lambda e, i=i, wc=wc: e.tensor_scalar(
                                out=cacc[i][:], in0=gcb[i][:, 0:NG], scalar1=wc, scalar2=None, op0=ALU.mult),
                                reads=[('cbuf', i), 'ffnp'], writes=[('cacc', i)])
                        else:
                            S.add('dve', lambda e, i=i, wc=wc, kk=kk: e.scalar_tensor_tensor(
                                out=cacc[i][:], in0=gcb[i][:, kk:kk + NG], scalar=wc, in1=cacc[i][:],
                                op0=ALU.mult, op1=ALU.add),
                                reads=[('cbuf', i), 'ffnp', ('cacc', i)], writes=[('cacc', i)])
                    S.add('act', lambda e, i=i, fc=fc: e.activation(out=glb[i][:], in_=cacc[i][:], func=AF.Gelu_apprx_tanh,
                                                                    bias=ffnp[:, 66 + fc:67 + fc]),
                          reads=[('cacc', i), 'ffnp'], writes=[('sfm', i)])
                    S.add('dve', lambda e, i=i, fc=fc, bu=bu: e.tensor_tensor(
                        out=aT[:, fc, :], in0=bank(bu)[:, 0:NG], in1=glb[i][:], op=ALU.mult),
                        reads=bkeys(bu) + [('sfm', i)],
                        writes=[('zs', jj) for jj in range(3)] + [('Xtok', jj) for jj in range(3)] + bkeys(bu))
                aTk = [('zs', jj) for jj in range(3)] + [('Xtok', jj) for jj in range(3)]
                for fc in range(NFC):
                    w, wkey = wload(BLK_DOWN + fc)
                    for j in range(3):
                        for hf in range(2):
                            S.add('pe', lambda e, j=j, hf=hf, w=w, fc=fc: e.matmul(
                                bank(2 * j + hf), lhsT=aT[:, fc, j * 128:(j + 1) * 128],
                                rhs=w[:, hf * 512:(hf + 1) * 512], start=(fc == 0), stop=(fc == NFC - 1)),
                                reads=[wkey] + aTk, writes=bkeys(2 * j + hf))
                run_rr([chain_post(j, gfpost_bc, 'gfpost_bc', 12, True) for j in range(3)])

            for g in range(11 if stage >= 4 else 1):
                do_group(g)

        tail = [op for q in ENGS for op in S.dma_ops[q][-NDSEM:]]
        S.add('sp', lambda e: e.nop(), extra=tail)
        es2.close()
        S.emit(nc, es)
    return nc


def _host_layout(inputs):
    f = np.float32
    w_in = np.asarray(inputs["w_in"], f)[0]
    w_out = np.asarray(inputs["w_out"], f)[0]
    w_up = np.asarray(inputs["w_up"], f)[0]
    w_down = np.asarray(inputs["w_down"], f)[0]

    def kblocks(w, c0, nb):
        sub = w[:, c0:c0 + nb * 128].reshape(8, 128, nb, 128)
        return np.ascontiguousarray(sub.transpose(2, 1, 0, 3)).reshape(nb * 128, 1024)

    blocks = [kblocks(w_in, OFF_Z, 8), kblocks(w_in, OFF_XBC, 12), kblocks(w_in, OFF_Q, 8),
              kblocks(w_in, OFF_K, 8), kblocks(w_in, OFF_V, 8),
              w_out.reshape(16 * 128, 1024),
              kblocks(w_up, 0, 22), kblocks(w_up, DFF, 22),
              w_down.reshape(22 * 128, 1024)]
    wblk = np.ascontiguousarray(np.concatenate(blocks, axis=0))
    assert wblk.shape == (NBLK * 128, 1024)
    wdt = np.ascontiguousarray(w_in[:, OFF_DT:OFF_DT + 16].reshape(8, 128, 16).transpose(1, 0, 2)).reshape(128, 128)
    vecs = np.zeros((8, D), f)
    vecs[0] = np.asarray(inputs["mix_pre_g"], f)[0]
    vecs[1] = np.asarray(inputs["mix_post_g"], f)[0]
    vecs[2] = np.asarray(inputs["ffn_pre_g"], f)[0]
    vecs[3] = np.asarray(inputs["ffn_post_g"], f)[0]
    pp = np.zeros((128, 128), f)
    pp[:, 0:8] = np.asarray(inputs["ssd_norm_g"], f)[0].reshape(8, 128).T
    pp[:, 8:16] = np.asarray(inputs["sb_norm_g"], f)[0].reshape(8, 128).T
    cw = np.asarray(inputs["ssd_conv_w"], f)[0]
    pp[:, 16:64] = cw.reshape(4, 12, 128).transpose(2, 1, 0).reshape(128, 48)
    pp[:, 64:76] = np.asarray(inputs["ssd_conv_b"], f)[0].reshape(12, 128).T
    pp[:, 76:84] = np.asarray(inputs["mix_pre_g"], f)[0].reshape(8, 128).T
    pp[:, 84:92] = np.asarray(inputs["ffn_pre_g"], f)[0].reshape(8, 128).T
    fw = np.asarray(inputs["ffn_conv_w"], f)[0]
    ffn = np.zeros((128, 128), f)
    ffn[:, 0:66] = fw.reshape(3, 22, 128).transpose(2, 1, 0).reshape(128, 66)
    ffn[:, 66:88] = np.asarray(inputs["ffn_conv_b"], f)[0].reshape(22, 128).T
    small = np.zeros((1, 64), f)
    small[0, 0:16] = np.asarray(inputs["ssd_dt_bias"], f)[0]
    small[0, 16:32] = np.asarray(inputs["ssd_a_log"], f)[0]
    small[0, 32:48] = np.asarray(inputs["ssd_d"], f)[0]
    return dict(wblk=wblk, wdt=wdt, vecs=vecs, ppd=pp, ffnpd=ffn, small=small,
                meta=np.ascontiguousarray(np.asarray(inputs["meta_tokens"], f)))


def kernel(**inputs):
    x = np.asarray(inputs["x"], np.float32)
    shared = _host_layout(inputs)
    nc = build_nc()
    in_maps = []
    for b in range(8):
        m = dict(shared)
        m["x"] = np.ascontiguousarray(x[b])
        in_maps.append(m)
    res = run_bass_kernel_spmd(nc, in_maps, core_ids=list(range(8)))
    return np.stack([r["out"] for r in res.results], axis=0)
```

```python
import numpy as np
from contextlib import ExitStack
import concourse.bass as bass
import concourse.mybir as mybir
from concourse.bass_utils import run_bass_kernel_spmd

F32 = mybir.dt.float32
BF16 = mybir.dt.bfloat16
AF = mybir.ActivationFunctionType
ALU = mybir.AluOpType

D = 1024
SEQ = 4096
NMETA = 16
NT = 33
LP = NT * 128
PAD = 112
H = 16
DFF = 2816
NFC = 22
EPS = 1e-6
OFF_Z, OFF_XBC, OFF_DT, OFF_Q, OFF_K, OFF_V = 0, 1024, 2560, 2576, 3600, 4624

BLK_Z = 0
BLK_XBC = 8
BLK_Q = 20
BLK_K = 28
BLK_V = 36
BLK_OUT = 44
BLK_GATE = 60
BLK_UP = 82
BLK_DOWN = 104
NBLK = 126

ENGS = ['pe', 'act', 'dve', 'pool', 'sp']
NDSEM = 8


class _Op:
    __slots__ = ('eng', 'fn', 'deps', 'is_dma', 'needed', 'val', 'sem', 'idx')


class Sched:
    def __init__(self):
        self.ops = {e: [] for e in ENGS}
        self.last_w = {}
        self.readers = {}
        self.seen_c = {e: {p: -1 for p in ENGS} for e in ENGS}
        self.seen_d = {e: set() for e in ENGS}
        self.ndma = {e: 0 for e in ENGS}
        self.dma_ops = {e: [] for e in ENGS}

    def add(self, eng, fn, reads=(), writes=(), dma=False, extra=()):
        op = _Op()
        op.eng = eng
        op.fn = fn
        op.is_dma = dma
        op.needed = False
        op.val = None
        op.sem = None
        op.idx = len(self.ops[eng])
        deps = list(extra)
        for k in reads:
            w = self.last_w.get(k)
            if w is not None:
                deps.append(w)
        for k in writes:
            w = self.last_w.get(k)
            if w is not None:
                deps.append(w)
            deps.extend(self.readers.get(k, ()))
        if dma:
            n = self.ndma[eng]
            if n >= NDSEM:
                deps.append(self.dma_ops[eng][n - NDSEM])
            op.sem = ('d', eng, n % NDSEM)
            op.val = 16 * (n // NDSEM + 1)
            self.ndma[eng] = n + 1
            self.dma_ops[eng].append(op)
        cdeps = {}
        ddeps = []
        for d in deps:
            if d is op:
                continue
            if d.is_dma:
                key = (d.eng, d.sem, d.val)
                if key in self.seen_d[eng]:
                    continue
                self.seen_d[eng].add(key)
                ddeps.append(d)
            else:
                if d.eng == eng and eng == 'pe':
                    continue
                if d.idx <= self.seen_c[eng][d.eng]:
                    continue
                if d.eng not in cdeps or cdeps[d.eng].idx < d.idx:
                    cdeps[d.eng] = d
        for p, d in cdeps.items():
            self.seen_c[eng][p] = d.idx
            d.needed = True
        op.deps = list(cdeps.values()) + ddeps
        self.ops[eng].append(op)
        for k in writes:
            self.last_w[k] = op
            self.readers[k] = []
        for k in reads:
            if k in writes:
                continue
            self.readers.setdefault(k, []).append(op)
        return op

    def emit(self, nc, es):
        sem_c = {e: es.enter_context(nc.semaphore('sc_' + e)) for e in ENGS}
        sem_d = {}
        for e in ENGS:
            for i in range(min(NDSEM, self.ndma[e])):
                sem_d[('d', e, i)] = es.enter_context(nc.semaphore('sd_%s_%d' % (e, i)))
        for e in ENGS:
            c = 0
            for op in self.ops[e]:
                if op.is_dma:
                    continue
                if op.needed:
                    c += 1
                    op.val = c
        block = es.enter_context(nc.Block())
        secs = {'pe': block.tensor, 'act': block.scalar, 'dve': block.vector,
                'pool': block.gpsimd, 'sp': block.sync}

        def mk(e):
            def body(eng):
                for op in self.ops[e]:
                    for d in op.deps:
                        if d.is_dma:
                            eng.wait_ge(sem_d[d.sem], d.val)
                        else:
                            eng.wait_ge(sem_c[d.eng], d.val)
                    inst = op.fn(eng)
                    if op.is_dma:
                        inst.then_inc(sem_d[op.sem], 16)
                    elif op.needed:
                        inst.then_inc(sem_c[e], 1)
            return body

        for e in ENGS:
            if self.ops[e]:
                secs[e](mk(e))


def build_nc(stage=99, debug=False):
    nc = bass.Bass("TRN2", target_bir_lowering=False)
    es = ExitStack()

    def din(name, shape):
        return nc.dram_tensor(name, list(shape), F32, kind="ExternalInput").ap()

    x_d = din("x", [SEQ, D])
    meta_d = din("meta", [NMETA, D])
    wblk_d = din("wblk", [NBLK * 128, 1024])
    wdt_d = din("wdt", [128, 8 * 16])
    vecs_d = din("vecs", [8, D])
    pp_d = din("ppd", [128, 128])
    ffnp_d = din("ffnpd", [128, 128])
    small_d = din("small", [1, 64])
    out_d = nc.dram_tensor("out", [SEQ, D], F32, kind="ExternalOutput").ap()
    wscr = nc.dram_tensor("wscr", [NBLK * 128, 1024], BF16, kind="Internal").ap()
    dbg = {}
    if debug:
        dbg['xnT'] = nc.dram_tensor("dbg_xnT", [128, 8 * LP], BF16, kind="ExternalOutput").ap()
        dbg['ysbT'] = nc.dram_tensor("dbg_ysbT", [128, 8 * LP], BF16, kind="ExternalOutput").ap()

    S = Sched()
    with es:
        es2 = ExitStack()

        def sb(name, shape, dt=F32):
            return es.enter_context(nc.sbuf_tensor(name, list(shape), dt))

        def sb2(name, shape, dt=F32):
            return es2.enter_context(nc.sbuf_tensor(name, list(shape), dt))

        ps = es.enter_context(nc.psum_tensor("ps", [128, 4096], F32))

        def bank(b, n=1):
            return ps[:, 512 * b:512 * (b + n)]

        def bkeys(b, n=1):
            return [('ps', b + i) for i in range(n)]

        def dbgdump(name, ap, shape, dt, reads):
            if not debug:
                return
            t = nc.dram_tensor("dbg_" + name, list(shape), dt, kind="ExternalOutput").ap()
            S.add('sp', lambda e: e.dma_start(out=t, in_=ap), reads=reads, dma=True)

        ident_b = sb("ident_b", [128, 128], BF16)
        tri_b = sb("tri_b", [128, 128], BF16)
        tric_b = sb("tric_b", [128, 128], BF16)
        mdiag_b = sb("mdiag_b", [128, 128], BF16)
        padm = sb("padm", [128, 1], F32)
        mhalf = sb("mhalf", [128, 1], F32)
        pp = sb("pp", [128, 128], F32)
        junk = sb("junk", [128, D], BF16)

        def cmask(t, pattern_mult, chan_mult, op, base=0):
            S.add('pool', lambda e: e.memset(t[:], 1.0), writes=[t.name])
            S.add('pool', lambda e: e.affine_select(
                out=t[:], in_=t[:], pattern=[[pattern_mult, t.shape[1]]], compare_op=op,
                fill=0.0, base=base, channel_multiplier=chan_mult), reads=[t.name], writes=[t.name])

        cmask(ident_b, -1, 1, ALU.is_equal)
        cmask(tri_b, -1, 1, ALU.is_ge)
        cmask(tric_b, 1, -1, ALU.is_gt)
        cmask(mdiag_b, 1, -1, ALU.is_gt)
        cmask(padm, 0, 1, ALU.is_ge, base=-PAD)
        S.add('pool', lambda e: e.memset(mhalf[:], -0.5), writes=['mhalf'])
        S.add('sp', lambda e: e.dma_start(out=pp[:], in_=pp_d[:, :]), writes=['pp'], dma=True)

        def prep_blocks(b0, nb):
            S.add('pool', lambda e: e.dma_start(out=wscr[b0 * 128:(b0 + nb) * 128, :],
                                                in_=wblk_d[b0 * 128:(b0 + nb) * 128, :]),
                  writes=[('wscr', b) for b in range(b0, b0 + nb)], dma=True)


        ysbT = sb("ysbT", [128, 8, LP], BF16)
        xnT = sb2("xnT", [128, 8, LP], BF16)
        gpre_bc = sb2("gpre_bc", [128, D], F32)

        S.add('sp', lambda e: e.dma_start(out=gpre_bc[:], in_=vecs_d[0:1, :].partition_broadcast(128)),
              writes=['gpre_bc'], dma=True)
        xt = [sb2("xt%d" % i, [128, D], F32) for i in range(2)]
        xnb = [sb2("xnb%d" % i, [128, D], BF16) for i in range(2)]
        st1 = sb2("st1", [128, NT * 4], F32)

        def load_x_tile(t, buf, key):
            if t == 0:
                S.add('pool', lambda e: e.memset(buf[:], 0.0), writes=[key])
                S.add('sp', lambda e: e.dma_start(out=buf[PAD:128, :], in_=meta_d[:, :]),
                      writes=[key], dma=True)
            else:
                S.add('sp', lambda e: e.dma_start(out=buf[:], in_=x_d[(t - 1) * 128:t * 128, :]),
                      writes=[key], dma=True)

        def rstd_from(src, srckey, col, stt, sttname, junkbuf, junkkey):
            S.add('act', lambda e: e.activation(out=junkbuf[:], in_=src, func=AF.Square,
                                                accum_out=stt[:, col:col + 1]),
                  reads=[srckey], writes=[(sttname, col)])
            S.add('dve', lambda e: e.tensor_scalar(out=stt[:, col + 1:col + 2], in0=stt[:, col:col + 1],
                                                   scalar1=1.0 / D, scalar2=EPS, op0=ALU.mult, op1=ALU.add),
                  reads=[(sttname, col)], writes=[(sttname, col + 1)])
            S.add('pool', lambda e: e.tensor_tensor(out=stt[:, col + 2:col + 3], in0=stt[:, col + 1:col + 2],
                                                    in1=mhalf[:], op=ALU.pow),
                  reads=[(sttname, col + 1), 'mhalf'], writes=[(sttname, col + 2)])

        def p1_stage(t, s):
            i = t % 2
            pb = 4 + i
            pbv = bank(pb).bitcast(BF16)
            col = 4 * t
            if s == 0:
                load_x_tile(t, xt[i], ('xt', i))
            elif s == 1:
                S.add('act', lambda e: e.activation(out=junk[:], in_=xt[i][:], func=AF.Square,
                                                    accum_out=st1[:, col:col + 1]),
                      reads=[('xt', i)], writes=[('st1', col), 'junk'])
            elif s == 2:
                S.add('dve', lambda e: e.tensor_scalar(out=st1[:, col + 1:col + 2], in0=st1[:, col:col + 1],
                                                       scalar1=1.0 / D, scalar2=EPS, op0=ALU.mult, op1=ALU.add),
                      reads=[('st1', col)], writes=[('st1', col + 1)])
            elif s == 3:
                S.add('pool', lambda e: e.tensor_tensor(out=st1[:, col + 2:col + 3], in0=st1[:, col + 1:col + 2],
                                                        in1=mhalf[:], op=ALU.pow),
                      reads=[('st1', col + 1), 'mhalf'], writes=[('st1', col + 2)])
            elif s == 4:
                S.add('dve', lambda e: e.scalar_tensor_tensor(
                    out=xnb[i][:], in0=xt[i][:], scalar=st1[:, col + 2:col + 3], in1=gpre_bc[:],
                    op0=ALU.mult, op1=ALU.mult),
                    reads=[('xt', i), ('st1', col + 2), 'gpre_bc'], writes=[('xnb', i)])
            elif s == 5:
                for k in range(8):
                    S.add('pe', lambda e, k=k: e.transpose(
                        out=pbv[:, k * 128:(k + 1) * 128], in_=xnb[i][:, k * 128:(k + 1) * 128], identity=ident_b[:]),
                        reads=[('xnb', i), 'ident_b'], writes=bkeys(pb))
            elif s == 6:
                S.add('act', lambda e: e.activation(
                    out=xnT[:, :, t * 128:(t + 1) * 128], in_=pbv.rearrange("p (k c) -> p k c", k=8), func=AF.Copy),
                    reads=bkeys(pb), writes=[('xnT', t)] + bkeys(pb))

        for t0_ in range(0, NT, 2):
            ts_ = [t for t in (t0_, t0_ + 1) if t < NT]
            for s in range(7):
                for t in ts_:
                    p1_stage(t, s)
            if t0_ == 4:
                for b0 in (BLK_Q, BLK_K, BLK_V):
                    prep_blocks(b0, 8)
        for b0 in range(0, 20, 4):
            prep_blocks(b0, 4)
        for b0 in range(BLK_OUT, NBLK, 4):
            prep_blocks(b0, min(4, NBLK - b0))

        if debug:
            S.add('sp', lambda e: e.dma_start(out=dbg['xnT'], in_=xnT[:].rearrange("p k c -> p (k c)")),
                  reads=[('xnT', t) for t in range(NT)], dma=True)

        if stage >= 2:
            wq = sb2("wq", [128, 8, 128], BF16)
            wk = sb2("wk", [128, 8, 128], BF16)
            wv = sb2("wv", [128, 8, 128], BF16)
            qT = sb2("qT", [128, LP], BF16)
            kT = sb2("kT", [128, LP], BF16)
            vv = sb2("vv", [128, NT, 128], BF16)
            eb = [sb2("eb%d" % i, [128, 2, 512], BF16) for i in range(4)]
            spb = [sb2("spb%d" % i, [128, 2, 512], BF16) for i in range(3)]
            gb = [sb2("gb%d" % i, [128, 2, 512], BF16) for i in range(2)]
            wb = [sb2("wb%d" % i, [128, 2, 512], BF16) for i in range(2)]
            allx = [('xnT', t) for t in range(NT)]

            def load_wblk(dst, key, blk):
                S.add('sp', lambda e: e.dma_start(
                    out=dst[:].rearrange("p k c -> p (k c)"), in_=wscr[blk * 128:(blk + 1) * 128, :]),
                    reads=[('wscr', blk)], writes=[key], dma=True)

            gstep = 0
            for hp in range(8):
                load_wblk(wq, 'wq', BLK_Q + hp)
                load_wblk(wk, 'wk', BLK_K + hp)
                load_wblk(wv, 'wv', BLK_V + hp)
                def blk_tiles(gq):
                    return [0] if gq == 0 else list(range(4 * gq - 3, 4 * gq + 1))

                def unit_qk(gq, which, b):
                    tiles = blk_tiles(gq)
                    c0 = tiles[0] * 128
                    n = len(tiles) * 128
                    dst, dkey, wt, wkey, scale = ((qT, 'qT', wq, 'wq', 0.125) if which == 'q'
                                                  else (kT, 'kT', wk, 'wk', 1.0))
                    for k in range(8):
                        S.add('pe', lambda e, k=k: e.matmul(
                            bank(b)[:, 0:n], lhsT=wt[:, k, :], rhs=xnT[:, k, c0:c0 + n],
                            start=(k == 0), stop=(k == 7)),
                            reads=[wkey] + [('xnT', t) for t in tiles], writes=bkeys(b))
                    S.add('dve', lambda e: e.tensor_scalar(
                        out=dst[:, c0:c0 + n], in0=bank(b)[:, 0:n], scalar1=scale, scalar2=None, op0=ALU.mult),
                        reads=bkeys(b), writes=[(dkey, t) for t in tiles] + bkeys(b))

                def unit_v(tiles, b):
                    for j, t in enumerate(tiles):
                        for k in range(8):
                            S.add('pe', lambda e, j=j, t=t, k=k: e.matmul(
                                bank(b)[:, j * 128:(j + 1) * 128], lhsT=xnT[:, k, t * 128:(t + 1) * 128],
                                rhs=wv[:, k, :], start=(k == 0), stop=(k == 7)),
                                reads=['wv', ('xnT', t)], writes=bkeys(b))
                    nt_ = len(tiles)
                    t0_ = tiles[0]
                    S.add('dve', lambda e: e.tensor_copy(
                        out=vv[:, t0_:t0_ + nt_, :], in_=bank(b)[:, 0:nt_ * 128].rearrange("p (t c) -> p t c", t=nt_)),
                        reads=bkeys(b), writes=[('vv', t) for t in tiles] + bkeys(b))

                def block_units(gq, b):
                    tiles = blk_tiles(gq)
                    us = [lambda: unit_qk(gq, 'q', b), lambda: unit_qk(gq, 'k', b)]
                    for h0 in range(0, len(tiles), 2):
                        tl = tiles[h0:h0 + 2]
                        us.append(lambda tl=tl: unit_v(tl, b))
                    return us

                pb_ = 0
                for gq in (0, 1):
                    tiles = blk_tiles(gq)
                    unit_qk(gq, 'q', pb_ % 4); pb_ += 1
                    unit_qk(gq, 'k', pb_ % 4); pb_ += 1
                    for h0 in range(0, len(tiles), 2):
                        unit_v(tiles[h0:h0 + 2], pb_ % 4); pb_ += 1

                steps = []
                for qg in range(9):
                    qb0 = 0 if qg == 0 else 4 * qg - 3
                    nq = 1 if qg == 0 else 4
                    qb1 = qb0 + nq - 1
                    for kb in range(qb1, -1, -1):
                        steps.append(dict(qg=qg, kb=kb, lidx=qb1 - kb, qb0=qb0, qb1=qb1, NQ=nq * 128, qc0=qb0 * 128,
                                          ob=6 + (qg % 2), co=max(0, kb - qb0) * 128,
                                          first=(kb == qb1), last=(kb == 0)))
                cb = 4
                c3 = bank(cb, 2).rearrange("p (h c) -> p h c", h=2)

                def fZ(st, gs):
                    co, NQ, kb, qc0 = st['co'], st['NQ'], st['kb'], st['qc0']
                    zs = (gs % 2) * 2
                    z3 = bank(zs, 2).rearrange("p (h c) -> p h c", h=2)
                    for h in range(2):
                        r0 = 64 * h
                        S.add('pe', lambda e, h=h, r0=r0, z3=z3: e.matmul(
                            z3[:, h, co:NQ], lhsT=kT[r0:r0 + 64, kb * 128:(kb + 1) * 128],
                            rhs=qT[r0:r0 + 64, qc0 + co:qc0 + NQ], start=True, stop=True),
                            reads=[('kT', kb)] + [('qT', j) for j in range(st['qb0'] + co // 128, st['qb1'] + 1)],
                            writes=bkeys(zs + h))

                def fE(st, gs):
                    co, NQ, kb = st['co'], st['NQ'], st['kb']
                    zs = (gs % 2) * 2
                    i = gs % 4
                    z3 = bank(zs, 2).rearrange("p (h c) -> p h c", h=2)
                    S.add('act', lambda e: e.activation(
                        out=eb[i][:, :, co:NQ], in_=z3[:, :, co:NQ], func=AF.Exp),
                        reads=bkeys(zs, 2), writes=[('eb', i)] + bkeys(zs, 2))
                    if kb >= st['qb0']:
                        S.add('dve', lambda e: e.tensor_tensor(
                            out=eb[i][:, :, co:co + 128], in0=eb[i][:, :, co:co + 128],
                            in1=mdiag_b[:].unsqueeze(1).broadcast_to([128, 2, 128]), op=ALU.mult),
                            reads=[('eb', i), 'mdiag_b'], writes=[('eb', i)])
                    if kb == 0:
                        S.add('dve', lambda e: e.tensor_scalar(
                            out=eb[i][:, :, co:NQ], in0=eb[i][:, :, co:NQ], scalar1=padm[:, 0:1],
                            scalar2=None, op0=ALU.mult),
                            reads=[('eb', i), 'padm'], writes=[('eb', i)])

                def fL(st, gs):
                    co, NQ = st['co'], st['NQ']
                    i = gs % 3
                    ie = gs % 4
                    S.add('act', lambda e: e.activation(
                        out=spb[i][:, :, co:NQ], in_=eb[ie][:, :, co:NQ], func=AF.Ln, bias=1.0),
                        reads=[('eb', ie)], writes=[('spb', i)])

                def fT(st, gs):
                    co, NQ = st['co'], st['NQ']
                    i = gs % 3
                    for h in range(2):
                        S.add('pe', lambda e, h=h: e.matmul(
                            c3[:, h, co:NQ], lhsT=tri_b[:], rhs=spb[i][:, h, co:NQ],
                            start=st['first'], stop=False, skip_group_check=True),
                            reads=[('spb', i), 'tri_b'], writes=bkeys(cb + h))

                def fG(st, gs):
                    co, NQ = st['co'], st['NQ']
                    i = gs % 2
                    S.add('act', lambda e: e.activation(
                        out=gb[i][:, :, co:NQ], in_=c3[:, :, co:NQ], func=AF.Exp, scale=-1.0),
                        reads=bkeys(cb, 2), writes=[('gb', i)] + bkeys(cb, 2))

                def fT2(st, gs):
                    co, NQ = st['co'], st['NQ']
                    i = gs % 3
                    if st['last']:
                        return
                    for h in range(2):
                        S.add('pe', lambda e, h=h: e.matmul(
                            c3[:, h, co:NQ], lhsT=tric_b[:], rhs=spb[i][:, h, co:NQ],
                            start=False, stop=False, skip_group_check=True),
                            reads=[('spb', i), 'tric_b'], writes=bkeys(cb + h))

                def fW(st, gs):
                    co, NQ = st['co'], st['NQ']
                    i = gs % 2
                    S.add('dve', lambda e: e.tensor_tensor(
                        out=wb[i][:, :, co:NQ], in0=eb[gs % 4][:, :, co:NQ], in1=gb[i][:, :, co:NQ], op=ALU.mult),
                        reads=[('eb', gs % 4), ('gb', i)], writes=[('wb', i)])

                def fV(st, gs, hp=hp):
                    co, NQ, kb, ob, qc0 = st['co'], st['NQ'], st['kb'], st['ob'], st['qc0']
                    i = gs % 2
                    for h in range(2):
                        r0 = 64 * h
                        S.add('pe', lambda e, h=h, r0=r0: e.matmul(
                            bank(ob)[r0:r0 + 64, co:NQ], lhsT=vv[:, kb, r0:r0 + 64], rhs=wb[i][:, h, co:NQ],
                            start=st['first'], stop=False, skip_group_check=True),
                            reads=[('wb', i), ('vv', kb)], writes=bkeys(ob))
                    if st['last']:
                        S.add('dve', lambda e: e.tensor_copy(
                            out=ysbT[:, hp, qc0:qc0 + NQ], in_=bank(ob)[:, 0:NQ]),
                            reads=bkeys(ob), writes=[('ysbT', hp, st['qg'])] + bkeys(ob))

                n = len(steps)
                for i in range(-2, n + 2):
                    if 0 <= i - 1 < n:
                        fT(steps[i - 1], gstep + i - 1)
                    if 0 <= i + 2 < n:
                        fZ(steps[i + 2], gstep + i + 2)
                    if 0 <= i - 2 < n:
                        fV(steps[i - 2], gstep + i - 2)
                    if 0 <= i < n:
                        st_ = steps[i]
                        if 1 <= st_['qg'] <= 7 and 1 <= st_['lidx'] <= 4:
                            us = block_units(st_['qg'] + 1, 6 + ((st_['qg'] + 1) % 2))
                            us[st_['lidx'] - 1]()
                    if 0 <= i + 1 < n:
                        fE(steps[i + 1], gstep + i + 1)
                    if 0 <= i < n:
                        fL(steps[i], gstep + i)
                    if 0 <= i - 1 < n:
                        fG(steps[i - 1], gstep + i - 1)
                        fT2(steps[i - 1], gstep + i - 1)
                        fW(steps[i - 1], gstep + i - 1)
                gstep += n

            if debug:
                S.add('sp', lambda e: e.dma_start(out=dbg['ysbT'], in_=ysbT[:].rearrange("p k c -> p (k c)")),
                      reads=[('ysbT', hp, qg) for hp in range(8) for qg in range(9)], dma=True)

        def barrier():
            tails = {e: [op for op in S.ops[e] if not op.is_dma][-1:] for e in ENGS}
            dts = [op for q in ENGS for op in S.dma_ops[q][-NDSEM:]]
            for e in ENGS:
                ex = [o for p in ENGS if p != e for o in tails[p]] + dts
                S.add(e, lambda eng: eng.nop(), extra=ex)

        if stage >= 3:
            barrier()
            es2.close()
            NG = 384
            ident_f = sb("ident_f", [128, 128], F32)
            UTf = sb("UTf", [128, 128], F32)
            SGf = sb("SGf", [128, 128], F32)
            ONESf = sb("ONESf", [128, 128], F32)
            ones_b = sb("ones_b", [128, 128], BF16)
            cmask(ident_f, -1, 1, ALU.is_equal)
            cmask(UTf, 1, -1, ALU.is_ge)
            cmask(SGf, -1, 1, ALU.is_gt)
            S.add('pool', lambda e: e.memset(ONESf[:], 1.0), writes=['ONESf'])
            S.add('pool', lambda e: e.memset(ones_b[:], 1.0), writes=['ones_b'])
            ffnp = sb("ffnp", [128, 128], F32)
            S.add('sp', lambda e: e.dma_start(out=ffnp[:], in_=ffnp_d[:, :]), writes=['ffnp'], dma=True)
            smallbc = sb("smallbc", [128, 64], F32)
            S.add('sp', lambda e: e.dma_start(out=smallbc[:], in_=small_d[0:1, :].partition_broadcast(128)),
                  writes=['smallbc'], dma=True)
            Abc = sb("Abc", [128, 16], F32)
            S.add('act', lambda e: e.activation(out=Abc[:], in_=smallbc[:, 16:32], func=AF.Exp),
                  reads=['smallbc'], writes=['Abc'])
            S.add('dve', lambda e: e.tensor_scalar(out=Abc[:], in0=Abc[:], scalar1=-1.0, scalar2=None, op0=ALU.mult),
                  reads=['Abc'], writes=['Abc'])
            wdt_b = sb("wdt_b", [128, 8, 16], BF16)
            S.add('pool', lambda e: e.dma_start(out=wdt_b[:].rearrange("p k c -> p (k c)"), in_=wdt_d[:, :]),
                  writes=['wdt_b'], dma=True)
            gpost_bc = sb("gpost_bc", [128, D], F32)
            gfpost_bc = sb("gfpost_bc", [128, D], F32)
            S.add('sp', lambda e: e.dma_start(out=gpost_bc[:], in_=vecs_d[1:2, :].partition_broadcast(128)),
                  writes=['gpost_bc'], dma=True)
            S.add('sp', lambda e: e.dma_start(out=gfpost_bc[:], in_=vecs_d[3:4, :].partition_broadcast(128)),
                  writes=['gfpost_bc'], dma=True)

            NWB = 4
            wbuf = [sb("wbuf%d" % i, [128, 1024], BF16) for i in range(NWB)]
            wcnt = [0]

            def wload(blk):
                i = wcnt[0] % NWB
                wcnt[0] += 1
                S.add('sp', lambda e: e.dma_start(out=wbuf[i][:], in_=wscr[blk * 128:(blk + 1) * 128, :]),
                      reads=[('wscr', blk)], writes=[('wbuf', i)], dma=True)
                return wbuf[i], ('wbuf', i)

            state = sb("state", [128, 1024], F32)
            state_b = sb("state_b", [128, 1024], BF16)
            S.add('pool', lambda e: e.memset(state[:], 0.0), writes=['state'])
            S.add('pool', lambda e: e.memset(state_b[:], 0.0), writes=['state_b'])
            halo = sb("halo", [128, 12, 3], F32)
            ghalo = sb("ghalo", [128, NFC, 2], F32)
            S.add('pool', lambda e: e.memset(halo[:], 0.0), writes=['halo'])
            S.add('pool', lambda e: e.memset(ghalo[:], 0.0), writes=['ghalo'])

            hbuf = sb("hbuf", [128, 3, D], F32)
            actT = sb("actT", [128, 8, NG], BF16)
            R1 = sb("R1", [128, 6 * D], F32)
            zs = R1[:, 0:3 * D].rearrange("p (j c) -> p j c", j=3)
            Xtok = R1[:, 3 * D:6 * D].rearrange("p (j c) -> p j c", j=3)
            aT = R1[:, 0:NFC * NG // 2].bitcast(BF16).rearrange("p (f c) -> p f c", f=NFC)
            BT = sb("BT", [128, 2, NG], BF16)
            CT = sb("CT", [128, 2, NG], BF16)
            Btok = sb("Btok", [128, 3, 256], BF16)
            cbuf = [sb("cbuf%d" % i, [128, NG + 3], F32) for i in range(4)]
            cacc = [sb("cacc%d" % i, [128, NG], F32) for i in range(4)]
            sfm = [sb("sfm%d" % i, [128, NG], F32) for i in range(4)]
            t1 = sb("t1", [128, D], F32)
            RH = sb("RH", [128, 16, 128], F32)
            Xdt = sb("Xdt", [128, D], BF16)
            Xdec = sb("Xdec", [128, D], BF16)
            Eb = [sb("Eb%d" % i, [128, 512], F32) for i in range(2)]
            MTb = [sb("MTb%d" % i, [128, 4, 128], BF16) for i in range(2)]
            CBm = sb("CBm", [128, 2, 128], F32)
            tokbfs = [sb("tokbf%d" % i, [128, D], BF16) for i in range(3)]
            ygnT = sb("ygnT", [128, 8, NG], BF16)
            ysbn = sb("ysbn", [128, 8, NG], BF16)
            r_bc = sb("r_bc", [128, NG], F32)
            lnr = r_bc
            sqb = [sb("sqb%d" % i, [128, NG], BF16) for i in range(2)]
            dtt = sb("dtt", [128, 3, 16], F32)
            att = sb("att", [128, 3, 16], F32)
            dte = sb("dte", [128, 3, 16], F32)
            ex48 = sb("ex48", [128, 48], F32)
            dtd = sb("dtd", [128, 16], F32)
            st3 = sb("st3", [128, 64], F32)
            gcb = cbuf
            RHf = RH[:].rearrange("p h c -> p (h c)")
            tmps = [(t1[:], ['t1']), (RHf[:, 0:D], ['RH', ('tmpf', 1)]), (RHf[:, D:2 * D], ['RH', ('tmpf', 2)])]
            glb = sfm

            def bc3(ap2, n):
                return ap2.unsqueeze(2).broadcast_to([128, 16, n])

            def v3(ap2):
                return ap2.rearrange("p (h d) -> p h d", h=16)

            def run_rr(gens):
                gens = list(gens)
                while gens:
                    nxt = []
                    for gq in gens:
                        try:
                            next(gq)
                            nxt.append(gq)
                        except StopIteration:
                            pass
                    gens = nxt

            def rinfo(slot):
                return st3[:, 4 * slot + 2:4 * slot + 3], (('st3', slot), 2)

            def g_rstd3(src, srcreads, slot):
                col = 4 * slot
                key = ('st3', slot)
                S.add('act', lambda e: e.activation(out=junk[:], in_=src, func=AF.Square,
                                                    accum_out=st3[:, col:col + 1]),
                      reads=srcreads, writes=[(key, 0), 'junk'])
                yield
                S.add('dve', lambda e: e.tensor_scalar(out=st3[:, col + 1:col + 2], in0=st3[:, col:col + 1],
                                                       scalar1=1.0 / D, scalar2=EPS, op0=ALU.mult, op1=ALU.add),
                      reads=[(key, 0)], writes=[(key, 1)])
                yield
                S.add('pool', lambda e: e.tensor_tensor(out=st3[:, col + 2:col + 3], in0=st3[:, col + 1:col + 2],
                                                        in1=mhalf[:], op=ALU.pow),
                      reads=[(key, 1), 'mhalf'], writes=[(key, 2)])
                yield

            def rstd3(src, srcreads, slot):
                for _ in g_rstd3(src, srcreads, slot):
                    pass
                return rinfo(slot)

            def g_norm_transpose(j, src, srckeys, slot, dstT, dkey, pb=7):
                rcol, rkey = rinfo(slot)
                tb = tokbfs[j % 3]
                tk = ('tokbf', j % 3)
                S.add('dve', lambda e: e.tensor_scalar(out=tb[:], in0=src, scalar1=rcol, scalar2=None,
                                                       op0=ALU.mult),
                      reads=srckeys + [rkey], writes=[tk])
                yield
                pbv = bank(pb).bitcast(BF16)
                for k in range(8):
                    S.add('pe', lambda e, k=k: e.transpose(
                        out=pbv[:, k * 128:(k + 1) * 128], in_=tb[:, k * 128:(k + 1) * 128], identity=ident_b[:]),
                        reads=[tk, 'ident_b'], writes=bkeys(pb))
                yield
                S.add('act', lambda e: e.activation(
                    out=dstT[:, :, j * 128:(j + 1) * 128], in_=pbv.rearrange("p (k c) -> p k c", k=8), func=AF.Copy),
                    reads=bkeys(pb), writes=[(dkey, j)] + bkeys(pb))
                yield

            def norm_transpose(j, src, srckeys, slot, dstT, dkey, pb=7):
                for _ in g_norm_transpose(j, src, srckeys, slot, dstT, dkey, pb):
                    pass

            def apply_gain(dstT, dkey, gcol):
                for k in range(8):
                    S.add('dve', lambda e, k=k: e.tensor_scalar(
                        out=dstT[:, k, :], in0=dstT[:, k, :], scalar1=pp[:, gcol + k:gcol + k + 1], scalar2=None,
                        op0=ALU.mult),
                        reads=[(dkey, jj) for jj in range(3)] + ['pp'], writes=[(dkey, jj) for jj in range(3)])

            def do_group(g):
                t0 = 3 * g
                gc0 = t0 * 128
                hk = [('hbuf', j) for j in range(3)]
                def chain_a(j):
                    load_x_tile(t0 + j, hbuf[:, j, :], ('hbuf', j))
                    yield
                    yield from g_rstd3(hbuf[:, j, :], [('hbuf', j)], j)
                    yield from g_norm_transpose(j, hbuf[:, j, :], [('hbuf', j)], j, actT, 'actT', pb=7 - j)

                run_rr([chain_a(j) for j in range(3)])
                apply_gain(actT, 'actT', 76)
                ak = [('actT', j) for j in range(3)]
                if g == 0:
                    dbgdump('actT', actT[:].rearrange("p k c -> p (k c)"), [128, 8 * NG], BF16, ak)
                for cbk in range(8):
                    w, wkey = wload(BLK_Z + cbk)
                    w3 = w[:].rearrange("p (k c) -> p k c", k=8)
                    for j in range(3):
                        b = 2 * j + cbk // 4
                        for k in range(8):
                            S.add('pe', lambda e, b=b, j=j, k=k, w3=w3, cbk=cbk: e.matmul(
                                bank(b)[:, (cbk % 4) * 128:(cbk % 4 + 1) * 128], lhsT=actT[:, k, j * 128:(j + 1) * 128],
                                rhs=w3[:, k, :], start=(k == 0), stop=(k == 7)),
                                reads=[wkey, ('actT', j)], writes=bkeys(b))
                for j in range(3):
                    S.add('act', lambda e, j=j: e.activation(out=zs[:, j, :], in_=bank(2 * j, 2), func=AF.Silu),
                          reads=bkeys(2 * j, 2), writes=[('zs', j)] + bkeys(2 * j, 2))
                for j in range(3):
                    for k in range(8):
                        S.add('pe', lambda e, j=j, k=k: e.matmul(
                            bank(6)[:, j * 16:(j + 1) * 16], lhsT=actT[:, k, j * 128:(j + 1) * 128],
                            rhs=wdt_b[:, k, :], start=(k == 0), stop=(k == 7)),
                            reads=['wdt_b', ('actT', j)], writes=bkeys(6))
                S.add('dve', lambda e: e.tensor_tensor(
                    out=dte[:], in0=bank(6)[:, 0:48].rearrange("p (j h) -> p j h", j=3),
                    in1=smallbc[:, 0:16].unsqueeze(1).broadcast_to([128, 3, 16]), op=ALU.add),
                    reads=bkeys(6) + ['smallbc'], writes=['dte'] + bkeys(6))
                S.add('act', lambda e: e.activation(out=dte[:], in_=dte[:], func=AF.Exp), reads=['dte'], writes=['dte'])
                S.add('act', lambda e: e.activation(out=dtt[:], in_=dte[:], func=AF.Ln, bias=1.0),
                      reads=['dte'], writes=['dtt'])
                if g == 0:
                    S.add('dve', lambda e: e.tensor_scalar(out=dtt[:, 0, :], in0=dtt[:, 0, :], scalar1=padm[:, 0:1],
                                                           scalar2=None, op0=ALU.mult),
                          reads=['dtt', 'padm'], writes=['dtt'])
                S.add('dve', lambda e: e.tensor_tensor(
                    out=att[:], in0=dtt[:], in1=Abc[:].unsqueeze(1).broadcast_to([128, 3, 16]), op=ALU.mult),
                    reads=['dtt', 'Abc'], writes=['att'])
                pend = []

                def emit_tr(jb, i):
                    for j in range(3):
                        S.add('pe', lambda e, i=i, j=j, jb=jb: e.transpose(
                            out=bank(2 * j + jb // 4)[:, (jb % 4) * 128:(jb % 4 + 1) * 128],
                            in_=sfm[i][:, j * 128:(j + 1) * 128], identity=ident_f[:]),
                            reads=[('sfm', i), 'ident_f'], writes=bkeys(2 * j + jb // 4))

                for jb in range(12):
                    w, wkey = wload(BLK_XBC + jb)
                    w3 = w[:].rearrange("p (k c) -> p k c", k=8)
                    b = 6 + (jb % 2)
                    i = jb % 4
                    for k in range(8):
                        S.add('pe', lambda e, b=b, k=k, w3=w3: e.matmul(
                            bank(b)[:, 0:NG], lhsT=w3[:, k, :], rhs=actT[:, k, :], start=(k == 0), stop=(k == 7)),
                            reads=[wkey] + ak, writes=bkeys(b))
                    if len(pend) >= 2:
                        emit_tr(*pend.pop(0))
                    S.add('act', lambda e, b=b, i=i: e.activation(out=cbuf[i][:, 3:3 + NG], in_=bank(b)[:, 0:NG],
                                                                  func=AF.Copy),
                          reads=bkeys(b), writes=[('cbuf', i)] + bkeys(b))
                    S.add('pool', lambda e, i=i, jb=jb: e.tensor_copy(out=cbuf[i][:, 0:3], in_=halo[:, jb, :]),
                          reads=['halo'], writes=[('cbuf', i)])
                    S.add('pool', lambda e, i=i, jb=jb: e.tensor_copy(out=halo[:, jb, :], in_=cbuf[i][:, NG:NG + 3]),
                          reads=[('cbuf', i)], writes=['halo'])
                    for kk in range(4):
                        wc = pp[:, 16 + jb * 4 + kk:16 + jb * 4 + kk + 1]
                        if kk == 0:
                            S.add('dve', lambda e, i=i, wc=wc: e.tensor_scalar(
                                out=cacc[i][:], in0=cbuf[i][:, 0:NG], scalar1=wc, scalar2=None, op0=ALU.mult),
                                reads=[('cbuf', i), 'pp'], writes=[('cacc', i)])
                        else:
                            S.add('dve', lambda e, i=i, wc=wc, kk=kk: e.scalar_tensor_tensor(
                                out=cacc[i][:], in0=cbuf[i][:, kk:kk + NG], scalar=wc, in1=cacc[i][:],
                                op0=ALU.mult, op1=ALU.add),
                                reads=[('cbuf', i), 'pp', ('cacc', i)], writes=[('cacc', i)])
                    bias = pp[:, 64 + jb:65 + jb]
                    if jb < 8:
                        S.add('act', lambda e, i=i, bias=bias: e.activation(out=sfm[i][:], in_=cacc[i][:], func=AF.Silu,
                                                                            bias=bias),
                              reads=[('cacc', i), 'pp'], writes=[('sfm', i)])
                        pend.append((jb, i))
                    else:
                        dst = BT if jb < 10 else CT
                        dk = 'BT' if jb < 10 else 'CT'
                        gg = jb % 2
                        S.add('act', lambda e, i=i, bias=bias, dst=dst, gg=gg: e.activation(
                            out=dst[:, gg, :], in_=cacc[i][:], func=AF.Silu, bias=bias),
                            reads=[('cacc', i), 'pp'], writes=[(dk, gg)])
                while pend:
                    emit_tr(*pend.pop(0))
                for j in range(3):
                    S.add('act', lambda e, j=j: e.activation(out=Xtok[:, j, :], in_=bank(2 * j, 2), func=AF.Copy),
                          reads=bkeys(2 * j, 2), writes=[('Xtok', j)] + bkeys(2 * j, 2))
                pbv = bank(7).bitcast(BF16)
                for gg in range(2):
                    for j in range(3):
                        S.add('pe', lambda e, j=j, gg=gg, pbv=pbv: e.transpose(
                            out=pbv[:, j * 256 + gg * 128:j * 256 + (gg + 1) * 128],
                            in_=BT[:, gg, j * 128:(j + 1) * 128], identity=ident_b[:]),
                            reads=[('BT', gg), 'ident_b'], writes=bkeys(7))
                S.add('dve', lambda e, pbv=pbv: e.tensor_copy(
                    out=Btok[:], in_=pbv[:, 0:768].rearrange("p (j c) -> p j c", j=3)),
                    reads=bkeys(7), writes=['Btok'] + bkeys(7))
                if g == 0:
                    dbgdump('zs', R1[:, 0:3 * D], [128, 3 * D], F32, [('zs', jj) for jj in range(3)])
                    dbgdump('Xtok', R1[:, 3 * D:6 * D], [128, 3 * D], F32, [('Xtok', jj) for jj in range(3)])
                    dbgdump('dtt', dtt[:].rearrange("p j h -> p (j h)"), [128, 48], F32, ['dtt'])
                    dbgdump('att', att[:].rearrange("p j h -> p (j h)"), [128, 48], F32, ['att'])
                    dbgdump('BT', BT[:].rearrange("p g c -> p (g c)"), [128, 2 * NG], BF16, [('BT', 0), ('BT', 1)])
                    dbgdump('CT', CT[:].rearrange("p g c -> p (g c)"), [128, 2 * NG], BF16, [('CT', 0), ('CT', 1)])
                    dbgdump('Btok', Btok[:].rearrange("p j c -> p (j c)"), [128, 768], BF16, ['Btok'])
                for j in range(3):
                    tc0 = j * 128
                    a_j = att[:, j, :]
                    for ci, lt in enumerate((UTf, SGf, ONESf)):
                        S.add('pe', lambda e, ci=ci, lt=lt, a_j=a_j: e.matmul(
                            bank(0)[:, ci * 16:(ci + 1) * 16], lhsT=lt[:], rhs=a_j, start=True, stop=True),
                            reads=['att', 'UTf', 'SGf', 'ONESf'], writes=bkeys(0))
                    S.add('act', lambda e: e.activation(out=ex48[:], in_=bank(0)[:, 0:48], func=AF.Exp),
                          reads=bkeys(0), writes=['ex48'] + bkeys(0))
                    S.add('dve', lambda e, a_j=a_j: e.tensor_tensor(
                        out=RH[:], in0=UTf[:].unsqueeze(1).broadcast_to([128, 16, 128]), in1=bc3(a_j, 128),
                        op=ALU.mult),
                        reads=['att', 'UTf'], writes=['RH'])
                    for gg in range(2):
                        S.add('pe', lambda e, gg=gg, tc0=tc0: e.matmul(
                            bank(0)[:, 128 + gg * 128:256 + gg * 128], lhsT=BT[:, gg, tc0:tc0 + 128],
                            rhs=CT[:, gg, tc0:tc0 + 128], start=True, stop=True),
                            reads=[('BT', gg), ('CT', gg)], writes=bkeys(0))
                    S.add('dve', lambda e: e.tensor_tensor(
                        out=CBm[:], in0=bank(0)[:, 128:384].rearrange("p (g c) -> p g c", g=2),
                        in1=UTf[:].unsqueeze(1).broadcast_to([128, 2, 128]), op=ALU.mult),
                        reads=bkeys(0) + ['UTf'], writes=['CBm'] + bkeys(0))
                    for gg in range(2):
                        S.add('pe', lambda e, gg=gg, tc0=tc0: e.matmul(
                            bank(5 + gg), lhsT=CT[:, gg, tc0:tc0 + 128], rhs=state_b[:, gg * 512:(gg + 1) * 512],
                            start=True, stop=True),
                            reads=[('CT', gg), 'state_b'], writes=bkeys(5 + gg))
                    S.add('dve', lambda e, j=j: e.tensor_tensor(out=dtd[:], in0=dtt[:, j, :], in1=ex48[:, 16:32],
                                                                op=ALU.mult),
                          reads=['dtt', 'ex48'], writes=['dtd'])
                    S.add('dve', lambda e, j=j: e.tensor_tensor(out=v3(Xdt[:]), in0=v3(Xtok[:, j, :]),
                                                                in1=bc3(dtt[:, j, :], 64), op=ALU.mult),
                          reads=[('Xtok', j), 'dtt'], writes=['Xdt'])
                    S.add('dve', lambda e, j=j: e.tensor_tensor(out=v3(Xdec[:]), in0=v3(Xtok[:, j, :]),
                                                                in1=bc3(dtd[:], 64), op=ALU.mult),
                          reads=[('Xtok', j), 'dtd'], writes=['Xdec'])
                    def fD(hq):
                        i = hq % 2
                        S.add('pe', lambda e, i=i, hq=hq: e.matmul(
                            bank(1 + i), lhsT=SGf[:], rhs=RH[:, 4 * hq:4 * hq + 4, :].rearrange("p h c -> p (h c)"),
                            start=True, stop=True),
                            reads=['RH', 'SGf'], writes=bkeys(1 + i))
                        S.add('act', lambda e, i=i: e.activation(out=Eb[i][:], in_=bank(1 + i), func=AF.Exp),
                              reads=bkeys(1 + i), writes=[('Eb', i)] + bkeys(1 + i))

                    def fY(hq):
                        i = hq % 2
                        gg = hq // 2
                        S.add('dve', lambda e, i=i, gg=gg: e.tensor_tensor(
                            out=MTb[i][:], in0=Eb[i][:].rearrange("p (h c) -> p h c", h=4),
                            in1=CBm[:, gg:gg + 1, :].broadcast_to([128, 4, 128]), op=ALU.mult),
                            reads=[('Eb', i), 'CBm'], writes=[('MTb', i)])
                        for hh in range(4):
                            h = 4 * hq + hh
                            S.add('pe', lambda e, i=i, hh=hh, h=h: e.matmul(
                                bank(3 + h // 8)[:, (h % 8) * 64:(h % 8 + 1) * 64], lhsT=MTb[i][:, hh, :],
                                rhs=Xdt[:, h * 64:(h + 1) * 64], start=True, stop=True),
                                reads=[('MTb', i), 'Xdt'], writes=bkeys(3 + h // 8))

                    fD(0)
                    fD(1)
                    fY(0)
                    fD(2)
                    fY(1)
                    fD(3)
                    fY(2)
                    fY(3)
                    S.add('dve', lambda e: e.tensor_tensor(out=v3(t1[:]), in0=v3(bank(5, 2)), in1=bc3(ex48[:, 0:16], 64),
                                                           op=ALU.mult),
                          reads=bkeys(5, 2) + ['ex48'], writes=['t1'] + bkeys(5, 2))
                    S.add('dve', lambda e: e.tensor_tensor(out=t1[:], in0=bank(3, 2), in1=t1[:], op=ALU.add),
                          reads=bkeys(3, 2) + ['t1'], writes=['t1'] + bkeys(3, 2))
                    for gg in range(2):
                        S.add('pe', lambda e, gg=gg, j=j: e.matmul(
                            bank(5 + gg), lhsT=Btok[:, j, gg * 128:(gg + 1) * 128], rhs=Xdec[:, gg * 512:(gg + 1) * 512],
                            start=True, stop=True),
                            reads=['Btok', 'Xdec'], writes=bkeys(5 + gg))
                    S.add('dve', lambda e, j=j: e.tensor_tensor(out=v3(Xtok[:, j, :]), in0=v3(Xtok[:, j, :]),
                                                                in1=bc3(smallbc[:, 32:48], 64), op=ALU.mult),
                          reads=[('Xtok', j), 'smallbc'], writes=[('Xtok', j)])
                    S.add('dve', lambda e, j=j: e.tensor_tensor(out=t1[:], in0=t1[:], in1=Xtok[:, j, :], op=ALU.add),
                          reads=[('Xtok', j), 't1'], writes=['t1'])
                    S.add('dve', lambda e, j=j: e.tensor_tensor(out=t1[:], in0=t1[:], in1=zs[:, j, :], op=ALU.mult),
                          reads=[('zs', j), 't1'], writes=['t1'])
                    S.add('dve', lambda e: e.tensor_tensor(out=v3(state[:]), in0=v3(state[:]), in1=bc3(ex48[:, 32:48], 64),
                                                           op=ALU.mult),
                          reads=['state', 'ex48'], writes=['state'])
                    S.add('dve', lambda e: e.tensor_tensor(out=state[:], in0=bank(5, 2), in1=state[:], op=ALU.add),
                          reads=bkeys(5, 2) + ['state'], writes=['state'] + bkeys(5, 2))
                    S.add('act', lambda e: e.activation(out=state_b[:], in_=state[:], func=AF.Copy),
                          reads=['state'], writes=['state_b'])
                    if g == 0:
                        dbgdump('yg%d' % j, t1[:], [128, D], F32, ['t1'])
                        dbgdump('ex48_%d' % j, ex48[:], [128, 48], F32, ['ex48'])
                    rstd3(t1[:], ['t1'], 3 + j)
                    norm_transpose(j, t1[:], ['t1'], 3 + j, ygnT, 'ygnT')
                apply_gain(ygnT, 'ygnT', 0)
                for k in range(8):
                    i = k % 2
                    S.add('act', lambda e, i=i, k=k: e.activation(out=sqb[i][:], in_=ysbT[:, k, gc0:gc0 + NG],
                                                                  func=AF.Square),
                          reads=[('ysbT', k, q) for q in range(9)], writes=[('sqb', i)])
                    S.add('pe', lambda e, i=i, k=k: e.matmul(bank(6)[:, 0:NG], lhsT=ones_b[:], rhs=sqb[i][:],
                                                             start=(k == 0), stop=(k == 7)),
                          reads=[('sqb', i), 'ones_b'], writes=bkeys(6))
                S.add('act', lambda e: e.activation(out=lnr[:], in_=bank(6)[:, 0:NG], func=AF.Ln, scale=1.0 / D, bias=EPS),
                      reads=bkeys(6), writes=['lnr', 'r_bc'] + bkeys(6))
                S.add('act', lambda e: e.activation(out=r_bc[:], in_=lnr[:], func=AF.Exp, scale=-0.5),
                      reads=['lnr', 'r_bc'], writes=['r_bc', 'lnr'])
                for k in range(8):
                    S.add('dve', lambda e, k=k: e.scalar_tensor_tensor(
                        out=ysbn[:, k, :], in0=ysbT[:, k, gc0:gc0 + NG], scalar=pp[:, 8 + k:9 + k], in1=r_bc[:],
                        op0=ALU.mult, op1=ALU.mult),
                        reads=[('ysbT', k, q) for q in range(9)] + ['pp', 'r_bc'], writes=[('ysbn', k)])
                for kc in range(16):
                    w, wkey = wload(BLK_OUT + kc)
                    src = ygnT if kc < 8 else ysbn
                    sk = [('ygnT', jj) for jj in range(3)] if kc < 8 else [('ysbn', kc - 8)]
                    for j in range(3):
                        for hf in range(2):
                            S.add('pe', lambda e, j=j, hf=hf, w=w, src=src, kc=kc: e.matmul(
                                bank(2 * j + hf), lhsT=src[:, kc % 8, j * 128:(j + 1) * 128],
                                rhs=w[:, hf * 512:(hf + 1) * 512], start=(kc == 0), stop=(kc == 15)),
                                reads=[wkey] + sk, writes=bkeys(2 * j + hf))
                def chain_post(j, gbc, gkey, slot0, final):
                    tmp, tkk = tmps[j]
                    S.add('dve', lambda e: e.tensor_tensor(
                        out=tmp, in0=bank(2 * j, 2), in1=gbc[:], op=ALU.mult),
                        reads=bkeys(2 * j, 2) + [gkey], writes=tkk + bkeys(2 * j, 2))
                    yield
                    yield from g_rstd3(bank(2 * j, 2), bkeys(2 * j, 2), slot0 + j)
                    rc, rk = rinfo(slot0 + j)
                    S.add('dve', lambda e: e.scalar_tensor_tensor(
                        out=hbuf[:, j, :], in0=tmp, scalar=rc, in1=hbuf[:, j, :], op0=ALU.mult, op1=ALU.add),
                        reads=tkk + [rk, ('hbuf', j)], writes=[('hbuf', j)])
                    yield
                    t = t0 + j
                    if final and t >= 1:
                        S.add('sp', lambda e: e.dma_start(out=out_d[(t - 1) * 128:t * 128, :], in_=hbuf[:, j, :]),
                              reads=[('hbuf', j)], dma=True)

                run_rr([chain_post(j, gpost_bc, 'gpost_bc', 6, False) for j in range(3)])
                if g == 0:
                    dbgdump('ygnT', ygnT[:].rearrange("p k c -> p (k c)"), [128, 8 * NG], BF16, [('ygnT', jj) for jj in range(3)])
                    dbgdump('ysbn', ysbn[:].rearrange("p k c -> p (k c)"), [128, 8 * NG], BF16, [('ysbn', k) for k in range(8)])
                    dbgdump('r_bc', r_bc[:], [128, NG], F32, ['r_bc'])
                    dbgdump('h1', hbuf[:].rearrange("p j c -> p (j c)"), [128, 3 * D], F32, hk)
                    dbgdump('state', state[:], [128, D], F32, ['state'])
                def chain_e(j):
                    yield from g_rstd3(hbuf[:, j, :], [('hbuf', j)], 9 + j)
                    yield from g_norm_transpose(j, hbuf[:, j, :], [('hbuf', j)], 9 + j, actT, 'actT', pb=7 - j)

                run_rr([chain_e(j) for j in range(3)])
                apply_gain(actT, 'actT', 84)
                for fc in range(NFC):
                    wg, wgk = wload(BLK_GATE + fc)
                    wu, wuk = wload(BLK_UP + fc)
                    wg3 = wg[:].rearrange("p (k c) -> p k c", k=8)
                    wu3 = wu[:].rearrange("p (k c) -> p k c", k=8)
                    i = fc % 4
                    bg = fc % 3
                    bu = 3 + fc % 3
                    for k in range(8):
                        S.add('pe', lambda e, bg=bg, k=k, wg3=wg3: e.matmul(
                            bank(bg)[:, 0:NG], lhsT=wg3[:, k, :], rhs=actT[:, k, :], start=(k == 0), stop=(k == 7)),
                            reads=[wgk] + ak, writes=bkeys(bg))
                    for k in range(8):
                        S.add('pe', lambda e, bu=bu, k=k, wu3=wu3: e.matmul(
                            bank(bu)[:, 0:NG], lhsT=wu3[:, k, :], rhs=actT[:, k, :], start=(k == 0), stop=(k == 7)),
                            reads=[wuk] + ak, writes=bkeys(bu))
                    S.add('act', lambda e, bg=bg, i=i: e.activation(out=gcb[i][:, 2:2 + NG], in_=bank(bg)[:, 0:NG],
                                                                    func=AF.Copy),
                          reads=bkeys(bg), writes=[('cbuf', i)] + bkeys(bg))
                    S.add('pool', lambda e, i=i, fc=fc: e.tensor_copy(out=gcb[i][:, 0:2], in_=ghalo[:, fc, :]),
                          reads=['ghalo'], writes=[('cbuf', i)])
                    S.add('pool', lambda e, i=i, fc=fc: e.tensor_copy(out=ghalo[:, fc, :], in_=gcb[i][:, NG:NG + 2]),
                          reads=[('cbuf', i)], writes=['ghalo'])
                    for kk in range(3):
                        wc = ffnp[:, fc * 3 + kk:fc * 3 + kk + 1]
                        if kk == 0:
                            S.add('dve', lambda e, i=i, wc=wc: e.tensor_scalar(
                                out=cacc[i][:], in0=gcb[i][:, 0:NG], scalar1=wc, scalar2=None, op0=ALU.mult),
                                reads=[('cbuf', i), 'ffnp'], writes=[('cacc', i)])
                        else:
                            S.add('dve', lambda e, i=i, wc=wc, kk=kk: e.scalar_tensor_tensor(
                                out=cacc[i][:], in0=gcb[i][:, kk:kk + NG], scalar=wc, in1=cacc[i][:],
                                op0=ALU.mult, op1=ALU.add),
                                reads=[('cbuf', i), 'ffnp', ('cacc', i)], writes=[('cacc', i)])
                    S.add('act', lambda e, i=i, fc=fc: e.activation(out=glb[i][:], in_=cacc[i][:], func=AF.Gelu_apprx_tanh,
                                                                    bias=ffnp[:, 66 + fc:67 + fc]),
                          reads=[('cacc', i), 'ffnp'], writes=[('sfm', i)])
                    S.add('dve', lambda e, i=i, fc=fc, bu=bu: e.tensor_tensor(
                        out=aT[:, fc, :], in0=bank(bu)[:, 0:NG], in1=glb[i][:], op=ALU.mult),
                        reads=bkeys(bu) + [('sfm', i)],
                        writes=[('zs', jj) for jj in range(3)] + [('Xtok', jj) for jj in range(3)] + bkeys(bu))
                aTk = [('zs', jj) for jj in range(3)] + [('Xtok', jj) for jj in range(3)]
                for fc in range(NFC):
                    w, wkey = wload(BLK_DOWN + fc)
                    for j in range(3):
                        for hf in range(2):
                            S.add('pe', lambda e, j=j, hf=hf, w=w, fc=fc: e.matmul(
                                bank(2 * j + hf), lhsT=aT[:, fc, j * 128:(j + 1) * 128],
                                rhs=w[:, hf * 512:(hf + 1) * 512], start=(fc == 0), stop=(fc == NFC - 1)),
                                reads=[wkey] + aTk, writes=bkeys(2 * j + hf))
                run_rr([chain_post(j, gfpost_bc, 'gfpost_bc', 12, True) for j in range(3)])

            for g in range(11 if stage >= 4 else 1):
                do_group(g)

        tail = [op for q in ENGS for op in S.dma_ops[q][-NDSEM:]]
        S.add('sp', lambda e: e.nop(), extra=tail)
        es2.close()
        S.emit(nc, es)
    return nc


def _host_layout(inputs):
    f = np.float32
    w_in = np.asarray(inputs["w_in"], f)[0]
    w_out = np.asarray(inputs["w_out"], f)[0]
    w_up = np.asarray(inputs["w_up"], f)[0]
    w_down = np.asarray(inputs["w_down"], f)[0]

    def kblocks(w, c0, nb):
        sub = w[:, c0:c0 + nb * 128].reshape(8, 128, nb, 128)
        return np.ascontiguousarray(sub.transpose(2, 1, 0, 3)).reshape(nb * 128, 1024)

    blocks = [kblocks(w_in, OFF_Z, 8), kblocks(w_in, OFF_XBC, 12), kblocks(w_in, OFF_Q, 8),
              kblocks(w_in, OFF_K, 8), kblocks(w_in, OFF_V, 8),
              w_out.reshape(16 * 128, 1024),
              kblocks(w_up, 0, 22), kblocks(w_up, DFF, 22),
              w_down.reshape(22 * 128, 1024)]
    wblk = np.ascontiguousarray(np.concatenate(blocks, axis=0))
    assert wblk.shape == (NBLK * 128, 1024)
    wdt = np.ascontiguousarray(w_in[:, OFF_DT:OFF_DT + 16].reshape(8, 128, 16).transpose(1, 0, 2)).reshape(128, 128)
    vecs = np.zeros((8, D), f)
    vecs[0] = np.asarray(inputs["mix_pre_g"], f)[0]
    vecs[1] = np.asarray(inputs["mix_post_g"], f)[0]
    vecs[2] = np.asarray(inputs["ffn_pre_g"], f)[0]
    vecs[3] = np.asarray(inputs["ffn_post_g"], f)[0]
    pp = np.zeros((128, 128), f)
    pp[:, 0:8] = np.asarray(inputs["ssd_norm_g"], f)[0].reshape(8, 128).T
    pp[:, 8:16] = np.asarray(inputs["sb_norm_g"], f)[0].reshape(8, 128).T
    cw = np.asarray(inputs["ssd_conv_w"], f)[0]
    pp[:, 16:64] = cw.reshape(4, 12, 128).transpose(2, 1, 0).reshape(128, 48)
    pp[:, 64:76] = np.asarray(inputs["ssd_conv_b"], f)[0].reshape(12, 128).T
    pp[:, 76:84] = np.asarray(inputs["mix_pre_g"], f)[0].reshape(8, 128).T
    pp[:, 84:92] = np.asarray(inputs["ffn_pre_g"], f)[0].reshape(8, 128).T
    fw = np.asarray(inputs["ffn_conv_w"], f)[0]
    ffn = np.zeros((128, 128), f)
    ffn[:, 0:66] = fw.reshape(3, 22, 128).transpose(2, 1, 0).reshape(128, 66)
    ffn[:, 66:88] = np.asarray(inputs["ffn_conv_b"], f)[0].reshape(22, 128).T
    small = np.zeros((1, 64), f)
    small[0, 0:16] = np.asarray(inputs["ssd_dt_bias"], f)[0]
    small[0, 16:32] = np.asarray(inputs["ssd_a_log"], f)[0]
    small[0, 32:48] = np.asarray(inputs["ssd_d"], f)[0]
    return dict(wblk=wblk, wdt=wdt, vecs=vecs, ppd=pp, ffnpd=ffn, small=small,
                meta=np.ascontiguousarray(np.asarray(inputs["meta_tokens"], f)))


def kernel(**inputs):
    x = np.asarray(inputs["x"], np.float32)
    shared = _host_layout(inputs)
    nc = build_nc()
    in_maps = []
    for b in range(8):
        m = dict(shared)
        m["x"] = np.ascontiguousarray(x[b])
        in_maps.append(m)
    res = run_bass_kernel_spmd(nc, in_maps, core_ids=list(range(8)))
    return np.stack([r["out"] for r in res.results], axis=0)
```

```python
import numpy as np
from contextlib import ExitStack
import concourse.bass as bass
import concourse.mybir as mybir
from concourse.bass_utils import run_bass_kernel_spmd

F32 = mybir.dt.float32
BF16 = mybir.dt.bfloat16
AF = mybir.ActivationFunctionType
ALU = mybir.AluOpType

D = 1024
SEQ = 4096
NMETA = 16
NT = 33
LP = NT * 128
PAD = 112
H = 16
DFF = 2816
NFC = 22
EPS = 1e-6
OFF_Z, OFF_XBC, OFF_DT, OFF_Q, OFF_K, OFF_V = 0, 1024, 2560, 2576, 3600, 4624

BLK_Z = 0
BLK_XBC = 8
BLK_Q = 20
BLK_K = 28
BLK_V = 36
BLK_OUT = 44
BLK_GATE = 60
BLK_UP = 82
BLK_DOWN = 104
NBLK = 126

ENGS = ['pe', 'act', 'dve', 'pool', 'sp']
NDSEM = 8


class _Op:
    __slots__ = ('eng', 'fn', 'deps', 'is_dma', 'needed', 'val', 'sem', 'idx')


class Sched:
    def __init__(self):
        self.ops = {e: [] for e in ENGS}
        self.last_w = {}
        self.readers = {}
        self.seen_c = {e: {p: -1 for p in ENGS} for e in ENGS}
        self.seen_d = {e: set() for e in ENGS}
        self.ndma = {e: 0 for e in ENGS}
        self.dma_ops = {e: [] for e in ENGS}

    def add(self, eng, fn, reads=(), writes=(), dma=False, extra=()):
        op = _Op()
        op.eng = eng
        op.fn = fn
        op.is_dma = dma
        op.needed = False
        op.val = None
        op.sem = None
        op.idx = len(self.ops[eng])
        deps = list(extra)
        for k in reads:
            w = self.last_w.get(k)
            if w is not None:
                deps.append(w)
        for k in writes:
            w = self.last_w.get(k)
            if w is not None:
                deps.append(w)
            deps.extend(self.readers.get(k, ()))
        if dma:
            n = self.ndma[eng]
            if n >= NDSEM:
                deps.append(self.dma_ops[eng][n - NDSEM])
            op.sem = ('d', eng, n % NDSEM)
            op.val = 16 * (n // NDSEM + 1)
            self.ndma[eng] = n + 1
            self.dma_ops[eng].append(op)
        cdeps = {}
        ddeps = []
        for d in deps:
            if d is op:
                continue
            if d.is_dma:
                key = (d.eng, d.sem, d.val)
                if key in self.seen_d[eng]:
                    continue
                self.seen_d[eng].add(key)
                ddeps.append(d)
            else:
                if d.eng == eng and eng == 'pe':
                    continue
                if d.idx <= self.seen_c[eng][d.eng]:
                    continue
                if d.eng not in cdeps or cdeps[d.eng].idx < d.idx:
                    cdeps[d.eng] = d
        for p, d in cdeps.items():
            self.seen_c[eng][p] = d.idx
            d.needed = True
        op.deps = list(cdeps.values()) + ddeps
        self.ops[eng].append(op)
        for k in writes:
            self.last_w[k] = op
            self.readers[k] = []
        for k in reads:
            if k in writes:
                continue
            self.readers.setdefault(k, []).append(op)
        return op

    def emit(self, nc, es):
        sem_c = {e: es.enter_context(nc.semaphore('sc_' + e)) for e in ENGS}
        sem_d = {}
        for e in ENGS:
            for i in range(min(NDSEM, self.ndma[e])):
                sem_d[('d', e, i)] = es.enter_context(nc.semaphore('sd_%s_%d' % (e, i)))
        for e in ENGS:
            c = 0
            for op in self.ops[e]:
                if op.is_dma:
                    continue
                if op.needed:
                    c += 1
                    op.val = c
        block = es.enter_context(nc.Block())
        secs = {'pe': block.tensor, 'act': block.scalar, 'dve': block.vector,
                'pool': block.gpsimd, 'sp': block.sync}

        def mk(e):
            def body(eng):
                for op in self.ops[e]:
                    for d in op.deps:
                        if d.is_dma:
                            eng.wait_ge(sem_d[d.sem], d.val)
                        else:
                            eng.wait_ge(sem_c[d.eng], d.val)
                    inst = op.fn(eng)
                    if op.is_dma:
                        inst.then_inc(sem_d[op.sem], 16)
                    elif op.needed:
                        inst.then_inc(sem_c[e], 1)
            return body

        for e in ENGS:
            if self.ops[e]:
                secs[e](mk(e))


def build_nc(stage=99, debug=False):
    nc = bass.Bass("TRN2", target_bir_lowering=False)
    es = ExitStack()

    def din(name, shape):
        return nc.dram_tensor(name, list(shape), F32, kind="ExternalInput").ap()

    x_d = din("x", [SEQ, D])
    meta_d = din("meta", [NMETA, D])
    wblk_d = din("wblk", [NBLK * 128, 1024])
    wdt_d = din("wdt", [128, 8 * 16])
    vecs_d = din("vecs", [8, D])
    pp_d = din("ppd", [128, 128])
    ffnp_d = din("ffnpd", [128, 128])
    small_d = din("small", [1, 64])
    out_d = nc.dram_tensor("out", [SEQ, D], F32, kind="ExternalOutput").ap()
    wscr = nc.dram_tensor("wscr", [NBLK * 128, 1024], BF16, kind="Internal").ap()
    dbg = {}
    if debug:
        dbg['xnT'] = nc.dram_tensor("dbg_xnT", [128, 8 * LP], BF16, kind="ExternalOutput").ap()
        dbg['ysbT'] = nc.dram_tensor("dbg_ysbT", [128, 8 * LP], BF16, kind="ExternalOutput").ap()

    S = Sched()
    with es:
        es2 = ExitStack()

        def sb(name, shape, dt=F32):
            return es.enter_context(nc.sbuf_tensor(name, list(shape), dt))

        def sb2(name, shape, dt=F32):
            return es2.enter_context(nc.sbuf_tensor(name, list(shape), dt))

        ps = es.enter_context(nc.psum_tensor("ps", [128, 4096], F32))

        def bank(b, n=1):
            return ps[:, 512 * b:512 * (b + n)]

        def bkeys(b, n=1):
            return [('ps', b + i) for i in range(n)]

        def dbgdump(name, ap, shape, dt, reads):
            if not debug:
                return
            t = nc.dram_tensor("dbg_" + name, list(shape), dt, kind="ExternalOutput").ap()
            S.add('sp', lambda e: e.dma_start(out=t, in_=ap), reads=reads, dma=True)

        ident_b = sb("ident_b", [128, 128], BF16)
        tri_b = sb("tri_b", [128, 128], BF16)
        tric_b = sb("tric_b", [128, 128], BF16)
        mdiag_b = sb("mdiag_b", [128, 128], BF16)
        padm = sb("padm", [128, 1], F32)
        mhalf = sb("mhalf", [128, 1], F32)
        pp = sb("pp", [128, 128], F32)
        junk = sb("junk", [128, D], BF16)

        def cmask(t, pattern_mult, chan_mult, op, base=0):
            S.add('pool', lambda e: e.memset(t[:], 1.0), writes=[t.name])
            S.add('pool', lambda e: e.affine_select(
                out=t[:], in_=t[:], pattern=[[pattern_mult, t.shape[1]]], compare_op=op,
                fill=0.0, base=base, channel_multiplier=chan_mult), reads=[t.name], writes=[t.name])

        cmask(ident_b, -1, 1, ALU.is_equal)
        cmask(tri_b, -1, 1, ALU.is_ge)
        cmask(tric_b, 1, -1, ALU.is_gt)
        cmask(mdiag_b, 1, -1, ALU.is_gt)
        cmask(padm, 0, 1, ALU.is_ge, base=-PAD)
        S.add('pool', lambda e: e.memset(mhalf[:], -0.5), writes=['mhalf'])
        S.add('sp', lambda e: e.dma_start(out=pp[:], in_=pp_d[:, :]), writes=['pp'], dma=True)

        def prep_blocks(b0, nb):
            S.add('pool', lambda e: e.dma_start(out=wscr[b0 * 128:(b0 + nb) * 128, :],
                                                in_=wblk_d[b0 * 128:(b0 + nb) * 128, :]),
                  writes=[('wscr', b) for b in range(b0, b0 + nb)], dma=True)


        ysbT = sb("ysbT", [128, 8, LP], BF16)
        xnT = sb2("xnT", [128, 8, LP], BF16)
        gpre_bc = sb2("gpre_bc", [128, D], F32)

        S.add('sp', lambda e: e.dma_start(out=gpre_bc[:], in_=vecs_d[0:1, :].partition_broadcast(128)),
              writes=['gpre_bc'], dma=True)
        xt = [sb2("xt%d" % i, [128, D], F32) for i in range(2)]
        xnb = [sb2("xnb%d" % i, [128, D], BF16) for i in range(2)]
        st1 = sb2("st1", [128, NT * 4], F32)

        def load_x_tile(t, buf, key):
            keys = key if isinstance(key, list) else [key]
            if t == 0:
                S.add('pool', lambda e: e.memset(buf, 0.0), writes=keys)
                S.add('sp', lambda e: e.dma_start(out=buf[PAD:128, :], in_=meta_d[:, :]),
                      writes=keys, dma=True)
            else:
                S.add('sp', lambda e: e.dma_start(out=buf, in_=x_d[(t - 1) * 128:t * 128, :]),
                      writes=keys, dma=True)

        def rstd_from(src, srckey, col, stt, sttname, junkbuf, junkkey):
            S.add('act', lambda e: e.activation(out=junkbuf[:], in_=src, func=AF.Square,
                                                accum_out=stt[:, col:col + 1]),
                  reads=[srckey], writes=[(sttname, col)])
            S.add('dve', lambda e: e.tensor_scalar(out=stt[:, col + 1:col + 2], in0=stt[:, col:col + 1],
                                                   scalar1=1.0 / D, scalar2=EPS, op0=ALU.mult, op1=ALU.add),
                  reads=[(sttname, col)], writes=[(sttname, col + 1)])
            S.add('pool', lambda e: e.tensor_tensor(out=stt[:, col + 2:col + 3], in0=stt[:, col + 1:col + 2],
                                                    in1=mhalf[:], op=ALU.pow),
                  reads=[(sttname, col + 1), 'mhalf'], writes=[(sttname, col + 2)])

        def p1_stage(t, s):
            i = t % 2
            pb = 4 + i
            pbv = bank(pb).bitcast(BF16)
            col = 4 * t
            if s == 0:
                load_x_tile(t, xt[i][:], ('xt', i))
            elif s == 1:
                S.add('act', lambda e: e.activation(out=junk[:], in_=xt[i][:], func=AF.Square,
                                                    accum_out=st1[:, col:col + 1]),
                      reads=[('xt', i)], writes=[('st1', col), 'junk'])
            elif s == 2:
                S.add('dve', lambda e: e.tensor_scalar(out=st1[:, col + 1:col + 2], in0=st1[:, col:col + 1],
                                                       scalar1=1.0 / D, scalar2=EPS, op0=ALU.mult, op1=ALU.add),
                      reads=[('st1', col)], writes=[('st1', col + 1)])
            elif s == 3:
                S.add('pool', lambda e: e.tensor_tensor(out=st1[:, col + 2:col + 3], in0=st1[:, col + 1:col + 2],
                                                        in1=mhalf[:], op=ALU.pow),
                      reads=[('st1', col + 1), 'mhalf'], writes=[('st1', col + 2)])
            elif s == 4:
                S.add('dve', lambda e: e.scalar_tensor_tensor(
                    out=xnb[i][:], in0=xt[i][:], scalar=st1[:, col + 2:col + 3], in1=gpre_bc[:],
                    op0=ALU.mult, op1=ALU.mult),
                    reads=[('xt', i), ('st1', col + 2), 'gpre_bc'], writes=[('xnb', i)])
            elif s == 5:
                for k in range(8):
                    S.add('pe', lambda e, k=k: e.transpose(
                        out=pbv[:, k * 128:(k + 1) * 128], in_=xnb[i][:, k * 128:(k + 1) * 128], identity=ident_b[:]),
                        reads=[('xnb', i), 'ident_b'], writes=bkeys(pb))
            elif s == 6:
                S.add('act', lambda e: e.activation(
                    out=xnT[:, :, t * 128:(t + 1) * 128], in_=pbv.rearrange("p (k c) -> p k c", k=8), func=AF.Copy),
                    reads=bkeys(pb), writes=[('xnT', t)] + bkeys(pb))

        for t0_ in range(0, NT, 2):
            ts_ = [t for t in (t0_, t0_ + 1) if t < NT]
            for s in range(7):
                for t in ts_:
                    p1_stage(t, s)
            if t0_ == 4:
                for b0 in (BLK_Q, BLK_K, BLK_V):
                    prep_blocks(b0, 8)
        for b0 in range(0, 20, 4):
            prep_blocks(b0, 4)
        for b0 in range(BLK_OUT, NBLK, 4):
            prep_blocks(b0, min(4, NBLK - b0))

        if debug:
            S.add('sp', lambda e: e.dma_start(out=dbg['xnT'], in_=xnT[:].rearrange("p k c -> p (k c)")),
                  reads=[('xnT', t) for t in range(NT)], dma=True)

        if stage >= 2:
            wq = sb2("wq", [128, 8, 128], BF16)
            wk = sb2("wk", [128, 8, 128], BF16)
            wv = sb2("wv", [128, 8, 128], BF16)
            qT = sb2("qT", [128, LP], BF16)
            kT = sb2("kT", [128, LP], BF16)
            vv = sb2("vv", [128, NT, 128], BF16)
            eb = [sb2("eb%d" % i, [128, 2, 512], BF16) for i in range(4)]
            spb = [sb2("spb%d" % i, [128, 2, 512], BF16) for i in range(3)]
            gb = [sb2("gb%d" % i, [128, 2, 512], BF16) for i in range(2)]
            wb = [sb2("wb%d" % i, [128, 2, 512], BF16) for i in range(2)]
            allx = [('xnT', t) for t in range(NT)]

            def load_wblk(dst, key, blk):
                S.add('sp', lambda e: e.dma_start(
                    out=dst[:].rearrange("p k c -> p (k c)"), in_=wscr[blk * 128:(blk + 1) * 128, :]),
                    reads=[('wscr', blk)], writes=[key], dma=True)

            gstep = 0
            for hp in range(8):
                load_wblk(wq, 'wq', BLK_Q + hp)
                load_wblk(wk, 'wk', BLK_K + hp)
                load_wblk(wv, 'wv', BLK_V + hp)
                def blk_tiles(gq):
                    return [0] if gq == 0 else list(range(4 * gq - 3, 4 * gq + 1))

                def unit_qk(gq, which, b):
                    tiles = blk_tiles(gq)
                    c0 = tiles[0] * 128
                    n = len(tiles) * 128
                    dst, dkey, wt, wkey, scale = ((qT, 'qT', wq, 'wq', 0.125) if which == 'q'
                                                  else (kT, 'kT', wk, 'wk', 1.0))
                    for k in range(8):
                        S.add('pe', lambda e, k=k: e.matmul(
                            bank(b)[:, 0:n], lhsT=wt[:, k, :], rhs=xnT[:, k, c0:c0 + n],
                            start=(k == 0), stop=(k == 7)),
                            reads=[wkey] + [('xnT', t) for t in tiles], writes=bkeys(b))
                    S.add('dve', lambda e: e.tensor_scalar(
                        out=dst[:, c0:c0 + n], in0=bank(b)[:, 0:n], scalar1=scale, scalar2=None, op0=ALU.mult),
                        reads=bkeys(b), writes=[(dkey, t) for t in tiles] + bkeys(b))

                def unit_v(tiles, b):
                    for j, t in enumerate(tiles):
                        for k in range(8):
                            S.add('pe', lambda e, j=j, t=t, k=k: e.matmul(
                                bank(b)[:, j * 128:(j + 1) * 128], lhsT=xnT[:, k, t * 128:(t + 1) * 128],
                                rhs=wv[:, k, :], start=(k == 0), stop=(k == 7)),
                                reads=['wv', ('xnT', t)], writes=bkeys(b))
                    nt_ = len(tiles)
                    t0_ = tiles[0]
                    S.add('dve', lambda e: e.tensor_copy(
                        out=vv[:, t0_:t0_ + nt_, :], in_=bank(b)[:, 0:nt_ * 128].rearrange("p (t c) -> p t c", t=nt_)),
                        reads=bkeys(b), writes=[('vv', t) for t in tiles] + bkeys(b))

                def block_units(gq, b):
                    tiles = blk_tiles(gq)
                    us = [lambda: unit_qk(gq, 'q', b), lambda: unit_qk(gq, 'k', b)]
                    for h0 in range(0, len(tiles), 2):
                        tl = tiles[h0:h0 + 2]
                        us.append(lambda tl=tl: unit_v(tl, b))
                    return us

                pb_ = 0
                for gq in (0, 1):
                    tiles = blk_tiles(gq)
                    unit_qk(gq, 'q', pb_ % 4); pb_ += 1
                    unit_qk(gq, 'k', pb_ % 4); pb_ += 1
                    for h0 in range(0, len(tiles), 2):
                        unit_v(tiles[h0:h0 + 2], pb_ % 4); pb_ += 1

                steps = []
                for qg in range(9):
                    qb0 = 0 if qg == 0 else 4 * qg - 3
                    nq = 1 if qg == 0 else 4
                    qb1 = qb0 + nq - 1
                    for kb in range(qb1, -1, -1):
                        steps.append(dict(qg=qg, kb=kb, lidx=qb1 - kb, qb0=qb0, qb1=qb1, NQ=nq * 128, qc0=qb0 * 128,
                                          ob=6 + (qg % 2), co=max(0, kb - qb0) * 128,
                                          first=(kb == qb1), last=(kb == 0)))
                cb = 4
                c3 = bank(cb, 2).rearrange("p (h c) -> p h c", h=2)

                def fZ(st, gs):
                    co, NQ, kb, qc0 = st['co'], st['NQ'], st['kb'], st['qc0']
                    zs = (gs % 2) * 2
                    z3 = bank(zs, 2).rearrange("p (h c) -> p h c", h=2)
                    for h in range(2):
                        r0 = 64 * h
                        S.add('pe', lambda e, h=h, r0=r0, z3=z3: e.matmul(
                            z3[:, h, co:NQ], lhsT=kT[r0:r0 + 64, kb * 128:(kb + 1) * 128],
                            rhs=qT[r0:r0 + 64, qc0 + co:qc0 + NQ], start=True, stop=True),
                            reads=[('kT', kb)] + [('qT', j) for j in range(st['qb0'] + co // 128, st['qb1'] + 1)],
                            writes=bkeys(zs + h))

                def fE(st, gs):
                    co, NQ, kb = st['co'], st['NQ'], st['kb']
                    zs = (gs % 2) * 2
                    i = gs % 4
                    z3 = bank(zs, 2).rearrange("p (h c) -> p h c", h=2)
                    S.add('act', lambda e: e.activation(
                        out=eb[i][:, :, co:NQ], in_=z3[:, :, co:NQ], func=AF.Exp),
                        reads=bkeys(zs, 2), writes=[('eb', i)] + bkeys(zs, 2))
                    if kb >= st['qb0']:
                        S.add('dve', lambda e: e.tensor_tensor(
                            out=eb[i][:, :, co:co + 128], in0=eb[i][:, :, co:co + 128],
                            in1=mdiag_b[:].unsqueeze(1).broadcast_to([128, 2, 128]), op=ALU.mult),
                            reads=[('eb', i), 'mdiag_b'], writes=[('eb', i)])
                    if kb == 0:
                        S.add('dve', lambda e: e.tensor_scalar(
                            out=eb[i][:, :, co:NQ], in0=eb[i][:, :, co:NQ], scalar1=padm[:, 0:1],
                            scalar2=None, op0=ALU.mult),
                            reads=[('eb', i), 'padm'], writes=[('eb', i)])

                def fL(st, gs):
                    co, NQ = st['co'], st['NQ']
                    i = gs % 3
                    ie = gs % 4
                    S.add('act', lambda e: e.activation(
                        out=spb[i][:, :, co:NQ], in_=eb[ie][:, :, co:NQ], func=AF.Ln, bias=1.0),
                        reads=[('eb', ie)], writes=[('spb', i)])

                def fT(st, gs):
                    co, NQ = st['co'], st['NQ']
                    i = gs % 3
                    for h in range(2):
                        S.add('pe', lambda e, h=h: e.matmul(
                            c3[:, h, co:NQ], lhsT=tri_b[:], rhs=spb[i][:, h, co:NQ],
                            start=st['first'], stop=False, skip_group_check=True),
                            reads=[('spb', i), 'tri_b'], writes=bkeys(cb + h))

                def fG(st, gs):
                    co, NQ = st['co'], st['NQ']
                    i = gs % 2
                    S.add('act', lambda e: e.activation(
                        out=gb[i][:, :, co:NQ], in_=c3[:, :, co:NQ], func=AF.Exp, scale=-1.0),
                        reads=bkeys(cb, 2), writes=[('gb', i)] + bkeys(cb, 2))

                def fT2(st, gs):
                    co, NQ = st['co'], st['NQ']
                    i = gs % 3
                    if st['last']:
                        return
                    for h in range(2):
                        S.add('pe', lambda e, h=h: e.matmul(
                            c3[:, h, co:NQ], lhsT=tric_b[:], rhs=spb[i][:, h, co:NQ],
                            start=False, stop=False, skip_group_check=True),
                            reads=[('spb', i), 'tric_b'], writes=bkeys(cb + h))

                def fW(st, gs):
                    co, NQ = st['co'], st['NQ']
                    i = gs % 2
                    S.add('dve', lambda e: e.tensor_tensor(
                        out=wb[i][:, :, co:NQ], in0=eb[gs % 4][:, :, co:NQ], in1=gb[i][:, :, co:NQ], op=ALU.mult),
                        reads=[('eb', gs % 4), ('gb', i)], writes=[('wb', i)])

                def fV(st, gs, hp=hp):
                    co, NQ, kb, ob, qc0 = st['co'], st['NQ'], st['kb'], st['ob'], st['qc0']
                    i = gs % 2
                    for h in range(2):
                        r0 = 64 * h
                        S.add('pe', lambda e, h=h, r0=r0: e.matmul(
                            bank(ob)[r0:r0 + 64, co:NQ], lhsT=vv[:, kb, r0:r0 + 64], rhs=wb[i][:, h, co:NQ],
                            start=st['first'], stop=False, skip_group_check=True),
                            reads=[('wb', i), ('vv', kb)], writes=bkeys(ob))
                    if st['last']:
                        S.add('dve', lambda e: e.tensor_copy(
                            out=ysbT[:, hp, qc0:qc0 + NQ], in_=bank(ob)[:, 0:NQ]),
                            reads=bkeys(ob), writes=[('ysbT', hp, st['qg'])] + bkeys(ob))

                n = len(steps)
                for i in range(-2, n + 2):
                    if 0 <= i - 1 < n:
                        fT(steps[i - 1], gstep + i - 1)
                    if 0 <= i + 2 < n:
                        fZ(steps[i + 2], gstep + i + 2)
                    if 0 <= i - 2 < n:
                        fV(steps[i - 2], gstep + i - 2)
                    if 0 <= i < n:
                        st_ = steps[i]
                        if 1 <= st_['qg'] <= 7 and 1 <= st_['lidx'] <= 4:
                            us = block_units(st_['qg'] + 1, 6 + ((st_['qg'] + 1) % 2))
                            us[st_['lidx'] - 1]()
                    if 0 <= i + 1 < n:
                        fE(steps[i + 1], gstep + i + 1)
                    if 0 <= i < n:
                        fL(steps[i], gstep + i)
                    if 0 <= i - 1 < n:
                        fG(steps[i - 1], gstep + i - 1)
                        fT2(steps[i - 1], gstep + i - 1)
                        fW(steps[i - 1], gstep + i - 1)
                gstep += n

            if debug:
                S.add('sp', lambda e: e.dma_start(out=dbg['ysbT'], in_=ysbT[:].rearrange("p k c -> p (k c)")),
                      reads=[('ysbT', hp, qg) for hp in range(8) for qg in range(9)], dma=True)

        def barrier():
            tails = {e: [op for op in S.ops[e] if not op.is_dma][-1:] for e in ENGS}
            dts = [op for q in ENGS for op in S.dma_ops[q][-NDSEM:]]
            for e in ENGS:
                ex = [o for p in ENGS if p != e for o in tails[p]] + dts
                S.add(e, lambda eng: eng.nop(), extra=ex)

        if stage >= 3:
            barrier()
            es2.close()
            NG = 384
            ident_f = sb("ident_f", [128, 128], F32)
            UTf = sb("UTf", [128, 128], F32)
            SGf = sb("SGf", [128, 128], F32)
            ONESf = sb("ONESf", [128, 128], F32)
            ones_b = sb("ones_b", [128, 128], BF16)
            cmask(ident_f, -1, 1, ALU.is_equal)
            cmask(UTf, 1, -1, ALU.is_ge)
            cmask(SGf, -1, 1, ALU.is_gt)
            S.add('pool', lambda e: e.memset(ONESf[:], 1.0), writes=['ONESf'])
            S.add('pool', lambda e: e.memset(ones_b[:], 1.0), writes=['ones_b'])
            ffnp = sb("ffnp", [128, 128], F32)
            S.add('sp', lambda e: e.dma_start(out=ffnp[:], in_=ffnp_d[:, :]), writes=['ffnp'], dma=True)
            smallbc = sb("smallbc", [128, 64], F32)
            S.add('sp', lambda e: e.dma_start(out=smallbc[:], in_=small_d[0:1, :].partition_broadcast(128)),
                  writes=['smallbc'], dma=True)
            Abc = sb("Abc", [128, 16], F32)
            S.add('act', lambda e: e.activation(out=Abc[:], in_=smallbc[:, 16:32], func=AF.Exp),
                  reads=['smallbc'], writes=['Abc'])
            S.add('dve', lambda e: e.tensor_scalar(out=Abc[:], in0=Abc[:], scalar1=-1.0, scalar2=None, op0=ALU.mult),
                  reads=['Abc'], writes=['Abc'])
            wdt_b = sb("wdt_b", [128, 8, 16], BF16)
            S.add('pool', lambda e: e.dma_start(out=wdt_b[:].rearrange("p k c -> p (k c)"), in_=wdt_d[:, :]),
                  writes=['wdt_b'], dma=True)
            gpost_bc = sb("gpost_bc", [128, D], F32)
            gfpost_bc = sb("gfpost_bc", [128, D], F32)
            S.add('sp', lambda e: e.dma_start(out=gpost_bc[:], in_=vecs_d[1:2, :].partition_broadcast(128)),
                  writes=['gpost_bc'], dma=True)
            S.add('sp', lambda e: e.dma_start(out=gfpost_bc[:], in_=vecs_d[3:4, :].partition_broadcast(128)),
                  writes=['gfpost_bc'], dma=True)

            NWB = 4
            wbuf = [sb("wbuf%d" % i, [128, 1024], BF16) for i in range(NWB)]
            wcnt = [0]

            def wload(blk):
                i = wcnt[0] % NWB
                wcnt[0] += 1
                S.add('sp', lambda e: e.dma_start(out=wbuf[i][:], in_=wscr[blk * 128:(blk + 1) * 128, :]),
                      reads=[('wscr', blk)], writes=[('wbuf', i)], dma=True)
                return wbuf[i], ('wbuf', i)

            state = sb("state", [128, 1024], F32)
            state_b = sb("state_b", [128, 1024], BF16)
            S.add('pool', lambda e: e.memset(state[:], 0.0), writes=['state'])
            S.add('pool', lambda e: e.memset(state_b[:], 0.0), writes=['state_b'])
            halo = sb("halo", [128, 12, 3], F32)
            ghalo = sb("ghalo", [128, NFC, 2], F32)
            S.add('pool', lambda e: e.memset(halo[:], 0.0), writes=['halo'])
            S.add('pool', lambda e: e.memset(ghalo[:], 0.0), writes=['ghalo'])

            hbuf = sb("hbuf", [128, 3, D], F32)
            actT = sb("actT", [128, 8, NG], BF16)
            R1 = sb("R1", [128, 6 * D], F32)
            zs = R1[:, 0:3 * D].rearrange("p (j c) -> p j c", j=3)
            Xtok = R1[:, 3 * D:6 * D].rearrange("p (j c) -> p j c", j=3)
            aT = R1[:, 0:NFC * NG // 2].bitcast(BF16).rearrange("p (f c) -> p f c", f=NFC)
            BT = sb("BT", [128, 2, NG], BF16)
            CT = sb("CT", [128, 2, NG], BF16)
            Btok = sb("Btok", [128, 3, 256], BF16)
            cbuf = [sb("cbuf%d" % i, [128, NG + 3], F32) for i in range(4)]
            cacc = [sb("cacc%d" % i, [128, NG], F32) for i in range(4)]
            sfm = [sb("sfm%d" % i, [128, NG], F32) for i in range(4)]
            t1 = sb("t1", [128, D], F32)
            RH = sb("RH", [128, 16, 128], F32)
            Xdt = sb("Xdt", [128, D], BF16)
            Xdec = sb("Xdec", [128, D], BF16)
            Eb = [sb("Eb%d" % i, [128, 512], F32) for i in range(2)]
            MTb = [sb("MTb%d" % i, [128, 4, 128], BF16) for i in range(2)]
            CBm = sb("CBm", [128, 2, 128], F32)
            tokbfs = [sb("tokbf%d" % i, [128, D], BF16) for i in range(3)]
            ygnT = sb("ygnT", [128, 8, NG], BF16)
            ysbn = sb("ysbn", [128, 8, NG], BF16)
            r_bc = sb("r_bc", [128, NG], F32)
            lnr = r_bc
            sqb = [sb("sqb%d" % i, [128, NG], BF16) for i in range(2)]
            dtt = sb("dtt", [128, 3, 16], F32)
            att = sb("att", [128, 3, 16], F32)
            dte = sb("dte", [128, 3, 16], F32)
            ex48 = sb("ex48", [128, 48], F32)
            dtd = sb("dtd", [128, 16], F32)
            st3 = sb("st3", [128, 64], F32)
            gcb = cbuf
            RHf = RH[:].rearrange("p h c -> p (h c)")
            tmps = [(t1[:], ['t1']), (RHf[:, 0:D], [('tmpf', 1)]), (RHf[:, D:2 * D], [('tmpf', 2)])]
            glb = sfm

            def bc3(ap2, n):
                return ap2.unsqueeze(2).broadcast_to([128, 16, n])

            def v3(ap2):
                return ap2.rearrange("p (h d) -> p h d", h=16)

            def run_rr(gens):
                gens = list(gens)
                while gens:
                    nxt = []
                    for gq in gens:
                        try:
                            next(gq)
                            nxt.append(gq)
                        except StopIteration:
                            pass
                    gens = nxt

            def rinfo(slot):
                return st3[:, 4 * slot + 2:4 * slot + 3], (('st3', slot), 2)

            def g_rstd3(src, srcreads, slot):
                col = 4 * slot
                key = ('st3', slot)
                S.add('act', lambda e: e.activation(out=junk[:], in_=src, func=AF.Square,
                                                    accum_out=st3[:, col:col + 1]),
                      reads=srcreads, writes=[(key, 0), 'junk'])
                yield
                S.add('dve', lambda e: e.tensor_scalar(out=st3[:, col + 1:col + 2], in0=st3[:, col:col + 1],
                                                       scalar1=1.0 / D, scalar2=EPS, op0=ALU.mult, op1=ALU.add),
                      reads=[(key, 0)], writes=[(key, 1)])
                yield
                S.add('pool', lambda e: e.tensor_tensor(out=st3[:, col + 2:col + 3], in0=st3[:, col + 1:col + 2],
                                                        in1=mhalf[:], op=ALU.pow),
                      reads=[(key, 1), 'mhalf'], writes=[(key, 2)])
                yield

            def rstd3(src, srcreads, slot):
                for _ in g_rstd3(src, srcreads, slot):
                    pass
                return rinfo(slot)

            def g_norm_transpose(j, src, srckeys, slot, dstT, dkey, pb=7):
                rcol, rkey = rinfo(slot)
                tb = tokbfs[j % 3]
                tk = ('tokbf', j % 3)
                S.add('dve', lambda e: e.tensor_scalar(out=tb[:], in0=src, scalar1=rcol, scalar2=None,
                                                       op0=ALU.mult),
                      reads=srckeys + [rkey], writes=[tk])
                yield
                pbv = bank(pb).bitcast(BF16)
                for k in range(8):
                    S.add('pe', lambda e, k=k: e.transpose(
                        out=pbv[:, k * 128:(k + 1) * 128], in_=tb[:, k * 128:(k + 1) * 128], identity=ident_b[:]),
                        reads=[tk, 'ident_b'], writes=bkeys(pb))
                yield
                S.add('act', lambda e: e.activation(
                    out=dstT[:, :, j * 128:(j + 1) * 128], in_=pbv.rearrange("p (k c) -> p k c", k=8), func=AF.Copy),
                    reads=bkeys(pb), writes=[(dkey, j)] + bkeys(pb))
                yield

            def norm_transpose(j, src, srckeys, slot, dstT, dkey, pb=7):
                for _ in g_norm_transpose(j, src, srckeys, slot, dstT, dkey, pb):
                    pass

            def apply_gain(dstT, dkey, gcol):
                for k in range(8):
                    S.add('dve', lambda e, k=k: e.tensor_scalar(
                        out=dstT[:, k, :], in0=dstT[:, k, :], scalar1=pp[:, gcol + k:gcol + k + 1], scalar2=None,
                        op0=ALU.mult),
                        reads=[(dkey, jj) for jj in range(3)] + ['pp'], writes=[(dkey, jj) for jj in range(3)])

            def do_group(g):
                t0 = 3 * g
                gc0 = t0 * 128
                hk = [('hbuf', j) for j in range(3)]
                def chain_a(gg_, j, xbuf, xkeys, pb):
                    load_x_tile(3 * gg_ + j, xbuf, xkeys)
                    yield
                    yield from g_rstd3(xbuf, xkeys, j)
                    yield from g_norm_transpose(j, xbuf, xkeys, j, actT, 'actT', pb=pb)

                if g == 0:
                    run_rr([chain_a(0, j, hbuf[:, j, :], [('hbuf', j)], 7 - j) for j in range(3)])
                else:
                    for j in range(3):
                        S.add('act', lambda e, j=j: e.activation(out=hbuf[:, j, :], in_=tmps[j][0], func=AF.Copy),
                              reads=tmps[j][1], writes=[('hbuf', j)])
                if g == 0:
                    apply_gain(actT, 'actT', 76)
                ak = [('actT', j) for j in range(3)]
                if g == 0:
                    dbgdump('actT', actT[:].rearrange("p k c -> p (k c)"), [128, 8 * NG], BF16, ak)
                for cbk in range(8):
                    w, wkey = wload(BLK_Z + cbk)
                    w3 = w[:].rearrange("p (k c) -> p k c", k=8)
                    for j in range(3):
                        b = 2 * j + cbk // 4
                        for k in range(8):
                            S.add('pe', lambda e, b=b, j=j, k=k, w3=w3, cbk=cbk: e.matmul(
                                bank(b)[:, (cbk % 4) * 128:(cbk % 4 + 1) * 128], lhsT=actT[:, k, j * 128:(j + 1) * 128],
                                rhs=w3[:, k, :], start=(k == 0), stop=(k == 7)),
                                reads=[wkey, ('actT', j)], writes=bkeys(b))
                for j in range(3):
                    S.add('act', lambda e, j=j: e.activation(out=zs[:, j, :], in_=bank(2 * j, 2), func=AF.Silu),
                          reads=bkeys(2 * j, 2), writes=[('zs', j)] + bkeys(2 * j, 2))
                for j in range(3):
                    for k in range(8):
                        S.add('pe', lambda e, j=j, k=k: e.matmul(
                            bank(6)[:, j * 16:(j + 1) * 16], lhsT=actT[:, k, j * 128:(j + 1) * 128],
                            rhs=wdt_b[:, k, :], start=(k == 0), stop=(k == 7)),
                            reads=['wdt_b', ('actT', j)], writes=bkeys(6))
                S.add('dve', lambda e: e.tensor_tensor(
                    out=dte[:], in0=bank(6)[:, 0:48].rearrange("p (j h) -> p j h", j=3),
                    in1=smallbc[:, 0:16].unsqueeze(1).broadcast_to([128, 3, 16]), op=ALU.add),
                    reads=bkeys(6) + ['smallbc'], writes=['dte'] + bkeys(6))
                S.add('act', lambda e: e.activation(out=dte[:], in_=dte[:], func=AF.Exp), reads=['dte'], writes=['dte'])
                S.add('act', lambda e: e.activation(out=dtt[:], in_=dte[:], func=AF.Ln, bias=1.0),
                      reads=['dte'], writes=['dtt'])
                if g == 0:
                    S.add('dve', lambda e: e.tensor_scalar(out=dtt[:, 0, :], in0=dtt[:, 0, :], scalar1=padm[:, 0:1],
                                                           scalar2=None, op0=ALU.mult),
                          reads=['dtt', 'padm'], writes=['dtt'])
                S.add('dve', lambda e: e.tensor_tensor(
                    out=att[:], in0=dtt[:], in1=Abc[:].unsqueeze(1).broadcast_to([128, 3, 16]), op=ALU.mult),
                    reads=['dtt', 'Abc'], writes=['att'])
                pend = []

                def emit_tr(jb, i):
                    for j in range(3):
                        S.add('pe', lambda e, i=i, j=j, jb=jb: e.transpose(
                            out=bank(2 * j + jb // 4)[:, (jb % 4) * 128:(jb % 4 + 1) * 128],
                            in_=sfm[i][:, j * 128:(j + 1) * 128], identity=ident_f[:]),
                            reads=[('sfm', i), 'ident_f'], writes=bkeys(2 * j + jb // 4))

                for jb in range(12):
                    w, wkey = wload(BLK_XBC + jb)
                    w3 = w[:].rearrange("p (k c) -> p k c", k=8)
                    b = 6 + (jb % 2)
                    i = jb % 4
                    for k in range(8):
                        S.add('pe', lambda e, b=b, k=k, w3=w3: e.matmul(
                            bank(b)[:, 0:NG], lhsT=w3[:, k, :], rhs=actT[:, k, :], start=(k == 0), stop=(k == 7)),
                            reads=[wkey] + ak, writes=bkeys(b))
                    if len(pend) >= 2:
                        emit_tr(*pend.pop(0))
                    S.add('act', lambda e, b=b, i=i: e.activation(out=cbuf[i][:, 3:3 + NG], in_=bank(b)[:, 0:NG],
                                                                  func=AF.Copy),
                          reads=bkeys(b), writes=[('cbuf', i)] + bkeys(b))
                    S.add('pool', lambda e, i=i, jb=jb: e.tensor_copy(out=cbuf[i][:, 0:3], in_=halo[:, jb, :]),
                          reads=['halo'], writes=[('cbuf', i)])
                    S.add('pool', lambda e, i=i, jb=jb: e.tensor_copy(out=halo[:, jb, :], in_=cbuf[i][:, NG:NG + 3]),
                          reads=[('cbuf', i)], writes=['halo'])
                    for kk in range(4):
                        wc = pp[:, 16 + jb * 4 + kk:16 + jb * 4 + kk + 1]
                        if kk == 0:
                            S.add('dve', lambda e, i=i, wc=wc: e.tensor_scalar(
                                out=cacc[i][:], in0=cbuf[i][:, 0:NG], scalar1=wc, scalar2=None, op0=ALU.mult),
                                reads=[('cbuf', i), 'pp'], writes=[('cacc', i)])
                        else:
                            S.add('dve', lambda e, i=i, wc=wc, kk=kk: e.scalar_tensor_tensor(
                                out=cacc[i][:], in0=cbuf[i][:, kk:kk + NG], scalar=wc, in1=cacc[i][:],
                                op0=ALU.mult, op1=ALU.add),
                                reads=[('cbuf', i), 'pp', ('cacc', i)], writes=[('cacc', i)])
                    bias = pp[:, 64 + jb:65 + jb]
                    if jb < 8:
                        S.add('act', lambda e, i=i, bias=bias: e.activation(out=sfm[i][:], in_=cacc[i][:], func=AF.Silu,
                                                                            bias=bias),
                              reads=[('cacc', i), 'pp'], writes=[('sfm', i)])
                        pend.append((jb, i))
                    else:
                        dst = BT if jb < 10 else CT
                        dk = 'BT' if jb < 10 else 'CT'
                        gg = jb % 2
                        S.add('act', lambda e, i=i, bias=bias, dst=dst, gg=gg: e.activation(
                            out=dst[:, gg, :], in_=cacc[i][:], func=AF.Silu, bias=bias),
                            reads=[('cacc', i), 'pp'], writes=[(dk, gg)])
                while pend:
                    emit_tr(*pend.pop(0))
                for j in range(3):
                    S.add('act', lambda e, j=j: e.activation(out=Xtok[:, j, :], in_=bank(2 * j, 2), func=AF.Copy),
                          reads=bkeys(2 * j, 2), writes=[('Xtok', j)] + bkeys(2 * j, 2))
                pbv = bank(7).bitcast(BF16)
                for gg in range(2):
                    for j in range(3):
                        S.add('pe', lambda e, j=j, gg=gg, pbv=pbv: e.transpose(
                            out=pbv[:, j * 256 + gg * 128:j * 256 + (gg + 1) * 128],
                            in_=BT[:, gg, j * 128:(j + 1) * 128], identity=ident_b[:]),
                            reads=[('BT', gg), 'ident_b'], writes=bkeys(7))
                S.add('dve', lambda e, pbv=pbv: e.tensor_copy(
                    out=Btok[:], in_=pbv[:, 0:768].rearrange("p (j c) -> p j c", j=3)),
                    reads=bkeys(7), writes=['Btok'] + bkeys(7))
                if g == 0:
                    dbgdump('zs', R1[:, 0:3 * D], [128, 3 * D], F32, [('zs', jj) for jj in range(3)])
                    dbgdump('Xtok', R1[:, 3 * D:6 * D], [128, 3 * D], F32, [('Xtok', jj) for jj in range(3)])
                    dbgdump('dtt', dtt[:].rearrange("p j h -> p (j h)"), [128, 48], F32, ['dtt'])
                    dbgdump('att', att[:].rearrange("p j h -> p (j h)"), [128, 48], F32, ['att'])
                    dbgdump('BT', BT[:].rearrange("p g c -> p (g c)"), [128, 2 * NG], BF16, [('BT', 0), ('BT', 1)])
                    dbgdump('CT', CT[:].rearrange("p g c -> p (g c)"), [128, 2 * NG], BF16, [('CT', 0), ('CT', 1)])
                    dbgdump('Btok', Btok[:].rearrange("p j c -> p (j c)"), [128, 768], BF16, ['Btok'])
                for j in range(3):
                    tc0 = j * 128
                    a_j = att[:, j, :]
                    for ci, lt in enumerate((UTf, SGf, ONESf)):
                        S.add('pe', lambda e, ci=ci, lt=lt, a_j=a_j: e.matmul(
                            bank(0)[:, ci * 16:(ci + 1) * 16], lhsT=lt[:], rhs=a_j, start=True, stop=True),
                            reads=['att', 'UTf', 'SGf', 'ONESf'], writes=bkeys(0))
                    S.add('act', lambda e: e.activation(out=ex48[:], in_=bank(0)[:, 0:48], func=AF.Exp),
                          reads=bkeys(0), writes=['ex48'] + bkeys(0))
                    S.add('dve', lambda e, a_j=a_j: e.tensor_tensor(
                        out=RH[:], in0=UTf[:].unsqueeze(1).broadcast_to([128, 16, 128]), in1=bc3(a_j, 128),
                        op=ALU.mult),
                        reads=['att', 'UTf'], writes=['RH', ('tmpf', 1), ('tmpf', 2)])
                    for gg in range(2):
                        S.add('pe', lambda e, gg=gg, tc0=tc0: e.matmul(
                            bank(0)[:, 128 + gg * 128:256 + gg * 128], lhsT=BT[:, gg, tc0:tc0 + 128],
                            rhs=CT[:, gg, tc0:tc0 + 128], start=True, stop=True),
                            reads=[('BT', gg), ('CT', gg)], writes=bkeys(0))
                    S.add('dve', lambda e: e.tensor_tensor(
                        out=CBm[:], in0=bank(0)[:, 128:384].rearrange("p (g c) -> p g c", g=2),
                        in1=UTf[:].unsqueeze(1).broadcast_to([128, 2, 128]), op=ALU.mult),
                        reads=bkeys(0) + ['UTf'], writes=['CBm'] + bkeys(0))
                    for gg in range(2):
                        S.add('pe', lambda e, gg=gg, tc0=tc0: e.matmul(
                            bank(5 + gg), lhsT=CT[:, gg, tc0:tc0 + 128], rhs=state_b[:, gg * 512:(gg + 1) * 512],
                            start=True, stop=True),
                            reads=[('CT', gg), 'state_b'], writes=bkeys(5 + gg))
                    S.add('dve', lambda e, j=j: e.tensor_tensor(out=dtd[:], in0=dtt[:, j, :], in1=ex48[:, 16:32],
                                                                op=ALU.mult),
                          reads=['dtt', 'ex48'], writes=['dtd'])
                    S.add('dve', lambda e, j=j: e.tensor_tensor(out=v3(Xdt[:]), in0=v3(Xtok[:, j, :]),
                                                                in1=bc3(dtt[:, j, :], 64), op=ALU.mult),
                          reads=[('Xtok', j), 'dtt'], writes=['Xdt'])
                    S.add('dve', lambda e, j=j: e.tensor_tensor(out=v3(Xdec[:]), in0=v3(Xtok[:, j, :]),
                                                                in1=bc3(dtd[:], 64), op=ALU.mult),
                          reads=[('Xtok', j), 'dtd'], writes=['Xdec'])
                    def fD(hq):
                        i = hq % 2
                        S.add('pe', lambda e, i=i, hq=hq: e.matmul(
                            bank(1 + i), lhsT=SGf[:], rhs=RH[:, 4 * hq:4 * hq + 4, :].rearrange("p h c -> p (h c)"),
                            start=True, stop=True),
                            reads=['RH', ('tmpf', 1), ('tmpf', 2), 'SGf'], writes=bkeys(1 + i))
                        S.add('act', lambda e, i=i: e.activation(out=Eb[i][:], in_=bank(1 + i), func=AF.Exp),
                              reads=bkeys(1 + i), writes=[('Eb', i)] + bkeys(1 + i))

                    def fY(hq):
                        i = hq % 2
                        gg = hq // 2
                        S.add('dve', lambda e, i=i, gg=gg: e.tensor_tensor(
                            out=MTb[i][:], in0=Eb[i][:].rearrange("p (h c) -> p h c", h=4),
                            in1=CBm[:, gg:gg + 1, :].broadcast_to([128, 4, 128]), op=ALU.mult),
                            reads=[('Eb', i), 'CBm'], writes=[('MTb', i)])
                        for hh in range(4):
                            h = 4 * hq + hh
                            S.add('pe', lambda e, i=i, hh=hh, h=h: e.matmul(
                                bank(3 + h // 8)[:, (h % 8) * 64:(h % 8 + 1) * 64], lhsT=MTb[i][:, hh, :],
                                rhs=Xdt[:, h * 64:(h + 1) * 64], start=True, stop=True),
                                reads=[('MTb', i), 'Xdt'], writes=bkeys(3 + h // 8))

                    fD(0)
                    fD(1)
                    fY(0)
                    fD(2)
                    fY(1)
                    fD(3)
                    fY(2)
                    fY(3)
                    S.add('dve', lambda e: e.tensor_tensor(out=v3(t1[:]), in0=v3(bank(5, 2)), in1=bc3(ex48[:, 0:16], 64),
                                                           op=ALU.mult),
                          reads=bkeys(5, 2) + ['ex48'], writes=['t1'] + bkeys(5, 2))
                    S.add('dve', lambda e: e.tensor_tensor(out=t1[:], in0=bank(3, 2), in1=t1[:], op=ALU.add),
                          reads=bkeys(3, 2) + ['t1'], writes=['t1'] + bkeys(3, 2))
                    for gg in range(2):
                        S.add('pe', lambda e, gg=gg, j=j: e.matmul(
                            bank(5 + gg), lhsT=Btok[:, j, gg * 128:(gg + 1) * 128], rhs=Xdec[:, gg * 512:(gg + 1) * 512],
                            start=True, stop=True),
                            reads=['Btok', 'Xdec'], writes=bkeys(5 + gg))
                    S.add('dve', lambda e, j=j: e.tensor_tensor(out=v3(Xtok[:, j, :]), in0=v3(Xtok[:, j, :]),
                                                                in1=bc3(smallbc[:, 32:48], 64), op=ALU.mult),
                          reads=[('Xtok', j), 'smallbc'], writes=[('Xtok', j)])
                    S.add('dve', lambda e, j=j: e.tensor_tensor(out=t1[:], in0=t1[:], in1=Xtok[:, j, :], op=ALU.add),
                          reads=[('Xtok', j), 't1'], writes=['t1'])
                    S.add('dve', lambda e, j=j: e.tensor_tensor(out=t1[:], in0=t1[:], in1=zs[:, j, :], op=ALU.mult),
                          reads=[('zs', j), 't1'], writes=['t1'])
                    S.add('dve', lambda e: e.tensor_tensor(out=v3(state[:]), in0=v3(state[:]), in1=bc3(ex48[:, 32:48], 64),
                                                           op=ALU.mult),
                          reads=['state', 'ex48'], writes=['state'])
                    S.add('dve', lambda e: e.tensor_tensor(out=state[:], in0=bank(5, 2), in1=state[:], op=ALU.add),
                          reads=bkeys(5, 2) + ['state'], writes=['state'] + bkeys(5, 2))
                    S.add('act', lambda e: e.activation(out=state_b[:], in_=state[:], func=AF.Copy),
                          reads=['state'], writes=['state_b'])
                    if g == 0:
                        dbgdump('yg%d' % j, t1[:], [128, D], F32, ['t1'])
                        dbgdump('ex48_%d' % j, ex48[:], [128, 48], F32, ['ex48'])
                    rstd3(t1[:], ['t1'], 3 + j)
                    norm_transpose(j, t1[:], ['t1'], 3 + j, ygnT, 'ygnT')
                apply_gain(ygnT, 'ygnT', 0)
                for k in range(8):
                    i = k % 2
                    S.add('act', lambda e, i=i, k=k: e.activation(out=sqb[i][:], in_=ysbT[:, k, gc0:gc0 + NG],
                                                                  func=AF.Square),
                          reads=[('ysbT', k, q) for q in range(9)], writes=[('sqb', i)])
                    S.add('pe', lambda e, i=i, k=k: e.matmul(bank(6)[:, 0:NG], lhsT=ones_b[:], rhs=sqb[i][:],
                                                             start=(k == 0), stop=(k == 7)),
                          reads=[('sqb', i), 'ones_b'], writes=bkeys(6))
                S.add('act', lambda e: e.activation(out=lnr[:], in_=bank(6)[:, 0:NG], func=AF.Ln, scale=1.0 / D, bias=EPS),
                      reads=bkeys(6), writes=['lnr', 'r_bc'] + bkeys(6))
                S.add('act', lambda e: e.activation(out=r_bc[:], in_=lnr[:], func=AF.Exp, scale=-0.5),
                      reads=['lnr', 'r_bc'], writes=['r_bc', 'lnr'])
                for k in range(8):
                    S.add('dve', lambda e, k=k: e.scalar_tensor_tensor(
                        out=ysbn[:, k, :], in0=ysbT[:, k, gc0:gc0 + NG], scalar=pp[:, 8 + k:9 + k], in1=r_bc[:],
                        op0=ALU.mult, op1=ALU.mult),
                        reads=[('ysbT', k, q) for q in range(9)] + ['pp', 'r_bc'], writes=[('ysbn', k)])
                for kc in range(16):
                    w, wkey = wload(BLK_OUT + kc)
                    src = ygnT if kc < 8 else ysbn
                    sk = [('ygnT', jj) for jj in range(3)] if kc < 8 else [('ysbn', kc - 8)]
                    for j in range(3):
                        for hf in range(2):
                            S.add('pe', lambda e, j=j, hf=hf, w=w, src=src, kc=kc: e.matmul(
                                bank(2 * j + hf), lhsT=src[:, kc % 8, j * 128:(j + 1) * 128],
                                rhs=w[:, hf * 512:(hf + 1) * 512], start=(kc == 0), stop=(kc == 15)),
                                reads=[wkey] + sk, writes=bkeys(2 * j + hf))
                def chain_post(j, gbc, gkey, slot0, final, tl=None):
                    tmp, tkk = (tl or tmps)[j]
                    S.add('dve', lambda e: e.tensor_tensor(
                        out=tmp, in0=bank(2 * j, 2), in1=gbc[:], op=ALU.mult),
                        reads=bkeys(2 * j, 2) + [gkey], writes=tkk + bkeys(2 * j, 2))
                    yield
                    yield from g_rstd3(bank(2 * j, 2), bkeys(2 * j, 2), slot0 + j)
                    rc, rk = rinfo(slot0 + j)
                    t = t0 + j
                    if not final:
                        S.add('dve', lambda e: e.scalar_tensor_tensor(
                            out=hbuf[:, j, :], in0=tmp, scalar=rc, in1=hbuf[:, j, :], op0=ALU.mult, op1=ALU.add),
                            reads=tkk + [rk, ('hbuf', j)], writes=[('hbuf', j)])
                        yield
                    else:
                        S.add('dve', lambda e: e.scalar_tensor_tensor(
                            out=tmp, in0=tmp, scalar=rc, in1=hbuf[:, j, :], op0=ALU.mult, op1=ALU.add),
                            reads=tkk + [rk, ('hbuf', j)], writes=tkk)
                        yield
                        if t >= 1:
                            S.add('sp', lambda e: e.dma_start(out=out_d[(t - 1) * 128:t * 128, :], in_=tmp),
                                  reads=tkk, dma=True)

                run_rr([chain_post(j, gpost_bc, 'gpost_bc', 6, False) for j in range(3)])
                if g == 0:
                    dbgdump('ygnT', ygnT[:].rearrange("p k c -> p (k c)"), [128, 8 * NG], BF16, [('ygnT', jj) for jj in range(3)])
                    dbgdump('ysbn', ysbn[:].rearrange("p k c -> p (k c)"), [128, 8 * NG], BF16, [('ysbn', k) for k in range(8)])
                    dbgdump('r_bc', r_bc[:], [128, NG], F32, ['r_bc'])
                    dbgdump('h1', hbuf[:].rearrange("p j c -> p (j c)"), [128, 3 * D], F32, hk)
                    dbgdump('state', state[:], [128, D], F32, ['state'])
                def chain_e(j):
                    yield from g_rstd3(hbuf[:, j, :], [('hbuf', j)], 9 + j)
                    yield from g_norm_transpose(j, hbuf[:, j, :], [('hbuf', j)], 9 + j, actT, 'actT', pb=7 - j)

                run_rr([chain_e(j) for j in range(3)])
                apply_gain(actT, 'actT', 84)
                for fc in range(NFC):
                    wg, wgk = wload(BLK_GATE + fc)
                    wu, wuk = wload(BLK_UP + fc)
                    wg3 = wg[:].rearrange("p (k c) -> p k c", k=8)
                    wu3 = wu[:].rearrange("p (k c) -> p k c", k=8)
                    i = fc % 4
                    bg = fc % 3
                    bu = 3 + fc % 3
                    for k in range(8):
                        S.add('pe', lambda e, bg=bg, k=k, wg3=wg3: e.matmul(
                            bank(bg)[:, 0:NG], lhsT=wg3[:, k, :], rhs=actT[:, k, :], start=(k == 0), stop=(k == 7)),
                            reads=[wgk] + ak, writes=bkeys(bg))
                    for k in range(8):
                        S.add('pe', lambda e, bu=bu, k=k, wu3=wu3: e.matmul(
                            bank(bu)[:, 0:NG], lhsT=wu3[:, k, :], rhs=actT[:, k, :], start=(k == 0), stop=(k == 7)),
                            reads=[wuk] + ak, writes=bkeys(bu))
                    S.add('act', lambda e, bg=bg, i=i: e.activation(out=gcb[i][:, 2:2 + NG], in_=bank(bg)[:, 0:NG],
                                                                    func=AF.Copy),
                          reads=bkeys(bg), writes=[('cbuf', i)] + bkeys(bg))
                    S.add('pool', lambda e, i=i, fc=fc: e.tensor_copy(out=gcb[i][:, 0:2], in_=ghalo[:, fc, :]),
                          reads=['ghalo'], writes=[('cbuf', i)])
                    S.add('pool', lambda e, i=i, fc=fc: e.tensor_copy(out=ghalo[:, fc, :], in_=gcb[i][:, NG:NG + 2]),
                          reads=[('cbuf', i)], writes=['ghalo'])
                    for kk in range(3):
                        wc = ffnp[:, fc * 3 + kk:fc * 3 + kk + 1]
                        if kk == 0:
                            S.add('dve', lambda e, i=i, wc=wc: e.tensor_scalar(
                                out=cacc[i][:], in0=gcb[i][:, 0:NG], scalar1=wc, scalar2=None, op0=ALU.mult),
                                reads=[('cbuf', i), 'ffnp'], writes=[('cacc', i)])
                        else:
                            S.add('dve', lambda e, i=i, wc=wc, kk=kk: e.scalar_tensor_tensor(
                                out=cacc[i][:], in0=gcb[i][:, kk:kk + NG], scalar=wc, in1=cacc[i][:],
                                op0=ALU.mult, op1=ALU.add),
                                reads=[('cbuf', i), 'ffnp', ('cacc', i)], writes=[('cacc', i)])
                    S.add('act', lambda e, i=i, fc=fc: e.activation(out=glb[i][:], in_=cacc[i][:], func=AF.Gelu_apprx_tanh,
                                                                    bias=ffnp[:, 66 + fc:67 + fc]),
                          reads=[('cacc', i), 'ffnp'], writes=[('sfm', i)])
                    S.add('dve', lambda e, i=i, fc=fc, bu=bu: e.tensor_tensor(
                        out=aT[:, fc, :], in0=bank(bu)[:, 0:NG], in1=glb[i][:], op=ALU.mult),
                        reads=bkeys(bu) + [('sfm', i)],
                        writes=[('zs', jj) for jj in range(3)] + [('Xtok', jj) for jj in range(3)] + bkeys(bu))
                aTk = [('zs', jj) for jj in range(3)] + [('Xtok', jj) for jj in range(3)]
                pre = []
                if g + 1 < ngroups:
                    pre = [chain_a(g + 1, j, tmps[j][0], tmps[j][1], 7 - (j % 2)) for j in range(3)]
                    for _ in range(5):
                        for gq in pre:
                            next(gq)
                sched = {6: 0, 8: 0, 10: 1, 12: 1, 14: 2, 16: 2}
                for fc in range(NFC):
                    w, wkey = wload(BLK_DOWN + fc)
                    for j in range(3):
                        for hf in range(2):
                            S.add('pe', lambda e, j=j, hf=hf, w=w, fc=fc: e.matmul(
                                bank(2 * j + hf), lhsT=aT[:, fc, j * 128:(j + 1) * 128],
                                rhs=w[:, hf * 512:(hf + 1) * 512], start=(fc == 0), stop=(fc == NFC - 1)),
                                reads=[wkey] + aTk, writes=bkeys(2 * j + hf))
                    if pre and fc in sched:
                        next(pre[sched[fc]])
                if pre:
                    apply_gain(actT, 'actT', 76)
                ftmps = [(zs[:, j, :], [('zs', j)]) for j in range(3)]
                run_rr([chain_post(j, gfpost_bc, 'gfpost_bc', 12, True, ftmps) for j in range(3)])

            ngroups = 11 if stage >= 4 else 1
            for g in range(ngroups):
                do_group(g)

        tail = [op for q in ENGS for op in S.dma_ops[q][-NDSEM:]]
        S.add('sp', lambda e: e.nop(), extra=tail)
        es2.close()
        S.emit(nc, es)
    return nc


def _host_layout(inputs):
    f = np.float32
    w_in = np.asarray(inputs["w_in"], f)[0]
    w_out = np.asarray(inputs["w_out"], f)[0]
    w_up = np.asarray(inputs["w_up"], f)[0]
    w_down = np.asarray(inputs["w_down"], f)[0]

    def kblocks(w, c0, nb):
        sub = w[:, c0:c0 + nb * 128].reshape(8, 128, nb, 128)
        return np.ascontiguousarray(sub.transpose(2, 1, 0, 3)).reshape(nb * 128, 1024)

    blocks = [kblocks(w_in, OFF_Z, 8), kblocks(w_in, OFF_XBC, 12), kblocks(w_in, OFF_Q, 8),
              kblocks(w_in, OFF_K, 8), kblocks(w_in, OFF_V, 8),
              w_out.reshape(16 * 128, 1024),
              kblocks(w_up, 0, 22), kblocks(w_up, DFF, 22),
              w_down.reshape(22 * 128, 1024)]
    wblk = np.ascontiguousarray(np.concatenate(blocks, axis=0))
    assert wblk.shape == (NBLK * 128, 1024)
    wdt = np.ascontiguousarray(w_in[:, OFF_DT:OFF_DT + 16].reshape(8, 128, 16).transpose(1, 0, 2)).reshape(128, 128)
    vecs = np.zeros((8, D), f)
    vecs[0] = np.asarray(inputs["mix_pre_g"], f)[0]
    vecs[1] = np.asarray(inputs["mix_post_g"], f)[0]
    vecs[2] = np.asarray(inputs["ffn_pre_g"], f)[0]
    vecs[3] = np.asarray(inputs["ffn_post_g"], f)[0]
    pp = np.zeros((128, 128), f)
    pp[:, 0:8] = np.asarray(inputs["ssd_norm_g"], f)[0].reshape(8, 128).T
    pp[:, 8:16] = np.asarray(inputs["sb_norm_g"], f)[0].reshape(8, 128).T
    cw = np.asarray(inputs["ssd_conv_w"], f)[0]
    pp[:, 16:64] = cw.reshape(4, 12, 128).transpose(2, 1, 0).reshape(128, 48)
    pp[:, 64:76] = np.asarray(inputs["ssd_conv_b"], f)[0].reshape(12, 128).T
    pp[:, 76:84] = np.asarray(inputs["mix_pre_g"], f)[0].reshape(8, 128).T
    pp[:, 84:92] = np.asarray(inputs["ffn_pre_g"], f)[0].reshape(8, 128).T
    fw = np.asarray(inputs["ffn_conv_w"], f)[0]
    ffn = np.zeros((128, 128), f)
    ffn[:, 0:66] = fw.reshape(3, 22, 128).transpose(2, 1, 0).reshape(128, 66)
    ffn[:, 66:88] = np.asarray(inputs["ffn_conv_b"], f)[0].reshape(22, 128).T
    small = np.zeros((1, 64), f)
    small[0, 0:16] = np.asarray(inputs["ssd_dt_bias"], f)[0]
    small[0, 16:32] = np.asarray(inputs["ssd_a_log"], f)[0]
    small[0, 32:48] = np.asarray(inputs["ssd_d"], f)[0]
    return dict(wblk=wblk, wdt=wdt, vecs=vecs, ppd=pp, ffnpd=ffn, small=small,
                meta=np.ascontiguousarray(np.asarray(inputs["meta_tokens"], f)))


def kernel(**inputs):
    x = np.asarray(inputs["x"], np.float32)
    shared = _host_layout(inputs)
    nc = build_nc()
    in_maps = []
    for b in range(8):
        m = dict(shared)
        m["x"] = np.ascontiguousarray(x[b])
        in_maps.append(m)
    res = run_bass_kernel_spmd(nc, in_maps, core_ids=list(range(8)))
    return np.stack([r["out"] for r in res.results], axis=0)
```

```python
import numpy as np
from contextlib import ExitStack
import concourse.bass as bass
import concourse.mybir as mybir
from concourse.bass_utils import run_bass_kernel_spmd

F32 = mybir.dt.float32
BF16 = mybir.dt.bfloat16
AF = mybir.ActivationFunctionType
ALU = mybir.AluOpType

D = 1024
SEQ = 4096
NMETA = 16
NT = 33
LP = NT * 128
PAD = 112
H = 16
DFF = 2816
NFC = 22
EPS = 1e-6
OFF_Z, OFF_XBC, OFF_DT, OFF_Q, OFF_K, OFF_V = 0, 1024, 2560, 2576, 3600, 4624

BLK_Z = 0
BLK_XBC = 8
BLK_Q = 20
BLK_K = 28
BLK_V = 36
BLK_OUT = 44
BLK_GATE = 60
BLK_UP = 82
BLK_DOWN = 104
NBLK = 126

ENGS = ['pe', 'act', 'dve', 'pool', 'sp']
NDSEM = 8


class _Op:
    __slots__ = ('eng', 'fn', 'deps', 'is_dma', 'needed', 'val', 'sem', 'idx')


class Sched:
    def __init__(self):
        self.ops = {e: [] for e in ENGS}
        self.last_w = {}
        self.readers = {}
        self.seen_c = {e: {p: -1 for p in ENGS} for e in ENGS}
        self.seen_d = {e: set() for e in ENGS}
        self.ndma = {e: 0 for e in ENGS}
        self.dma_ops = {e: [] for e in ENGS}

    def add(self, eng, fn, reads=(), writes=(), dma=False, extra=()):
        op = _Op()
        op.eng = eng
        op.fn = fn
        op.is_dma = dma
        op.needed = False
        op.val = None
        op.sem = None
        op.idx = len(self.ops[eng])
        deps = list(extra)
        for k in reads:
            w = self.last_w.get(k)
            if w is not None:
                deps.append(w)
        for k in writes:
            w = self.last_w.get(k)
            if w is not None:
                deps.append(w)
            deps.extend(self.readers.get(k, ()))
        if dma:
            n = self.ndma[eng]
            if n >= NDSEM:
                deps.append(self.dma_ops[eng][n - NDSEM])
            op.sem = ('d', eng, n % NDSEM)
            op.val = 16 * (n // NDSEM + 1)
            self.ndma[eng] = n + 1
            self.dma_ops[eng].append(op)
        cdeps = {}
        ddeps = []
        for d in deps:
            if d is op:
                continue
            if d.is_dma:
                key = (d.eng, d.sem, d.val)
                if key in self.seen_d[eng]:
                    continue
                self.seen_d[eng].add(key)
                ddeps.append(d)
            else:
                if d.eng == eng and eng == 'pe':
                    continue
                if d.idx <= self.seen_c[eng][d.eng]:
                    continue
                if d.eng not in cdeps or cdeps[d.eng].idx < d.idx:
                    cdeps[d.eng] = d
        for p, d in cdeps.items():
            self.seen_c[eng][p] = d.idx
            d.needed = True
        op.deps = list(cdeps.values()) + ddeps
        self.ops[eng].append(op)
        for k in writes:
            self.last_w[k] = op
            self.readers[k] = []
        for k in reads:
            if k in writes:
                continue
            self.readers.setdefault(k, []).append(op)
        return op

    def emit(self, nc, es):
        sem_c = {e: es.enter_context(nc.semaphore('sc_' + e)) for e in ENGS}
        sem_d = {}
        for e in ENGS:
            for i in range(min(NDSEM, self.ndma[e])):
                sem_d[('d', e, i)] = es.enter_context(nc.semaphore('sd_%s_%d' % (e, i)))
        for e in ENGS:
            c = 0
            for op in self.ops[e]:
                if op.is_dma:
                    continue
                if op.needed:
                    c += 1
                    op.val = c
        block = es.enter_context(nc.Block())
        secs = {'pe': block.tensor, 'act': block.scalar, 'dve': block.vector,
                'pool': block.gpsimd, 'sp': block.sync}

        def mk(e):
            def body(eng):
                for op in self.ops[e]:
                    for d in op.deps:
                        if d.is_dma:
                            eng.wait_ge(sem_d[d.sem], d.val)
                        else:
                            eng.wait_ge(sem_c[d.eng], d.val)
                    inst = op.fn(eng)
                    if op.is_dma:
                        inst.then_inc(sem_d[op.sem], 16)
                    elif op.needed:
                        inst.then_inc(sem_c[e], 1)
            return body

        for e in ENGS:
            if self.ops[e]:
                secs[e](mk(e))


def build_nc(stage=99, debug=False):
    nc = bass.Bass("TRN2", target_bir_lowering=False)
    es = ExitStack()

    def din(name, shape):
        return nc.dram_tensor(name, list(shape), F32, kind="ExternalInput").ap()

    x_d = din("x", [SEQ, D])
    meta_d = din("meta", [NMETA, D])
    wblk_d = din("wblk", [NBLK * 128, 1024])
    wdt_d = din("wdt", [128, 8 * 16])
    vecs_d = din("vecs", [8, D])
    pp_d = din("ppd", [128, 128])
    ffnp_d = din("ffnpd", [128, 128])
    small_d = din("small", [1, 64])
    out_d = nc.dram_tensor("out", [SEQ, D], F32, kind="ExternalOutput").ap()
    wscr = nc.dram_tensor("wscr", [NBLK * 128, 1024], BF16, kind="Internal").ap()
    dbg = {}
    if debug:
        dbg['xnT'] = nc.dram_tensor("dbg_xnT", [128, 8 * LP], BF16, kind="ExternalOutput").ap()
        dbg['ysbT'] = nc.dram_tensor("dbg_ysbT", [128, 8 * LP], BF16, kind="ExternalOutput").ap()

    S = Sched()
    with es:
        es2 = ExitStack()

        def sb(name, shape, dt=F32):
            return es.enter_context(nc.sbuf_tensor(name, list(shape), dt))

        def sb2(name, shape, dt=F32):
            return es2.enter_context(nc.sbuf_tensor(name, list(shape), dt))

        ps = es.enter_context(nc.psum_tensor("ps", [128, 4096], F32))

        def bank(b, n=1):
            return ps[:, 512 * b:512 * (b + n)]

        def bkeys(b, n=1):
            return [('ps', b + i) for i in range(n)]

        def dbgdump(name, ap, shape, dt, reads):
            if not debug:
                return
            t = nc.dram_tensor("dbg_" + name, list(shape), dt, kind="ExternalOutput").ap()
            S.add('sp', lambda e: e.dma_start(out=t, in_=ap), reads=reads, dma=True)

        ident_b = sb("ident_b", [128, 128], BF16)
        tri_b = sb("tri_b", [128, 128], BF16)
        tric_b = sb("tric_b", [128, 128], BF16)
        mdiag_b = sb("mdiag_b", [128, 128], BF16)
        padm = sb("padm", [128, 1], F32)
        mhalf = sb("mhalf", [128, 1], F32)
        pp = sb("pp", [128, 128], F32)
        junk = sb("junk", [128, D], BF16)

        def cmask(t, pattern_mult, chan_mult, op, base=0):
            S.add('pool', lambda e: e.memset(t[:], 1.0), writes=[t.name])
            S.add('pool', lambda e: e.affine_select(
                out=t[:], in_=t[:], pattern=[[pattern_mult, t.shape[1]]], compare_op=op,
                fill=0.0, base=base, channel_multiplier=chan_mult), reads=[t.name], writes=[t.name])

        cmask(ident_b, -1, 1, ALU.is_equal)
        cmask(tri_b, -1, 1, ALU.is_ge)
        cmask(tric_b, 1, -1, ALU.is_gt)
        cmask(mdiag_b, 1, -1, ALU.is_gt)
        cmask(padm, 0, 1, ALU.is_ge, base=-PAD)
        S.add('pool', lambda e: e.memset(mhalf[:], -0.5), writes=['mhalf'])
        S.add('sp', lambda e: e.dma_start(out=pp[:], in_=pp_d[:, :]), writes=['pp'], dma=True)

        def prep_blocks(b0, nb):
            S.add('pool', lambda e: e.dma_start(out=wscr[b0 * 128:(b0 + nb) * 128, :],
                                                in_=wblk_d[b0 * 128:(b0 + nb) * 128, :]),
                  writes=[('wscr', b) for b in range(b0, b0 + nb)], dma=True)


        ysbT = sb("ysbT", [128, 8, LP], BF16)
        xnT = sb2("xnT", [128, 8, LP], BF16)
        gpre_bc = sb2("gpre_bc", [128, D], F32)

        S.add('sp', lambda e: e.dma_start(out=gpre_bc[:], in_=vecs_d[0:1, :].partition_broadcast(128)),
              writes=['gpre_bc'], dma=True)
        xt = [sb2("xt%d" % i, [128, D], F32) for i in range(2)]
        xnb = [sb2("xnb%d" % i, [128, D], BF16) for i in range(2)]
        st1 = sb2("st1", [128, NT * 4], F32)

        def load_x_tile(t, buf, key):
            keys = key if isinstance(key, list) else [key]
            if t == 0:
                S.add('pool', lambda e: e.memset(buf, 0.0), writes=keys)
                S.add('sp', lambda e: e.dma_start(out=buf[PAD:128, :], in_=meta_d[:, :]),
                      writes=keys, dma=True)
            else:
                S.add('sp', lambda e: e.dma_start(out=buf, in_=x_d[(t - 1) * 128:t * 128, :]),
                      writes=keys, dma=True)

        def rstd_from(src, srckey, col, stt, sttname, junkbuf, junkkey):
            S.add('act', lambda e: e.activation(out=junkbuf[:], in_=src, func=AF.Square,
                                                accum_out=stt[:, col:col + 1]),
                  reads=[srckey], writes=[(sttname, col)])
            S.add('dve', lambda e: e.tensor_scalar(out=stt[:, col + 1:col + 2], in0=stt[:, col:col + 1],
                                                   scalar1=1.0 / D, scalar2=EPS, op0=ALU.mult, op1=ALU.add),
                  reads=[(sttname, col)], writes=[(sttname, col + 1)])
            S.add('pool', lambda e: e.tensor_tensor(out=stt[:, col + 2:col + 3], in0=stt[:, col + 1:col + 2],
                                                    in1=mhalf[:], op=ALU.pow),
                  reads=[(sttname, col + 1), 'mhalf'], writes=[(sttname, col + 2)])

        def p1_stage(t, s):
            i = t % 2
            pb = 4 + i
            pbv = bank(pb).bitcast(BF16)
            col = 4 * t
            if s == 0:
                load_x_tile(t, xt[i][:], ('xt', i))
            elif s == 1:
                S.add('act', lambda e: e.activation(out=junk[:], in_=xt[i][:], func=AF.Square,
                                                    accum_out=st1[:, col:col + 1]),
                      reads=[('xt', i)], writes=[('st1', col), 'junk'])
            elif s == 2:
                S.add('dve', lambda e: e.tensor_scalar(out=st1[:, col + 1:col + 2], in0=st1[:, col:col + 1],
                                                       scalar1=1.0 / D, scalar2=EPS, op0=ALU.mult, op1=ALU.add),
                      reads=[('st1', col)], writes=[('st1', col + 1)])
            elif s == 3:
                S.add('pool', lambda e: e.tensor_tensor(out=st1[:, col + 2:col + 3], in0=st1[:, col + 1:col + 2],
                                                        in1=mhalf[:], op=ALU.pow),
                      reads=[('st1', col + 1), 'mhalf'], writes=[('st1', col + 2)])
            elif s == 4:
                S.add('dve', lambda e: e.scalar_tensor_tensor(
                    out=xnb[i][:], in0=xt[i][:], scalar=st1[:, col + 2:col + 3], in1=gpre_bc[:],
                    op0=ALU.mult, op1=ALU.mult),
                    reads=[('xt', i), ('st1', col + 2), 'gpre_bc'], writes=[('xnb', i)])
            elif s == 5:
                for k in range(8):
                    S.add('pe', lambda e, k=k: e.transpose(
                        out=pbv[:, k * 128:(k + 1) * 128], in_=xnb[i][:, k * 128:(k + 1) * 128], identity=ident_b[:]),
                        reads=[('xnb', i), 'ident_b'], writes=bkeys(pb))
            elif s == 6:
                S.add('act', lambda e: e.activation(
                    out=xnT[:, :, t * 128:(t + 1) * 128], in_=pbv.rearrange("p (k c) -> p k c", k=8), func=AF.Copy),
                    reads=bkeys(pb), writes=[('xnT', t)] + bkeys(pb))

        for t0_ in range(0, NT, 2):
            ts_ = [t for t in (t0_, t0_ + 1) if t < NT]
            for s in range(7):
                for t in ts_:
                    p1_stage(t, s)
            if t0_ == 4:
                for b0 in (BLK_Q, BLK_K, BLK_V):
                    prep_blocks(b0, 8)
        for b0 in range(0, 20, 4):
            prep_blocks(b0, 4)
        for b0 in range(BLK_OUT, NBLK, 4):
            prep_blocks(b0, min(4, NBLK - b0))

        if debug:
            S.add('sp', lambda e: e.dma_start(out=dbg['xnT'], in_=xnT[:].rearrange("p k c -> p (k c)")),
                  reads=[('xnT', t) for t in range(NT)], dma=True)

        if stage >= 2:
            wq = sb2("wq", [128, 8, 128], BF16)
            wk = sb2("wk", [128, 8, 128], BF16)
            wv = sb2("wv", [128, 8, 128], BF16)
            qT = sb2("qT", [128, LP], BF16)
            kT = sb2("kT", [128, LP], BF16)
            vv = sb2("vv", [128, NT, 128], BF16)
            eb = [sb2("eb%d" % i, [128, 2, 512], BF16) for i in range(4)]
            spb = [sb2("spb%d" % i, [128, 2, 512], BF16) for i in range(3)]
            gb = [sb2("gb%d" % i, [128, 2, 512], BF16) for i in range(2)]
            wb = [sb2("wb%d" % i, [128, 2, 512], BF16) for i in range(2)]
            allx = [('xnT', t) for t in range(NT)]

            def load_wblk(dst, key, blk):
                S.add('sp', lambda e: e.dma_start(
                    out=dst[:].rearrange("p k c -> p (k c)"), in_=wscr[blk * 128:(blk + 1) * 128, :]),
                    reads=[('wscr', blk)], writes=[key], dma=True)

            gstep = 0
            for hp in range(8):
                load_wblk(wq, 'wq', BLK_Q + hp)
                load_wblk(wk, 'wk', BLK_K + hp)
                load_wblk(wv, 'wv', BLK_V + hp)
                def blk_tiles(gq):
                    return [0] if gq == 0 else list(range(4 * gq - 3, 4 * gq + 1))

                def unit_qk(gq, which, b):
                    tiles = blk_tiles(gq)
                    c0 = tiles[0] * 128
                    n = len(tiles) * 128
                    dst, dkey, wt, wkey, scale = ((qT, 'qT', wq, 'wq', 0.125) if which == 'q'
                                                  else (kT, 'kT', wk, 'wk', 1.0))
                    for k in range(8):
                        S.add('pe', lambda e, k=k: e.matmul(
                            bank(b)[:, 0:n], lhsT=wt[:, k, :], rhs=xnT[:, k, c0:c0 + n],
                            start=(k == 0), stop=(k == 7)),
                            reads=[wkey] + [('xnT', t) for t in tiles], writes=bkeys(b))
                    S.add('dve', lambda e: e.tensor_scalar(
                        out=dst[:, c0:c0 + n], in0=bank(b)[:, 0:n], scalar1=scale, scalar2=None, op0=ALU.mult),
                        reads=bkeys(b), writes=[(dkey, t) for t in tiles] + bkeys(b))

                def unit_v(tiles, b):
                    for j, t in enumerate(tiles):
                        for k in range(8):
                            S.add('pe', lambda e, j=j, t=t, k=k: e.matmul(
                                bank(b)[:, j * 128:(j + 1) * 128], lhsT=xnT[:, k, t * 128:(t + 1) * 128],
                                rhs=wv[:, k, :], start=(k == 0), stop=(k == 7)),
                                reads=['wv', ('xnT', t)], writes=bkeys(b))
                    nt_ = len(tiles)
                    t0_ = tiles[0]
                    S.add('dve', lambda e: e.tensor_copy(
                        out=vv[:, t0_:t0_ + nt_, :], in_=bank(b)[:, 0:nt_ * 128].rearrange("p (t c) -> p t c", t=nt_)),
                        reads=bkeys(b), writes=[('vv', t) for t in tiles] + bkeys(b))

                def block_units(gq, b):
                    tiles = blk_tiles(gq)
                    us = [lambda: unit_qk(gq, 'q', b), lambda: unit_qk(gq, 'k', b)]
                    for h0 in range(0, len(tiles), 2):
                        tl = tiles[h0:h0 + 2]
                        us.append(lambda tl=tl: unit_v(tl, b))
                    return us

                pb_ = 0
                for gq in (0, 1):
                    tiles = blk_tiles(gq)
                    unit_qk(gq, 'q', pb_ % 4); pb_ += 1
                    unit_qk(gq, 'k', pb_ % 4); pb_ += 1
                    for h0 in range(0, len(tiles), 2):
                        unit_v(tiles[h0:h0 + 2], pb_ % 4); pb_ += 1

                steps = []
                for qg in range(9):
                    qb0 = 0 if qg == 0 else 4 * qg - 3
                    nq = 1 if qg == 0 else 4
                    qb1 = qb0 + nq - 1
                    for kb in range(qb1, -1, -1):
                        steps.append(dict(qg=qg, kb=kb, lidx=qb1 - kb, qb0=qb0, qb1=qb1, NQ=nq * 128, qc0=qb0 * 128,
                                          ob=6 + (qg % 2), co=max(0, kb - qb0) * 128,
                                          first=(kb == qb1), last=(kb == 0)))
                cb = 4
                c3 = bank(cb, 2).rearrange("p (h c) -> p h c", h=2)

                def fZ(st, gs):
                    co, NQ, kb, qc0 = st['co'], st['NQ'], st['kb'], st['qc0']
                    zs = (gs % 2) * 2
                    z3 = bank(zs, 2).rearrange("p (h c) -> p h c", h=2)
                    for h in range(2):
                        r0 = 64 * h
                        S.add('pe', lambda e, h=h, r0=r0, z3=z3: e.matmul(
                            z3[:, h, co:NQ], lhsT=kT[r0:r0 + 64, kb * 128:(kb + 1) * 128],
                            rhs=qT[r0:r0 + 64, qc0 + co:qc0 + NQ], start=True, stop=True),
                            reads=[('kT', kb)] + [('qT', j) for j in range(st['qb0'] + co // 128, st['qb1'] + 1)],
                            writes=bkeys(zs + h))

                def fE(st, gs):
                    co, NQ, kb = st['co'], st['NQ'], st['kb']
                    zs = (gs % 2) * 2
                    i = gs % 4
                    z3 = bank(zs, 2).rearrange("p (h c) -> p h c", h=2)
                    S.add('act', lambda e: e.activation(
                        out=eb[i][:, :, co:NQ], in_=z3[:, :, co:NQ], func=AF.Exp),
                        reads=bkeys(zs, 2), writes=[('eb', i)] + bkeys(zs, 2))
                    if kb >= st['qb0']:
                        S.add('dve', lambda e: e.tensor_tensor(
                            out=eb[i][:, :, co:co + 128], in0=eb[i][:, :, co:co + 128],
                            in1=mdiag_b[:].unsqueeze(1).broadcast_to([128, 2, 128]), op=ALU.mult),
                            reads=[('eb', i), 'mdiag_b'], writes=[('eb', i)])
                    if kb == 0:
                        S.add('dve', lambda e: e.tensor_scalar(
                            out=eb[i][:, :, co:NQ], in0=eb[i][:, :, co:NQ], scalar1=padm[:, 0:1],
                            scalar2=None, op0=ALU.mult),
                            reads=[('eb', i), 'padm'], writes=[('eb', i)])

                def fL(st, gs):
                    co, NQ = st['co'], st['NQ']
                    i = gs % 3
                    ie = gs % 4
                    S.add('act', lambda e: e.activation(
                        out=spb[i][:, :, co:NQ], in_=eb[ie][:, :, co:NQ], func=AF.Ln, bias=1.0),
                        reads=[('eb', ie)], writes=[('spb', i)])

                def fT(st, gs):
                    co, NQ = st['co'], st['NQ']
                    i = gs % 3
                    for h in range(2):
                        S.add('pe', lambda e, h=h: e.matmul(
                            c3[:, h, co:NQ], lhsT=tri_b[:], rhs=spb[i][:, h, co:NQ],
                            start=st['first'], stop=False, skip_group_check=True),
                            reads=[('spb', i), 'tri_b'], writes=bkeys(cb + h))

                def fG(st, gs):
                    co, NQ = st['co'], st['NQ']
                    i = gs % 2
                    S.add('act', lambda e: e.activation(
                        out=gb[i][:, :, co:NQ], in_=c3[:, :, co:NQ], func=AF.Exp, scale=-1.0),
                        reads=bkeys(cb, 2), writes=[('gb', i)] + bkeys(cb, 2))

                def fT2(st, gs):
                    co, NQ = st['co'], st['NQ']
                    i = gs % 3
                    if st['last']:
                        return
                    for h in range(2):
                        S.add('pe', lambda e, h=h: e.matmul(
                            c3[:, h, co:NQ], lhsT=tric_b[:], rhs=spb[i][:, h, co:NQ],
                            start=False, stop=False, skip_group_check=True),
                            reads=[('spb', i), 'tric_b'], writes=bkeys(cb + h))

                def fW(st, gs):
                    co, NQ = st['co'], st['NQ']
                    i = gs % 2
                    S.add('dve', lambda e: e.tensor_tensor(
                        out=wb[i][:, :, co:NQ], in0=eb[gs % 4][:, :, co:NQ], in1=gb[i][:, :, co:NQ], op=ALU.mult),
                        reads=[('eb', gs % 4), ('gb', i)], writes=[('wb', i)])

                def fV(st, gs, hp=hp):
                    co, NQ, kb, ob, qc0 = st['co'], st['NQ'], st['kb'], st['ob'], st['qc0']
                    i = gs % 2
                    for h in range(2):
                        r0 = 64 * h
                        S.add('pe', lambda e, h=h, r0=r0: e.matmul(
                            bank(ob)[r0:r0 + 64, co:NQ], lhsT=vv[:, kb, r0:r0 + 64], rhs=wb[i][:, h, co:NQ],
                            start=st['first'], stop=False, skip_group_check=True),
                            reads=[('wb', i), ('vv', kb)], writes=bkeys(ob))
                    if st['last']:
                        S.add('dve', lambda e: e.tensor_copy(
                            out=ysbT[:, hp, qc0:qc0 + NQ], in_=bank(ob)[:, 0:NQ]),
                            reads=bkeys(ob), writes=[('ysbT', hp, st['qg'])] + bkeys(ob))

                n = len(steps)
                for i in range(-2, n + 2):
                    if 0 <= i - 1 < n:
                        fT(steps[i - 1], gstep + i - 1)
                    if 0 <= i + 2 < n:
                        fZ(steps[i + 2], gstep + i + 2)
                    if 0 <= i - 2 < n:
                        fV(steps[i - 2], gstep + i - 2)
                    if 0 <= i < n:
                        st_ = steps[i]
                        if 1 <= st_['qg'] <= 7 and 1 <= st_['lidx'] <= 4:
                            us = block_units(st_['qg'] + 1, 6 + ((st_['qg'] + 1) % 2))
                            us[st_['lidx'] - 1]()
                    if 0 <= i + 1 < n:
                        fE(steps[i + 1], gstep + i + 1)
                    if 0 <= i < n:
                        fL(steps[i], gstep + i)
                    if 0 <= i - 1 < n:
                        fG(steps[i - 1], gstep + i - 1)
                        fT2(steps[i - 1], gstep + i - 1)
                        fW(steps[i - 1], gstep + i - 1)
                gstep += n

            if debug:
                S.add('sp', lambda e: e.dma_start(out=dbg['ysbT'], in_=ysbT[:].rearrange("p k c -> p (k c)")),
                      reads=[('ysbT', hp, qg) for hp in range(8) for qg in range(9)], dma=True)

        def barrier():
            tails = {e: [op for op in S.ops[e] if not op.is_dma][-1:] for e in ENGS}
            dts = [op for q in ENGS for op in S.dma_ops[q][-NDSEM:]]
            for e in ENGS:
                ex = [o for p in ENGS if p != e for o in tails[p]] + dts
                S.add(e, lambda eng: eng.nop(), extra=ex)

        if stage >= 3:
            barrier()
            es2.close()
            NG = 384
            ident_f = sb("ident_f", [128, 128], F32)
            UTf = sb("UTf", [128, 128], F32)
            SGf = sb("SGf", [128, 128], F32)
            ONESf = sb("ONESf", [128, 128], F32)
            ones_b = sb("ones_b", [128, 128], BF16)
            cmask(ident_f, -1, 1, ALU.is_equal)
            cmask(UTf, 1, -1, ALU.is_ge)
            cmask(SGf, -1, 1, ALU.is_gt)
            S.add('pool', lambda e: e.memset(ONESf[:], 1.0), writes=['ONESf'])
            S.add('pool', lambda e: e.memset(ones_b[:], 1.0), writes=['ones_b'])
            ffnp = sb("ffnp", [128, 128], F32)
            S.add('sp', lambda e: e.dma_start(out=ffnp[:], in_=ffnp_d[:, :]), writes=['ffnp'], dma=True)
            smallbc = sb("smallbc", [128, 64], F32)
            S.add('sp', lambda e: e.dma_start(out=smallbc[:], in_=small_d[0:1, :].partition_broadcast(128)),
                  writes=['smallbc'], dma=True)
            Abc = sb("Abc", [128, 16], F32)
            S.add('act', lambda e: e.activation(out=Abc[:], in_=smallbc[:, 16:32], func=AF.Exp),
                  reads=['smallbc'], writes=['Abc'])
            S.add('dve', lambda e: e.tensor_scalar(out=Abc[:], in0=Abc[:], scalar1=-1.0, scalar2=None, op0=ALU.mult),
                  reads=['Abc'], writes=['Abc'])
            wdt_b = sb("wdt_b", [128, 8, 16], BF16)
            S.add('pool', lambda e: e.dma_start(out=wdt_b[:].rearrange("p k c -> p (k c)"), in_=wdt_d[:, :]),
                  writes=['wdt_b'], dma=True)
            gpost_bc = sb("gpost_bc", [128, D], F32)
            gfpost_bc = sb("gfpost_bc", [128, D], F32)
            S.add('sp', lambda e: e.dma_start(out=gpost_bc[:], in_=vecs_d[1:2, :].partition_broadcast(128)),
                  writes=['gpost_bc'], dma=True)
            S.add('sp', lambda e: e.dma_start(out=gfpost_bc[:], in_=vecs_d[3:4, :].partition_broadcast(128)),
                  writes=['gfpost_bc'], dma=True)

            NWB = 4
            wbuf = [sb("wbuf%d" % i, [128, 1024], BF16) for i in range(NWB)]
            wcnt = [0]

            def wload(blk):
                i = wcnt[0] % NWB
                wcnt[0] += 1
                S.add('sp', lambda e: e.dma_start(out=wbuf[i][:], in_=wscr[blk * 128:(blk + 1) * 128, :]),
                      reads=[('wscr', blk)], writes=[('wbuf', i)], dma=True)
                return wbuf[i], ('wbuf', i)

            state = sb("state", [128, 1024], F32)
            state_b = sb("state_b", [128, 1024], BF16)
            S.add('pool', lambda e: e.memset(state[:], 0.0), writes=['state'])
            S.add('pool', lambda e: e.memset(state_b[:], 0.0), writes=['state_b'])
            halo = sb("halo", [128, 12, 3], F32)
            ghalo = sb("ghalo", [128, NFC, 2], F32)
            S.add('pool', lambda e: e.memset(halo[:], 0.0), writes=['halo'])
            S.add('pool', lambda e: e.memset(ghalo[:], 0.0), writes=['ghalo'])

            hbuf = sb("hbuf", [128, 3, D], F32)
            actT = sb("actT", [128, 8, NG], BF16)
            R1 = sb("R1", [128, 6 * D], F32)
            zs = R1[:, 0:3 * D].rearrange("p (j c) -> p j c", j=3)
            Xtok = R1[:, 3 * D:6 * D].rearrange("p (j c) -> p j c", j=3)
            aT = R1[:, 0:NFC * NG // 2].bitcast(BF16).rearrange("p (f c) -> p f c", f=NFC)
            BT = sb("BT", [128, 2, NG], BF16)
            CT = sb("CT", [128, 2, NG], BF16)
            Btok = sb("Btok", [128, 3, 256], BF16)
            cbuf = [sb("cbuf%d" % i, [128, NG + 3], F32) for i in range(4)]
            cacc = [sb("cacc%d" % i, [128, NG], F32) for i in range(4)]
            sfm = [sb("sfm%d" % i, [128, NG], F32) for i in range(4)]
            t1 = sb("t1", [128, D], F32)
            RH = sb("RH", [128, 16, 128], F32)
            Xdt = sb("Xdt", [128, D], BF16)
            Xdec = sb("Xdec", [128, D], BF16)
            Eb = [sb("Eb%d" % i, [128, 512], F32) for i in range(2)]
            MTb = [sb("MTb%d" % i, [128, 4, 128], BF16) for i in range(2)]
            CBm = sb("CBm", [128, 2, 128], F32)
            tokbfs = [sb("tokbf%d" % i, [128, D], BF16) for i in range(3)]
            ygnT = sb("ygnT", [128, 8, NG], BF16)
            ysbn = sb("ysbn", [128, 8, NG], BF16)
            r_bc = sb("r_bc", [128, NG], F32)
            lnr = r_bc
            sqb = [sb("sqb%d" % i, [128, NG], BF16) for i in range(2)]
            dtt = sb("dtt", [128, 3, 16], F32)
            att = sb("att", [128, 3, 16], F32)
            dte = sb("dte", [128, 3, 16], F32)
            ex48 = sb("ex48", [128, 48], F32)
            dtd = sb("dtd", [128, 16], F32)
            st3 = sb("st3", [128, 64], F32)
            gcb = cbuf
            RHf = RH[:].rearrange("p h c -> p (h c)")
            tmps = [(t1[:], ['t1']), (RHf[:, 0:D], [('tmpf', 1)]), (RHf[:, D:2 * D], [('tmpf', 2)])]
            glb = sfm

            def bc3(ap2, n):
                return ap2.unsqueeze(2).broadcast_to([128, 16, n])

            def v3(ap2):
                return ap2.rearrange("p (h d) -> p h d", h=16)

            def run_rr(gens):
                gens = list(gens)
                while gens:
                    nxt = []
                    for gq in gens:
                        try:
                            next(gq)
                            nxt.append(gq)
                        except StopIteration:
                            pass
                    gens = nxt

            def rinfo(slot):
                return st3[:, 4 * slot + 2:4 * slot + 3], (('st3', slot), 2)

            def g_rstd3(src, srcreads, slot):
                col = 4 * slot
                key = ('st3', slot)
                S.add('act', lambda e: e.activation(out=junk[:], in_=src, func=AF.Square,
                                                    accum_out=st3[:, col:col + 1]),
                      reads=srcreads, writes=[(key, 0), 'junk'])
                yield
                S.add('dve', lambda e: e.tensor_scalar(out=st3[:, col + 1:col + 2], in0=st3[:, col:col + 1],
                                                       scalar1=1.0 / D, scalar2=EPS, op0=ALU.mult, op1=ALU.add),
                      reads=[(key, 0)], writes=[(key, 1)])
                yield
                S.add('pool', lambda e: e.tensor_tensor(out=st3[:, col + 2:col + 3], in0=st3[:, col + 1:col + 2],
                                                        in1=mhalf[:], op=ALU.pow),
                      reads=[(key, 1), 'mhalf'], writes=[(key, 2)])
                yield

            def rstd3(src, srcreads, slot):
                for _ in g_rstd3(src, srcreads, slot):
                    pass
                return rinfo(slot)

            def g_norm_transpose(j, src, srckeys, slot, dstT, dkey, pb=7):
                rcol, rkey = rinfo(slot)
                tb = tokbfs[j % 3]
                tk = ('tokbf', j % 3)
                S.add('dve', lambda e: e.tensor_scalar(out=tb[:], in0=src, scalar1=rcol, scalar2=None,
                                                       op0=ALU.mult),
                      reads=srckeys + [rkey], writes=[tk])
                yield
                pbv = bank(pb).bitcast(BF16)
                for k in range(8):
                    S.add('pe', lambda e, k=k: e.transpose(
                        out=pbv[:, k * 128:(k + 1) * 128], in_=tb[:, k * 128:(k + 1) * 128], identity=ident_b[:]),
                        reads=[tk, 'ident_b'], writes=bkeys(pb))
                yield
                S.add('act', lambda e: e.activation(
                    out=dstT[:, :, j * 128:(j + 1) * 128], in_=pbv.rearrange("p (k c) -> p k c", k=8), func=AF.Copy),
                    reads=bkeys(pb), writes=[(dkey, j)] + bkeys(pb))
                yield

            def norm_transpose(j, src, srckeys, slot, dstT, dkey, pb=7):
                for _ in g_norm_transpose(j, src, srckeys, slot, dstT, dkey, pb):
                    pass

            def apply_gain(dstT, dkey, gcol):
                for k in range(8):
                    S.add('dve', lambda e, k=k: e.tensor_scalar(
                        out=dstT[:, k, :], in0=dstT[:, k, :], scalar1=pp[:, gcol + k:gcol + k + 1], scalar2=None,
                        op0=ALU.mult),
                        reads=[(dkey, jj) for jj in range(3)] + ['pp'], writes=[(dkey, jj) for jj in range(3)])

            def do_group(g):
                t0 = 3 * g
                gc0 = t0 * 128
                hk = [('hbuf', j) for j in range(3)]
                def chain_a(gg_, j, xbuf, xkeys, pb):
                    load_x_tile(3 * gg_ + j, xbuf, xkeys)
                    yield
                    yield from g_rstd3(xbuf, xkeys, j)
                    yield from g_norm_transpose(j, xbuf, xkeys, j, actT, 'actT', pb=pb)

                if g == 0:
                    run_rr([chain_a(0, j, hbuf[:, j, :], [('hbuf', j)], 7 - j) for j in range(3)])
                else:
                    for j in range(3):
                        S.add('act', lambda e, j=j: e.activation(out=hbuf[:, j, :], in_=tmps[j][0], func=AF.Copy),
                              reads=tmps[j][1], writes=[('hbuf', j)])
                if g == 0:
                    apply_gain(actT, 'actT', 76)
                ak = [('actT', j) for j in range(3)]
                if g == 0:
                    dbgdump('actT', actT[:].rearrange("p k c -> p (k c)"), [128, 8 * NG], BF16, ak)
                for cbk in range(8):
                    w, wkey = wload(BLK_Z + cbk)
                    w3 = w[:].rearrange("p (k c) -> p k c", k=8)
                    for j in range(3):
                        b = 2 * j + cbk // 4
                        for k in range(8):
                            S.add('pe', lambda e, b=b, j=j, k=k, w3=w3, cbk=cbk: e.matmul(
                                bank(b)[:, (cbk % 4) * 128:(cbk % 4 + 1) * 128], lhsT=actT[:, k, j * 128:(j + 1) * 128],
                                rhs=w3[:, k, :], start=(k == 0), stop=(k == 7)),
                                reads=[wkey, ('actT', j)], writes=bkeys(b))
                for j in range(3):
                    S.add('act', lambda e, j=j: e.activation(out=zs[:, j, :], in_=bank(2 * j, 2), func=AF.Silu),
                          reads=bkeys(2 * j, 2), writes=[('zs', j)] + bkeys(2 * j, 2))
                for j in range(3):
                    for k in range(8):
                        S.add('pe', lambda e, j=j, k=k: e.matmul(
                            bank(6)[:, j * 16:(j + 1) * 16], lhsT=actT[:, k, j * 128:(j + 1) * 128],
                            rhs=wdt_b[:, k, :], start=(k == 0), stop=(k == 7)),
                            reads=['wdt_b', ('actT', j)], writes=bkeys(6))
                S.add('dve', lambda e: e.tensor_tensor(
                    out=dte[:], in0=bank(6)[:, 0:48].rearrange("p (j h) -> p j h", j=3),
                    in1=smallbc[:, 0:16].unsqueeze(1).broadcast_to([128, 3, 16]), op=ALU.add),
                    reads=bkeys(6) + ['smallbc'], writes=['dte'] + bkeys(6))
                S.add('act', lambda e: e.activation(out=dte[:], in_=dte[:], func=AF.Exp), reads=['dte'], writes=['dte'])
                S.add('act', lambda e: e.activation(out=dtt[:], in_=dte[:], func=AF.Ln, bias=1.0),
                      reads=['dte'], writes=['dtt'])
                if g == 0:
                    S.add('dve', lambda e: e.tensor_scalar(out=dtt[:, 0, :], in0=dtt[:, 0, :], scalar1=padm[:, 0:1],
                                                           scalar2=None, op0=ALU.mult),
                          reads=['dtt', 'padm'], writes=['dtt'])
                S.add('dve', lambda e: e.tensor_tensor(
                    out=att[:], in0=dtt[:], in1=Abc[:].unsqueeze(1).broadcast_to([128, 3, 16]), op=ALU.mult),
                    reads=['dtt', 'Abc'], writes=['att'])
                pend = []
                postq = []

                def emit_tr(jb, i):
                    for j in range(3):
                        S.add('pe', lambda e, i=i, j=j, jb=jb: e.transpose(
                            out=bank(2 * j + jb // 4)[:, (jb % 4) * 128:(jb % 4 + 1) * 128],
                            in_=sfm[i][:, j * 128:(j + 1) * 128], identity=ident_f[:]),
                            reads=[('sfm', i), 'ident_f'], writes=bkeys(2 * j + jb // 4))

                for jb in range(12):
                    w, wkey = wload(BLK_XBC + jb)
                    w3 = w[:].rearrange("p (k c) -> p k c", k=8)
                    b = 6 + (jb % 2)
                    i = jb % 4
                    for k in range(8):
                        S.add('pe', lambda e, b=b, k=k, w3=w3: e.matmul(
                            bank(b)[:, 0:NG], lhsT=w3[:, k, :], rhs=actT[:, k, :], start=(k == 0), stop=(k == 7)),
                            reads=[wkey] + ak, writes=bkeys(b))
                    if len(pend) >= 2:
                        emit_tr(*pend.pop(0))
                    S.add('act', lambda e, b=b, i=i: e.activation(out=cbuf[i][:, 3:3 + NG], in_=bank(b)[:, 0:NG],
                                                                  func=AF.Copy),
                          reads=bkeys(b), writes=[('cbuf', i)] + bkeys(b))
                    S.add('pool', lambda e, i=i, jb=jb: e.tensor_copy(out=cbuf[i][:, 0:3], in_=halo[:, jb, :]),
                          reads=['halo'], writes=[('cbuf', i)])
                    S.add('pool', lambda e, i=i, jb=jb: e.tensor_copy(out=halo[:, jb, :], in_=cbuf[i][:, NG:NG + 3]),
                          reads=[('cbuf', i)], writes=['halo'])
                    for kk in range(4):
                        wc = pp[:, 16 + jb * 4 + kk:16 + jb * 4 + kk + 1]
                        if kk == 0:
                            S.add('dve', lambda e, i=i, wc=wc: e.tensor_scalar(
                                out=cacc[i][:], in0=cbuf[i][:, 0:NG], scalar1=wc, scalar2=None, op0=ALU.mult),
                                reads=[('cbuf', i), 'pp'], writes=[('cacc', i)])
                        else:
                            S.add('dve', lambda e, i=i, wc=wc, kk=kk: e.scalar_tensor_tensor(
                                out=cacc[i][:], in0=cbuf[i][:, kk:kk + NG], scalar=wc, in1=cacc[i][:],
                                op0=ALU.mult, op1=ALU.add),
                                reads=[('cbuf', i), 'pp', ('cacc', i)], writes=[('cacc', i)])
                    def post_conv(jb=jb, i=i):
                        bias = pp[:, 64 + jb:65 + jb]
                        if jb < 8:
                            S.add('act', lambda e, i=i, bias=bias: e.activation(out=sfm[i][:], in_=cacc[i][:], func=AF.Silu,
                                                                                bias=bias),
                                  reads=[('cacc', i), 'pp'], writes=[('sfm', i)])
                            pend.append((jb, i))
                        else:
                            dst = BT if jb < 10 else CT
                            dk = 'BT' if jb < 10 else 'CT'
                            gg = jb % 2
                            S.add('act', lambda e, i=i, bias=bias, dst=dst, gg=gg: e.activation(
                                out=dst[:, gg, :], in_=cacc[i][:], func=AF.Silu, bias=bias),
                                reads=[('cacc', i), 'pp'], writes=[(dk, gg)])
                    if postq:
                        postq.pop(0)()
                    postq.append(post_conv)
                while postq:
                    postq.pop(0)()
                while pend:
                    emit_tr(*pend.pop(0))
                for j in range(3):
                    S.add('act', lambda e, j=j: e.activation(out=Xtok[:, j, :], in_=bank(2 * j, 2), func=AF.Copy),
                          reads=bkeys(2 * j, 2), writes=[('Xtok', j)] + bkeys(2 * j, 2))
                pbv = bank(7).bitcast(BF16)
                for gg in range(2):
                    for j in range(3):
                        S.add('pe', lambda e, j=j, gg=gg, pbv=pbv: e.transpose(
                            out=pbv[:, j * 256 + gg * 128:j * 256 + (gg + 1) * 128],
                            in_=BT[:, gg, j * 128:(j + 1) * 128], identity=ident_b[:]),
                            reads=[('BT', gg), 'ident_b'], writes=bkeys(7))
                S.add('dve', lambda e, pbv=pbv: e.tensor_copy(
                    out=Btok[:], in_=pbv[:, 0:768].rearrange("p (j c) -> p j c", j=3)),
                    reads=bkeys(7), writes=['Btok'] + bkeys(7))
                if g == 0:
                    dbgdump('zs', R1[:, 0:3 * D], [128, 3 * D], F32, [('zs', jj) for jj in range(3)])
                    dbgdump('Xtok', R1[:, 3 * D:6 * D], [128, 3 * D], F32, [('Xtok', jj) for jj in range(3)])
                    dbgdump('dtt', dtt[:].rearrange("p j h -> p (j h)"), [128, 48], F32, ['dtt'])
                    dbgdump('att', att[:].rearrange("p j h -> p (j h)"), [128, 48], F32, ['att'])
                    dbgdump('BT', BT[:].rearrange("p g c -> p (g c)"), [128, 2 * NG], BF16, [('BT', 0), ('BT', 1)])
                    dbgdump('CT', CT[:].rearrange("p g c -> p (g c)"), [128, 2 * NG], BF16, [('CT', 0), ('CT', 1)])
                    dbgdump('Btok', Btok[:].rearrange("p j c -> p (j c)"), [128, 768], BF16, ['Btok'])
                for j in range(3):
                    tc0 = j * 128
                    a_j = att[:, j, :]
                    for ci, lt in enumerate((UTf, SGf, ONESf)):
                        S.add('pe', lambda e, ci=ci, lt=lt, a_j=a_j: e.matmul(
                            bank(0)[:, ci * 16:(ci + 1) * 16], lhsT=lt[:], rhs=a_j, start=True, stop=True),
                            reads=['att', 'UTf', 'SGf', 'ONESf'], writes=bkeys(0))
                    S.add('act', lambda e: e.activation(out=ex48[:], in_=bank(0)[:, 0:48], func=AF.Exp),
                          reads=bkeys(0), writes=['ex48'] + bkeys(0))
                    S.add('dve', lambda e, a_j=a_j: e.tensor_tensor(
                        out=RH[:], in0=UTf[:].unsqueeze(1).broadcast_to([128, 16, 128]), in1=bc3(a_j, 128),
                        op=ALU.mult),
                        reads=['att', 'UTf'], writes=['RH', ('tmpf', 1), ('tmpf', 2)])
                    for gg in range(2):
                        S.add('pe', lambda e, gg=gg, tc0=tc0: e.matmul(
                            bank(0)[:, 128 + gg * 128:256 + gg * 128], lhsT=BT[:, gg, tc0:tc0 + 128],
                            rhs=CT[:, gg, tc0:tc0 + 128], start=True, stop=True),
                            reads=[('BT', gg), ('CT', gg)], writes=bkeys(0))
                    S.add('dve', lambda e: e.tensor_tensor(
                        out=CBm[:], in0=bank(0)[:, 128:384].rearrange("p (g c) -> p g c", g=2),
                        in1=UTf[:].unsqueeze(1).broadcast_to([128, 2, 128]), op=ALU.mult),
                        reads=bkeys(0) + ['UTf'], writes=['CBm'] + bkeys(0))
                    for gg in range(2):
                        S.add('pe', lambda e, gg=gg, tc0=tc0: e.matmul(
                            bank(5 + gg), lhsT=CT[:, gg, tc0:tc0 + 128], rhs=state_b[:, gg * 512:(gg + 1) * 512],
                            start=True, stop=True),
                            reads=[('CT', gg), 'state_b'], writes=bkeys(5 + gg))
                    S.add('dve', lambda e, j=j: e.tensor_tensor(out=dtd[:], in0=dtt[:, j, :], in1=ex48[:, 16:32],
                                                                op=ALU.mult),
                          reads=['dtt', 'ex48'], writes=['dtd'])
                    S.add('dve', lambda e, j=j: e.tensor_tensor(out=v3(Xdt[:]), in0=v3(Xtok[:, j, :]),
                                                                in1=bc3(dtt[:, j, :], 64), op=ALU.mult),
                          reads=[('Xtok', j), 'dtt'], writes=['Xdt'])
                    S.add('dve', lambda e, j=j: e.tensor_tensor(out=v3(Xdec[:]), in0=v3(Xtok[:, j, :]),
                                                                in1=bc3(dtd[:], 64), op=ALU.mult),
                          reads=[('Xtok', j), 'dtd'], writes=['Xdec'])
                    def fD(hq):
                        i = hq % 2
                        S.add('pe', lambda e, i=i, hq=hq: e.matmul(
                            bank(1 + i), lhsT=SGf[:], rhs=RH[:, 4 * hq:4 * hq + 4, :].rearrange("p h c -> p (h c)"),
                            start=True, stop=True),
                            reads=['RH', ('tmpf', 1), ('tmpf', 2), 'SGf'], writes=bkeys(1 + i))
                        S.add('act', lambda e, i=i: e.activation(out=Eb[i][:], in_=bank(1 + i), func=AF.Exp),
                              reads=bkeys(1 + i), writes=[('Eb', i)] + bkeys(1 + i))

                    def fY(hq):
                        i = hq % 2
                        gg = hq // 2
                        S.add('dve', lambda e, i=i, gg=gg: e.tensor_tensor(
                            out=MTb[i][:], in0=Eb[i][:].rearrange("p (h c) -> p h c", h=4),
                            in1=CBm[:, gg:gg + 1, :].broadcast_to([128, 4, 128]), op=ALU.mult),
                            reads=[('Eb', i), 'CBm'], writes=[('MTb', i)])
                        for hh in range(4):
                            h = 4 * hq + hh
                            S.add('pe', lambda e, i=i, hh=hh, h=h: e.matmul(
                                bank(3 + h // 8)[:, (h % 8) * 64:(h % 8 + 1) * 64], lhsT=MTb[i][:, hh, :],
                                rhs=Xdt[:, h * 64:(h + 1) * 64], start=True, stop=True),
                                reads=[('MTb', i), 'Xdt'], writes=bkeys(3 + h // 8))

                    fD(0)
                    fD(1)
                    fY(0)
                    fD(2)
                    fY(1)
                    fD(3)
                    fY(2)
                    fY(3)
                    S.add('dve', lambda e: e.tensor_tensor(out=v3(t1[:]), in0=v3(bank(5, 2)), in1=bc3(ex48[:, 0:16], 64),
                                                           op=ALU.mult),
                          reads=bkeys(5, 2) + ['ex48'], writes=['t1'] + bkeys(5, 2))
                    S.add('dve', lambda e: e.tensor_tensor(out=t1[:], in0=bank(3, 2), in1=t1[:], op=ALU.add),
                          reads=bkeys(3, 2) + ['t1'], writes=['t1'] + bkeys(3, 2))
                    for gg in range(2):
                        S.add('pe', lambda e, gg=gg, j=j: e.matmul(
                            bank(5 + gg), lhsT=Btok[:, j, gg * 128:(gg + 1) * 128], rhs=Xdec[:, gg * 512:(gg + 1) * 512],
                            start=True, stop=True),
                            reads=['Btok', 'Xdec'], writes=bkeys(5 + gg))
                    S.add('dve', lambda e, j=j: e.tensor_tensor(out=v3(Xtok[:, j, :]), in0=v3(Xtok[:, j, :]),
                                                                in1=bc3(smallbc[:, 32:48], 64), op=ALU.mult),
                          reads=[('Xtok', j), 'smallbc'], writes=[('Xtok', j)])
                    S.add('dve', lambda e, j=j: e.tensor_tensor(out=t1[:], in0=t1[:], in1=Xtok[:, j, :], op=ALU.add),
                          reads=[('Xtok', j), 't1'], writes=['t1'])
                    S.add('dve', lambda e, j=j: e.tensor_tensor(out=t1[:], in0=t1[:], in1=zs[:, j, :], op=ALU.mult),
                          reads=[('zs', j), 't1'], writes=['t1'])
                    S.add('dve', lambda e: e.tensor_tensor(out=v3(state[:]), in0=v3(state[:]), in1=bc3(ex48[:, 32:48], 64),
                                                           op=ALU.mult),
                          reads=['state', 'ex48'], writes=['state'])
                    S.add('dve', lambda e: e.tensor_tensor(out=state[:], in0=bank(5, 2), in1=state[:], op=ALU.add),
                          reads=bkeys(5, 2) + ['state'], writes=['state'] + bkeys(5, 2))
                    S.add('act', lambda e: e.activation(out=state_b[:], in_=state[:], func=AF.Copy),
                          reads=['state'], writes=['state_b'])
                    if g == 0:
                        dbgdump('yg%d' % j, t1[:], [128, D], F32, ['t1'])
                        dbgdump('ex48_%d' % j, ex48[:], [128, 48], F32, ['ex48'])
                    rstd3(t1[:], ['t1'], 3 + j)
                    norm_transpose(j, t1[:], ['t1'], 3 + j, ygnT, 'ygnT')
                apply_gain(ygnT, 'ygnT', 0)
                for k in range(8):
                    i = k % 2
                    S.add('act', lambda e, i=i, k=k: e.activation(out=sqb[i][:], in_=ysbT[:, k, gc0:gc0 + NG],
                                                                  func=AF.Square),
                          reads=[('ysbT', k, q) for q in range(9)], writes=[('sqb', i)])
                    S.add('pe', lambda e, i=i, k=k: e.matmul(bank(6)[:, 0:NG], lhsT=ones_b[:], rhs=sqb[i][:],
                                                             start=(k == 0), stop=(k == 7)),
                          reads=[('sqb', i), 'ones_b'], writes=bkeys(6))
                S.add('act', lambda e: e.activation(out=lnr[:], in_=bank(6)[:, 0:NG], func=AF.Ln, scale=1.0 / D, bias=EPS),
                      reads=bkeys(6), writes=['lnr', 'r_bc'] + bkeys(6))
                S.add('act', lambda e: e.activation(out=r_bc[:], in_=lnr[:], func=AF.Exp, scale=-0.5),
                      reads=['lnr', 'r_bc'], writes=['r_bc', 'lnr'])
                for k in range(8):
                    S.add('dve', lambda e, k=k: e.scalar_tensor_tensor(
                        out=ysbn[:, k, :], in0=ysbT[:, k, gc0:gc0 + NG], scalar=pp[:, 8 + k:9 + k], in1=r_bc[:],
                        op0=ALU.mult, op1=ALU.mult),
                        reads=[('ysbT', k, q) for q in range(9)] + ['pp', 'r_bc'], writes=[('ysbn', k)])
                for kc in range(16):
                    w, wkey = wload(BLK_OUT + kc)
                    src = ygnT if kc < 8 else ysbn
                    sk = [('ygnT', jj) for jj in range(3)] if kc < 8 else [('ysbn', kc - 8)]
                    for j in range(3):
                        for hf in range(2):
                            S.add('pe', lambda e, j=j, hf=hf, w=w, src=src, kc=kc: e.matmul(
                                bank(2 * j + hf), lhsT=src[:, kc % 8, j * 128:(j + 1) * 128],
                                rhs=w[:, hf * 512:(hf + 1) * 512], start=(kc == 0), stop=(kc == 15)),
                                reads=[wkey] + sk, writes=bkeys(2 * j + hf))
                def chain_post(j, gbc, gkey, slot0, final, tl=None):
                    tmp, tkk = (tl or tmps)[j]
                    S.add('dve', lambda e: e.tensor_tensor(
                        out=tmp, in0=bank(2 * j, 2), in1=gbc[:], op=ALU.mult),
                        reads=bkeys(2 * j, 2) + [gkey], writes=tkk + bkeys(2 * j, 2))
                    yield
                    yield from g_rstd3(bank(2 * j, 2), bkeys(2 * j, 2), slot0 + j)
                    rc, rk = rinfo(slot0 + j)
                    t = t0 + j
                    if not final:
                        S.add('dve', lambda e: e.scalar_tensor_tensor(
                            out=hbuf[:, j, :], in0=tmp, scalar=rc, in1=hbuf[:, j, :], op0=ALU.mult, op1=ALU.add),
                            reads=tkk + [rk, ('hbuf', j)], writes=[('hbuf', j)])
                        yield
                    else:
                        S.add('dve', lambda e: e.scalar_tensor_tensor(
                            out=tmp, in0=tmp, scalar=rc, in1=hbuf[:, j, :], op0=ALU.mult, op1=ALU.add),
                            reads=tkk + [rk, ('hbuf', j)], writes=tkk)
                        yield
                        if t >= 1:
                            S.add('sp', lambda e: e.dma_start(out=out_d[(t - 1) * 128:t * 128, :], in_=tmp),
                                  reads=tkk, dma=True)

                run_rr([chain_post(j, gpost_bc, 'gpost_bc', 6, False) for j in range(3)])
                if g == 0:
                    dbgdump('ygnT', ygnT[:].rearrange("p k c -> p (k c)"), [128, 8 * NG], BF16, [('ygnT', jj) for jj in range(3)])
                    dbgdump('ysbn', ysbn[:].rearrange("p k c -> p (k c)"), [128, 8 * NG], BF16, [('ysbn', k) for k in range(8)])
                    dbgdump('r_bc', r_bc[:], [128, NG], F32, ['r_bc'])
                    dbgdump('h1', hbuf[:].rearrange("p j c -> p (j c)"), [128, 3 * D], F32, hk)
                    dbgdump('state', state[:], [128, D], F32, ['state'])
                def chain_e(j):
                    yield from g_rstd3(hbuf[:, j, :], [('hbuf', j)], 9 + j)
                    yield from g_norm_transpose(j, hbuf[:, j, :], [('hbuf', j)], 9 + j, actT, 'actT', pb=7 - j)

                run_rr([chain_e(j) for j in range(3)])
                apply_gain(actT, 'actT', 84)
                fpost = []
                for fc in range(NFC):
                    wg, wgk = wload(BLK_GATE + fc)
                    wu, wuk = wload(BLK_UP + fc)
                    wg3 = wg[:].rearrange("p (k c) -> p k c", k=8)
                    wu3 = wu[:].rearrange("p (k c) -> p k c", k=8)
                    i = fc % 4
                    bg = fc % 3
                    bu = 3 + fc % 3
                    for k in range(8):
                        S.add('pe', lambda e, bg=bg, k=k, wg3=wg3: e.matmul(
                            bank(bg)[:, 0:NG], lhsT=wg3[:, k, :], rhs=actT[:, k, :], start=(k == 0), stop=(k == 7)),
                            reads=[wgk] + ak, writes=bkeys(bg))
                    for k in range(8):
                        S.add('pe', lambda e, bu=bu, k=k, wu3=wu3: e.matmul(
                            bank(bu)[:, 0:NG], lhsT=wu3[:, k, :], rhs=actT[:, k, :], start=(k == 0), stop=(k == 7)),
                            reads=[wuk] + ak, writes=bkeys(bu))
                    S.add('act', lambda e, bg=bg, i=i: e.activation(out=gcb[i][:, 2:2 + NG], in_=bank(bg)[:, 0:NG],
                                                                    func=AF.Copy),
                          reads=bkeys(bg), writes=[('cbuf', i)] + bkeys(bg))
                    S.add('pool', lambda e, i=i, fc=fc: e.tensor_copy(out=gcb[i][:, 0:2], in_=ghalo[:, fc, :]),
                          reads=['ghalo'], writes=[('cbuf', i)])
                    S.add('pool', lambda e, i=i, fc=fc: e.tensor_copy(out=ghalo[:, fc, :], in_=gcb[i][:, NG:NG + 2]),
                          reads=[('cbuf', i)], writes=['ghalo'])
                    for kk in range(3):
                        wc = ffnp[:, fc * 3 + kk:fc * 3 + kk + 1]
                        if kk == 0:
                            S.add('dve', lambda e, i=i, wc=wc: e.tensor_scalar(
                                out=cacc[i][:], in0=gcb[i][:, 0:NG], scalar1=wc, scalar2=None, op0=ALU.mult),
                                reads=[('cbuf', i), 'ffnp'], writes=[('cacc', i)])
                        else:
                            S.add('dve', lambda e, i=i, wc=wc, kk=kk: e.scalar_tensor_tensor(
                                out=cacc[i][:], in0=gcb[i][:, kk:kk + NG], scalar=wc, in1=cacc[i][:],
                                op0=ALU.mult, op1=ALU.add),
                                reads=[('cbuf', i), 'ffnp', ('cacc', i)], writes=[('cacc', i)])
                    def post_ffn(fc=fc, i=i, bu=bu):
                        S.add('act', lambda e, i=i, fc=fc: e.activation(out=glb[i][:], in_=cacc[i][:], func=AF.Gelu_apprx_tanh,
                                                                        bias=ffnp[:, 66 + fc:67 + fc]),
                              reads=[('cacc', i), 'ffnp'], writes=[('sfm', i)])
                        S.add('dve', lambda e, i=i, fc=fc, bu=bu: e.tensor_tensor(
                            out=aT[:, fc, :], in0=bank(bu)[:, 0:NG], in1=glb[i][:], op=ALU.mult),
                            reads=bkeys(bu) + [('sfm', i)],
                            writes=[('zs', jj) for jj in range(3)] + [('Xtok', jj) for jj in range(3)] + bkeys(bu))
                    if fpost:
                        fpost.pop(0)()
                    fpost.append(post_ffn)
                while fpost:
                    fpost.pop(0)()
                aTk = [('zs', jj) for jj in range(3)] + [('Xtok', jj) for jj in range(3)]
                pre = []
                if g + 1 < ngroups:
                    pre = [chain_a(g + 1, j, tmps[j][0], tmps[j][1], 7 - (j % 2)) for j in range(3)]
                    for _ in range(5):
                        for gq in pre:
                            next(gq)
                sched = {6: 0, 8: 0, 10: 1, 12: 1, 14: 2, 16: 2}
                for fc in range(NFC):
                    w, wkey = wload(BLK_DOWN + fc)
                    for j in range(3):
                        for hf in range(2):
                            S.add('pe', lambda e, j=j, hf=hf, w=w, fc=fc: e.matmul(
                                bank(2 * j + hf), lhsT=aT[:, fc, j * 128:(j + 1) * 128],
                                rhs=w[:, hf * 512:(hf + 1) * 512], start=(fc == 0), stop=(fc == NFC - 1)),
                                reads=[wkey] + aTk, writes=bkeys(2 * j + hf))
                    if pre and fc in sched:
                        next(pre[sched[fc]])
                if pre:
                    apply_gain(actT, 'actT', 76)
                ftmps = [(zs[:, j, :], [('zs', j)]) for j in range(3)]
                run_rr([chain_post(j, gfpost_bc, 'gfpost_bc', 12, True, ftmps) for j in range(3)])

            ngroups = 11 if stage >= 4 else 1
            for g in range(ngroups):
                do_group(g)

        tail = [op for q in ENGS for op in S.dma_ops[q][-NDSEM:]]
        S.add('sp', lambda e: e.nop(), extra=tail)
        es2.close()
        S.emit(nc, es)
    return nc


def _host_layout(inputs):
    f = np.float32
    w_in = np.asarray(inputs["w_in"], f)[0]
    w_out = np.asarray(inputs["w_out"], f)[0]
    w_up = np.asarray(inputs["w_up"], f)[0]
    w_down = np.asarray(inputs["w_down"], f)[0]

    def kblocks(w, c0, nb):
        sub = w[:, c0:c0 + nb * 128].reshape(8, 128, nb, 128)
        return np.ascontiguousarray(sub.transpose(2, 1, 0, 3)).reshape(nb * 128, 1024)

    blocks = [kblocks(w_in, OFF_Z, 8), kblocks(w_in, OFF_XBC, 12), kblocks(w_in, OFF_Q, 8),
              kblocks(w_in, OFF_K, 8), kblocks(w_in, OFF_V, 8),
              w_out.reshape(16 * 128, 1024),
              kblocks(w_up, 0, 22), kblocks(w_up, DFF, 22),
              w_down.reshape(22 * 128, 1024)]
    wblk = np.ascontiguousarray(np.concatenate(blocks, axis=0))
    assert wblk.shape == (NBLK * 128, 1024)
    wdt = np.ascontiguousarray(w_in[:, OFF_DT:OFF_DT + 16].reshape(8, 128, 16).transpose(1, 0, 2)).reshape(128, 128)
    vecs = np.zeros((8, D), f)
    vecs[0] = np.asarray(inputs["mix_pre_g"], f)[0]
    vecs[1] = np.asarray(inputs["mix_post_g"], f)[0]
    vecs[2] = np.asarray(inputs["ffn_pre_g"], f)[0]
    vecs[3] = np.asarray(inputs["ffn_post_g"], f)[0]
    pp = np.zeros((128, 128), f)
    pp[:, 0:8] = np.asarray(inputs["ssd_norm_g"], f)[0].reshape(8, 128).T
    pp[:, 8:16] = np.asarray(inputs["sb_norm_g"], f)[0].reshape(8, 128).T
    cw = np.asarray(inputs["ssd_conv_w"], f)[0]
    pp[:, 16:64] = cw.reshape(4, 12, 128).transpose(2, 1, 0).reshape(128, 48)
    pp[:, 64:76] = np.asarray(inputs["ssd_conv_b"], f)[0].reshape(12, 128).T
    pp[:, 76:84] = np.asarray(inputs["mix_pre_g"], f)[0].reshape(8, 128).T
    pp[:, 84:92] = np.asarray(inputs["ffn_pre_g"], f)[0].reshape(8, 128).T
    fw = np.asarray(inputs["ffn_conv_w"], f)[0]
    ffn = np.zeros((128, 128), f)
    ffn[:, 0:66] = fw.reshape(3, 22, 128).transpose(2, 1, 0).reshape(128, 66)
    ffn[:, 66:88] = np.asarray(inputs["ffn_conv_b"], f)[0].reshape(22, 128).T
    small = np.zeros((1, 64), f)
    small[0, 0:16] = np.asarray(inputs["ssd_dt_bias"], f)[0]
    small[0, 16:32] = np.asarray(inputs["ssd_a_log"], f)[0]
    small[0, 32:48] = np.asarray(inputs["ssd_d"], f)[0]
    return dict(wblk=wblk, wdt=wdt, vecs=vecs, ppd=pp, ffnpd=ffn, small=small,
                meta=np.ascontiguousarray(np.asarray(inputs["meta_tokens"], f)))


def kernel(**inputs):
    x = np.asarray(inputs["x"], np.float32)
    shared = _host_layout(inputs)
    nc = build_nc()
    in_maps = []
    for b in range(8):
        m = dict(shared)
        m["x"] = np.ascontiguousarray(x[b])
        in_maps.append(m)
    res = run_bass_kernel_spmd(nc, in_maps, core_ids=list(range(8)))
    return np.stack([r["out"] for r in res.results], axis=0)
```

```python
import numpy as np
from contextlib import ExitStack
import concourse.bass as bass
import concourse.mybir as mybir
from concourse.bass_utils import run_bass_kernel_spmd

F32 = mybir.dt.float32
BF16 = mybir.dt.bfloat16
AF = mybir.ActivationFunctionType
ALU = mybir.AluOpType

D = 1024
SEQ = 4096
NMETA = 16
NT = 33
LP = NT * 128
PAD = 112
H = 16
DFF = 2816
NFC = 22
EPS = 1e-6
OFF_Z, OFF_XBC, OFF_DT, OFF_Q, OFF_K, OFF_V = 0, 1024, 2560, 2576, 3600, 4624

BLK_Z = 0
BLK_XBC = 8
BLK_Q = 20
BLK_K = 28
BLK_V = 36
BLK_OUT = 44
BLK_GATE = 60
BLK_UP = 82
BLK_DOWN = 104
NBLK = 126

ENGS = ['pe', 'act', 'dve', 'pool', 'sp']
NDSEM = 8


class _Op:
    __slots__ = ('eng', 'fn', 'deps', 'is_dma', 'needed', 'val', 'sem', 'idx')


class Sched:
    def __init__(self):
        self.ops = {e: [] for e in ENGS}
        self.last_w = {}
        self.readers = {}
        self.seen_c = {e: {p: -1 for p in ENGS} for e in ENGS}
        self.seen_d = {e: set() for e in ENGS}
        self.ndma = {e: 0 for e in ENGS}
        self.dma_ops = {e: [] for e in ENGS}

    def add(self, eng, fn, reads=(), writes=(), dma=False, extra=()):
        op = _Op()
        op.eng = eng
        op.fn = fn
        op.is_dma = dma
        op.needed = False
        op.val = None
        op.sem = None
        op.idx = len(self.ops[eng])
        deps = list(extra)
        for k in reads:
            w = self.last_w.get(k)
            if w is not None:
                deps.append(w)
        for k in writes:
            w = self.last_w.get(k)
            if w is not None:
                deps.append(w)
            deps.extend(self.readers.get(k, ()))
        if dma:
            n = self.ndma[eng]
            if n >= NDSEM:
                deps.append(self.dma_ops[eng][n - NDSEM])
            op.sem = ('d', eng, n % NDSEM)
            op.val = 16 * (n // NDSEM + 1)
            self.ndma[eng] = n + 1
            self.dma_ops[eng].append(op)
        cdeps = {}
        ddeps = []
        for d in deps:
            if d is op:
                continue
            if d.is_dma:
                key = (d.eng, d.sem, d.val)
                if key in self.seen_d[eng]:
                    continue
                self.seen_d[eng].add(key)
                ddeps.append(d)
            else:
                if d.eng == eng and eng == 'pe':
                    continue
                if d.idx <= self.seen_c[eng][d.eng]:
                    continue
                if d.eng not in cdeps or cdeps[d.eng].idx < d.idx:
                    cdeps[d.eng] = d
        for p, d in cdeps.items():
            self.seen_c[eng][p] = d.idx
            d.needed = True
        op.deps = list(cdeps.values()) + ddeps
        self.ops[eng].append(op)
        for k in writes:
            self.last_w[k] = op
            self.readers[k] = []
        for k in reads:
            if k in writes:
                continue
            self.readers.setdefault(k, []).append(op)
        return op

    def emit(self, nc, es):
        sem_c = {e: es.enter_context(nc.semaphore('sc_' + e)) for e in ENGS}
        sem_d = {}
        for e in ENGS:
            for i in range(min(NDSEM, self.ndma[e])):
                sem_d[('d', e, i)] = es.enter_context(nc.semaphore('sd_%s_%d' % (e, i)))
        for e in ENGS:
            c = 0
            for op in self.ops[e]:
                if op.is_dma:
                    continue
                if op.needed:
                    c += 1
                    op.val = c
        block = es.enter_context(nc.Block())
        secs = {'pe': block.tensor, 'act': block.scalar, 'dve': block.vector,
                'pool': block.gpsimd, 'sp': block.sync}

        def mk(e):
            def body(eng):
                for op in self.ops[e]:
                    for d in op.deps:
                        if d.is_dma:
                            eng.wait_ge(sem_d[d.sem], d.val)
                        else:
                            eng.wait_ge(sem_c[d.eng], d.val)
                    inst = op.fn(eng)
                    if op.is_dma:
                        inst.then_inc(sem_d[op.sem], 16)
                    elif op.needed:
                        inst.then_inc(sem_c[e], 1)
            return body

        for e in ENGS:
            if self.ops[e]:
                secs[e](mk(e))


def build_nc(stage=99, debug=False):
    nc = bass.Bass("TRN2", target_bir_lowering=False)
    es = ExitStack()

    def din(name, shape):
        return nc.dram_tensor(name, list(shape), F32, kind="ExternalInput").ap()

    x_d = din("x", [SEQ, D])
    meta_d = din("meta", [NMETA, D])
    wblk_d = din("wblk", [NBLK * 128, 1024])
    wdt_d = din("wdt", [128, 8 * 16])
    vecs_d = din("vecs", [8, D])
    pp_d = din("ppd", [128, 128])
    ffnp_d = din("ffnpd", [128, 128])
    small_d = din("small", [1, 64])
    out_d = nc.dram_tensor("out", [SEQ, D], F32, kind="ExternalOutput").ap()
    wscr = nc.dram_tensor("wscr", [NBLK * 128, 1024], BF16, kind="Internal").ap()
    dbg = {}
    if debug:
        dbg['xnT'] = nc.dram_tensor("dbg_xnT", [128, 8 * LP], BF16, kind="ExternalOutput").ap()
        dbg['ysbT'] = nc.dram_tensor("dbg_ysbT", [128, 8 * LP], BF16, kind="ExternalOutput").ap()

    S = Sched()
    with es:
        es2 = ExitStack()

        def sb(name, shape, dt=F32):
            return es.enter_context(nc.sbuf_tensor(name, list(shape), dt))

        def sb2(name, shape, dt=F32):
            return es2.enter_context(nc.sbuf_tensor(name, list(shape), dt))

        ps = es.enter_context(nc.psum_tensor("ps", [128, 4096], F32))

        def bank(b, n=1):
            return ps[:, 512 * b:512 * (b + n)]

        def bkeys(b, n=1):
            return [('ps', b + i) for i in range(n)]

        def dbgdump(name, ap, shape, dt, reads):
            if not debug:
                return
            t = nc.dram_tensor("dbg_" + name, list(shape), dt, kind="ExternalOutput").ap()
            S.add('sp', lambda e: e.dma_start(out=t, in_=ap), reads=reads, dma=True)

        ident_b = sb("ident_b", [128, 128], BF16)
        tri_b = sb("tri_b", [128, 128], BF16)
        tric_b = sb("tric_b", [128, 128], BF16)
        mdiag_b = sb("mdiag_b", [128, 128], BF16)
        padm = sb("padm", [128, 1], F32)
        mhalf = sb("mhalf", [128, 1], F32)
        pp = sb("pp", [128, 128], F32)
        junk = sb("junk", [128, D], BF16)

        def cmask(t, pattern_mult, chan_mult, op, base=0):
            S.add('pool', lambda e: e.memset(t[:], 1.0), writes=[t.name])
            S.add('pool', lambda e: e.affine_select(
                out=t[:], in_=t[:], pattern=[[pattern_mult, t.shape[1]]], compare_op=op,
                fill=0.0, base=base, channel_multiplier=chan_mult), reads=[t.name], writes=[t.name])

        cmask(ident_b, -1, 1, ALU.is_equal)
        cmask(tri_b, -1, 1, ALU.is_ge)
        cmask(tric_b, 1, -1, ALU.is_gt)
        cmask(mdiag_b, 1, -1, ALU.is_gt)
        cmask(padm, 0, 1, ALU.is_ge, base=-PAD)
        S.add('pool', lambda e: e.memset(mhalf[:], -0.5), writes=['mhalf'])
        S.add('sp', lambda e: e.dma_start(out=pp[:], in_=pp_d[:, :]), writes=['pp'], dma=True)

        def prep_blocks(b0, nb):
            S.add('pool', lambda e: e.dma_start(out=wscr[b0 * 128:(b0 + nb) * 128, :],
                                                in_=wblk_d[b0 * 128:(b0 + nb) * 128, :]),
                  writes=[('wscr', b) for b in range(b0, b0 + nb)], dma=True)


        ysbT = sb("ysbT", [128, 8, LP], BF16)
        xnT = sb2("xnT", [128, 8, LP], BF16)
        gpre_bc = sb2("gpre_bc", [128, D], F32)

        S.add('sp', lambda e: e.dma_start(out=gpre_bc[:], in_=vecs_d[0:1, :].partition_broadcast(128)),
              writes=['gpre_bc'], dma=True)
        xt = [sb2("xt%d" % i, [128, D], F32) for i in range(2)]
        xnb = [sb2("xnb%d" % i, [128, D], BF16) for i in range(2)]
        st1 = sb2("st1", [128, NT * 4], F32)

        def load_x_tile(t, buf, key):
            keys = key if isinstance(key, list) else [key]
            if t == 0:
                S.add('pool', lambda e: e.memset(buf, 0.0), writes=keys)
                S.add('sp', lambda e: e.dma_start(out=buf[PAD:128, :], in_=meta_d[:, :]),
                      writes=keys, dma=True)
            else:
                S.add('sp', lambda e: e.dma_start(out=buf, in_=x_d[(t - 1) * 128:t * 128, :]),
                      writes=keys, dma=True)

        def rstd_from(src, srckey, col, stt, sttname, junkbuf, junkkey):
            S.add('act', lambda e: e.activation(out=junkbuf[:], in_=src, func=AF.Square,
                                                accum_out=stt[:, col:col + 1]),
                  reads=[srckey], writes=[(sttname, col)])
            S.add('dve', lambda e: e.tensor_scalar(out=stt[:, col + 1:col + 2], in0=stt[:, col:col + 1],
                                                   scalar1=1.0 / D, scalar2=EPS, op0=ALU.mult, op1=ALU.add),
                  reads=[(sttname, col)], writes=[(sttname, col + 1)])
            S.add('pool', lambda e: e.tensor_tensor(out=stt[:, col + 2:col + 3], in0=stt[:, col + 1:col + 2],
                                                    in1=mhalf[:], op=ALU.pow),
                  reads=[(sttname, col + 1), 'mhalf'], writes=[(sttname, col + 2)])

        def p1_stage(t, s):
            i = t % 2
            pb = 4 + i
            pbv = bank(pb).bitcast(BF16)
            col = 4 * t
            if s == 0:
                load_x_tile(t, xt[i][:], ('xt', i))
            elif s == 1:
                S.add('act', lambda e: e.activation(out=junk[:], in_=xt[i][:], func=AF.Square,
                                                    accum_out=st1[:, col:col + 1]),
                      reads=[('xt', i)], writes=[('st1', col), 'junk'])
            elif s == 2:
                S.add('dve', lambda e: e.tensor_scalar(out=st1[:, col + 1:col + 2], in0=st1[:, col:col + 1],
                                                       scalar1=1.0 / D, scalar2=EPS, op0=ALU.mult, op1=ALU.add),
                      reads=[('st1', col)], writes=[('st1', col + 1)])
            elif s == 3:
                S.add('pool', lambda e: e.tensor_tensor(out=st1[:, col + 2:col + 3], in0=st1[:, col + 1:col + 2],
                                                        in1=mhalf[:], op=ALU.pow),
                      reads=[('st1', col + 1), 'mhalf'], writes=[('st1', col + 2)])
            elif s == 4:
                S.add('dve', lambda e: e.scalar_tensor_tensor(
                    out=xnb[i][:], in0=xt[i][:], scalar=st1[:, col + 2:col + 3], in1=gpre_bc[:],
                    op0=ALU.mult, op1=ALU.mult),
                    reads=[('xt', i), ('st1', col + 2), 'gpre_bc'], writes=[('xnb', i)])
            elif s == 5:
                for k in range(8):
                    S.add('pe', lambda e, k=k: e.transpose(
                        out=pbv[:, k * 128:(k + 1) * 128], in_=xnb[i][:, k * 128:(k + 1) * 128], identity=ident_b[:]),
                        reads=[('xnb', i), 'ident_b'], writes=bkeys(pb))
            elif s == 6:
                S.add('act', lambda e: e.activation(
                    out=xnT[:, :, t * 128:(t + 1) * 128], in_=pbv.rearrange("p (k c) -> p k c", k=8), func=AF.Copy),
                    reads=bkeys(pb), writes=[('xnT', t)] + bkeys(pb))

        for t0_ in range(0, NT, 2):
            ts_ = [t for t in (t0_, t0_ + 1) if t < NT]
            for s in range(7):
                for t in ts_:
                    p1_stage(t, s)
            if t0_ == 4:
                for b0 in (BLK_Q, BLK_K, BLK_V):
                    prep_blocks(b0, 8)
        for b0 in range(0, 20, 4):
            prep_blocks(b0, 4)
        for b0 in range(BLK_OUT, NBLK, 4):
            prep_blocks(b0, min(4, NBLK - b0))

        if debug:
            S.add('sp', lambda e: e.dma_start(out=dbg['xnT'], in_=xnT[:].rearrange("p k c -> p (k c)")),
                  reads=[('xnT', t) for t in range(NT)], dma=True)

        if stage >= 2:
            wq = sb2("wq", [128, 8, 128], BF16)
            wk = sb2("wk", [128, 8, 128], BF16)
            wv = sb2("wv", [128, 8, 128], BF16)
            qT = sb2("qT", [128, LP], BF16)
            kT = sb2("kT", [128, LP], BF16)
            vv = sb2("vv", [128, NT, 128], BF16)
            eb = [sb2("eb%d" % i, [128, 2, 512], BF16) for i in range(4)]
            spb = [sb2("spb%d" % i, [128, 2, 512], BF16) for i in range(3)]
            gb = [sb2("gb%d" % i, [128, 2, 512], BF16) for i in range(2)]
            wb = [sb2("wb%d" % i, [128, 2, 512], BF16) for i in range(2)]
            allx = [('xnT', t) for t in range(NT)]

            def load_wblk(dst, key, blk):
                S.add('sp', lambda e: e.dma_start(
                    out=dst[:].rearrange("p k c -> p (k c)"), in_=wscr[blk * 128:(blk + 1) * 128, :]),
                    reads=[('wscr', blk)], writes=[key], dma=True)

            gstep = 0
            for hp in range(8):
                load_wblk(wq, 'wq', BLK_Q + hp)
                load_wblk(wk, 'wk', BLK_K + hp)
                load_wblk(wv, 'wv', BLK_V + hp)
                def blk_tiles(gq):
                    return [0] if gq == 0 else list(range(4 * gq - 3, 4 * gq + 1))

                def unit_qk(gq, which, b):
                    tiles = blk_tiles(gq)
                    c0 = tiles[0] * 128
                    n = len(tiles) * 128
                    dst, dkey, wt, wkey, scale = ((qT, 'qT', wq, 'wq', 0.125) if which == 'q'
                                                  else (kT, 'kT', wk, 'wk', 1.0))
                    for k in range(8):
                        S.add('pe', lambda e, k=k: e.matmul(
                            bank(b)[:, 0:n], lhsT=wt[:, k, :], rhs=xnT[:, k, c0:c0 + n],
                            start=(k == 0), stop=(k == 7)),
                            reads=[wkey] + [('xnT', t) for t in tiles], writes=bkeys(b))
                    S.add('dve', lambda e: e.tensor_scalar(
                        out=dst[:, c0:c0 + n], in0=bank(b)[:, 0:n], scalar1=scale, scalar2=None, op0=ALU.mult),
                        reads=bkeys(b), writes=[(dkey, t) for t in tiles] + bkeys(b))

                def unit_v(tiles, b):
                    for j, t in enumerate(tiles):
                        for k in range(8):
                            S.add('pe', lambda e, j=j, t=t, k=k: e.matmul(
                                bank(b)[:, j * 128:(j + 1) * 128], lhsT=xnT[:, k, t * 128:(t + 1) * 128],
                                rhs=wv[:, k, :], start=(k == 0), stop=(k == 7)),
                                reads=['wv', ('xnT', t)], writes=bkeys(b))
                    nt_ = len(tiles)
                    t0_ = tiles[0]
                    S.add('dve', lambda e: e.tensor_copy(
                        out=vv[:, t0_:t0_ + nt_, :], in_=bank(b)[:, 0:nt_ * 128].rearrange("p (t c) -> p t c", t=nt_)),
                        reads=bkeys(b), writes=[('vv', t) for t in tiles] + bkeys(b))

                def block_units(gq, b):
                    tiles = blk_tiles(gq)
                    us = [lambda: unit_qk(gq, 'q', b), lambda: unit_qk(gq, 'k', b)]
                    for h0 in range(0, len(tiles), 2):
                        tl = tiles[h0:h0 + 2]
                        us.append(lambda tl=tl: unit_v(tl, b))
                    return us

                pb_ = 0
                for gq in (0, 1):
                    tiles = blk_tiles(gq)
                    unit_qk(gq, 'q', pb_ % 4); pb_ += 1
                    unit_qk(gq, 'k', pb_ % 4); pb_ += 1
                    for h0 in range(0, len(tiles), 2):
                        unit_v(tiles[h0:h0 + 2], pb_ % 4); pb_ += 1

                steps = []
                for qg in range(9):
                    qb0 = 0 if qg == 0 else 4 * qg - 3
                    nq = 1 if qg == 0 else 4
                    qb1 = qb0 + nq - 1
                    for kb in range(qb1, -1, -1):
                        steps.append(dict(qg=qg, kb=kb, lidx=qb1 - kb, qb0=qb0, qb1=qb1, NQ=nq * 128, qc0=qb0 * 128,
                                          ob=6 + (qg % 2), co=max(0, kb - qb0) * 128,
                                          first=(kb == qb1), last=(kb == 0)))
                cb = 4
                c3 = bank(cb, 2).rearrange("p (h c) -> p h c", h=2)

                def fZ(st, gs):
                    co, NQ, kb, qc0 = st['co'], st['NQ'], st['kb'], st['qc0']
                    zs = (gs % 2) * 2
                    z3 = bank(zs, 2).rearrange("p (h c) -> p h c", h=2)
                    for h in range(2):
                        r0 = 64 * h
                        S.add('pe', lambda e, h=h, r0=r0, z3=z3: e.matmul(
                            z3[:, h, co:NQ], lhsT=kT[r0:r0 + 64, kb * 128:(kb + 1) * 128],
                            rhs=qT[r0:r0 + 64, qc0 + co:qc0 + NQ], start=True, stop=True),
                            reads=[('kT', kb)] + [('qT', j) for j in range(st['qb0'] + co // 128, st['qb1'] + 1)],
                            writes=bkeys(zs + h))

                def fE(st, gs):
                    co, NQ, kb = st['co'], st['NQ'], st['kb']
                    zs = (gs % 2) * 2
                    i = gs % 4
                    z3 = bank(zs, 2).rearrange("p (h c) -> p h c", h=2)
                    S.add('act', lambda e: e.activation(
                        out=eb[i][:, :, co:NQ], in_=z3[:, :, co:NQ], func=AF.Exp),
                        reads=bkeys(zs, 2), writes=[('eb', i)] + bkeys(zs, 2))
                    if kb >= st['qb0']:
                        S.add('dve', lambda e: e.tensor_tensor(
                            out=eb[i][:, :, co:co + 128], in0=eb[i][:, :, co:co + 128],
                            in1=mdiag_b[:].unsqueeze(1).broadcast_to([128, 2, 128]), op=ALU.mult),
                            reads=[('eb', i), 'mdiag_b'], writes=[('eb', i)])
                    if kb == 0:
                        S.add('dve', lambda e: e.tensor_scalar(
                            out=eb[i][:, :, co:NQ], in0=eb[i][:, :, co:NQ], scalar1=padm[:, 0:1],
                            scalar2=None, op0=ALU.mult),
                            reads=[('eb', i), 'padm'], writes=[('eb', i)])

                def fL(st, gs):
                    co, NQ = st['co'], st['NQ']
                    i = gs % 3
                    ie = gs % 4
                    S.add('act', lambda e: e.activation(
                        out=spb[i][:, :, co:NQ], in_=eb[ie][:, :, co:NQ], func=AF.Ln, bias=1.0),
                        reads=[('eb', ie)], writes=[('spb', i)])

                def fT(st, gs):
                    co, NQ = st['co'], st['NQ']
                    i = gs % 3
                    for h in range(2):
                        S.add('pe', lambda e, h=h: e.matmul(
                            c3[:, h, co:NQ], lhsT=tri_b[:], rhs=spb[i][:, h, co:NQ],
                            start=st['first'], stop=False, skip_group_check=True),
                            reads=[('spb', i), 'tri_b'], writes=bkeys(cb + h))

                def fG(st, gs):
                    co, NQ = st['co'], st['NQ']
                    i = gs % 2
                    S.add('act', lambda e: e.activation(
                        out=gb[i][:, :, co:NQ], in_=c3[:, :, co:NQ], func=AF.Exp, scale=-1.0),
                        reads=bkeys(cb, 2), writes=[('gb', i)] + bkeys(cb, 2))

                def fT2(st, gs):
                    co, NQ = st['co'], st['NQ']
                    i = gs % 3
                    if st['last']:
                        return
                    for h in range(2):
                        S.add('pe', lambda e, h=h: e.matmul(
                            c3[:, h, co:NQ], lhsT=tric_b[:], rhs=spb[i][:, h, co:NQ],
                            start=False, stop=False, skip_group_check=True),
                            reads=[('spb', i), 'tric_b'], writes=bkeys(cb + h))

                def fW(st, gs):
                    co, NQ = st['co'], st['NQ']
                    i = gs % 2
                    S.add('dve', lambda e: e.tensor_tensor(
                        out=wb[i][:, :, co:NQ], in0=eb[gs % 4][:, :, co:NQ], in1=gb[i][:, :, co:NQ], op=ALU.mult),
                        reads=[('eb', gs % 4), ('gb', i)], writes=[('wb', i)])

                def fV(st, gs, hp=hp):
                    co, NQ, kb, ob, qc0 = st['co'], st['NQ'], st['kb'], st['ob'], st['qc0']
                    i = gs % 2
                    for h in range(2):
                        r0 = 64 * h
                        S.add('pe', lambda e, h=h, r0=r0: e.matmul(
                            bank(ob)[r0:r0 + 64, co:NQ], lhsT=vv[:, kb, r0:r0 + 64], rhs=wb[i][:, h, co:NQ],
                            start=st['first'], stop=False, skip_group_check=True),
                            reads=[('wb', i), ('vv', kb)], writes=bkeys(ob))
                    if st['last']:
                        S.add('dve', lambda e: e.tensor_copy(
                            out=ysbT[:, hp, qc0:qc0 + NQ], in_=bank(ob)[:, 0:NQ]),
                            reads=bkeys(ob), writes=[('ysbT', hp, st['qg'])] + bkeys(ob))

                n = len(steps)
                for i in range(-2, n + 2):
                    if 0 <= i - 1 < n:
                        fT(steps[i - 1], gstep + i - 1)
                    if 0 <= i + 2 < n:
                        fZ(steps[i + 2], gstep + i + 2)
                    if 0 <= i - 2 < n:
                        fV(steps[i - 2], gstep + i - 2)
                    if 0 <= i < n:
                        st_ = steps[i]
                        if 1 <= st_['qg'] <= 7 and 1 <= st_['lidx'] <= 4:
                            us = block_units(st_['qg'] + 1, 6 + ((st_['qg'] + 1) % 2))
                            us[st_['lidx'] - 1]()
                    if 0 <= i + 1 < n:
                        fE(steps[i + 1], gstep + i + 1)
                    if 0 <= i < n:
                        fL(steps[i], gstep + i)
                    if 0 <= i - 1 < n:
                        fG(steps[i - 1], gstep + i - 1)
                        fT2(steps[i - 1], gstep + i - 1)
                        fW(steps[i - 1], gstep + i - 1)
                gstep += n

            if debug:
                S.add('sp', lambda e: e.dma_start(out=dbg['ysbT'], in_=ysbT[:].rearrange("p k c -> p (k c)")),
                      reads=[('ysbT', hp, qg) for hp in range(8) for qg in range(9)], dma=True)

        def barrier():
            tails = {e: [op for op in S.ops[e] if not op.is_dma][-1:] for e in ENGS}
            dts = [op for q in ENGS for op in S.dma_ops[q][-NDSEM:]]
            for e in ENGS:
                ex = [o for p in ENGS if p != e for o in tails[p]] + dts
                S.add(e, lambda eng: eng.nop(), extra=ex)

        if stage >= 3:
            barrier()
            es2.close()
            NG = 384
            ident_f = sb("ident_f", [128, 128], F32)
            UTf = sb("UTf", [128, 128], F32)
            SGf = sb("SGf", [128, 128], F32)
            ONESf = sb("ONESf", [128, 128], F32)
            ones_b = sb("ones_b", [128, 128], BF16)
            cmask(ident_f, -1, 1, ALU.is_equal)
            cmask(UTf, 1, -1, ALU.is_ge)
            cmask(SGf, -1, 1, ALU.is_gt)
            S.add('pool', lambda e: e.memset(ONESf[:], 1.0), writes=['ONESf'])
            S.add('pool', lambda e: e.memset(ones_b[:], 1.0), writes=['ones_b'])
            ffnp = sb("ffnp", [128, 128], F32)
            S.add('sp', lambda e: e.dma_start(out=ffnp[:], in_=ffnp_d[:, :]), writes=['ffnp'], dma=True)
            smallbc = sb("smallbc", [128, 64], F32)
            S.add('sp', lambda e: e.dma_start(out=smallbc[:], in_=small_d[0:1, :].partition_broadcast(128)),
                  writes=['smallbc'], dma=True)
            Abc = sb("Abc", [128, 16], F32)
            S.add('act', lambda e: e.activation(out=Abc[:], in_=smallbc[:, 16:32], func=AF.Exp),
                  reads=['smallbc'], writes=['Abc'])
            S.add('dve', lambda e: e.tensor_scalar(out=Abc[:], in0=Abc[:], scalar1=-1.0, scalar2=None, op0=ALU.mult),
                  reads=['Abc'], writes=['Abc'])
            wdt_b = sb("wdt_b", [128, 8, 16], BF16)
            S.add('pool', lambda e: e.dma_start(out=wdt_b[:].rearrange("p k c -> p (k c)"), in_=wdt_d[:, :]),
                  writes=['wdt_b'], dma=True)
            gpost_bc = sb("gpost_bc", [128, D], F32)
            gfpost_bc = sb("gfpost_bc", [128, D], F32)
            S.add('sp', lambda e: e.dma_start(out=gpost_bc[:], in_=vecs_d[1:2, :].partition_broadcast(128)),
                  writes=['gpost_bc'], dma=True)
            S.add('sp', lambda e: e.dma_start(out=gfpost_bc[:], in_=vecs_d[3:4, :].partition_broadcast(128)),
                  writes=['gfpost_bc'], dma=True)

            NWB = 4
            wbuf = [sb("wbuf%d" % i, [128, 1024], BF16) for i in range(NWB)]
            wcnt = [0]

            def wload(blk):
                i = wcnt[0] % NWB
                wcnt[0] += 1
                S.add('sp', lambda e: e.dma_start(out=wbuf[i][:], in_=wscr[blk * 128:(blk + 1) * 128, :]),
                      reads=[('wscr', blk)], writes=[('wbuf', i)], dma=True)
                return wbuf[i], ('wbuf', i)

            state = sb("state", [128, 1024], F32)
            state_b = sb("state_b", [128, 1024], BF16)
            S.add('pool', lambda e: e.memset(state[:], 0.0), writes=['state'])
            S.add('pool', lambda e: e.memset(state_b[:], 0.0), writes=['state_b'])
            halo = sb("halo", [128, 12, 3], F32)
            ghalo = sb("ghalo", [128, NFC, 2], F32)
            S.add('pool', lambda e: e.memset(halo[:], 0.0), writes=['halo'])
            S.add('pool', lambda e: e.memset(ghalo[:], 0.0), writes=['ghalo'])

            hbuf = sb("hbuf", [128, 3, D], F32)
            actT = sb("actT", [128, 8, NG], BF16)
            R1 = sb("R1", [128, 6 * D], F32)
            zs = R1[:, 0:3 * D].rearrange("p (j c) -> p j c", j=3)
            Xtok = R1[:, 3 * D:6 * D].rearrange("p (j c) -> p j c", j=3)
            aT = R1[:, 0:NFC * NG // 2].bitcast(BF16).rearrange("p (f c) -> p f c", f=NFC)
            BT = sb("BT", [128, 2, NG], BF16)
            CT = sb("CT", [128, 2, NG], BF16)
            Btok = sb("Btok", [128, 3, 256], BF16)
            cbuf = [sb("cbuf%d" % i, [128, NG + 3], F32) for i in range(4)]
            cacc = [sb("cacc%d" % i, [128, NG], F32) for i in range(4)]
            sfm = [sb("sfm%d" % i, [128, NG], F32) for i in range(4)]
            t1 = sb("t1", [128, D], F32)
            RH = sb("RH", [128, 16, 128], F32)
            Xdt = sb("Xdt", [128, D], BF16)
            Xdec = sb("Xdec", [128, D], BF16)
            Eb = [sb("Eb%d" % i, [128, 512], F32) for i in range(2)]
            MTb = [sb("MTb%d" % i, [128, 4, 128], BF16) for i in range(2)]
            CBm = sb("CBm", [128, 2, 128], F32)
            tokbfs = [sb("tokbf%d" % i, [128, D], BF16) for i in range(3)]
            ygnT = sb("ygnT", [128, 8, NG], BF16)
            ysbn = sb("ysbn", [128, 8, NG], BF16)
            r_bc = sb("r_bc", [128, NG], F32)
            lnr = r_bc
            sqb = [sb("sqb%d" % i, [128, NG], BF16) for i in range(2)]
            dtt = sb("dtt", [128, 3, 16], F32)
            att = sb("att", [128, 3, 16], F32)
            dte = sb("dte", [128, 3, 16], F32)
            ex48 = sb("ex48", [128, 48], F32)
            dtd = sb("dtd", [128, 16], F32)
            st3 = sb("st3", [128, 64], F32)
            gcb = cbuf
            RHf = RH[:].rearrange("p h c -> p (h c)")
            tmps = [(t1[:], ['t1']), (RHf[:, 0:D], [('tmpf', 1)]), (RHf[:, D:2 * D], [('tmpf', 2)])]
            glb = sfm

            def bc3(ap2, n):
                return ap2.unsqueeze(2).broadcast_to([128, 16, n])

            def v3(ap2):
                return ap2.rearrange("p (h d) -> p h d", h=16)

            def run_rr(gens):
                gens = list(gens)
                while gens:
                    nxt = []
                    for gq in gens:
                        try:
                            next(gq)
                            nxt.append(gq)
                        except StopIteration:
                            pass
                    gens = nxt

            def rinfo(slot):
                return st3[:, 4 * slot + 2:4 * slot + 3], (('st3', slot), 2)

            def g_rstd3(src, srcreads, slot):
                col = 4 * slot
                key = ('st3', slot)
                S.add('act', lambda e: e.activation(out=junk[:], in_=src, func=AF.Square,
                                                    accum_out=st3[:, col:col + 1]),
                      reads=srcreads, writes=[(key, 0), 'junk'])
                yield
                S.add('dve', lambda e: e.tensor_scalar(out=st3[:, col + 1:col + 2], in0=st3[:, col:col + 1],
                                                       scalar1=1.0 / D, scalar2=EPS, op0=ALU.mult, op1=ALU.add),
                      reads=[(key, 0)], writes=[(key, 1)])
                yield
                S.add('pool', lambda e: e.tensor_tensor(out=st3[:, col + 2:col + 3], in0=st3[:, col + 1:col + 2],
                                                        in1=mhalf[:], op=ALU.pow),
                      reads=[(key, 1), 'mhalf'], writes=[(key, 2)])
                yield

            def rstd3(src, srcreads, slot):
                for _ in g_rstd3(src, srcreads, slot):
                    pass
                return rinfo(slot)

            def g_norm_transpose(j, src, srckeys, slot, dstT, dkey, pb=7):
                rcol, rkey = rinfo(slot)
                tb = tokbfs[j % 3]
                tk = ('tokbf', j % 3)
                S.add('dve', lambda e: e.tensor_scalar(out=tb[:], in0=src, scalar1=rcol, scalar2=None,
                                                       op0=ALU.mult),
                      reads=srckeys + [rkey], writes=[tk])
                yield
                pbv = bank(pb).bitcast(BF16)
                for k in range(8):
                    S.add('pe', lambda e, k=k: e.transpose(
                        out=pbv[:, k * 128:(k + 1) * 128], in_=tb[:, k * 128:(k + 1) * 128], identity=ident_b[:]),
                        reads=[tk, 'ident_b'], writes=bkeys(pb))
                yield
                S.add('act', lambda e: e.activation(
                    out=dstT[:, :, j * 128:(j + 1) * 128], in_=pbv.rearrange("p (k c) -> p k c", k=8), func=AF.Copy),
                    reads=bkeys(pb), writes=[(dkey, j)] + bkeys(pb))
                yield

            def norm_transpose(j, src, srckeys, slot, dstT, dkey, pb=7):
                for _ in g_norm_transpose(j, src, srckeys, slot, dstT, dkey, pb):
                    pass

            def apply_gain(dstT, dkey, gcol):
                for k in range(8):
                    S.add('dve', lambda e, k=k: e.tensor_scalar(
                        out=dstT[:, k, :], in0=dstT[:, k, :], scalar1=pp[:, gcol + k:gcol + k + 1], scalar2=None,
                        op0=ALU.mult),
                        reads=[(dkey, jj) for jj in range(3)] + ['pp'], writes=[(dkey, jj) for jj in range(3)])

            def do_group(g):
                t0 = 3 * g
                gc0 = t0 * 128
                hk = [('hbuf', j) for j in range(3)]
                def chain_a(gg_, j, xbuf, xkeys, pb):
                    load_x_tile(3 * gg_ + j, xbuf, xkeys)
                    yield
                    yield from g_rstd3(xbuf, xkeys, j)
                    yield from g_norm_transpose(j, xbuf, xkeys, j, actT, 'actT', pb=pb)

                if g == 0:
                    run_rr([chain_a(0, j, hbuf[:, j, :], [('hbuf', j)], 7 - j) for j in range(3)])
                else:
                    for j in range(3):
                        S.add('act', lambda e, j=j: e.activation(out=hbuf[:, j, :], in_=tmps[j][0], func=AF.Copy),
                              reads=tmps[j][1], writes=[('hbuf', j)])
                if g == 0:
                    apply_gain(actT, 'actT', 76)
                ak = [('actT', j) for j in range(3)]
                if g == 0:
                    dbgdump('actT', actT[:].rearrange("p k c -> p (k c)"), [128, 8 * NG], BF16, ak)
                for cbk in range(8):
                    w, wkey = wload(BLK_Z + cbk)
                    w3 = w[:].rearrange("p (k c) -> p k c", k=8)
                    for j in range(3):
                        b = 2 * j + cbk // 4
                        for k in range(8):
                            S.add('pe', lambda e, b=b, j=j, k=k, w3=w3, cbk=cbk: e.matmul(
                                bank(b)[:, (cbk % 4) * 128:(cbk % 4 + 1) * 128], lhsT=actT[:, k, j * 128:(j + 1) * 128],
                                rhs=w3[:, k, :], start=(k == 0), stop=(k == 7)),
                                reads=[wkey, ('actT', j)], writes=bkeys(b))
                for j in range(3):
                    S.add('act', lambda e, j=j: e.activation(out=zs[:, j, :], in_=bank(2 * j, 2), func=AF.Silu),
                          reads=bkeys(2 * j, 2), writes=[('zs', j)] + bkeys(2 * j, 2))
                for j in range(3):
                    for k in range(8):
                        S.add('pe', lambda e, j=j, k=k: e.matmul(
                            bank(6)[:, j * 16:(j + 1) * 16], lhsT=actT[:, k, j * 128:(j + 1) * 128],
                            rhs=wdt_b[:, k, :], start=(k == 0), stop=(k == 7)),
                            reads=['wdt_b', ('actT', j)], writes=bkeys(6))
                S.add('dve', lambda e: e.tensor_tensor(
                    out=dte[:], in0=bank(6)[:, 0:48].rearrange("p (j h) -> p j h", j=3),
                    in1=smallbc[:, 0:16].unsqueeze(1).broadcast_to([128, 3, 16]), op=ALU.add),
                    reads=bkeys(6) + ['smallbc'], writes=['dte'] + bkeys(6))
                S.add('act', lambda e: e.activation(out=dte[:], in_=dte[:], func=AF.Exp), reads=['dte'], writes=['dte'])
                S.add('act', lambda e: e.activation(out=dtt[:], in_=dte[:], func=AF.Ln, bias=1.0),
                      reads=['dte'], writes=['dtt'])
                if g == 0:
                    S.add('dve', lambda e: e.tensor_scalar(out=dtt[:, 0, :], in0=dtt[:, 0, :], scalar1=padm[:, 0:1],
                                                           scalar2=None, op0=ALU.mult),
                          reads=['dtt', 'padm'], writes=['dtt'])
                S.add('dve', lambda e: e.tensor_tensor(
                    out=att[:], in0=dtt[:], in1=Abc[:].unsqueeze(1).broadcast_to([128, 3, 16]), op=ALU.mult),
                    reads=['dtt', 'Abc'], writes=['att'])
                pend = []
                postq = []

                def emit_tr(jb, i):
                    for j in range(3):
                        S.add('pe', lambda e, i=i, j=j, jb=jb: e.transpose(
                            out=bank(2 * j + jb // 4)[:, (jb % 4) * 128:(jb % 4 + 1) * 128],
                            in_=sfm[i][:, j * 128:(j + 1) * 128], identity=ident_f[:]),
                            reads=[('sfm', i), 'ident_f'], writes=bkeys(2 * j + jb // 4))

                for jb in range(12):
                    w, wkey = wload(BLK_XBC + jb)
                    w3 = w[:].rearrange("p (k c) -> p k c", k=8)
                    b = 6 + (jb % 2)
                    i = jb % 4
                    for k in range(8):
                        S.add('pe', lambda e, b=b, k=k, w3=w3: e.matmul(
                            bank(b)[:, 0:NG], lhsT=w3[:, k, :], rhs=actT[:, k, :], start=(k == 0), stop=(k == 7)),
                            reads=[wkey] + ak, writes=bkeys(b))
                    if len(pend) >= 2:
                        emit_tr(*pend.pop(0))
                    S.add('act', lambda e, b=b, i=i: e.activation(out=cbuf[i][:, 3:3 + NG], in_=bank(b)[:, 0:NG],
                                                                  func=AF.Copy),
                          reads=bkeys(b), writes=[('cbuf', i)] + bkeys(b))
                    S.add('pool', lambda e, i=i, jb=jb: e.tensor_copy(out=cbuf[i][:, 0:3], in_=halo[:, jb, :]),
                          reads=['halo'], writes=[('cbuf', i)])
                    S.add('pool', lambda e, i=i, jb=jb: e.tensor_copy(out=halo[:, jb, :], in_=cbuf[i][:, NG:NG + 3]),
                          reads=[('cbuf', i)], writes=['halo'])
                    for kk in range(4):
                        wc = pp[:, 16 + jb * 4 + kk:16 + jb * 4 + kk + 1]
                        if kk == 0:
                            S.add('dve', lambda e, i=i, wc=wc: e.tensor_scalar(
                                out=cacc[i][:], in0=cbuf[i][:, 0:NG], scalar1=wc, scalar2=None, op0=ALU.mult),
                                reads=[('cbuf', i), 'pp'], writes=[('cacc', i)])
                        else:
                            S.add('dve', lambda e, i=i, wc=wc, kk=kk: e.scalar_tensor_tensor(
                                out=cacc[i][:], in0=cbuf[i][:, kk:kk + NG], scalar=wc, in1=cacc[i][:],
                                op0=ALU.mult, op1=ALU.add),
                                reads=[('cbuf', i), 'pp', ('cacc', i)], writes=[('cacc', i)])
                    def post_conv(jb=jb, i=i):
                        bias = pp[:, 64 + jb:65 + jb]
                        if jb < 8:
                            S.add('act', lambda e, i=i, bias=bias: e.activation(out=sfm[i][:], in_=cacc[i][:], func=AF.Silu,
                                                                                bias=bias),
                                  reads=[('cacc', i), 'pp'], writes=[('sfm', i)])
                            pend.append((jb, i))
                        else:
                            dst = BT if jb < 10 else CT
                            dk = 'BT' if jb < 10 else 'CT'
                            gg = jb % 2
                            S.add('act', lambda e, i=i, bias=bias, dst=dst, gg=gg: e.activation(
                                out=dst[:, gg, :], in_=cacc[i][:], func=AF.Silu, bias=bias),
                                reads=[('cacc', i), 'pp'], writes=[(dk, gg)])
                    if postq:
                        postq.pop(0)()
                    postq.append(post_conv)
                while postq:
                    postq.pop(0)()
                while pend:
                    emit_tr(*pend.pop(0))
                for j in range(3):
                    S.add('act', lambda e, j=j: e.activation(out=Xtok[:, j, :], in_=bank(2 * j, 2), func=AF.Copy),
                          reads=bkeys(2 * j, 2), writes=[('Xtok', j)] + bkeys(2 * j, 2))
                pbv = bank(7).bitcast(BF16)
                for gg in range(2):
                    for j in range(3):
                        S.add('pe', lambda e, j=j, gg=gg, pbv=pbv: e.transpose(
                            out=pbv[:, j * 256 + gg * 128:j * 256 + (gg + 1) * 128],
                            in_=BT[:, gg, j * 128:(j + 1) * 128], identity=ident_b[:]),
                            reads=[('BT', gg), 'ident_b'], writes=bkeys(7))
                S.add('dve', lambda e, pbv=pbv: e.tensor_copy(
                    out=Btok[:], in_=pbv[:, 0:768].rearrange("p (j c) -> p j c", j=3)),
                    reads=bkeys(7), writes=['Btok'] + bkeys(7))
                if g == 0:
                    dbgdump('zs', R1[:, 0:3 * D], [128, 3 * D], F32, [('zs', jj) for jj in range(3)])
                    dbgdump('Xtok', R1[:, 3 * D:6 * D], [128, 3 * D], F32, [('Xtok', jj) for jj in range(3)])
                    dbgdump('dtt', dtt[:].rearrange("p j h -> p (j h)"), [128, 48], F32, ['dtt'])
                    dbgdump('att', att[:].rearrange("p j h -> p (j h)"), [128, 48], F32, ['att'])
                    dbgdump('BT', BT[:].rearrange("p g c -> p (g c)"), [128, 2 * NG], BF16, [('BT', 0), ('BT', 1)])
                    dbgdump('CT', CT[:].rearrange("p g c -> p (g c)"), [128, 2 * NG], BF16, [('CT', 0), ('CT', 1)])
                    dbgdump('Btok', Btok[:].rearrange("p j c -> p (j c)"), [128, 768], BF16, ['Btok'])
                for j in range(3):
                    tc0 = j * 128
                    a_j = att[:, j, :]
                    for ci, lt in enumerate((UTf, SGf, ONESf)):
                        S.add('pe', lambda e, ci=ci, lt=lt, a_j=a_j: e.matmul(
                            bank(0)[:, ci * 16:(ci + 1) * 16], lhsT=lt[:], rhs=a_j, start=True, stop=True),
                            reads=['att', 'UTf', 'SGf', 'ONESf'], writes=bkeys(0))
                    S.add('act', lambda e: e.activation(out=ex48[:], in_=bank(0)[:, 0:48], func=AF.Exp),
                          reads=bkeys(0), writes=['ex48'] + bkeys(0))
                    S.add('dve', lambda e, a_j=a_j: e.tensor_tensor(
                        out=RH[:], in0=UTf[:].unsqueeze(1).broadcast_to([128, 16, 128]), in1=bc3(a_j, 128),
                        op=ALU.mult),
                        reads=['att', 'UTf'], writes=['RH', ('tmpf', 1), ('tmpf', 2)])
                    for gg in range(2):
                        S.add('pe', lambda e, gg=gg, tc0=tc0: e.matmul(
                            bank(0)[:, 128 + gg * 128:256 + gg * 128], lhsT=BT[:, gg, tc0:tc0 + 128],
                            rhs=CT[:, gg, tc0:tc0 + 128], start=True, stop=True),
                            reads=[('BT', gg), ('CT', gg)], writes=bkeys(0))
                    S.add('dve', lambda e: e.tensor_tensor(
                        out=CBm[:], in0=bank(0)[:, 128:384].rearrange("p (g c) -> p g c", g=2),
                        in1=UTf[:].unsqueeze(1).broadcast_to([128, 2, 128]), op=ALU.mult),
                        reads=bkeys(0) + ['UTf'], writes=['CBm'] + bkeys(0))
                    for gg in range(2):
                        S.add('pe', lambda e, gg=gg, tc0=tc0: e.matmul(
                            bank(5 + gg), lhsT=CT[:, gg, tc0:tc0 + 128], rhs=state_b[:, gg * 512:(gg + 1) * 512],
                            start=True, stop=True),
                            reads=[('CT', gg), 'state_b'], writes=bkeys(5 + gg))
                    S.add('dve', lambda e, j=j: e.tensor_tensor(out=dtd[:], in0=dtt[:, j, :], in1=ex48[:, 16:32],
                                                                op=ALU.mult),
                          reads=['dtt', 'ex48'], writes=['dtd'])
                    S.add('dve', lambda e, j=j: e.tensor_tensor(out=v3(Xdt[:]), in0=v3(Xtok[:, j, :]),
                                                                in1=bc3(dtt[:, j, :], 64), op=ALU.mult),
                          reads=[('Xtok', j), 'dtt'], writes=['Xdt'])
                    S.add('dve', lambda e, j=j: e.tensor_tensor(out=v3(Xdec[:]), in0=v3(Xtok[:, j, :]),
                                                                in1=bc3(dtd[:], 64), op=ALU.mult),
                          reads=[('Xtok', j), 'dtd'], writes=['Xdec'])
                    def fD(hq):
                        i = hq % 2
                        S.add('pe', lambda e, i=i, hq=hq: e.matmul(
                            bank(1 + i), lhsT=SGf[:], rhs=RH[:, 4 * hq:4 * hq + 4, :].rearrange("p h c -> p (h c)"),
                            start=True, stop=True),
                            reads=['RH', ('tmpf', 1), ('tmpf', 2), 'SGf'], writes=bkeys(1 + i))
                        S.add('act', lambda e, i=i: e.activation(out=Eb[i][:], in_=bank(1 + i), func=AF.Exp),
                              reads=bkeys(1 + i), writes=[('Eb', i)] + bkeys(1 + i))

                    def fY(hq):
                        i = hq % 2
                        gg = hq // 2
                        S.add('dve', lambda e, i=i, gg=gg: e.tensor_tensor(
                            out=MTb[i][:], in0=Eb[i][:].rearrange("p (h c) -> p h c", h=4),
                            in1=CBm[:, gg:gg + 1, :].broadcast_to([128, 4, 128]), op=ALU.mult),
                            reads=[('Eb', i), 'CBm'], writes=[('MTb', i)])
                        for hh in range(4):
                            h = 4 * hq + hh
                            S.add('pe', lambda e, i=i, hh=hh, h=h: e.matmul(
                                bank(3 + h // 8)[:, (h % 8) * 64:(h % 8 + 1) * 64], lhsT=MTb[i][:, hh, :],
                                rhs=Xdt[:, h * 64:(h + 1) * 64], start=True, stop=True),
                                reads=[('MTb', i), 'Xdt'], writes=bkeys(3 + h // 8))

                    fD(0)
                    fD(1)
                    fY(0)
                    fD(2)
                    fY(1)
                    fD(3)
                    fY(2)
                    fY(3)
                    S.add('dve', lambda e: e.tensor_tensor(out=v3(t1[:]), in0=v3(bank(5, 2)), in1=bc3(ex48[:, 0:16], 64),
                                                           op=ALU.mult),
                          reads=bkeys(5, 2) + ['ex48'], writes=['t1'] + bkeys(5, 2))
                    S.add('dve', lambda e: e.tensor_tensor(out=t1[:], in0=bank(3, 2), in1=t1[:], op=ALU.add),
                          reads=bkeys(3, 2) + ['t1'], writes=['t1'] + bkeys(3, 2))
                    for gg in range(2):
                        S.add('pe', lambda e, gg=gg, j=j: e.matmul(
                            bank(5 + gg), lhsT=Btok[:, j, gg * 128:(gg + 1) * 128], rhs=Xdec[:, gg * 512:(gg + 1) * 512],
                            start=True, stop=True),
                            reads=['Btok', 'Xdec'], writes=bkeys(5 + gg))
                    S.add('dve', lambda e, j=j: e.tensor_tensor(out=v3(Xtok[:, j, :]), in0=v3(Xtok[:, j, :]),
                                                                in1=bc3(smallbc[:, 32:48], 64), op=ALU.mult),
                          reads=[('Xtok', j), 'smallbc'], writes=[('Xtok', j)])
                    S.add('dve', lambda e, j=j: e.tensor_tensor(out=t1[:], in0=t1[:], in1=Xtok[:, j, :], op=ALU.add),
                          reads=[('Xtok', j), 't1'], writes=['t1'])
                    S.add('dve', lambda e, j=j: e.tensor_tensor(out=t1[:], in0=t1[:], in1=zs[:, j, :], op=ALU.mult),
                          reads=[('zs', j), 't1'], writes=['t1'])
                    grs = g_rstd3(t1[:], ['t1'], 3 + j)
                    next(grs)
                    S.add('dve', lambda e: e.tensor_tensor(out=v3(state[:]), in0=v3(state[:]), in1=bc3(ex48[:, 32:48], 64),
                                                           op=ALU.mult),
                          reads=['state', 'ex48'], writes=['state'])
                    S.add('dve', lambda e: e.tensor_tensor(out=state[:], in0=bank(5, 2), in1=state[:], op=ALU.add),
                          reads=bkeys(5, 2) + ['state'], writes=['state'] + bkeys(5, 2))
                    for _ in grs:
                        pass
                    S.add('act', lambda e: e.activation(out=state_b[:], in_=state[:], func=AF.Copy),
                          reads=['state'], writes=['state_b'])
                    if g == 0:
                        dbgdump('yg%d' % j, t1[:], [128, D], F32, ['t1'])
                        dbgdump('ex48_%d' % j, ex48[:], [128, 48], F32, ['ex48'])
                    norm_transpose(j, t1[:], ['t1'], 3 + j, ygnT, 'ygnT')
                apply_gain(ygnT, 'ygnT', 0)
                for k in range(8):
                    i = k % 2
                    S.add('act', lambda e, i=i, k=k: e.activation(out=sqb[i][:], in_=ysbT[:, k, gc0:gc0 + NG],
                                                                  func=AF.Square),
                          reads=[('ysbT', k, q) for q in range(9)], writes=[('sqb', i)])
                    S.add('pe', lambda e, i=i, k=k: e.matmul(bank(6)[:, 0:NG], lhsT=ones_b[:], rhs=sqb[i][:],
                                                             start=(k == 0), stop=(k == 7)),
                          reads=[('sqb', i), 'ones_b'], writes=bkeys(6))
                S.add('act', lambda e: e.activation(out=lnr[:], in_=bank(6)[:, 0:NG], func=AF.Ln, scale=1.0 / D, bias=EPS),
                      reads=bkeys(6), writes=['lnr', 'r_bc'] + bkeys(6))
                S.add('act', lambda e: e.activation(out=r_bc[:], in_=lnr[:], func=AF.Exp, scale=-0.5),
                      reads=['lnr', 'r_bc'], writes=['r_bc', 'lnr'])
                for k in range(8):
                    S.add('dve', lambda e, k=k: e.scalar_tensor_tensor(
                        out=ysbn[:, k, :], in0=ysbT[:, k, gc0:gc0 + NG], scalar=pp[:, 8 + k:9 + k], in1=r_bc[:],
                        op0=ALU.mult, op1=ALU.mult),
                        reads=[('ysbT', k, q) for q in range(9)] + ['pp', 'r_bc'], writes=[('ysbn', k)])
                for kc in range(16):
                    w, wkey = wload(BLK_OUT + kc)
                    src = ygnT if kc < 8 else ysbn
                    sk = [('ygnT', jj) for jj in range(3)] if kc < 8 else [('ysbn', kc - 8)]
                    for j in range(3):
                        for hf in range(2):
                            S.add('pe', lambda e, j=j, hf=hf, w=w, src=src, kc=kc: e.matmul(
                                bank(2 * j + hf), lhsT=src[:, kc % 8, j * 128:(j + 1) * 128],
                                rhs=w[:, hf * 512:(hf + 1) * 512], start=(kc == 0), stop=(kc == 15)),
                                reads=[wkey] + sk, writes=bkeys(2 * j + hf))
                def chain_post(j, gbc, gkey, slot0, final, tl=None):
                    tmp, tkk = (tl or tmps)[j]
                    S.add('dve', lambda e: e.tensor_tensor(
                        out=tmp, in0=bank(2 * j, 2), in1=gbc[:], op=ALU.mult),
                        reads=bkeys(2 * j, 2) + [gkey], writes=tkk + bkeys(2 * j, 2))
                    yield
                    yield from g_rstd3(bank(2 * j, 2), bkeys(2 * j, 2), slot0 + j)
                    rc, rk = rinfo(slot0 + j)
                    t = t0 + j
                    if not final:
                        S.add('dve', lambda e: e.scalar_tensor_tensor(
                            out=hbuf[:, j, :], in0=tmp, scalar=rc, in1=hbuf[:, j, :], op0=ALU.mult, op1=ALU.add),
                            reads=tkk + [rk, ('hbuf', j)], writes=[('hbuf', j)])
                        yield
                    else:
                        S.add('dve', lambda e: e.scalar_tensor_tensor(
                            out=tmp, in0=tmp, scalar=rc, in1=hbuf[:, j, :], op0=ALU.mult, op1=ALU.add),
                            reads=tkk + [rk, ('hbuf', j)], writes=tkk)
                        yield
                        if t >= 1:
                            S.add('sp', lambda e: e.dma_start(out=out_d[(t - 1) * 128:t * 128, :], in_=tmp),
                                  reads=tkk, dma=True)

                run_rr([chain_post(j, gpost_bc, 'gpost_bc', 6, False) for j in range(3)])
                if g == 0:
                    dbgdump('ygnT', ygnT[:].rearrange("p k c -> p (k c)"), [128, 8 * NG], BF16, [('ygnT', jj) for jj in range(3)])
                    dbgdump('ysbn', ysbn[:].rearrange("p k c -> p (k c)"), [128, 8 * NG], BF16, [('ysbn', k) for k in range(8)])
                    dbgdump('r_bc', r_bc[:], [128, NG], F32, ['r_bc'])
                    dbgdump('h1', hbuf[:].rearrange("p j c -> p (j c)"), [128, 3 * D], F32, hk)
                    dbgdump('state', state[:], [128, D], F32, ['state'])
                def chain_e(j):
                    yield from g_rstd3(hbuf[:, j, :], [('hbuf', j)], 9 + j)
                    yield from g_norm_transpose(j, hbuf[:, j, :], [('hbuf', j)], 9 + j, actT, 'actT', pb=7 - j)

                run_rr([chain_e(j) for j in range(3)])
                apply_gain(actT, 'actT', 84)
                fpost = []
                for fc in range(NFC):
                    wg, wgk = wload(BLK_GATE + fc)
                    wu, wuk = wload(BLK_UP + fc)
                    wg3 = wg[:].rearrange("p (k c) -> p k c", k=8)
                    wu3 = wu[:].rearrange("p (k c) -> p k c", k=8)
                    i = fc % 4
                    bg = fc % 3
                    bu = 3 + fc % 3
                    for k in range(8):
                        S.add('pe', lambda e, bg=bg, k=k, wg3=wg3: e.matmul(
                            bank(bg)[:, 0:NG], lhsT=wg3[:, k, :], rhs=actT[:, k, :], start=(k == 0), stop=(k == 7)),
                            reads=[wgk] + ak, writes=bkeys(bg))
                    for k in range(8):
                        S.add('pe', lambda e, bu=bu, k=k, wu3=wu3: e.matmul(
                            bank(bu)[:, 0:NG], lhsT=wu3[:, k, :], rhs=actT[:, k, :], start=(k == 0), stop=(k == 7)),
                            reads=[wuk] + ak, writes=bkeys(bu))
                    S.add('act', lambda e, bg=bg, i=i: e.activation(out=gcb[i][:, 2:2 + NG], in_=bank(bg)[:, 0:NG],
                                                                    func=AF.Copy),
                          reads=bkeys(bg), writes=[('cbuf', i)] + bkeys(bg))
                    S.add('pool', lambda e, i=i, fc=fc: e.tensor_copy(out=gcb[i][:, 0:2], in_=ghalo[:, fc, :]),
                          reads=['ghalo'], writes=[('cbuf', i)])
                    S.add('pool', lambda e, i=i, fc=fc: e.tensor_copy(out=ghalo[:, fc, :], in_=gcb[i][:, NG:NG + 2]),
                          reads=[('cbuf', i)], writes=['ghalo'])
                    for kk in range(3):
                        wc = ffnp[:, fc * 3 + kk:fc * 3 + kk + 1]
                        if kk == 0:
                            S.add('dve', lambda e, i=i, wc=wc: e.tensor_scalar(
                                out=cacc[i][:], in0=gcb[i][:, 0:NG], scalar1=wc, scalar2=None, op0=ALU.mult),
                                reads=[('cbuf', i), 'ffnp'], writes=[('cacc', i)])
                        else:
                            S.add('dve', lambda e, i=i, wc=wc, kk=kk: e.scalar_tensor_tensor(
                                out=cacc[i][:], in0=gcb[i][:, kk:kk + NG], scalar=wc, in1=cacc[i][:],
                                op0=ALU.mult, op1=ALU.add),
                                reads=[('cbuf', i), 'ffnp', ('cacc', i)], writes=[('cacc', i)])
                    def post_ffn(fc=fc, i=i, bu=bu):
                        S.add('act', lambda e, i=i, fc=fc: e.activation(out=glb[i][:], in_=cacc[i][:], func=AF.Gelu_apprx_tanh,
                                                                        bias=ffnp[:, 66 + fc:67 + fc]),
                              reads=[('cacc', i), 'ffnp'], writes=[('sfm', i)])
                        S.add('dve', lambda e, i=i, fc=fc, bu=bu: e.tensor_tensor(
                            out=aT[:, fc, :], in0=bank(bu)[:, 0:NG], in1=glb[i][:], op=ALU.mult),
                            reads=bkeys(bu) + [('sfm', i)],
                            writes=[('zs', jj) for jj in range(3)] + [('Xtok', jj) for jj in range(3)] + bkeys(bu))
                    if fpost:
                        fpost.pop(0)()
                    fpost.append(post_ffn)
                while fpost:
                    fpost.pop(0)()
                aTk = [('zs', jj) for jj in range(3)] + [('Xtok', jj) for jj in range(3)]
                pre = []
                if g + 1 < ngroups:
                    pre = [chain_a(g + 1, j, tmps[j][0], tmps[j][1], 7 - (j % 2)) for j in range(3)]
                    for _ in range(5):
                        for gq in pre:
                            next(gq)
                sched = {6: 0, 8: 0, 10: 1, 12: 1, 14: 2, 16: 2}
                for fc in range(NFC):
                    w, wkey = wload(BLK_DOWN + fc)
                    for j in range(3):
                        for hf in range(2):
                            S.add('pe', lambda e, j=j, hf=hf, w=w, fc=fc: e.matmul(
                                bank(2 * j + hf), lhsT=aT[:, fc, j * 128:(j + 1) * 128],
                                rhs=w[:, hf * 512:(hf + 1) * 512], start=(fc == 0), stop=(fc == NFC - 1)),
                                reads=[wkey] + aTk, writes=bkeys(2 * j + hf))
                    if pre and fc in sched:
                        next(pre[sched[fc]])
                if pre:
                    apply_gain(actT, 'actT', 76)
                ftmps = [(zs[:, j, :], [('zs', j)]) for j in range(3)]
                run_rr([chain_post(j, gfpost_bc, 'gfpost_bc', 12, True, ftmps) for j in range(3)])

            ngroups = 11 if stage >= 4 else 1
            for g in range(ngroups):
                do_group(g)

        tail = [op for q in ENGS for op in S.dma_ops[q][-NDSEM:]]
        S.add('sp', lambda e: e.nop(), extra=tail)
        es2.close()
        S.emit(nc, es)
    return nc


def _host_layout(inputs):
    f = np.float32
    w_in = np.asarray(inputs["w_in"], f)[0]
    w_out = np.asarray(inputs["w_out"], f)[0]
    w_up = np.asarray(inputs["w_up"], f)[0]
    w_down = np.asarray(inputs["w_down"], f)[0]

    def kblocks(w, c0, nb):
        sub = w[:, c0:c0 + nb * 128].reshape(8, 128, nb, 128)
        return np.ascontiguousarray(sub.transpose(2, 1, 0, 3)).reshape(nb * 128, 1024)

    blocks = [kblocks(w_in, OFF_Z, 8), kblocks(w_in, OFF_XBC, 12), kblocks(w_in, OFF_Q, 8),
              kblocks(w_in, OFF_K, 8), kblocks(w_in, OFF_V, 8),
              w_out.reshape(16 * 128, 1024),
              kblocks(w_up, 0, 22), kblocks(w_up, DFF, 22),
              w_down.reshape(22 * 128, 1024)]
    wblk = np.ascontiguousarray(np.concatenate(blocks, axis=0))
    assert wblk.shape == (NBLK * 128, 1024)
    wdt = np.ascontiguousarray(w_in[:, OFF_DT:OFF_DT + 16].reshape(8, 128, 16).transpose(1, 0, 2)).reshape(128, 128)
    vecs = np.zeros((8, D), f)
    vecs[0] = np.asarray(inputs["mix_pre_g"], f)[0]
    vecs[1] = np.asarray(inputs["mix_post_g"], f)[0]
    vecs[2] = np.asarray(inputs["ffn_pre_g"], f)[0]
    vecs[3] = np.asarray(inputs["ffn_post_g"], f)[0]
    pp = np.zeros((128, 128), f)
    pp[:, 0:8] = np.asarray(inputs["ssd_norm_g"], f)[0].reshape(8, 128).T
    pp[:, 8:16] = np.asarray(inputs["sb_norm_g"], f)[0].reshape(8, 128).T
    cw = np.asarray(inputs["ssd_conv_w"], f)[0]
    pp[:, 16:64] = cw.reshape(4, 12, 128).transpose(2, 1, 0).reshape(128, 48)
    pp[:, 64:76] = np.asarray(inputs["ssd_conv_b"], f)[0].reshape(12, 128).T
    pp[:, 76:84] = np.asarray(inputs["mix_pre_g"], f)[0].reshape(8, 128).T
    pp[:, 84:92] = np.asarray(inputs["ffn_pre_g"], f)[0].reshape(8, 128).T
    fw = np.asarray(inputs["ffn_conv_w"], f)[0]
    ffn = np.zeros((128, 128), f)
    ffn[:, 0:66] = fw.reshape(3, 22, 128).transpose(2, 1, 0).reshape(128, 66)
    ffn[:, 66:88] = np.asarray(inputs["ffn_conv_b"], f)[0].reshape(22, 128).T
    small = np.zeros((1, 64), f)
    small[0, 0:16] = np.asarray(inputs["ssd_dt_bias"], f)[0]
    small[0, 16:32] = np.asarray(inputs["ssd_a_log"], f)[0]
    small[0, 32:48] = np.asarray(inputs["ssd_d"], f)[0]
    return dict(wblk=wblk, wdt=wdt, vecs=vecs, ppd=pp, ffnpd=ffn, small=small,
                meta=np.ascontiguousarray(np.asarray(inputs["meta_tokens"], f)))


def kernel(**inputs):
    x = np.asarray(inputs["x"], np.float32)
    shared = _host_layout(inputs)
    nc = build_nc()
    in_maps = []
    for b in range(8):
        m = dict(shared)
        m["x"] = np.ascontiguousarray(x[b])
        in_maps.append(m)
    res = run_bass_kernel_spmd(nc, in_maps, core_ids=list(range(8)))
    return np.stack([r["out"] for r in res.results], axis=0)
```
